# Optimizing a Trainium2 kernel written in Bass

```python
import math
import jax
import jax.numpy as jnp
from jax import lax
import numpy as np

D_MODEL = 1024
BATCH = 8
SEQ = 4096
DEPTH = 2

ATTN_HEAD_DIM = 64
DILATION_PATTERNS = ((128, 1), (512, 4), (2048, 16))
HEADS_PER_PATTERN = 8
N_ATTN_HEADS = HEADS_PER_PATTERN * len(DILATION_PATTERNS)
ATTN_WIDTH = N_ATTN_HEADS * ATTN_HEAD_DIM
ROT_DIM = ATTN_HEAD_DIM // 4
ROPE_THETA = 500000.0
NEG_BIG = -1e30

SSD_INNER = 2 * D_MODEL
SSD_HEAD_DIM = 64
SSD_HEADS = SSD_INNER // SSD_HEAD_DIM
SSD_GROUPS = 8
SSD_STATE = 128
CONV_WIDTH = 5
CHUNK = 128
SSD_XBC = SSD_INNER + 2 * SSD_GROUPS * SSD_STATE
SSD_IN = SSD_INNER + SSD_XBC + 2 * SSD_HEADS

NORM_EPS = 1e-6

kernel_name = "hybrid_dilated_attn_bissd_encoder"


def rms_norm(x, w):
    xf = x.astype(jnp.float32)
    y = xf * lax.rsqrt(jnp.mean(xf * xf, axis=-1, keepdims=True) + NORM_EPS)
    return (y * w.astype(jnp.float32)).astype(x.dtype)


def partial_rotary(t, cos, sin):
    half = ROT_DIM // 2
    t1, t2, rest = t[..., :half], t[..., half:ROT_DIM], t[..., ROT_DIM:]
    return jnp.concatenate([t1 * cos - t2 * sin, t2 * cos + t1 * sin, rest], axis=-1)


def banded_attention(q, k, v, half):
    n, l, h, dh = q.shape
    blk = half
    nb = -(-l // blk)
    pad = nb * blk - l
    qb = jnp.pad(q, ((0, 0), (0, pad), (0, 0), (0, 0))).reshape(n, nb, blk, h, dh)

    def windows(t):
        tp = jnp.pad(t, ((0, 0), (blk, pad + blk), (0, 0), (0, 0))).reshape(n, nb + 2, blk, h, dh)
        return jnp.concatenate([tp[:, :-2], tp[:, 1:-1], tp[:, 2:]], axis=2)

    kw, vw = windows(k), windows(v)
    s = jnp.einsum('njqhd,njkhd->njhqk', qb, kw).astype(jnp.float32) / math.sqrt(dh)
    qpos = jnp.arange(nb)[:, None] * blk + jnp.arange(blk)[None, :]
    kpos = jnp.arange(nb)[:, None] * blk - blk + jnp.arange(3 * blk)[None, :]
    rel = kpos[:, None, :] - qpos[:, :, None]
    valid = (jnp.abs(rel) <= half) & (kpos[:, None, :] >= 0) & (kpos[:, None, :] < l)
    s = jnp.where(valid[None, :, None], s, NEG_BIG)
    m = jnp.max(s, axis=-1, keepdims=True)
    p = jnp.exp(s - m)
    den = jnp.sum(p, axis=-1)
    o = jnp.einsum('njhqk,njkhd->njqhd', p, vw.astype(jnp.float32))
    o = o / den.transpose(0, 1, 3, 2)[..., None]
    lse = (m[..., 0] + jnp.log(den)).transpose(0, 1, 3, 2)
    o = o.reshape(n, nb * blk, h, dh)[:, :l]
    lse = lse.reshape(n, nb * blk, h)[:, :l]
    return o, lse


def dilated_group(q, k, v, window, dilation):
    b, s, h, dh = q.shape
    l = s // dilation
    half = (window // 2) // dilation

    def to_sub(t):
        return t.reshape(b, l, dilation, h, dh).transpose(0, 2, 1, 3, 4).reshape(b * dilation, l, h, dh)

    o, lse = banded_attention(to_sub(q), to_sub(k), to_sub(v), half)
    o = o.reshape(b, dilation, l, h, dh).transpose(0, 2, 1, 3, 4).reshape(b, s, h, dh)
    lse = lse.reshape(b, dilation, l, h).transpose(0, 2, 1, 3).reshape(b, s, h)
    return o, lse


def dilated_attention_mixer(h, positions, w_in, w_out):
    b, s, _ = h.shape
    proj = h @ w_in
    q, k, v, z = jnp.split(proj, 4, axis=-1)
    q = q.reshape(b, s, N_ATTN_HEADS, ATTN_HEAD_DIM)
    k = k.reshape(b, s, N_ATTN_HEADS, ATTN_HEAD_DIM)
    v = v.reshape(b, s, N_ATTN_HEADS, ATTN_HEAD_DIM)
    inv_freq = ROPE_THETA ** (-jnp.arange(0, ROT_DIM, 2, dtype=jnp.float32) / ROT_DIM)
    ang = positions.astype(jnp.float32)[..., None] * inv_freq
    cos = jnp.cos(ang)[:, :, None, :].astype(q.dtype)
    sin = jnp.sin(ang)[:, :, None, :].astype(q.dtype)
    q = partial_rotary(q, cos, sin)
    k = partial_rotary(k, cos, sin)
    outs, lses = [], []
    for g, (window, dilation) in enumerate(DILATION_PATTERNS):
        sl = slice(g * HEADS_PER_PATTERN, (g + 1) * HEADS_PER_PATTERN)
        o, l = dilated_group(q[:, :, sl], k[:, :, sl], v[:, :, sl], window, dilation)
        outs.append(o)
        lses.append(l)
    alpha = jax.nn.softmax(jnp.stack(lses, axis=0), axis=0)
    o = jnp.concatenate([outs[g] * alpha[g][..., None] for g in range(len(DILATION_PATTERNS))], axis=2)
    o = o.reshape(b, s, ATTN_WIDTH).astype(h.dtype)
    return (o * jax.nn.silu(z)) @ w_out


def ssd_scan(x, dt, a, b_mat, c_mat):
    bsz, s, h, p = x.shape
    g, n = b_mat.shape[2], b_mat.shape[3]
    r = h // g
    c = s // CHUNK
    l = CHUNK
    xdt = (x.astype(jnp.float32) * dt[..., None]).reshape(bsz, c, l, g, r, p)
    da = (dt * a.astype(jnp.float32)).reshape(bsz, c, l, g, r)
    acs = jnp.cumsum(da, axis=2)
    bc = b_mat.astype(jnp.float32).reshape(bsz, c, l, g, n)
    cc = c_mat.astype(jnp.float32).reshape(bsz, c, l, g, n)
    acs_t = acs.transpose(0, 1, 3, 4, 2)
    seg = acs_t[..., :, None] - acs_t[..., None, :]
    lower = jnp.tril(jnp.ones((l, l), dtype=bool))
    decay = jnp.exp(jnp.where(lower, seg, -jnp.inf))
    cb = jnp.einsum('bclgn,bcsgn->bcgls', cc, bc)
    y_diag = jnp.einsum('bcgrls,bcsgrp->bclgrp', cb[:, :, :, None] * decay, xdt)
    decay_states = jnp.exp(acs[:, :, -1:] - acs)
    states = jnp.einsum('bclgn,bclgrp->bcgrpn', bc, xdt * decay_states[..., None])
    chunk_decay = jnp.exp(acs[:, :, -1])

    def step(carry, inp):
        st, dec = inp
        return carry * dec[..., None, None] + st, carry

    init = jnp.zeros((bsz, g, r, p, n), jnp.float32)
    _, prev = lax.scan(step, init, (states.transpose(1, 0, 2, 3, 4, 5), chunk_decay.transpose(1, 0, 2, 3)))
    y_off = jnp.einsum('bclgn,cbgrpn->bclgrp', cc, prev) * jnp.exp(acs)[..., None]
    return (y_diag + y_off).reshape(bsz, s, h, p)


def bi_ssd_mixer(h, w_in, conv_w, conv_b, dt_bias, a_log, d_skip, norm_w, w_out):
    b, s, _ = h.shape
    proj = h @ w_in
    z = proj[..., :SSD_INNER]
    xbc = proj[..., SSD_INNER:SSD_INNER + SSD_XBC]
    dt_raw = proj[..., SSD_INNER + SSD_XBC:].reshape(b, s, 2, SSD_HEADS)
    pad = CONV_WIDTH // 2
    xbc = lax.conv_general_dilated(
        xbc, conv_w.reshape(CONV_WIDTH, 1, SSD_XBC), window_strides=(1,),
        padding=[(pad, pad)], dimension_numbers=('NWC', 'WIO', 'NWC'),
        feature_group_count=SSD_XBC)
    xbc = jax.nn.silu(xbc + conv_b)
    xs = xbc[..., :SSD_INNER].reshape(b, s, SSD_HEADS, SSD_HEAD_DIM)
    bm = xbc[..., SSD_INNER:SSD_INNER + SSD_GROUPS * SSD_STATE].reshape(b, s, SSD_GROUPS, SSD_STATE)
    cm = xbc[..., SSD_INNER + SSD_GROUPS * SSD_STATE:].reshape(b, s, SSD_GROUPS, SSD_STATE)
    dt = jax.nn.softplus(dt_raw.astype(jnp.float32) + dt_bias.astype(jnp.float32))
    a = -jnp.exp(a_log.astype(jnp.float32))
    y_f = ssd_scan(xs, dt[:, :, 0], a[0], bm, cm)
    y_b = jnp.flip(ssd_scan(jnp.flip(xs, 1), jnp.flip(dt[:, :, 1], 1), a[1],
                            jnp.flip(bm, 1), jnp.flip(cm, 1)), 1)
    y = y_f + y_b + xs.astype(jnp.float32) * d_skip.astype(jnp.float32)[:, None]
    y = y.reshape(b, s, SSD_INNER) * jax.nn.silu(z.astype(jnp.float32))
    y = rms_norm(y, norm_w).astype(h.dtype)
    return y @ w_out


def setup_inputs(seed: int = 0) -> dict:
    key = jax.random.key(seed)
    ks = jax.random.split(key, 20)
    n_attn = (DEPTH + 1) // 2
    n_ssd = DEPTH // 2
    f32 = jnp.float32
    x = jax.random.normal(ks[0], (BATCH, SEQ, D_MODEL), f32)
    c = jax.random.normal(ks[1], (BATCH, D_MODEL), f32)
    positions = (jnp.arange(SEQ, dtype=jnp.int32)[None, :]
                 + jax.random.randint(ks[2], (BATCH, 1), 0, 1024, dtype=jnp.int32))
    norm_w = 1.0 + 0.02 * jax.random.normal(ks[3], (DEPTH, D_MODEL), f32)
    mod_w = jax.random.normal(ks[4], (DEPTH, D_MODEL, 3 * D_MODEL), f32) * D_MODEL ** -0.5
    mod_b = 0.02 * jax.random.normal(ks[5], (DEPTH, 3 * D_MODEL), f32)
    attn_w_in = jax.random.normal(ks[6], (n_attn, D_MODEL, 4 * ATTN_WIDTH), f32) * D_MODEL ** -0.5
    attn_w_out = jax.random.normal(ks[7], (n_attn, ATTN_WIDTH, D_MODEL), f32) * ATTN_WIDTH ** -0.5
    ssd_w_in = jax.random.normal(ks[8], (n_ssd, D_MODEL, SSD_IN), f32) * D_MODEL ** -0.5
    ssd_conv_w = jax.random.normal(ks[9], (n_ssd, CONV_WIDTH, SSD_XBC), f32) * CONV_WIDTH ** -0.5
    ssd_conv_b = 0.02 * jax.random.normal(ks[10], (n_ssd, SSD_XBC), f32)
    dt0 = jnp.exp(jax.random.uniform(ks[11], (n_ssd, 2, SSD_HEADS), f32)
                  * (math.log(0.1) - math.log(0.001)) + math.log(0.001))
    ssd_dt_bias = dt0 + jnp.log(-jnp.expm1(-dt0))
    ssd_a_log = jnp.log(jax.random.uniform(ks[12], (n_ssd, 2, SSD_HEADS), f32, 1.0, 16.0))
    ssd_d = 1.0 + 0.1 * jax.random.normal(ks[13], (n_ssd, SSD_HEADS), f32)
    ssd_norm_w = 1.0 + 0.02 * jax.random.normal(ks[14], (n_ssd, SSD_INNER), f32)
    ssd_w_out = jax.random.normal(ks[15], (n_ssd, SSD_INNER, D_MODEL), f32) * SSD_INNER ** -0.5
    final_norm_w = 1.0 + 0.02 * jax.random.normal(ks[16], (D_MODEL,), f32)
    return {"x": x, "c": c, "positions": positions, "norm_w": norm_w, "mod_w": mod_w,
            "mod_b": mod_b, "attn_w_in": attn_w_in, "attn_w_out": attn_w_out,
            "ssd_w_in": ssd_w_in, "ssd_conv_w": ssd_conv_w, "ssd_conv_b": ssd_conv_b,
            "ssd_dt_bias": ssd_dt_bias, "ssd_a_log": ssd_a_log, "ssd_d": ssd_d,
            "ssd_norm_w": ssd_norm_w, "ssd_w_out": ssd_w_out, "final_norm_w": final_norm_w}


def reference(x, c, positions, norm_w, mod_w, mod_b, attn_w_in, attn_w_out, ssd_w_in,
              ssd_conv_w, ssd_conv_b, ssd_dt_bias, ssd_a_log, ssd_d, ssd_norm_w, ssd_w_out,
              final_norm_w):
    cond = jax.nn.silu(c)
    for i in range(DEPTH):
        mod = cond @ mod_w[i] + mod_b[i]
        shift, scale, gate = jnp.split(mod, 3, axis=-1)
        hn = rms_norm(x, norm_w[i]) * (1.0 + scale[:, None, :]) + shift[:, None, :]
        j = i // 2
        if i % 2 == 0:
            y = dilated_attention_mixer(hn, positions, attn_w_in[j], attn_w_out[j])
        else:
            y = bi_ssd_mixer(hn, ssd_w_in[j], ssd_conv_w[j], ssd_conv_b[j], ssd_dt_bias[j],
                             ssd_a_log[j], ssd_d[j], ssd_norm_w[j], ssd_w_out[j])
        x = x + gate[:, None, :] * y.astype(x.dtype)
    return rms_norm(x, final_norm_w)
```

```python
import contextlib
import math
import numpy as np
import ml_dtypes
import concourse.bass as bass
import concourse.mybir as mybir
from concourse.bass_utils import run_bass_kernel_spmd

F32 = mybir.dt.float32
BF16 = mybir.dt.bfloat16
I32 = mybir.dt.int32
AF = mybir.ActivationFunctionType
ALU = mybir.AluOpType

D = 1024
SEQ = 4096
NT = SEQ // 128
AW = 1536
PATTERNS = ((128, 1), (512, 4), (2048, 16))
SSD_INNER = 2048
SSD_IN = 6208
EPS = 1e-6
import os
KSTEP = int(os.environ.get('KSTEP', '9'))
KPOOL = int(os.environ.get('KPOOL', '0'))


class Sched:
    NDMA = 10

    def __init__(self, nc, es, needed=None):
        self.nc = nc
        self.es = es
        self.needed = needed
        self.record = set()
        self.engs = {"pe": nc.tensor, "dve": nc.vector, "act": nc.scalar,
                     "pool": nc.gpsimd, "sp": nc.sync}
        self.sem = {}
        self.raw = {}
        self.pub = {}
        for e in self.engs:
            self.sem[e] = self.es.enter_context(nc.semaphore("s_" + e))
            self.raw[e] = 0
            self.pub[e] = 0
        self.dsem, self.dcnt, self.dnext = {}, {}, {}
        for q in ("sp",):
            self.dsem[q] = [self.es.enter_context(nc.semaphore(f"d_{q}{i}")) for i in range(self.NDMA)]
            self.dcnt[q] = [0] * self.NDMA
            self.dnext[q] = 0
        self.seen = {e: {} for e in self.engs}
        self.lastw = {}
        self.readers = {}
        self.ninstr = 0
        self.nwaits = 0

    def _wait(self, e, ev):
        sem, val, src, rid = ev
        if src == "pe" and e == "pe":
            return
        k = id(sem)
        if self.seen[e].get(k, 0) >= val:
            return
        if rid is not None:
            self.record.add(rid)
        self.engs[e].wait_ge(sem, val)
        self.seen[e][k] = val
        self.nwaits += 1

    def _deps(self, e, reads, writes):
        for k in reads:
            ev = self.lastw.get(k)
            if ev is not None:
                self._wait(e, ev)
            if k[:2] in ("ps", "pt"):
                for ev in self.readers.get(k, ()):
                    self._wait(e, ev)
        for k in writes:
            ev = self.lastw.get(k)
            if ev is not None:
                self._wait(e, ev)
            for ev in self.readers.get(k, ()):
                self._wait(e, ev)

    def _commit(self, ev, reads, writes):
        for k in reads:
            self.readers.setdefault(k, []).append(ev)
        for k in writes:
            self.lastw[k] = ev
            self.readers[k] = []

    def op(self, e, fn, reads=(), writes=()):
        self._deps(e, reads, writes)
        ins = fn()
        self.raw[e] += 1
        rid = (e, self.raw[e])
        if self.needed is None or rid in self.needed:
            self.pub[e] += 1
            ins.then_inc(self.sem[e], 1)
            ev = (self.sem[e], self.pub[e], e, rid)
        else:
            ev = (self.sem[e], self.pub[e] + 1, e, rid)
        self._commit(ev, reads, writes)
        self.ninstr += 1
        return ev

    def dma(self, q, out, in_, reads=(), writes=()):
        i = self.dnext[q]
        self.dnext[q] = (i + 1) % self.NDMA
        sem = self.dsem[q][i]
        if self.dcnt[q][i] > 0:
            self._wait(q, (sem, self.dcnt[q][i], None, None))
        self._deps(q, reads, writes)
        ins = self.engs[q].dma_start(out=out, in_=in_)
        self.dcnt[q][i] += 16
        ins.then_inc(sem, 16)
        ev = (sem, self.dcnt[q][i], None, None)
        self._commit(ev, reads, writes)
        self.ninstr += 1
        return ev

    def barrier(self):
        evs = []
        for e in self.engs:
            if self.raw[e] > 0:
                rid = (e, self.raw[e])
                pubd = self.needed is None or rid in self.needed
                evs.append((self.sem[e], self.pub[e] if pubd else self.pub[e] + 1, e, rid))
        for q in self.dsem:
            for i, sem in enumerate(self.dsem[q]):
                if self.dcnt[q][i] > 0:
                    evs.append((sem, self.dcnt[q][i], None, None))
        for e in self.engs:
            for ev in evs:
                if ev[2] == e:
                    continue
                self._wait(e, ev)
        self.lastw = {}
        self.readers = {}

    def finish(self):
        for q in self.dsem:
            for i, sem in enumerate(self.dsem[q]):
                if self.dcnt[q][i] > 0:
                    self._wait("sp", (sem, self.dcnt[q][i], None, None))


class Rot:
    def __init__(self, tiles, name):
        self.tiles = tiles
        self.name = name
        self.i = -1

    def next(self):
        self.i = (self.i + 1) % len(self.tiles)
        return self.tiles[self.i], f"{self.name}{self.i}"


def _ap(t, rowsize, p0, npart, col, dims):
    return bass.AP(t, p0 * rowsize + col, [[rowsize, npart]] + [list(d) for d in dims])


def _bf(a):
    return np.asarray(a, dtype=np.float32).astype(ml_dtypes.bfloat16)


def host_consts():
    c = {}
    p = np.arange(128)
    c["ident"] = _bf(np.eye(128))
    c["bdones"] = _bf((p[:, None] // 64) == (p[None, :] // 64))
    pm = np.zeros((128, 128), np.float32)
    for hb in (0, 64):
        for j in range(8):
            pm[hb + j + 8, hb + j] = 1.0
            pm[hb + j, hb + 8 + j] = 1.0
    c["pswap"] = _bf(pm)
    i = (p % 64)[:, None, None]
    i4 = np.arange(4)[None, :, None]
    jq = np.arange(128)[None, None, :]
    c["mask4"] = _bf(np.abs(64 * (i4 - 1) + i - jq) <= 64).reshape(128, 512)
    k = p[:, None]
    l = p[None, :]
    c["triL"] = (k <= l).astype(np.float32)
    c["triU"] = (k >= l).astype(np.float32)
    c["nmaskf"] = _bf(np.where(l <= k, 0.0, -30000.0))
    c["nmaskb"] = _bf(np.where(l >= k, 0.0, -30000.0))
    c["maskf"] = _bf(k <= l)
    c["maskb"] = _bf(k >= l)
    return c


def host_pcol(inp):
    cols = {}
    p = np.arange(128)
    inv = (500000.0 ** (-np.arange(0, 16, 2, dtype=np.float32) / 16.0)).astype(np.float32)
    pp = p % 64
    invf = np.where(pp < 16, inv[pp % 8], 0.0).astype(np.float32)
    sg = np.where(pp < 8, -1.0, np.where(pp < 16, 1.0, 0.0)).astype(np.float32)
    parts = [invf[:, None], sg[:, None]]
    parts.append(np.asarray(inp["norm_w"], np.float32).reshape(2, 8, 128).transpose(2, 0, 1).reshape(128, 16))
    parts.append(np.asarray(inp["mod_b"], np.float32).reshape(2, 24, 128).transpose(2, 0, 1).reshape(128, 48))
    parts.append(np.asarray(inp["ssd_conv_w"], np.float32).reshape(5, 32, 128).transpose(2, 0, 1).reshape(128, 160))
    parts.append(np.asarray(inp["ssd_conv_b"], np.float32).reshape(32, 128).T)
    parts.append(np.asarray(inp["ssd_norm_w"], np.float32).reshape(16, 128).T)
    return np.ascontiguousarray(np.concatenate(parts, axis=1), dtype=np.float32)


PC_INVF, PC_SG, PC_NW, PC_MODB, PC_CW, PC_CB, PC_SNW, PC_N = 0, 1, 2, 18, 66, 226, 258, 274


def build(stop_after=None):
    M = contextlib.ExitStack()
    rec = {}
    _build(M, stop_after, None, rec)
    M.close()
    M = contextlib.ExitStack()
    nc, dbg = _build(M, stop_after, rec["needed"], {})
    M.close()
    return nc, dbg


def _build(M, stop_after=None, needed=None, rec=None):
    nc = bass.Bass("TRN2", target_bir_lowering=False)
    S = Sched(nc, M, needed)
    rec["needed"] = S.record
    dbg = {}

    def din(name, shape, dt=F32):
        return nc.dram_tensor(name, list(shape), dt, kind="ExternalInput").ap()

    x_d = din("x", [SEQ, D])
    c_d = din("c", [128, 8])
    pos_d = din("pos", [1, SEQ], I32)
    modw_d = din("mod_w", [2, D, 3 * D])
    modb_d = din("mod_b", [2, 3 * D])
    awin_d = din("attn_w_in", [D, 4 * AW])
    awout_d = din("attn_w_out", [AW, D])
    swin_d = din("ssd_w_in", [D, SSD_IN])
    dtb_d = din("ssd_dt_bias", [1, 64])
    alog_d = din("ssd_a_log", [1, 64])
    sd_d = din("ssd_d", [1, 32])
    snw_d = din("ssd_norm_w", [1, SSD_INNER])
    swout_d = din("ssd_w_out", [SSD_INNER, D])
    fnw_d = din("final_norm_w", [1, D])
    pcol_d = din("pcol", [128, PC_N])
    cst = host_consts()
    cst_d = {k: din("k_" + k, list(v.shape), BF16 if v.dtype != np.float32 else F32) for k, v in cst.items()}
    out_d = nc.dram_tensor("out", [SEQ, D], F32, kind="ExternalOutput").ap()

    def scratch(name, shape, dt):
        kind = "ExternalOutput" if stop_after is not None else "Internal"
        t = nc.dram_tensor(name, list(shape), dt, kind=kind).ap()
        dbg[name] = t
        return t

    o_d = scratch("o_d", [AW, SEQ], BF16)
    x1_d = scratch("x1_d", [SEQ, D], F32)

    P = M

    uid = [0]

    def sb(stack, name, shape, dt):
        uid[0] += 1
        return stack.enter_context(nc.sbuf_tensor(f"sb{uid[0]}_{name}", list(shape), dt))

    ps = [P.enter_context(nc.psum_tensor(f"ps{i}", [128, 512], F32)) for i in range(6)]
    pt = [P.enter_context(nc.psum_tensor(f"pt{i}", [128, 8, 128], BF16)) for i in range(2)]
    psr = Rot(ps, "ps")
    ptr = Rot(pt, "pt")

    def load_cast(stg_rot, dst, dkey, src, shape):
        stg, stgk = stg_rot.next()
        n = 1
        for d_ in shape[1:]:
            n *= d_
        sv = stg[:, 0:n]
        if len(shape) == 3:
            sv = sv.rearrange("p (a b) -> p a b", a=shape[1])
        S.dma("sp", sv, src, writes=[stgk])
        S.op("act", lambda: nc.scalar.activation(out=dst, in_=sv, func=AF.Copy), reads=[stgk], writes=[dkey])

    pcol = sb(P, "pcol", [128, PC_N], F32)
    S.dma("sp", pcol[:], pcol_d, writes=["pcol"])
    K = {}
    for k, v in cst.items():
        K[k] = sb(P, "k_" + k, list(v.shape), BF16 if v.dtype != np.float32 else F32)
        S.dma("sp", K[k][:], cst_d[k], writes=["k_" + k])
    if stop_after is not None:
        junk = sb(P, "junk", [1, 16], F32)
        junki = sb(P, "junki", [1, 16], I32)
        for t_ in (awin_d, awout_d, swin_d, dtb_d, alog_d, sd_d, snw_d, swout_d, fnw_d):
            S.dma("sp", junk[:], t_[0:1, 0:16], writes=["junk"])
        S.dma("sp", junki[:], pos_d[0:1, 0:16], writes=["junki"])
    epsc = sb(P, "epsc", [128, 1], F32)
    S.op("dve", lambda: nc.vector.memset(epsc[:], EPS), writes=["epsc"])
    one11 = sb(P, "one11", [1, 1], F32)
    S.op("dve", lambda: nc.vector.memset(one11[:], 1.0), writes=["one11"])
    modA = sb(P, "modA", [128, 16], F32)
    modB = sb(P, "modB", [128, 16], F32)
    gate_b = [sb(P, f"gate_b{i}", [128, D], F32) for i in range(2)]
    nrm_junk = sb(P, "nrm_junk", [128, D], BF16)
    xn_t = [sb(P, f"xn{i}", [128, D], BF16) for i in range(2)]
    st_t = [sb(P, f"nst{i}", [128, 4], F32) for i in range(2)]
    onec = sb(P, "onec", [128, 1], F32)
    S.op("dve", lambda: nc.vector.memset(onec[:], 1.0), writes=["onec"])
    H = contextlib.ExitStack()
    M.enter_context(H)
    hnT = sb(H, "hnT", [128, 8, SEQ], BF16)

    with contextlib.ExitStack() as A:
        c_fm = sb(A, "c_fm", [128, 8], F32)
        S.dma("sp", c_fm[:], c_d, writes=["c_fm"])
        cond_f = sb(A, "cond_f", [128, 8], F32)
        cond_b = sb(A, "cond_b", [128, 8], BF16)
        condB = sb(A, "condB", [128, 8, 128], F32)
        S.op("act", lambda: nc.scalar.activation(out=cond_f[:], in_=c_fm[:], func=AF.Silu), reads=["c_fm"], writes=["cond_f"])
        if stop_after == "A0":
            dd = nc.dram_tensor("cond_dbg", [128, 8], F32, kind="ExternalOutput").ap()
            S.dma("sp", dd, cond_f[:], reads=["cond_f"])
            S.finish()
            return nc, dbg
        S.op("dve", lambda: nc.vector.tensor_copy(out=cond_b[:], in_=cond_f[:]), reads=["cond_f"], writes=["cond_b"])
        S.op("dve", lambda: nc.vector.tensor_copy(out=condB[:], in_=cond_f[:].unsqueeze(2).to_broadcast([128, 8, 128])),
             reads=["cond_f"], writes=["condB"])
        modw = sb(A, "modw", [128, 8, 3 * D], F32)
        modT = sb(A, "modT", [128, 24], F32)
        gb_bias = sb(A, "gb_bias", [128, D], F32)
        for li in range(2):
            for kc in range(8):
                S.dma("sp", modw[:, kc, :], modw_d[li, kc * 128:(kc + 1) * 128, :], writes=[f"modw{kc}"])
            pm, pmk = psr.next()
            for fc in range(24):
                for kc in range(8):
                    S.op("pe", lambda: nc.tensor.matmul(pm[:, fc:fc + 1], lhsT=modw[:, kc, fc * 128:(fc + 1) * 128],
                                                        rhs=cond_f[:, kc:kc + 1], start=(kc == 0), stop=(kc == 7)),
                         reads=[f"modw{kc}", "cond_f"], writes=[pmk])
            S.op("dve", lambda: nc.vector.tensor_tensor(out=modT[:], in0=pm[:, 0:24],
                                                        in1=pcol[:, PC_MODB + li * 24:PC_MODB + (li + 1) * 24], op=ALU.add),
                 reads=[pmk, "pcol"], writes=["modT"])
            S.op("dve", lambda: nc.vector.tensor_copy(out=modB[:, li * 8:(li + 1) * 8], in_=modT[:, 0:8]),
                 reads=["modT"], writes=["modB"])
            S.op("dve", lambda: nc.vector.scalar_tensor_tensor(out=modA[:, li * 8:(li + 1) * 8], in0=modT[:, 8:16], scalar=1.0,
                                                               in1=pcol[:, PC_NW + li * 8:PC_NW + (li + 1) * 8],
                                                               op0=ALU.add, op1=ALU.mult),
                 reads=["modT", "pcol"], writes=["modA"])
            S.dma("sp", gb_bias[:], modb_d[li:li + 1, 2 * D:3 * D].partition_broadcast(128), writes=["gb_bias"])
            for n in range(2):
                pg, pgk = psr.next()
                for kc in range(8):
                    S.op("pe", lambda: nc.tensor.matmul(pg[:], lhsT=condB[:, kc, :],
                                                        rhs=modw[:, kc, 2 * D + n * 512:2 * D + (n + 1) * 512],
                                                        start=(kc == 0), stop=(kc == 7)),
                         reads=[f"modw{kc}", "condB"], writes=[pgk])
                S.op("dve", lambda: nc.vector.tensor_tensor(out=gate_b[li][:, n * 512:(n + 1) * 512], in0=pg[:],
                                                            in1=gb_bias[:, n * 512:(n + 1) * 512], op=ALU.add),
                     reads=[pgk, "gb_bias"], writes=[f"gate_b{li}"])
        S.barrier()

    if stop_after == "A":
        for nm, t_, shp in (("modA_dbg", modA, [128, 16]), ("modB_dbg", modB, [128, 16]), ("gate0_dbg", gate_b[0], [128, D]), ("gate1_dbg", gate_b[1], [128, D])):
            dd = nc.dram_tensor(nm, shp, F32, kind="ExternalOutput").ap()
            S.dma("sp", dd, t_[:])
        S.finish()
        return nc, dbg

    xnr = Rot(xn_t, "xn")
    str_ = Rot(st_t, "nst")

    def norm_tile(xt, xk, tt, li):
        st, stk = str_.next()
        xn, xnk = xnr.next()
        S.op("act", lambda: nc.scalar.activation(out=nrm_junk[:], in_=xt[:], func=AF.Square, accum_out=st[:, 0:1]),
             reads=[xk], writes=["nrm_junk", stk])
        S.op("act", lambda: nc.scalar.activation(out=st[:, 1:2], in_=st[:, 0:1], func=AF.Sqrt, bias=epsc[:, 0:1], scale=1.0 / D),
             reads=[stk, "epsc"], writes=[stk])
        S.op("dve", lambda: nc.vector.reciprocal(out=st[:, 2:3], in_=st[:, 1:2]), reads=[stk], writes=[stk])
        S.op("dve", lambda: nc.vector.tensor_scalar(out=xn[:], in0=xt[:], scalar1=st[:, 2:3], scalar2=None, op0=ALU.mult),
             reads=[xk, stk], writes=[xnk])
        tp, tpk = ptr.next()
        for kc in range(8):
            S.op("pe", lambda: nc.tensor.transpose(tp[:, kc, :], xn[:, kc * 128:(kc + 1) * 128], K["ident"][:]),
                 reads=[xnk, "k_ident"], writes=[tpk])
        for kc in range(8):
            col = li * 8 + kc
            if True:
                S.op("dve", lambda: nc.vector.tensor_scalar(out=hnT[:, kc, tt * 128:(tt + 1) * 128], in0=tp[:, kc, :],
                                                            scalar1=modA[:, col:col + 1], scalar2=modB[:, col:col + 1],
                                                            op0=ALU.mult, op1=ALU.add),
                     reads=[tpk, "modA", "modB"], writes=[f"hnT{kc}"])
            else:
                S.op("act", lambda: nc.scalar.activation(out=hnT[:, kc, tt * 128:(tt + 1) * 128], in_=tp[:, kc, :],
                                                         func=AF.Identity, scale=modA[:, col:col + 1], bias=modB[:, col:col + 1]),
                     reads=[tpk, "modA", "modB"], writes=[f"hnT{kc}"])

    HN = [f"hnT{kc}" for kc in range(8)]

    with contextlib.ExitStack() as B:
        xts = [sb(B, f"xt{i}", [128, D], F32) for i in range(3)]
        xr = Rot(xts, "xt")
        for tt in range(NT):
            xt, xk = xr.next()
            S.dma("sp", xt[:], x_d[tt * 128:(tt + 1) * 128, :], writes=[xk])
            norm_tile(xt, xk, tt, 0)
        S.barrier()

    if stop_after == "B":
        hn_dbg = nc.dram_tensor("hn_dbg", [128, 8, SEQ], BF16, kind="ExternalOutput").ap()
        dbg["hn_dbg"] = hn_dbg
        for kc in range(8):
            S.dma("sp", hn_dbg[:, kc, :], hnT[:, kc, :], reads=HN)
        S.finish()
        return nc, dbg

    with contextlib.ExitStack() as C:
        Ct = sb(C, "Ct", [128, SEQ], BF16)
        St = sb(C, "St", [128, SEQ], BF16)
        with contextlib.ExitStack() as R:
            RB = 1024
            posi = sb(R, "posi", [128, RB], I32)
            posf = sb(R, "posf", [128, RB], F32)
            ang = sb(R, "ang", [128, RB], F32)
            ki = sb(R, "ki", [128, RB], I32)
            kf = sb(R, "kf", [128, RB], F32)
            for cb in range(SEQ // RB):
                csl = slice(cb * RB, (cb + 1) * RB)
                S.dma("sp", posi[:], pos_d[:, csl].partition_broadcast(128), writes=["posi"])
                S.op("dve", lambda: nc.vector.tensor_copy(out=posf[:], in_=posi[:]), reads=["posi"], writes=["posf"])
                for which, phase in ((0, 0.0), (1, math.pi / 2)):
                    S.op("dve", lambda: nc.vector.tensor_scalar(out=ang[:], in0=posf[:], scalar1=pcol[:, PC_INVF:PC_INVF + 1],
                                                                scalar2=phase, op0=ALU.mult, op1=ALU.add),
                         reads=["posf", "pcol"], writes=["ang"])
                    S.op("dve", lambda: nc.vector.tensor_scalar(out=ki[:], in0=ang[:], scalar1=1.0 / (2 * math.pi), scalar2=None,
                                                                op0=ALU.mult), reads=["ang"], writes=["ki"])
                    S.op("dve", lambda: nc.vector.tensor_copy(out=kf[:], in_=ki[:]), reads=["ki"], writes=["kf"])
                    S.op("dve", lambda: nc.vector.scalar_tensor_tensor(out=ang[:], in0=kf[:], scalar=-2 * math.pi, in1=ang[:],
                                                                       op0=ALU.mult, op1=ALU.add),
                         reads=["kf", "ang"], writes=["ang"])
                    if which == 0:
                        S.op("act", lambda: nc.scalar.activation(out=kf[:], in_=ang[:], func=AF.Sin), reads=["ang"], writes=["kf"])
                        S.op("dve", lambda: nc.vector.tensor_scalar(out=St[:, csl], in0=kf[:], scalar1=pcol[:, PC_SG:PC_SG + 1],
                                                                    scalar2=None, op0=ALU.mult), reads=["kf", "pcol"], writes=["St"])
                    else:
                        S.op("act", lambda: nc.scalar.activation(out=Ct[:, csl], in_=ang[:], func=AF.Sin), reads=["ang"], writes=["Ct"])
            S.barrier()
        def dump(nm, t_, shape, dt):
            dd = nc.dram_tensor(nm, shape, dt, kind="ExternalOutput").ap()
            if len(shape) == 2 and shape[1] * (2 if dt == BF16 else 4) > 32768:
                h = shape[1] // 2
                S.dma("sp", dd[:, 0:h], t_[:, 0:h])
                S.dma("sp", dd[:, h:], t_[:, h:])
            else:
                S.dma("sp", dd, t_[:])

        if stop_after == "Crope":
            dump("Ct_dbg", Ct, [128, SEQ], BF16)
            dump("St_dbg", St, [128, SEQ], BF16)
            S.finish()
            return nc, dbg
        Dacc = sb(C, "Dacc", [128, SEQ], F32)
        Rinv = Dacc
        tmpfs = [sb(C, f"tmpf{i}", [128, 512], F32) for i in range(1)]
        tmpfr = Rot(tmpfs, "tmpf")
        Ug = [sb(C, f"Ug{i}", [128, SEQ], BF16) for i in range(3)]
        QT = sb(C, "QT", [128, SEQ], BF16)
        Kz = sb(C, "Kz", [128, 2 * SEQ], BF16)
        Vbd = sb(C, "Vbd", [128, 64, 128], BF16)
        S.op("dve", lambda: nc.vector.memset(Kz[:], 0.0), writes=["Kz"])
        S.op("dve", lambda: nc.vector.memset(Vbd[:], 0.0), writes=["Vbd"])
        wts = [sb(C, f"wt{i}", [128, 8, 128], BF16) for i in range(3)]
        wstg = [sb(C, f"wstg{i}", [128, 1024], F32) for i in range(2)]
        wstgr = Rot(wstg, "wstg")
        wr = Rot(wts, "wt")
        stA = [sb(C, f"stA{i}", [128, 512], BF16) for i in range(2)]
        stAr = Rot(stA, "stA")
        st1 = [sb(C, f"st1{i}", [128, 512], BF16) for i in range(2)]
        st1r = Rot(st1, "st1")
        st2 = [sb(C, f"st2{i}", [128, 512], BF16) for i in range(2)]
        st2r = Rot(st2, "st2")
        Et = [sb(C, f"Et{i}", [128, 512], BF16) for i in range(2)]
        Etr = Rot(Et, "Et")
        PTt = [sb(C, f"PT{i}", [128, 512], BF16) for i in range(2)]
        PTr = Rot(PTt, "PT")
        Ost = [sb(C, f"Ost{i}", [128, 512], BF16) for i in range(2)]
        Ostr = Rot(Ost, "Ost")
        awv = awin_d.rearrange("(kc p) n -> p kc n", p=128)

        def load_w(col0):
            w, wk = wr.next()
            load_cast(wstgr, w[:], wk, awv[:, :, col0:col0 + 128], [128, 8, 128])
            return w, wk

        def proj_tile(w, wk, tn):
            pj, pjk = psr.next()
            for kc in range(8):
                S.op("pe", lambda: nc.tensor.matmul(pj[:], lhsT=w[:, kc, :], rhs=hnT[:, kc, tn * 512:(tn + 1) * 512],
                                                    start=(kc == 0), stop=(kc == 7)),
                     reads=[wk, f"hnT{kc}"], writes=[pjk])
            return pj, pjk

        jobs = []
        for s in range(4):
            for g in range(3):
                jobs += [(s, g, 0), (s, g, 1), (s, g, 2)]
            for g in range(3):
                jobs.append((s, g, 3))
        wq = {}
        PF = 2

        def prefetch(i):
            if i < len(jobs) and i not in wq:
                s_, g_, wh_ = jobs[i]
                wq[i] = load_w(wh_ * AW + (g_ * 4 + s_) * 128)

        for i in range(PF):
            prefetch(i)
        ji = [0]

        def next_w():
            w, wk = wq.pop(ji[0])
            prefetch(ji[0] + PF)
            ji[0] += 1
            return w, wk

        for s in range(4):
            for g in range(3):
                win, dil = PATTERNS[g]
                L = SEQ // dil
                j = g * 4 + s
                ntile = L // 64
                for which in range(2):
                    w, wk = next_w()

                    def post(tn, a, ak):
                        p2, p2k = psr.next()
                        S.op("pe", lambda: nc.tensor.matmul(p2[:], lhsT=K["pswap"][:], rhs=a[:], start=True, stop=True),
                             reads=["k_pswap", ak], writes=[p2k])
                        t1, t1k = st1r.next()
                        t2, t2k = st2r.next()
                        S.op("dve", lambda: nc.vector.tensor_tensor(out=t1[:], in0=a[:], in1=Ct[:, tn * 512:(tn + 1) * 512], op=ALU.mult),
                             reads=[ak, "Ct"], writes=[t1k])
                        S.op("dve", lambda: nc.vector.tensor_tensor(out=t2[:], in0=p2[:], in1=St[:, tn * 512:(tn + 1) * 512], op=ALU.mult),
                             reads=[p2k, "St"], writes=[t2k])
                        ni = 512 // dil
                        i0_ = tn * ni
                        if which == 0:
                            dst = QT[:].rearrange("p (r i) -> p i r", r=dil)[:, i0_:i0_ + ni, :]
                            S.op("dve", lambda: nc.vector.tensor_tensor(out=dst, in0=t1[:].rearrange("p (i r) -> p i r", r=dil),
                                                                        in1=t2[:].rearrange("p (i r) -> p i r", r=dil), op=ALU.add),
                                 reads=[t1k, t2k], writes=["QT"])
                        else:
                            bc_ = min(64, ni)
                            ac_ = max(1, ni // 64)
                            a0, b0 = divmod(i0_, 64)
                            for hh in range(2):
                                dst = bass.AP(Kz, hh * 64 * (2 * SEQ) + a0 * 128 + hh * 64 + b0,
                                              [[2 * SEQ, 64], [128, ac_], [1, bc_], [2 * L, dil]])
                                eng = "dve"
                                fn = nc.vector.tensor_tensor
                                S.op(eng, lambda: fn(out=dst, in0=t1[hh * 64:(hh + 1) * 64, :].rearrange("p (a b r) -> p a b r", a=ac_, b=bc_),
                                                     in1=t2[hh * 64:(hh + 1) * 64, :].rearrange("p (a b r) -> p a b r", a=ac_, b=bc_), op=ALU.add),
                                     reads=[t1k, t2k], writes=["Kz"])

                    pend = None
                    for tn in range(8):
                        pj, pjk = proj_tile(w, wk, tn)
                        a, ak = stAr.next()
                        S.op("act", lambda: nc.scalar.activation(out=a[:], in_=pj[:], func=AF.Copy), reads=[pjk], writes=[ak])
                        if pend is not None:
                            post(*pend)
                        pend = (tn, a, ak)
                    post(*pend)
                w, wk = next_w()
                VT, VTk = Ug[g], f"Ug{g}"
                for tn in range(8):
                    pj, pjk = proj_tile(w, wk, tn)
                    S.op("act", lambda: nc.scalar.activation(out=VT[:, tn * 512:(tn + 1) * 512], in_=pj[:], func=AF.Copy),
                         reads=[pjk], writes=[VTk])
                for t0 in range(0, 64, 4):
                    tp, tpk = psr.next()
                    for jj in range(4):
                        t = t0 + jj
                        r, cc = divmod(t, ntile)
                        src = _ap(VT, SEQ, 0, 128, r + dil * 64 * cc, [[dil, 64]])
                        for hh in range(2):
                            S.op("pe", lambda: nc.tensor.matmul(tp[hh * 64:(hh + 1) * 64, jj * 128:(jj + 1) * 128], lhsT=src, rhs=K["ident"][:],
                                                                start=True, stop=True),
                                 reads=[VTk, "k_ident"], writes=[tpk])
                    tpv = tp[:].rearrange("p (j f) -> p j f", j=4)
                    S.op("dve", lambda: nc.vector.tensor_copy(out=Vbd[0:64, t0:t0 + 4, 0:64], in_=tpv[0:64, :, 0:64]),
                         reads=[tpk], writes=["Vbd"])
                    S.op("act", lambda: nc.scalar.activation(out=Vbd[64:128, t0:t0 + 4, 64:128], in_=tpv[64:128, :, 64:128], func=AF.Copy),
                         reads=[tpk], writes=["Vbd"])
                nb = L // 128
                blocks = [(r, b) for r in range(dil) for b in range(nb)]

                def stage1(r, b):
                    lo = 1 if b == 0 else 0
                    hi = 3 if b == nb - 1 else 4
                    s4, s4k = psr.next()
                    qcol = r * L + 128 * b
                    for i4 in range(lo, hi):
                        cc = 2 * b - 1 + i4
                        S.op("pe", lambda: nc.tensor.matmul(s4[:, i4 * 128:(i4 + 1) * 128],
                                                            lhsT=Kz[:, (r * ntile + cc) * 128:(r * ntile + cc + 1) * 128],
                                                            rhs=QT[:, qcol:qcol + 128], start=True, stop=True),
                             reads=["Kz", "QT"], writes=[s4k])
                    e, ek = Etr.next()
                    S.op("act", lambda: nc.scalar.activation(out=e[:, lo * 128:hi * 128], in_=s4[:, lo * 128:hi * 128],
                                                             func=AF.Exp, scale=0.125), reads=[s4k], writes=[ek])
                    pt_, ptk = PTr.next()
                    S.op("dve", lambda: nc.vector.tensor_tensor(out=pt_[:, lo * 128:hi * 128], in0=e[:, lo * 128:hi * 128],
                                                                in1=K["mask4"][:, lo * 128:hi * 128], op=ALU.mult),
                         reads=[ek, "k_mask4"], writes=[ptk])
                    return (r, b, lo, hi, pt_, ptk)

                def stage2(r, b, lo, hi, pt_, ptk):
                    ud, udk = psr.next()
                    for i4 in range(lo, hi):
                        cc = 2 * b - 1 + i4
                        S.op("pe", lambda: nc.tensor.matmul(ud[:, 0:128], lhsT=Vbd[:, r * ntile + cc, :], rhs=pt_[:, i4 * 128:(i4 + 1) * 128],
                                                            start=(i4 == lo), stop=(i4 == hi - 1)),
                             reads=["Vbd", ptk], writes=[udk])
                    for i4 in range(lo, hi):
                        S.op("pe", lambda: nc.tensor.matmul(ud[:, 128:256], lhsT=K["bdones"][:], rhs=pt_[:, i4 * 128:(i4 + 1) * 128],
                                                            start=(i4 == lo), stop=(i4 == hi - 1)),
                             reads=["k_bdones", ptk], writes=[udk])
                    tok0 = r + dil * 128 * b
                    udst = _ap(Ug[g], SEQ, 0, 128, tok0, [[dil, 128]])
                    S.op("act", lambda: nc.scalar.activation(out=udst, in_=ud[:, 0:128], func=AF.Copy), reads=[udk], writes=[f"Ug{g}"])
                    ddst = _ap(Dacc, SEQ, 0, 128, tok0, [[dil, 128]])
                    if g == 0:
                        S.op("dve", lambda: nc.vector.tensor_copy(out=ddst, in_=ud[:, 128:256]), reads=[udk], writes=["Dacc"])
                    else:
                        S.op("dve", lambda: nc.vector.tensor_tensor(out=ddst, in0=ud[:, 128:256], in1=ddst, op=ALU.add),
                             reads=[udk, "Dacc"], writes=["Dacc"])

                pend = stage1(*blocks[0])
                for bi in range(len(blocks)):
                    nxt = stage1(*blocks[bi + 1]) if bi + 1 < len(blocks) else None
                    stage2(*pend)
                    pend = nxt
            S.op("dve", lambda: nc.vector.reciprocal(out=Rinv[:], in_=Dacc[:]), reads=["Dacc"], writes=["Dacc"])
            for g in range(3):
                j = g * 4 + s
                w, wk = next_w()
                for tn in range(8):
                    pj, pjk = proj_tile(w, wk, tn)
                    a, ak = stAr.next()
                    S.op("act", lambda: nc.scalar.activation(out=a[:], in_=pj[:], func=AF.Silu), reads=[pjk], writes=[ak])
                    sl = slice(tn * 512, (tn + 1) * 512)
                    tf, tfk = tmpfr.next()
                    S.op("dve", lambda: nc.vector.tensor_tensor(out=tf[:], in0=Ug[g][:, sl], in1=Rinv[:, sl], op=ALU.mult),
                         reads=[f"Ug{g}", "Dacc"], writes=[tfk])
                    o, ok = Ostr.next()
                    S.op("dve", lambda: nc.vector.tensor_tensor(out=o[:], in0=tf[:], in1=a[:], op=ALU.mult),
                         reads=[tfk, ak], writes=[ok])
                    S.dma("sp", o_d[j * 128:(j + 1) * 128, sl], o[:], reads=[ok], writes=["o_d"])
        S.barrier()

    if stop_after == "C":
        S.finish()
        return nc, dbg

    with contextlib.ExitStack() as Dx:
        wout = sb(Dx, "wout", [128, 12, D], BF16)
        wstg = [sb(Dx, f"wstgD{i}", [128, 1024], F32) for i in range(2)]
        wstgr = Rot(wstg, "wstgD")
        for j in range(12):
            load_cast(wstgr, wout[:, j, :], "wout", awout_d[j * 128:(j + 1) * 128, :], [128, 1024])
        OTs = [sb(Dx, f"OT{i}", [128, 12, 512], BF16) for i in range(2)]
        OTr = Rot(OTs, "OT")
        xts = [sb(Dx, f"xt{i}", [128, D], F32) for i in range(2)]
        xr = Rot(xts, "xt")
        ytm = [sb(Dx, f"ytm{i}", [128, D], F32) for i in range(2)]
        ytr = Rot(ytm, "ytm")
        x1s = [sb(Dx, f"x1t{i}", [128, D], F32) for i in range(2)]
        x1r = Rot(x1s, "x1t")
        o_v = o_d.rearrange("(j p) t -> p j t", p=128)
        for tb in range(8):
            ot, otk = OTr.next()
            S.dma("sp", ot[:], o_v[:, :, tb * 512:(tb + 1) * 512], reads=["o_d"], writes=[otk])
            for t4 in range(4):
                tt = tb * 4 + t4
                xt, xk = xr.next()
                S.dma("sp", xt[:], x_d[tt * 128:(tt + 1) * 128, :], writes=[xk])
                yt, ytk = ytr.next()
                for n in range(2):
                    py, pyk = psr.next()
                    for j in range(12):
                        S.op("pe", lambda: nc.tensor.matmul(py[:], lhsT=ot[:, j, t4 * 128:(t4 + 1) * 128], rhs=wout[:, j, n * 512:(n + 1) * 512],
                                                            start=(j == 0), stop=(j == 11)),
                             reads=[otk, "wout"], writes=[pyk])
                    S.op("dve", lambda: nc.vector.tensor_tensor(out=yt[:, n * 512:(n + 1) * 512], in0=py[:], in1=gate_b[0][:, n * 512:(n + 1) * 512],
                                                                op=ALU.mult), reads=[pyk, "gate_b0"], writes=[ytk])
                x1, x1k = x1r.next()
                S.op("dve", lambda: nc.vector.tensor_tensor(out=x1[:], in0=yt[:], in1=xt[:], op=ALU.add), reads=[ytk, xk], writes=[x1k])
                S.dma("sp", x1_d[tt * 128:(tt + 1) * 128, :], x1[:], reads=[x1k], writes=["x1_d"])
                norm_tile(x1, x1k, tt, 1)
        S.barrier()

    if stop_after == "D":
        hn_dbg = nc.dram_tensor("hn_dbg", [128, 8, SEQ], BF16, kind="ExternalOutput").ap()
        dbg["hn_dbg"] = hn_dbg
        for kc in range(8):
            S.dma("sp", hn_dbg[:, kc, :], hnT[:, kc, :], reads=HN)
        S.finish()
        return nc, dbg

    sz_d = scratch("sz_d", [SEQ, SSD_INNER], BF16)
    xs_d = scratch("xs_d", [SEQ, SSD_INNER], BF16)
    bt_d = scratch("bt_d", [SEQ, 1024], BF16)
    bT_d = scratch("bT_d", [1024, SEQ], BF16)
    cT_d = scratch("cT_d", [1024, SEQ], BF16)
    pb_d = scratch("pb_d", [NT, 128, SSD_INNER], BF16)
    dt_d = scratch("dt_d", [NT, 128, 64], F32)
    swv = swin_d.rearrange("(kc p) n -> p kc n", p=128)

    with contextlib.ExitStack() as E:
        dtb_b = sb(E, "dtb_b", [128, 64], F32)
        S.dma("sp", dtb_b[:], dtb_d.partition_broadcast(128), writes=["dtb_b"])
        with contextlib.ExitStack() as E1:
            wz = sb(E1, "wz", [128, 8, SSD_INNER], BF16)
            wdt = sb(E1, "wdt", [128, 8, 64], BF16)
            wstg1 = [sb(E1, f"wstgE1{i}", [128, 1024], F32) for i in range(2)]
            wstg1r = Rot(wstg1, "wstgE1")
            for kc in range(8):
                for hh in range(2):
                    load_cast(wstg1r, wz[:, kc, hh * 1024:(hh + 1) * 1024], f"wz{kc}", swin_d[kc * 128:(kc + 1) * 128, hh * 1024:(hh + 1) * 1024], [128, 1024])
            load_cast(wstg1r, wdt[:], "wdt", swv[:, :, 6144:6208], [128, 8, 64])
            szts = [sb(E1, f"szt{i}", [128, SSD_INNER], BF16) for i in range(2)]
            sztr = Rot(szts, "szt")
            dtt = [sb(E1, f"dtt{i}", [128, 64], F32) for i in range(2)]
            dttr = Rot(dtt, "dtt")
            dto = [sb(E1, f"dto{i}", [128, 64], F32) for i in range(2)]
            dtor = Rot(dto, "dto")
            for tt in range(NT):
                szt, sztk = sztr.next()
                for n in range(4):
                    pz, pzk = psr.next()
                    for kc in range(8):
                        S.op("pe", lambda: nc.tensor.matmul(pz[:], lhsT=hnT[:, kc, tt * 128:(tt + 1) * 128], rhs=wz[:, kc, n * 512:(n + 1) * 512],
                                                            start=(kc == 0), stop=(kc == 7)), reads=[f"hnT{kc}", f"wz{kc}"], writes=[pzk])
                    S.op("act", lambda: nc.scalar.activation(out=szt[:, n * 512:(n + 1) * 512], in_=pz[:], func=AF.Silu), reads=[pzk], writes=[sztk])
                S.dma("sp", sz_d[tt * 128:(tt + 1) * 128, :], szt[:], reads=[sztk], writes=["sz_d"])
                pd, pdk = psr.next()
                for kc in range(8):
                    S.op("pe", lambda: nc.tensor.matmul(pd[:, 0:64], lhsT=hnT[:, kc, tt * 128:(tt + 1) * 128], rhs=wdt[:, kc, :],
                                                        start=(kc == 0), stop=(kc == 7)), reads=[f"hnT{kc}", "wdt"], writes=[pdk])
                dtmp, dtk = dttr.next()
                S.op("dve", lambda: nc.vector.tensor_tensor(out=dtmp[:], in0=pd[:, 0:64], in1=dtb_b[:], op=ALU.add), reads=[pdk, "dtb_b"], writes=[dtk])
                S.op("act", lambda: nc.scalar.activation(out=dtmp[:], in_=dtmp[:], func=AF.Exp), reads=[dtk], writes=[dtk])
                dto_, dtok = dtor.next()
                S.op("act", lambda: nc.scalar.activation(out=dto_[:], in_=dtmp[:], func=AF.Ln, bias=onec[:, 0:1], scale=1.0),
                     reads=[dtk, "onec"], writes=[dtok])
                S.dma("sp", dt_d[tt], dto_[:], reads=[dtok], writes=["dt_d"])
            S.barrier()
        raws = [sb(E, f"raw{i}", [128, SEQ + 4], F32) for i in range(2)]
        for i in range(2):
            S.op("dve", lambda: nc.vector.memset(raws[i][:], 0.0), writes=[f"raw{i}"])
        rawr = Rot(raws, "raw")
        accs = [sb(E, f"acc{i}", [128, 1024], F32) for i in range(3)]
        accr = Rot(accs, "acc")
        xos = [sb(E, f"xo{i}", [128, SEQ], BF16) for i in range(4)]
        xor_ = Rot(xos, "xo")
        xtoks = [sb(E, f"xtok{i}", [128, 8, 128], BF16) for i in range(2)]
        xtokr = Rot(xtoks, "xtok")
        wxs = [sb(E, f"wx{i}", [128, 8, 128], BF16) for i in range(3)]
        wstgE = [sb(E, f"wstgE{i}", [128, 1024], F32) for i in range(2)]
        wstgEr = Rot(wstgE, "wstgE")
        wxr = Rot(wxs, "wx")
        xs_v = xs_d.rearrange("(t p) c -> p t c", p=128)
        bt_v = bt_d.rearrange("(t p) c -> p t c", p=128)
        def load_wx(cc_):
            wx_, wxk_ = wxr.next()
            load_cast(wstgEr, wx_[:], wxk_, swv[:, :, SSD_INNER + cc_ * 128:SSD_INNER + (cc_ + 1) * 128], [128, 8, 128])
            return wx_, wxk_

        pend_tok = []

        def to_tok(cc_, xo_, xok_):
            for tb in range(4):
                tp, tpk = ptr.next()
                for jj in range(8):
                    tt = tb * 8 + jj
                    S.op("pe", lambda: nc.tensor.transpose(tp[:, jj, :], xo_[:, tt * 128:(tt + 1) * 128], K["ident"][:]),
                         reads=[xok_, "k_ident"], writes=[tpk])
                xtok, xtokk = xtokr.next()
                S.op("act", lambda: nc.scalar.activation(out=xtok[:], in_=tp[:], func=AF.Copy), reads=[tpk], writes=[xtokk])
                if cc_ < 16:
                    S.dma("sp", xs_v[:, tb * 8:(tb + 1) * 8, cc_ * 128:(cc_ + 1) * 128], xtok[:], reads=[xtokk], writes=["xs_d"])
                else:
                    S.dma("sp", bt_v[:, tb * 8:(tb + 1) * 8, (cc_ - 16) * 128:(cc_ - 15) * 128], xtok[:], reads=[xtokk], writes=["bt_d"])

        wxq = {0: load_wx(0), 1: load_wx(1)}
        def e_proj(cc):
            wx, wxk = wxq.pop(cc)
            if cc + 2 < 32:
                wxq[cc + 2] = load_wx(cc + 2)
            raw, rawk = rawr.next()
            for tn in range(8):
                pj, pjk = psr.next()
                for kc in range(8):
                    S.op("pe", lambda: nc.tensor.matmul(pj[:], lhsT=wx[:, kc, :], rhs=hnT[:, kc, tn * 512:(tn + 1) * 512],
                                                        start=(kc == 0), stop=(kc == 7)), reads=[wxk, f"hnT{kc}"], writes=[pjk])
                S.op("act", lambda: nc.scalar.activation(out=raw[:, 2 + tn * 512:2 + (tn + 1) * 512], in_=pj[:], func=AF.Copy),
                     reads=[pjk], writes=[rawk])
            return raw, rawk

        def e_conv(cc, raw, rawk):
            xo, xok = xor_.next()
            for cb in range(4):
                acc, acck = accr.next()
                c0 = cb * 1024
                S.op("dve", lambda: nc.vector.tensor_scalar(out=acc[:], in0=raw[:, c0:c0 + 1024], scalar1=pcol[:, PC_CW + cc:PC_CW + cc + 1],
                                                            scalar2=None, op0=ALU.mult), reads=[rawk, "pcol"], writes=[acck])
                for k in range(1, 5):
                    S.op("dve", lambda: nc.vector.scalar_tensor_tensor(out=acc[:], in0=raw[:, c0 + k:c0 + k + 1024],
                                                                       scalar=pcol[:, PC_CW + k * 32 + cc:PC_CW + k * 32 + cc + 1],
                                                                       in1=acc[:], op0=ALU.mult, op1=ALU.add),
                         reads=[rawk, "pcol", acck], writes=[acck])
                S.op("act", lambda: nc.scalar.activation(out=xo[:, c0:c0 + 1024], in_=acc[:], func=AF.Silu,
                                                         bias=pcol[:, PC_CB + cc:PC_CB + cc + 1], scale=1.0), reads=[acck, "pcol"], writes=[xok])
            if cc >= 24:
                for hh in range(2):
                    S.dma("sp", cT_d[(cc - 24) * 128:(cc - 23) * 128, hh * 2048:(hh + 1) * 2048], xo[:, hh * 2048:(hh + 1) * 2048], reads=[xok], writes=["cT_d"])
            elif cc >= 16:
                for hh in range(2):
                    S.dma("sp", bT_d[(cc - 16) * 128:(cc - 15) * 128, hh * 2048:(hh + 1) * 2048], xo[:, hh * 2048:(hh + 1) * 2048], reads=[xok], writes=["bT_d"])
            if cc < 24:
                pend_tok.append((cc, xo, xok))

        prev_raw = None
        for cc in range(32):
            cur = e_proj(cc)
            if prev_raw is not None:
                e_conv(cc - 1, *prev_raw)
            prev_raw = cur
            while len(pend_tok) > 2 or (pend_tok and cc >= 24):
                to_tok(*pend_tok.pop(0))
        e_conv(31, *prev_raw)
        while pend_tok:
            to_tok(*pend_tok.pop(0))
        S.barrier()
    H.close()

    with contextlib.ExitStack() as G:
        arow = sb(G, "arow", [128, 64], F32)
        S.dma("sp", arow[:], alog_d.partition_broadcast(128), writes=["arow"])
        S.op("act", lambda: nc.scalar.activation(out=arow[:], in_=arow[:], func=AF.Exp), reads=["arow"], writes=["arow"])
        S.op("dve", lambda: nc.vector.tensor_scalar(out=arow[:], in0=arow[:], scalar1=-1.0, scalar2=None, op0=ALU.mult), reads=["arow"], writes=["arow"])
        drow = sb(G, "drow", [128, 32], F32)
        S.dma("sp", drow[:], sd_d.partition_broadcast(128), writes=["drow"])
        Dident = sb(G, "Dident", [128, 32, 128], BF16)
        for h in range(32):
            S.op("dve", lambda: nc.vector.tensor_scalar(out=Dident[:, h, :], in0=K["ident"][:], scalar1=drow[:, h:h + 1], scalar2=None, op0=ALU.mult),
                 reads=["k_ident", "drow"], writes=["Dident"])
        fnw_b = sb(G, "fnw_b", [128, D], F32)
        S.dma("sp", fnw_b[:], fnw_d.partition_broadcast(128), writes=["fnw_b"])
        ones128 = sb(G, "ones128", [128, 128], F32)
        S.op("dve", lambda: nc.vector.memset(ones128[:], 1.0), writes=["ones128"])
        ones1 = sb(G, "ones1", [1, 128], BF16)
        S.op("dve", lambda: nc.vector.memset(ones1[:], 1.0), writes=["ones1"])
        wout1 = sb(G, "wout1", [128, 16, D], BF16)
        with contextlib.ExitStack() as G0:
            wstg = [sb(G0, f"wstgG{i}", [128, 1024], F32) for i in range(2)]
            wstgr = Rot(wstg, "wstgG")
            for j in range(16):
                stg, stgk = wstgr.next()
                S.dma("sp", stg[:], swout_d[j * 128:(j + 1) * 128, :], writes=[stgk])
                S.op("dve", lambda: nc.vector.tensor_scalar(out=wout1[:, j, :], in0=stg[:], scalar1=pcol[:, PC_SNW + j:PC_SNW + j + 1], scalar2=None, op0=ALU.mult),
                     reads=[stgk, "pcol"], writes=["wout1"])
            S.barrier()
        carry = sb(G, "carry", [128, SSD_INNER], F32)
        xts_ = [sb(G, f"xc{i}", [128, SSD_INNER], BF16) for i in range(2)]
        xcr = Rot(xts_, "xc")
        bts_ = [sb(G, f"bc{i}", [128, 1024], BF16) for i in range(2)]
        bcr = Rot(bts_, "bc")
        dtcs = [sb(G, f"dtc{i}", [128, 64], F32) for i in range(4)]
        dtcr = Rot(dtcs, "dtc")
        das = [sb(G, f"da{i}", [128, 64], F32) for i in range(2)]
        dar = Rot(das, "da")
        scs = [sb(G, f"sc{i}", [128, 128], F32) for i in range(2)]
        scr_ = Rot(scs, "sc")
        smalls = [sb(G, f"sm{i}", [128, 6, 64], F32) for i in range(3)]
        smr = Rot(smalls, "sm")
        xws = [sb(G, f"xw{i}", [128, SSD_INNER], BF16) for i in range(1)]
        xwr = Rot(xws, "xw")
        cbfs = [sb(G, f"cbf{i}", [128, SSD_INNER], BF16) for i in range(1)]
        cbfr = Rot(cbfs, "cbf")

        def load_dt(c):
            dtc, dtck = dtcr.next()
            S.dma("sp", dtc[:], dt_d[c], reads=["dt_d"], writes=[dtck])
            return dtc, dtck

        def chunk_scalars(c, need_b_only, pre=None):
            dtc, dtck = pre if pre is not None else load_dt(c)
            da, dak = dar.next()
            S.op("dve", lambda: nc.vector.tensor_tensor(out=da[:], in0=dtc[:], in1=arow[:], op=ALU.mult), reads=[dtck, "arow"], writes=[dak])
            p0, p0k = psr.next()
            S.op("pe", lambda: nc.tensor.matmul(p0[:, 0:32], lhsT=K["triL"][:], rhs=da[:, 0:32], start=True, stop=True), reads=["k_triL", dak], writes=[p0k])
            S.op("pe", lambda: nc.tensor.matmul(p0[:, 32:64], lhsT=K["triU"][:], rhs=da[:, 32:64], start=True, stop=True), reads=["k_triU", dak], writes=[p0k])
            S.op("pe", lambda: nc.tensor.matmul(p0[:, 64:128], lhsT=ones128[:], rhs=da[:, 0:64], start=True, stop=True), reads=["ones128", dak], writes=[p0k])
            if not need_b_only:
                S.op("pe", lambda: nc.tensor.matmul(p0[0:32, 128:256], lhsT=da[:, 0:32], rhs=K["triL"][:], start=True, stop=True), reads=["k_triL", dak], writes=[p0k])
                S.op("pe", lambda: nc.tensor.matmul(p0[32:64, 128:256], lhsT=da[:, 32:64], rhs=K["triU"][:], start=True, stop=True), reads=["k_triU", dak], writes=[p0k])
            sc, sck = scr_.next()
            S.op("act", lambda: nc.scalar.activation(out=sc[:], in_=p0[:, 0:128], func=AF.Copy), reads=[p0k], writes=[sck])
            sm, smk = smr.next()
            S.op("dve", lambda: nc.vector.tensor_scalar(out=sm[:, 0, :], in0=sc[:, 0:64], scalar1=-1.0, scalar2=None, op0=ALU.mult), reads=[sck], writes=[smk])
            S.op("act", lambda: nc.scalar.activation(out=sm[:, 1, :], in_=sc[:, 0:64], func=AF.Exp), reads=[sck], writes=[smk])
            S.op("dve", lambda: nc.vector.tensor_tensor(out=sm[:, 4, :], in0=sc[:, 64:128], in1=sc[:, 0:64], op=ALU.subtract), reads=[sck], writes=[smk])
            S.op("act", lambda: nc.scalar.activation(out=sm[:, 5, :], in_=sm[:, 4, :], func=AF.Exp), reads=[smk], writes=[smk])
            S.op("dve", lambda: nc.vector.tensor_tensor(out=sm[:, 2, :], in0=sm[:, 5, :], in1=dtc[:], op=ALU.mult), reads=[smk, dtck], writes=[smk])
            S.op("act", lambda: nc.scalar.activation(out=sm[:, 3, :], in_=sc[:, 64:128], func=AF.Exp), reads=[sck], writes=[smk])
            return sm, smk, p0, p0k, dtc, dtck

        def state_update(c, xc, xck, bc, bck, sm, smk, d):
            xw, xwk = xwr.next()
            S.op("dve", lambda: nc.vector.tensor_tensor(out=xw[:].rearrange("p (h q) -> p h q", q=64), in0=xc[:].rearrange("p (h q) -> p h q", q=64),
                                                        in1=sm[:, 2, d * 32:(d + 1) * 32].unsqueeze(2).to_broadcast([128, 32, 64]), op=ALU.mult),
                 reads=[xck, smk], writes=[xwk])
            S.op("dve", lambda: nc.vector.tensor_tensor(out=carry[:].rearrange("p (h q) -> p h q", q=64), in0=carry[:].rearrange("p (h q) -> p h q", q=64),
                                                        in1=sm[:, 3, d * 32:(d + 1) * 32].unsqueeze(2).to_broadcast([128, 32, 64]), op=ALU.mult),
                 reads=["carry", smk], writes=["carry"])
            for q4 in range(4):
                pS, pSk = psr.next()
                for g2 in range(2):
                    g = q4 * 2 + g2
                    S.op("pe", lambda: nc.tensor.matmul(pS[:, g2 * 256:(g2 + 1) * 256], lhsT=bc[:, g * 128:(g + 1) * 128], rhs=xw[:, g * 256:(g + 1) * 256],
                                                        start=True, stop=True), reads=[bck, xwk], writes=[pSk])
                S.op("dve", lambda: nc.vector.tensor_tensor(out=carry[:, q4 * 512:(q4 + 1) * 512], in0=pS[:], in1=carry[:, q4 * 512:(q4 + 1) * 512], op=ALU.add),
                     reads=[pSk, "carry"], writes=["carry"])

        S.op("dve", lambda: nc.vector.memset(carry[:], 0.0), writes=["carry"])
        def f_prep(c):
            xc, xck = xcr.next()
            S.dma("sp", xc[:], xs_d[c * 128:(c + 1) * 128, :], reads=["xs_d"], writes=[xck])
            bc, bck = bcr.next()
            S.dma("sp", bc[:], bt_d[c * 128:(c + 1) * 128, :], reads=["bt_d"], writes=[bck])
            sm, smk, _, _, _, _ = chunk_scalars(c, True)
            return xc, xck, bc, bck, sm, smk

        fp_ = f_prep(NT - 1)
        for c in range(NT - 1, -1, -1):
            nfp = f_prep(c - 1) if c > 0 else None
            cbf, cbfk = cbfr.next()
            S.op("act", lambda: nc.scalar.activation(out=cbf[:], in_=carry[:], func=AF.Copy), reads=["carry"], writes=[cbfk])
            S.dma("sp", pb_d[c], cbf[:], reads=[cbfk], writes=["pb_d"])
            state_update(c, *fp_, 1)
            fp_ = nfp

        if stop_after == "F":
            S.barrier()
            S.finish()
            return nc, dbg

        S.op("dve", lambda: nc.vector.memset(carry[:], 0.0), writes=["carry"])
        BTs = [sb(G, f"BT{i}", [128, 8, 128], BF16) for i in range(1)]
        BTr = Rot(BTs, "BT")
        CTs = [sb(G, f"CT{i}", [128, 8, 128], BF16) for i in range(2)]
        CTr = Rot(CTs, "CT")
        szs = [sb(G, f"szc{i}", [128, SSD_INNER], BF16) for i in range(1)]
        szr = Rot(szs, "szc")
        pbs = [sb(G, f"pbc{i}", [128, SSD_INNER], BF16) for i in range(1)]
        pbr = Rot(pbs, "pbc")
        hls = [sb(G, f"hl{i}", [64, 2, 128], BF16) for i in range(3)]
        hlr = Rot(hls, "hl")
        rows = [sb(G, f"row{i}", [1, 8 * 256], BF16) for i in range(2)]
        rowr = Rot(rows, "row")
        Lts = [sb(G, f"Lt{i}", [128, 32, 128], BF16) for i in range(4)]
        Ltr = Rot(Lts, "Lt")
        CBs = [sb(G, f"CBs{i}", [128, 8, 128], BF16) for i in range(1)]
        CBr = Rot(CBs, "CBs")
        xdts = [sb(G, f"xdt{i}", [128, SSD_INNER], BF16) for i in range(2)]
        ycs = [sb(G, f"yc{i}", [128, 512], F32) for i in range(1)]
        ycr = Rot(ycs, "yc")
        yts = [sb(G, f"ytmp{i}", [128, 512], F32) for i in range(1)]
        ytr = Rot(yts, "ytmp")
        yg = sb(G, "yg", [128, SSD_INNER], F32)
        ynb = sb(G, "ynb", [128, SSD_INNER], BF16)
        ynT = sb(G, "ynT", [128, 16, 128], BF16)
        nst2 = [sb(G, f"nst2{i}", [128, 8], F32) for i in range(2)]
        nst2r = Rot(nst2, "nst2")
        x2s = [sb(G, f"x2{i}", [128, D], F32) for i in range(2)]
        x2r = Rot(x2s, "x2")
        outs = [sb(G, f"ot{i}", [128, D], F32) for i in range(2)]
        outr = Rot(outs, "ot")
        bT_v = bT_d.rearrange("(g p) t -> p g t", p=128)
        cT_v = cT_d.rearrange("(g p) t -> p g t", p=128)
        ident4 = K["ident"][:].unsqueeze(1).to_broadcast([128, 4, 128])
        def g_loads(c):
            tsl = slice(c * 128, (c + 1) * 128)
            xc, xck = xcr.next()
            S.dma("sp", xc[:], xs_d[tsl, :], reads=["xs_d"], writes=[xck])
            bc, bck = bcr.next()
            S.dma("sp", bc[:], bt_d[tsl, :], reads=["bt_d"], writes=[bck])
            BT, BTk = BTr.next()
            S.dma("sp", BT[:], bT_v[:, :, tsl], reads=["bT_d"], writes=[BTk])
            CT, CTk = CTr.next()
            S.dma("sp", CT[:], cT_v[:, :, tsl], reads=["cT_d"], writes=[CTk])
            dt_ = load_dt(c)
            return (tsl, xc, xck, bc, bck, BT, BTk, CT, CTk, dt_)

        def g_scal(c, dt_):
            sm, smk, p0, p0k, dtc, dtck = chunk_scalars(c, False, dt_)
            hl, hlk = hlr.next()
            S.op("act", lambda: nc.scalar.activation(out=hl[:, 0, :], in_=p0[0:64, 128:256], func=AF.Copy), reads=[p0k], writes=[hlk])
            S.op("dve", lambda: nc.vector.tensor_tensor(out=hl[:, 1, :], in0=p0[0:64, 128:256], in1=hl[:, 0, :], op=ALU.subtract), reads=[p0k, hlk], writes=[hlk])
            return (sm, smk, dtc, dtck, hl, hlk)

        def g_head(c, L, SC):
            tsl, xc, xck, bc, bck, BT, BTk, CT, CTk, dt_ = L
            sm, smk, dtc, dtck, hl, hlk = SC
            CB, CBk = CBr.next()
            for gh in range(2):
                pcb, pcbk = psr.next()
                for g4 in range(4):
                    g = gh * 4 + g4
                    S.op("pe", lambda: nc.tensor.matmul(pcb[:, g4 * 128:(g4 + 1) * 128], lhsT=BT[:, g, :], rhs=CT[:, g, :], start=True, stop=True),
                         reads=[BTk, CTk], writes=[pcbk])
                S.op("act", lambda: nc.scalar.activation(out=CB[:, gh * 4:(gh + 1) * 4, :].rearrange("p g l -> p (g l)"), in_=pcb[:], func=AF.Copy),
                     reads=[pcbk], writes=[CBk])
            Ms = []
            tasks = []
            mtasks = []
            rowbox = {}
            for d in range(2):
                Lt, Ltk = Ltr.next()
                Ms.append((Lt, Ltk))
                for r16 in range(4):
                    for hb4 in range(2):
                        def task(d=d, r16=r16, hb4=hb4, Lt=Lt, Ltk=Ltk):
                            nm = K["nmaskf"] if d == 0 else K["nmaskb"]
                            nmk = "k_nmaskf" if d == 0 else "k_nmaskb"
                            if hb4 == 0:
                                row, rowk = rowr.next()
                                p0_ = d * 32 + r16 * 8
                                S.dma("sp", row[:].rearrange("o (p f) -> o p f", p=8), hl[p0_:p0_ + 8, :, :].rearrange("p j l -> p (j l)"),
                                      reads=[hlk], writes=[rowk])
                                rowbox[(d, r16)] = (row, rowk)
                            row, rowk = rowbox[(d, r16)]
                            hb = r16 * 2 + hb4
                            lp, lpk = psr.next()
                            base = hb4 * 4 * 256
                            hi_ap = bass.AP(row, base, [[8 * 256, 1], [256, 4], [1, 128]])
                            lo_ap = bass.AP(row, base + 128, [[8 * 256, 1], [256, 4], [1, 128]])
                            S.op("pe", lambda: nc.tensor.matmul(lp[:], lhsT=ones1[:], rhs=hi_ap, start=True, stop=False), reads=["ones1", rowk], writes=[lpk])
                            S.op("pe", lambda: nc.tensor.matmul(lp[:], lhsT=ones1[:], rhs=lo_ap, start=False, stop=False), reads=["ones1", rowk], writes=[lpk])
                            S.op("pe", lambda: nc.tensor.matmul(lp[:], lhsT=nm[:], rhs=ident4, start=False, stop=True), reads=[nmk, "k_ident"], writes=[lpk])
                            for j4 in range(4):
                                h = hb * 4 + j4
                                S.op("act", lambda: nc.scalar.activation(out=Lt[:, h, :], in_=lp[:, j4 * 128:(j4 + 1) * 128], func=AF.Exp,
                                                                         bias=sm[:, 0, d * 32 + h:d * 32 + h + 1], scale=1.0), reads=[lpk, smk], writes=[Ltk])
                        tasks.append(task)

                        def mtask(d=d, r16=r16, hb4=hb4, Lt=Lt, Ltk=Ltk):
                            hb = r16 * 2 + hb4
                            S.op("dve", lambda: nc.vector.tensor_tensor(out=Lt[:, hb * 4:(hb + 1) * 4, :], in0=Lt[:, hb * 4:(hb + 1) * 4, :],
                                                                        in1=CB[:, hb:hb + 1, :].to_broadcast([128, 4, 128]), op=ALU.mult),
                                 reads=[Ltk, CBk], writes=[Ltk])
                        mtasks.append(mtask)

            def fin():
                for d in range(2):
                    S.op("dve", lambda: nc.vector.tensor_tensor(out=xdts[d][:].rearrange("p (h q) -> p h q", q=64), in0=xc[:].rearrange("p (h q) -> p h q", q=64),
                                                                in1=dtc[:, d * 32:(d + 1) * 32].unsqueeze(2).to_broadcast([128, 32, 64]), op=ALU.mult),
                         reads=[xck, dtck], writes=[f"xdt{d}"])

            def late_loads():
                szc, szk = szr.next()
                S.dma("sp", szc[:], sz_d[tsl, :], reads=["sz_d"], writes=[szk])
                pbc, pbk = pbr.next()
                S.dma("sp", pbc[:], pb_d[c], reads=["pb_d"], writes=[pbk])
                x2, x2k = x2r.next()
                S.dma("sp", x2[:], x1_d[tsl, :], reads=["x1_d"], writes=[x2k])
                X.update(szc=szc, szk=szk, pbc=pbc, pbk=pbk, x2=x2, x2k=x2k)

            X = dict(c=c, tsl=tsl, xc=xc, xck=xck, bc=bc, bck=bck, CT=CT, CTk=CTk,
                     sm=sm, smk=smk, Ms=Ms, tasks=tasks, mtasks=mtasks, fin=fin, late_loads=late_loads)
            return X

        def g_mid(X, run):
            c, tsl, xc, xck, bc, bck, CT, CTk = X["c"], X["tsl"], X["xc"], X["xck"], X["bc"], X["bck"], X["CT"], X["CTk"]
            szc, szk, pbc, pbk, sm, smk, Ms = X["szc"], X["szk"], X["pbc"], X["pbk"], X["sm"], X["smk"], X["Ms"]
            cbf, cbfk = X["cbf"], X["cbfk"]
            for q4 in range(4):
                csl = slice(q4 * 512, (q4 + 1) * 512)
                pyd, pydk = psr.next()
                for h8 in range(8):
                    h = q4 * 8 + h8
                    for d in range(2):
                        S.op("pe", lambda: nc.tensor.matmul(pyd[:, h8 * 64:(h8 + 1) * 64], lhsT=Ms[d][0][:, h, :], rhs=xdts[d][:, h * 64:(h + 1) * 64],
                                                            start=(d == 0), stop=False), reads=[Ms[d][1], f"xdt{d}"], writes=[pydk])
                    S.op("pe", lambda: nc.tensor.matmul(pyd[:, h8 * 64:(h8 + 1) * 64], lhsT=Dident[:, h, :], rhs=xc[:, h * 64:(h + 1) * 64],
                                                        start=False, stop=True), reads=["Dident", xck], writes=[pydk])
                pof, pofk = psr.next()
                pob, pobk = psr.next()
                for g2 in range(2):
                    g = q4 * 2 + g2
                    S.op("pe", lambda: nc.tensor.matmul(pof[:, g2 * 256:(g2 + 1) * 256], lhsT=CT[:, g, :], rhs=cbf[:, g * 256:(g + 1) * 256], start=True, stop=True),
                         reads=[CTk, cbfk], writes=[pofk])
                    S.op("pe", lambda: nc.tensor.matmul(pob[:, g2 * 256:(g2 + 1) * 256], lhsT=CT[:, g, :], rhs=pbc[:, g * 256:(g + 1) * 256], start=True, stop=True),
                         reads=[CTk, pbk], writes=[pobk])
                yc, yck = ycr.next()
                yt_, ytk = ytr.next()
                v3 = lambda t_: t_[:].rearrange("p (h q) -> p h q", q=64)
                ef = sm[:, 1, q4 * 8:q4 * 8 + 8].unsqueeze(2).to_broadcast([128, 8, 64])
                eb = sm[:, 1, 32 + q4 * 8:32 + q4 * 8 + 8].unsqueeze(2).to_broadcast([128, 8, 64])
                S.op("dve", lambda: nc.vector.tensor_tensor(out=v3(yc), in0=pof[:].rearrange("p (h q) -> p h q", q=64), in1=ef, op=ALU.mult), reads=[pofk, smk], writes=[yck])
                S.op("dve", lambda: nc.vector.tensor_tensor(out=v3(yt_), in0=pob[:].rearrange("p (h q) -> p h q", q=64), in1=eb, op=ALU.mult), reads=[pobk, smk], writes=[ytk])
                S.op("dve", lambda: nc.vector.tensor_tensor(out=yc[:], in0=yc[:], in1=yt_[:], op=ALU.add), reads=[yck, ytk], writes=[yck])
                S.op("dve", lambda: nc.vector.tensor_tensor(out=yc[:], in0=pyd[:], in1=yc[:], op=ALU.add), reads=[pydk, yck], writes=[yck])
                S.op("dve", lambda: nc.vector.tensor_tensor(out=yg[:, csl], in0=yc[:], in1=szc[:, csl], op=ALU.mult), reads=[yck, szk], writes=["yg"])
                run(3)
            state_update(c, xc, xck, bc, bck, sm, smk, 0)

        def g_tail(X):
            c, tsl, x2, x2k = X["c"], X["tsl"], X["x2"], X["x2k"]
            ns, nsk = nst2r.next()
            S.op("act", lambda: nc.scalar.activation(out=ynb[:], in_=yg[:], func=AF.Square, accum_out=ns[:, 0:1]), reads=["yg"], writes=["ynb", nsk])
            S.op("act", lambda: nc.scalar.activation(out=ns[:, 1:2], in_=ns[:, 0:1], func=AF.Sqrt, bias=epsc[:, 0:1], scale=1.0 / SSD_INNER), reads=[nsk, "epsc"], writes=[nsk])
            S.op("dve", lambda: nc.vector.reciprocal(out=ns[:, 2:3], in_=ns[:, 1:2]), reads=[nsk], writes=[nsk])
            S.op("dve", lambda: nc.vector.tensor_scalar(out=ynb[:], in0=yg[:], scalar1=ns[:, 2:3], scalar2=None, op0=ALU.mult),
                 reads=["yg", nsk], writes=["ynb"])
            for jh in range(2):
                tp, tpk = ptr.next()
                for jj in range(8):
                    j = jh * 8 + jj
                    S.op("pe", lambda: nc.tensor.transpose(tp[:, jj, :], ynb[:, j * 128:(j + 1) * 128], K["ident"][:]), reads=["ynb", "k_ident"], writes=[tpk])
                S.op("act", lambda: nc.scalar.activation(out=ynT[:, jh * 8:(jh + 1) * 8, :], in_=tp[:], func=AF.Copy), reads=[tpk], writes=["ynT"])
            y2, y2k = yg, "yg"
            for n in range(2):
                po, pok = psr.next()
                for j in range(16):
                    S.op("pe", lambda: nc.tensor.matmul(po[:], lhsT=ynT[:, j, :], rhs=wout1[:, j, n * 512:(n + 1) * 512], start=(j == 0), stop=(j == 15)),
                         reads=["ynT", "wout1"], writes=[pok])
                S.op("dve", lambda: nc.vector.tensor_tensor(out=y2[:, n * 512:(n + 1) * 512], in0=po[:], in1=gate_b[1][:, n * 512:(n + 1) * 512], op=ALU.mult),
                     reads=[pok, "gate_b1"], writes=[y2k])
            S.op("dve", lambda: nc.vector.tensor_tensor(out=x2[:], in0=y2[:, 0:D], in1=x2[:], op=ALU.add), reads=[y2k, x2k], writes=[x2k])
            ns, nsk = nst2r.next()
            S.op("act", lambda: nc.scalar.activation(out=nrm_junk[:], in_=x2[:], func=AF.Square, accum_out=ns[:, 0:1]), reads=[x2k], writes=["nrm_junk", nsk])
            S.op("act", lambda: nc.scalar.activation(out=ns[:, 1:2], in_=ns[:, 0:1], func=AF.Sqrt, bias=epsc[:, 0:1], scale=1.0 / D), reads=[nsk, "epsc"], writes=[nsk])
            S.op("dve", lambda: nc.vector.reciprocal(out=ns[:, 2:3], in_=ns[:, 1:2]), reads=[nsk], writes=[nsk])
            ot, otk = outr.next()
            S.op("dve", lambda: nc.vector.scalar_tensor_tensor(out=ot[:], in0=x2[:], scalar=ns[:, 2:3], in1=fnw_b[:], op0=ALU.mult, op1=ALU.mult),
                 reads=[x2k, nsk, "fnw_b"], writes=[otk])
            X["store"] = lambda: S.dma("sp", out_d[tsl, :], ot[:], reads=[otk], writes=["out_d"])

        def make_runner(X):
            tl = list(X["tasks"]) if X is not None else []
            ml = list(X["mtasks"]) if X is not None else []
            done = [0]

            def run(n):
                for _ in range(n):
                    if tl:
                        tl.pop(0)()
                        done[0] += 1
                        if done[0] > 2 and ml:
                            ml.pop(0)()

            def flush():
                run(len(tl))
                while ml:
                    ml.pop(0)()
            return run, flush

        Ld = {0: g_loads(0)}
        Sc = {0: g_scal(0, Ld[0][-1])}
        ctx = g_head(0, Ld.pop(0), Sc.pop(0))
        run, flush = make_runner(ctx)
        flush()
        ctx["fin"]()
        ctx["late_loads"]()
        if NT > 1:
            Ld[1] = g_loads(1)
            Sc[1] = g_scal(1, Ld[1][-1])
        pend_store = None
        for c in range(NT):
            cbf, cbfk = cbfr.next()
            S.op("act", lambda: nc.scalar.activation(out=cbf[:], in_=carry[:], func=AF.Copy), reads=["carry"], writes=[cbfk])
            ctx["cbf"], ctx["cbfk"] = cbf, cbfk
            nxt = g_head(c + 1, Ld.pop(c + 1), Sc.pop(c + 1)) if c + 1 < NT else None
            run, flush = make_runner(nxt)
            g_mid(ctx, run)
            flush()
            if c + 2 < NT:
                Ld[c + 2] = g_loads(c + 2)
                Sc[c + 2] = g_scal(c + 2, Ld[c + 2][-1])
            if nxt is not None:
                nxt["late_loads"]()
            if pend_store is not None:
                pend_store()
            g_tail(ctx)
            pend_store = ctx["store"]
            if nxt is not None:
                nxt["fin"]()
            ctx = nxt
        pend_store()
        S.barrier()
    S.finish()
    return nc, dbg


def make_in_maps(inputs):
    cst = host_consts()
    pcol = host_pcol(inputs)
    f = lambda a: np.ascontiguousarray(np.asarray(a), dtype=np.float32)
    shared = {
        "mod_w": f(inputs["mod_w"]), "mod_b": f(inputs["mod_b"]),
        "attn_w_in": f(inputs["attn_w_in"][0]), "attn_w_out": f(inputs["attn_w_out"][0]),
        "ssd_w_in": f(inputs["ssd_w_in"][0]),
        "ssd_dt_bias": f(inputs["ssd_dt_bias"]).reshape(1, 64), "ssd_a_log": f(inputs["ssd_a_log"]).reshape(1, 64),
        "ssd_d": f(inputs["ssd_d"]).reshape(1, 32), "ssd_norm_w": f(inputs["ssd_norm_w"]).reshape(1, SSD_INNER),
        "ssd_w_out": f(inputs["ssd_w_out"][0]), "final_norm_w": f(inputs["final_norm_w"]).reshape(1, D),
        "pcol": pcol,
    }
    for k, v in cst.items():
        shared["k_" + k] = v
    x = f(inputs["x"])
    c = f(inputs["c"])
    pos = np.ascontiguousarray(np.asarray(inputs["positions"]), dtype=np.int32)
    maps = []
    for b in range(x.shape[0]):
        m = dict(shared)
        m["x"] = x[b]
        m["c"] = np.ascontiguousarray(c[b].reshape(8, 128).T)
        m["pos"] = pos[b:b + 1]
        maps.append(m)
    return maps


def kernel(**inputs):
    nc, _ = build()
    maps = make_in_maps(inputs)
    res = run_bass_kernel_spmd(nc, maps, core_ids=list(range(8)))
    return np.stack([np.asarray(r["out"], dtype=np.float32) for r in res.results], axis=0)
```

```python
import contextlib
import math
import numpy as np
import ml_dtypes
import concourse.bass as bass
import concourse.mybir as mybir
from concourse.bass_utils import run_bass_kernel_spmd

F32 = mybir.dt.float32
BF16 = mybir.dt.bfloat16
I32 = mybir.dt.int32
AF = mybir.ActivationFunctionType
ALU = mybir.AluOpType

D = 1024
SEQ = 4096
NT = SEQ // 128
AW = 1536
PATTERNS = ((128, 1), (512, 4), (2048, 16))
SSD_INNER = 2048
SSD_IN = 6208
EPS = 1e-6
import os
KSTEP = int(os.environ.get('KSTEP', '9'))
KPOOL = int(os.environ.get('KPOOL', '0'))


class Sched:
    NDMA = 10

    def __init__(self, nc, es, needed=None):
        self.nc = nc
        self.es = es
        self.needed = needed
        self.record = set()
        self.engs = {"pe": nc.tensor, "dve": nc.vector, "act": nc.scalar,
                     "pool": nc.gpsimd, "sp": nc.sync}
        self.sem = {}
        self.raw = {}
        self.pub = {}
        for e in self.engs:
            self.sem[e] = self.es.enter_context(nc.semaphore("s_" + e))
            self.raw[e] = 0
            self.pub[e] = 0
        self.dsem, self.dcnt, self.dnext = {}, {}, {}
        for q in ("sp",):
            self.dsem[q] = [self.es.enter_context(nc.semaphore(f"d_{q}{i}")) for i in range(self.NDMA)]
            self.dcnt[q] = [0] * self.NDMA
            self.dnext[q] = 0
        self.seen = {e: {} for e in self.engs}
        self.lastw = {}
        self.readers = {}
        self.ninstr = 0
        self.nwaits = 0

    def _wait(self, e, ev):
        sem, val, src, rid = ev
        if src == "pe" and e == "pe":
            return
        k = id(sem)
        if self.seen[e].get(k, 0) >= val:
            return
        if rid is not None:
            self.record.add(rid)
        self.engs[e].wait_ge(sem, val)
        self.seen[e][k] = val
        self.nwaits += 1

    def _deps(self, e, reads, writes):
        for k in reads:
            ev = self.lastw.get(k)
            if ev is not None:
                self._wait(e, ev)
            if k[:2] in ("ps", "pt"):
                for ev in self.readers.get(k, ()):
                    self._wait(e, ev)
        for k in writes:
            ev = self.lastw.get(k)
            if ev is not None:
                self._wait(e, ev)
            for ev in self.readers.get(k, ()):
                self._wait(e, ev)

    def _commit(self, ev, reads, writes):
        for k in reads:
            self.readers.setdefault(k, []).append(ev)
        for k in writes:
            self.lastw[k] = ev
            self.readers[k] = []

    def op(self, e, fn, reads=(), writes=()):
        self._deps(e, reads, writes)
        ins = fn()
        self.raw[e] += 1
        rid = (e, self.raw[e])
        if self.needed is None or rid in self.needed:
            self.pub[e] += 1
            ins.then_inc(self.sem[e], 1)
            ev = (self.sem[e], self.pub[e], e, rid)
        else:
            ev = (self.sem[e], self.pub[e] + 1, e, rid)
        self._commit(ev, reads, writes)
        self.ninstr += 1
        return ev

    def dma(self, q, out, in_, reads=(), writes=()):
        i = self.dnext[q]
        self.dnext[q] = (i + 1) % self.NDMA
        sem = self.dsem[q][i]
        if self.dcnt[q][i] > 0:
            self._wait(q, (sem, self.dcnt[q][i], None, None))
        self._deps(q, reads, writes)
        ins = self.engs[q].dma_start(out=out, in_=in_)
        self.dcnt[q][i] += 16
        ins.then_inc(sem, 16)
        ev = (sem, self.dcnt[q][i], None, None)
        self._commit(ev, reads, writes)
        self.ninstr += 1
        return ev

    def barrier(self):
        evs = []
        for e in self.engs:
            if self.raw[e] > 0:
                rid = (e, self.raw[e])
                pubd = self.needed is None or rid in self.needed
                evs.append((self.sem[e], self.pub[e] if pubd else self.pub[e] + 1, e, rid))
        for q in self.dsem:
            for i, sem in enumerate(self.dsem[q]):
                if self.dcnt[q][i] > 0:
                    evs.append((sem, self.dcnt[q][i], None, None))
        for e in self.engs:
            for ev in evs:
                if ev[2] == e:
                    continue
                self._wait(e, ev)
        self.lastw = {}
        self.readers = {}

    def finish(self):
        for q in self.dsem:
            for i, sem in enumerate(self.dsem[q]):
                if self.dcnt[q][i] > 0:
                    self._wait("sp", (sem, self.dcnt[q][i], None, None))


class Rot:
    def __init__(self, tiles, name):
        self.tiles = tiles
        self.name = name
        self.i = -1

    def next(self):
        self.i = (self.i + 1) % len(self.tiles)
        return self.tiles[self.i], f"{self.name}{self.i}"


def _ap(t, rowsize, p0, npart, col, dims):
    return bass.AP(t, p0 * rowsize + col, [[rowsize, npart]] + [list(d) for d in dims])


def _bf(a):
    return np.asarray(a, dtype=np.float32).astype(ml_dtypes.bfloat16)


def host_consts():
    c = {}
    p = np.arange(128)
    c["ident"] = _bf(np.eye(128))
    c["bdones"] = _bf((p[:, None] // 64) == (p[None, :] // 64))
    pm = np.zeros((128, 128), np.float32)
    for hb in (0, 64):
        for j in range(8):
            pm[hb + j + 8, hb + j] = 1.0
            pm[hb + j, hb + 8 + j] = 1.0
    c["pswap"] = _bf(pm)
    i = (p % 64)[:, None, None]
    i4 = np.arange(4)[None, :, None]
    jq = np.arange(128)[None, None, :]
    c["mask4"] = _bf(np.abs(64 * (i4 - 1) + i - jq) <= 64).reshape(128, 512)
    k = p[:, None]
    l = p[None, :]
    c["triL"] = (k <= l).astype(np.float32)
    c["triU"] = (k >= l).astype(np.float32)
    c["nmaskf"] = _bf(np.where(l <= k, 0.0, -30000.0))
    c["nmaskb"] = _bf(np.where(l >= k, 0.0, -30000.0))
    c["maskf"] = _bf(k <= l)
    c["maskb"] = _bf(k >= l)
    return c


def host_pcol(inp):
    cols = {}
    p = np.arange(128)
    inv = (500000.0 ** (-np.arange(0, 16, 2, dtype=np.float32) / 16.0)).astype(np.float32)
    pp = p % 64
    invf = np.where(pp < 16, inv[pp % 8], 0.0).astype(np.float32)
    sg = np.where(pp < 8, -1.0, np.where(pp < 16, 1.0, 0.0)).astype(np.float32)
    parts = [invf[:, None], sg[:, None]]
    parts.append(np.asarray(inp["norm_w"], np.float32).reshape(2, 8, 128).transpose(2, 0, 1).reshape(128, 16))
    parts.append(np.asarray(inp["mod_b"], np.float32).reshape(2, 24, 128).transpose(2, 0, 1).reshape(128, 48))
    parts.append(np.asarray(inp["ssd_conv_w"], np.float32).reshape(5, 32, 128).transpose(2, 0, 1).reshape(128, 160))
    parts.append(np.asarray(inp["ssd_conv_b"], np.float32).reshape(32, 128).T)
    parts.append(np.asarray(inp["ssd_norm_w"], np.float32).reshape(16, 128).T)
    return np.ascontiguousarray(np.concatenate(parts, axis=1), dtype=np.float32)


PC_INVF, PC_SG, PC_NW, PC_MODB, PC_CW, PC_CB, PC_SNW, PC_N = 0, 1, 2, 18, 66, 226, 258, 274


def build(stop_after=None):
    M = contextlib.ExitStack()
    rec = {}
    _build(M, stop_after, None, rec)
    M.close()
    M = contextlib.ExitStack()
    nc, dbg = _build(M, stop_after, rec["needed"], {})
    M.close()
    return nc, dbg


def _build(M, stop_after=None, needed=None, rec=None):
    nc = bass.Bass("TRN2", target_bir_lowering=False)
    S = Sched(nc, M, needed)
    rec["needed"] = S.record
    dbg = {}

    def din(name, shape, dt=F32):
        return nc.dram_tensor(name, list(shape), dt, kind="ExternalInput").ap()

    x_d = din("x", [SEQ, D])
    c_d = din("c", [128, 8])
    pos_d = din("pos", [1, SEQ], I32)
    modw_d = din("mod_w", [2, D, 3 * D])
    modb_d = din("mod_b", [2, 3 * D])
    awin_d = din("attn_w_in", [D, 4 * AW])
    awout_d = din("attn_w_out", [AW, D])
    swin_d = din("ssd_w_in", [D, SSD_IN])
    dtb_d = din("ssd_dt_bias", [1, 64])
    alog_d = din("ssd_a_log", [1, 64])
    sd_d = din("ssd_d", [1, 32])
    snw_d = din("ssd_norm_w", [1, SSD_INNER])
    swout_d = din("ssd_w_out", [SSD_INNER, D])
    fnw_d = din("final_norm_w", [1, D])
    pcol_d = din("pcol", [128, PC_N])
    cst = host_consts()
    cst_d = {k: din("k_" + k, list(v.shape), BF16 if v.dtype != np.float32 else F32) for k, v in cst.items()}
    out_d = nc.dram_tensor("out", [SEQ, D], F32, kind="ExternalOutput").ap()

    def scratch(name, shape, dt):
        kind = "ExternalOutput" if stop_after is not None else "Internal"
        t = nc.dram_tensor(name, list(shape), dt, kind=kind).ap()
        dbg[name] = t
        return t

    o_d = scratch("o_d", [AW, SEQ], BF16)
    x1_d = scratch("x1_d", [SEQ, D], F32)

    P = M

    uid = [0]

    def sb(stack, name, shape, dt):
        uid[0] += 1
        return stack.enter_context(nc.sbuf_tensor(f"sb{uid[0]}_{name}", list(shape), dt))

    ps = [P.enter_context(nc.psum_tensor(f"ps{i}", [128, 512], F32)) for i in range(6)]
    pt = [P.enter_context(nc.psum_tensor(f"pt{i}", [128, 8, 128], BF16)) for i in range(2)]
    psr = Rot(ps, "ps")
    ptr = Rot(pt, "pt")

    def load_cast(stg_rot, dst, dkey, src, shape):
        stg, stgk = stg_rot.next()
        n = 1
        for d_ in shape[1:]:
            n *= d_
        sv = stg[:, 0:n]
        if len(shape) == 3:
            sv = sv.rearrange("p (a b) -> p a b", a=shape[1])
        S.dma("sp", sv, src, writes=[stgk])
        S.op("act", lambda: nc.scalar.activation(out=dst, in_=sv, func=AF.Copy), reads=[stgk], writes=[dkey])

    pcol = sb(P, "pcol", [128, PC_N], F32)
    S.dma("sp", pcol[:], pcol_d, writes=["pcol"])
    K = {}
    for k, v in cst.items():
        K[k] = sb(P, "k_" + k, list(v.shape), BF16 if v.dtype != np.float32 else F32)
        S.dma("sp", K[k][:], cst_d[k], writes=["k_" + k])
    if stop_after is not None:
        junk = sb(P, "junk", [1, 16], F32)
        junki = sb(P, "junki", [1, 16], I32)
        for t_ in (awin_d, awout_d, swin_d, dtb_d, alog_d, sd_d, snw_d, swout_d, fnw_d):
            S.dma("sp", junk[:], t_[0:1, 0:16], writes=["junk"])
        S.dma("sp", junki[:], pos_d[0:1, 0:16], writes=["junki"])
    epsc = sb(P, "epsc", [128, 1], F32)
    S.op("dve", lambda: nc.vector.memset(epsc[:], EPS), writes=["epsc"])
    one11 = sb(P, "one11", [1, 1], F32)
    S.op("dve", lambda: nc.vector.memset(one11[:], 1.0), writes=["one11"])
    modA = sb(P, "modA", [128, 16], F32)
    modB = sb(P, "modB", [128, 16], F32)
    gate_b = [sb(P, f"gate_b{i}", [128, D], F32) for i in range(2)]
    nrm_junk = sb(P, "nrm_junk", [128, D], BF16)
    xn_t = [sb(P, f"xn{i}", [128, D], BF16) for i in range(2)]
    st_t = [sb(P, f"nst{i}", [128, 4], F32) for i in range(2)]
    onec = sb(P, "onec", [128, 1], F32)
    S.op("dve", lambda: nc.vector.memset(onec[:], 1.0), writes=["onec"])
    H = contextlib.ExitStack()
    M.enter_context(H)
    hnT = sb(H, "hnT", [128, 8, SEQ], BF16)

    with contextlib.ExitStack() as A:
        c_fm = sb(A, "c_fm", [128, 8], F32)
        S.dma("sp", c_fm[:], c_d, writes=["c_fm"])
        cond_f = sb(A, "cond_f", [128, 8], F32)
        cond_b = sb(A, "cond_b", [128, 8], BF16)
        condB = sb(A, "condB", [128, 8, 128], F32)
        S.op("act", lambda: nc.scalar.activation(out=cond_f[:], in_=c_fm[:], func=AF.Silu), reads=["c_fm"], writes=["cond_f"])
        if stop_after == "A0":
            dd = nc.dram_tensor("cond_dbg", [128, 8], F32, kind="ExternalOutput").ap()
            S.dma("sp", dd, cond_f[:], reads=["cond_f"])
            S.finish()
            return nc, dbg
        S.op("dve", lambda: nc.vector.tensor_copy(out=cond_b[:], in_=cond_f[:]), reads=["cond_f"], writes=["cond_b"])
        S.op("dve", lambda: nc.vector.tensor_copy(out=condB[:], in_=cond_f[:].unsqueeze(2).to_broadcast([128, 8, 128])),
             reads=["cond_f"], writes=["condB"])
        modw = sb(A, "modw", [128, 8, 3 * D], F32)
        modT = sb(A, "modT", [128, 24], F32)
        gb_bias = sb(A, "gb_bias", [128, D], F32)
        for li in range(2):
            for kc in range(8):
                S.dma("sp", modw[:, kc, :], modw_d[li, kc * 128:(kc + 1) * 128, :], writes=[f"modw{kc}"])
            pm, pmk = psr.next()
            for fc in range(24):
                for kc in range(8):
                    S.op("pe", lambda: nc.tensor.matmul(pm[:, fc:fc + 1], lhsT=modw[:, kc, fc * 128:(fc + 1) * 128],
                                                        rhs=cond_f[:, kc:kc + 1], start=(kc == 0), stop=(kc == 7)),
                         reads=[f"modw{kc}", "cond_f"], writes=[pmk])
            S.op("dve", lambda: nc.vector.tensor_tensor(out=modT[:], in0=pm[:, 0:24],
                                                        in1=pcol[:, PC_MODB + li * 24:PC_MODB + (li + 1) * 24], op=ALU.add),
                 reads=[pmk, "pcol"], writes=["modT"])
            S.op("dve", lambda: nc.vector.tensor_copy(out=modB[:, li * 8:(li + 1) * 8], in_=modT[:, 0:8]),
                 reads=["modT"], writes=["modB"])
            S.op("dve", lambda: nc.vector.scalar_tensor_tensor(out=modA[:, li * 8:(li + 1) * 8], in0=modT[:, 8:16], scalar=1.0,
                                                               in1=pcol[:, PC_NW + li * 8:PC_NW + (li + 1) * 8],
                                                               op0=ALU.add, op1=ALU.mult),
                 reads=["modT", "pcol"], writes=["modA"])
            S.dma("sp", gb_bias[:], modb_d[li:li + 1, 2 * D:3 * D].partition_broadcast(128), writes=["gb_bias"])
            for n in range(2):
                pg, pgk = psr.next()
                for kc in range(8):
                    S.op("pe", lambda: nc.tensor.matmul(pg[:], lhsT=condB[:, kc, :],
                                                        rhs=modw[:, kc, 2 * D + n * 512:2 * D + (n + 1) * 512],
                                                        start=(kc == 0), stop=(kc == 7)),
                         reads=[f"modw{kc}", "condB"], writes=[pgk])
                S.op("dve", lambda: nc.vector.tensor_tensor(out=gate_b[li][:, n * 512:(n + 1) * 512], in0=pg[:],
                                                            in1=gb_bias[:, n * 512:(n + 1) * 512], op=ALU.add),
                     reads=[pgk, "gb_bias"], writes=[f"gate_b{li}"])
        S.barrier()

    if stop_after == "A":
        for nm, t_, shp in (("modA_dbg", modA, [128, 16]), ("modB_dbg", modB, [128, 16]), ("gate0_dbg", gate_b[0], [128, D]), ("gate1_dbg", gate_b[1], [128, D])):
            dd = nc.dram_tensor(nm, shp, F32, kind="ExternalOutput").ap()
            S.dma("sp", dd, t_[:])
        S.finish()
        return nc, dbg

    xnr = Rot(xn_t, "xn")
    str_ = Rot(st_t, "nst")

    def norm_tile(xt, xk, tt, li):
        st, stk = str_.next()
        xn, xnk = xnr.next()
        S.op("act", lambda: nc.scalar.activation(out=nrm_junk[:], in_=xt[:], func=AF.Square, accum_out=st[:, 0:1]),
             reads=[xk], writes=["nrm_junk", stk])
        S.op("act", lambda: nc.scalar.activation(out=st[:, 1:2], in_=st[:, 0:1], func=AF.Sqrt, bias=epsc[:, 0:1], scale=1.0 / D),
             reads=[stk, "epsc"], writes=[stk])
        S.op("dve", lambda: nc.vector.reciprocal(out=st[:, 2:3], in_=st[:, 1:2]), reads=[stk], writes=[stk])
        S.op("dve", lambda: nc.vector.tensor_scalar(out=xn[:], in0=xt[:], scalar1=st[:, 2:3], scalar2=None, op0=ALU.mult),
             reads=[xk, stk], writes=[xnk])
        tp, tpk = ptr.next()
        for kc in range(8):
            S.op("pe", lambda: nc.tensor.transpose(tp[:, kc, :], xn[:, kc * 128:(kc + 1) * 128], K["ident"][:]),
                 reads=[xnk, "k_ident"], writes=[tpk])
        for kc in range(8):
            col = li * 8 + kc
            if True:
                S.op("dve", lambda: nc.vector.tensor_scalar(out=hnT[:, kc, tt * 128:(tt + 1) * 128], in0=tp[:, kc, :],
                                                            scalar1=modA[:, col:col + 1], scalar2=modB[:, col:col + 1],
                                                            op0=ALU.mult, op1=ALU.add),
                     reads=[tpk, "modA", "modB"], writes=[f"hnT{kc}"])
            else:
                S.op("act", lambda: nc.scalar.activation(out=hnT[:, kc, tt * 128:(tt + 1) * 128], in_=tp[:, kc, :],
                                                         func=AF.Identity, scale=modA[:, col:col + 1], bias=modB[:, col:col + 1]),
                     reads=[tpk, "modA", "modB"], writes=[f"hnT{kc}"])

    HN = [f"hnT{kc}" for kc in range(8)]

    with contextlib.ExitStack() as B:
        xts = [sb(B, f"xt{i}", [128, D], F32) for i in range(3)]
        xr = Rot(xts, "xt")
        for tt in range(NT):
            xt, xk = xr.next()
            S.dma("sp", xt[:], x_d[tt * 128:(tt + 1) * 128, :], writes=[xk])
            norm_tile(xt, xk, tt, 0)
        S.barrier()

    if stop_after == "B":
        hn_dbg = nc.dram_tensor("hn_dbg", [128, 8, SEQ], BF16, kind="ExternalOutput").ap()
        dbg["hn_dbg"] = hn_dbg
        for kc in range(8):
            S.dma("sp", hn_dbg[:, kc, :], hnT[:, kc, :], reads=HN)
        S.finish()
        return nc, dbg

    with contextlib.ExitStack() as C:
        Ct = sb(C, "Ct", [128, SEQ], BF16)
        St = sb(C, "St", [128, SEQ], BF16)
        with contextlib.ExitStack() as R:
            RB = 1024
            posi = sb(R, "posi", [128, RB], I32)
            posf = sb(R, "posf", [128, RB], F32)
            ang = sb(R, "ang", [128, RB], F32)
            ki = sb(R, "ki", [128, RB], I32)
            kf = sb(R, "kf", [128, RB], F32)
            for cb in range(SEQ // RB):
                csl = slice(cb * RB, (cb + 1) * RB)
                S.dma("sp", posi[:], pos_d[:, csl].partition_broadcast(128), writes=["posi"])
                S.op("dve", lambda: nc.vector.tensor_copy(out=posf[:], in_=posi[:]), reads=["posi"], writes=["posf"])
                for which, phase in ((0, 0.0), (1, math.pi / 2)):
                    S.op("dve", lambda: nc.vector.tensor_scalar(out=ang[:], in0=posf[:], scalar1=pcol[:, PC_INVF:PC_INVF + 1],
                                                                scalar2=phase, op0=ALU.mult, op1=ALU.add),
                         reads=["posf", "pcol"], writes=["ang"])
                    S.op("dve", lambda: nc.vector.tensor_scalar(out=ki[:], in0=ang[:], scalar1=1.0 / (2 * math.pi), scalar2=None,
                                                                op0=ALU.mult), reads=["ang"], writes=["ki"])
                    S.op("dve", lambda: nc.vector.tensor_copy(out=kf[:], in_=ki[:]), reads=["ki"], writes=["kf"])
                    S.op("dve", lambda: nc.vector.scalar_tensor_tensor(out=ang[:], in0=kf[:], scalar=-2 * math.pi, in1=ang[:],
                                                                       op0=ALU.mult, op1=ALU.add),
                         reads=["kf", "ang"], writes=["ang"])
                    if which == 0:
                        S.op("act", lambda: nc.scalar.activation(out=kf[:], in_=ang[:], func=AF.Sin), reads=["ang"], writes=["kf"])
                        S.op("dve", lambda: nc.vector.tensor_scalar(out=St[:, csl], in0=kf[:], scalar1=pcol[:, PC_SG:PC_SG + 1],
                                                                    scalar2=None, op0=ALU.mult), reads=["kf", "pcol"], writes=["St"])
                    else:
                        S.op("act", lambda: nc.scalar.activation(out=Ct[:, csl], in_=ang[:], func=AF.Sin), reads=["ang"], writes=["Ct"])
            S.barrier()
        def dump(nm, t_, shape, dt):
            dd = nc.dram_tensor(nm, shape, dt, kind="ExternalOutput").ap()
            if len(shape) == 2 and shape[1] * (2 if dt == BF16 else 4) > 32768:
                h = shape[1] // 2
                S.dma("sp", dd[:, 0:h], t_[:, 0:h])
                S.dma("sp", dd[:, h:], t_[:, h:])
            else:
                S.dma("sp", dd, t_[:])

        if stop_after == "Crope":
            dump("Ct_dbg", Ct, [128, SEQ], BF16)
            dump("St_dbg", St, [128, SEQ], BF16)
            S.finish()
            return nc, dbg
        Dacc = sb(C, "Dacc", [128, SEQ], F32)
        Rinv = Dacc
        tmpfs = [sb(C, f"tmpf{i}", [128, 512], F32) for i in range(1)]
        tmpfr = Rot(tmpfs, "tmpf")
        Ug = [sb(C, f"Ug{i}", [128, SEQ], BF16) for i in range(3)]
        QT = sb(C, "QT", [128, SEQ], BF16)
        Kz = sb(C, "Kz", [128, 2 * SEQ], BF16)
        Vbd = sb(C, "Vbd", [128, 64, 128], BF16)
        S.op("dve", lambda: nc.vector.memset(Kz[:], 0.0), writes=["Kz"])
        S.op("dve", lambda: nc.vector.memset(Vbd[:], 0.0), writes=["Vbd"])
        wts = [sb(C, f"wt{i}", [128, 8, 128], BF16) for i in range(3)]
        wstg = [sb(C, f"wstg{i}", [128, 1024], F32) for i in range(2)]
        wstgr = Rot(wstg, "wstg")
        wr = Rot(wts, "wt")
        stA = [sb(C, f"stA{i}", [128, 512], BF16) for i in range(2)]
        stAr = Rot(stA, "stA")
        st1 = [sb(C, f"st1{i}", [128, 512], BF16) for i in range(2)]
        st1r = Rot(st1, "st1")
        st2 = [sb(C, f"st2{i}", [128, 512], BF16) for i in range(2)]
        st2r = Rot(st2, "st2")
        Et = [sb(C, f"Et{i}", [128, 512], BF16) for i in range(2)]
        Etr = Rot(Et, "Et")
        PTt = [sb(C, f"PT{i}", [128, 512], BF16) for i in range(2)]
        PTr = Rot(PTt, "PT")
        Ost = [sb(C, f"Ost{i}", [128, 512], BF16) for i in range(2)]
        Ostr = Rot(Ost, "Ost")
        awv = awin_d.rearrange("(kc p) n -> p kc n", p=128)

        def load_w(col0):
            w, wk = wr.next()
            load_cast(wstgr, w[:], wk, awv[:, :, col0:col0 + 128], [128, 8, 128])
            return w, wk

        def proj_tile(w, wk, tn):
            pj, pjk = psr.next()
            for kc in range(8):
                S.op("pe", lambda: nc.tensor.matmul(pj[:], lhsT=w[:, kc, :], rhs=hnT[:, kc, tn * 512:(tn + 1) * 512],
                                                    start=(kc == 0), stop=(kc == 7)),
                     reads=[wk, f"hnT{kc}"], writes=[pjk])
            return pj, pjk

        jobs = []
        for s in range(4):
            for g in range(3):
                jobs += [(s, g, 0), (s, g, 1), (s, g, 2)]
            for g in range(3):
                jobs.append((s, g, 3))
        wq = {}
        PF = 2

        def prefetch(i):
            if i < len(jobs) and i not in wq:
                s_, g_, wh_ = jobs[i]
                wq[i] = load_w(wh_ * AW + (g_ * 4 + s_) * 128)

        for i in range(PF):
            prefetch(i)
        ji = [0]

        def next_w():
            w, wk = wq.pop(ji[0])
            prefetch(ji[0] + PF)
            ji[0] += 1
            return w, wk

        for s in range(4):
            for g in range(3):
                win, dil = PATTERNS[g]
                L = SEQ // dil
                j = g * 4 + s
                ntile = L // 64
                for which in range(2):
                    w, wk = next_w()

                    def post(tn, a, ak):
                        p2, p2k = psr.next()
                        S.op("pe", lambda: nc.tensor.matmul(p2[:], lhsT=K["pswap"][:], rhs=a[:], start=True, stop=True),
                             reads=["k_pswap", ak], writes=[p2k])
                        t1, t1k = st1r.next()
                        t2, t2k = st2r.next()
                        S.op("dve", lambda: nc.vector.tensor_tensor(out=t1[:], in0=a[:], in1=Ct[:, tn * 512:(tn + 1) * 512], op=ALU.mult),
                             reads=[ak, "Ct"], writes=[t1k])
                        S.op("dve", lambda: nc.vector.tensor_tensor(out=t2[:], in0=p2[:], in1=St[:, tn * 512:(tn + 1) * 512], op=ALU.mult),
                             reads=[p2k, "St"], writes=[t2k])
                        ni = 512 // dil
                        i0_ = tn * ni
                        if which == 0:
                            dst = QT[:].rearrange("p (r i) -> p i r", r=dil)[:, i0_:i0_ + ni, :]
                            S.op("dve", lambda: nc.vector.tensor_tensor(out=dst, in0=t1[:].rearrange("p (i r) -> p i r", r=dil),
                                                                        in1=t2[:].rearrange("p (i r) -> p i r", r=dil), op=ALU.add),
                                 reads=[t1k, t2k], writes=["QT"])
                        else:
                            bc_ = min(64, ni)
                            ac_ = max(1, ni // 64)
                            a0, b0 = divmod(i0_, 64)
                            for hh in range(2):
                                dst = bass.AP(Kz, hh * 64 * (2 * SEQ) + a0 * 128 + hh * 64 + b0,
                                              [[2 * SEQ, 64], [128, ac_], [1, bc_], [2 * L, dil]])
                                eng = "dve"
                                fn = nc.vector.tensor_tensor
                                S.op(eng, lambda: fn(out=dst, in0=t1[hh * 64:(hh + 1) * 64, :].rearrange("p (a b r) -> p a b r", a=ac_, b=bc_),
                                                     in1=t2[hh * 64:(hh + 1) * 64, :].rearrange("p (a b r) -> p a b r", a=ac_, b=bc_), op=ALU.add),
                                     reads=[t1k, t2k], writes=["Kz"])

                    pend = None
                    for tn in range(8):
                        pj, pjk = proj_tile(w, wk, tn)
                        a, ak = stAr.next()
                        S.op("act", lambda: nc.scalar.activation(out=a[:], in_=pj[:], func=AF.Copy), reads=[pjk], writes=[ak])
                        if pend is not None:
                            post(*pend)
                        pend = (tn, a, ak)
                    post(*pend)
                w, wk = next_w()
                VT, VTk = Ug[g], f"Ug{g}"
                for tn in range(8):
                    pj, pjk = proj_tile(w, wk, tn)
                    S.op("act", lambda: nc.scalar.activation(out=VT[:, tn * 512:(tn + 1) * 512], in_=pj[:], func=AF.Copy),
                         reads=[pjk], writes=[VTk])
                for t0 in range(0, 64, 4):
                    tp, tpk = psr.next()
                    for jj in range(4):
                        t = t0 + jj
                        r, cc = divmod(t, ntile)
                        src = _ap(VT, SEQ, 0, 128, r + dil * 64 * cc, [[dil, 64]])
                        for hh in range(2):
                            S.op("pe", lambda: nc.tensor.matmul(tp[hh * 64:(hh + 1) * 64, jj * 128:(jj + 1) * 128], lhsT=src, rhs=K["ident"][:],
                                                                start=True, stop=True),
                                 reads=[VTk, "k_ident"], writes=[tpk])
                    tpv = tp[:].rearrange("p (j f) -> p j f", j=4)
                    S.op("dve", lambda: nc.vector.tensor_copy(out=Vbd[0:64, t0:t0 + 4, 0:64], in_=tpv[0:64, :, 0:64]),
                         reads=[tpk], writes=["Vbd"])
                    S.op("act", lambda: nc.scalar.activation(out=Vbd[64:128, t0:t0 + 4, 64:128], in_=tpv[64:128, :, 64:128], func=AF.Copy),
                         reads=[tpk], writes=["Vbd"])
                nb = L // 128
                blocks = [(r, b) for r in range(dil) for b in range(nb)]

                def stage1(r, b):
                    lo = 1 if b == 0 else 0
                    hi = 3 if b == nb - 1 else 4
                    s4, s4k = psr.next()
                    qcol = r * L + 128 * b
                    for i4 in range(lo, hi):
                        cc = 2 * b - 1 + i4
                        S.op("pe", lambda: nc.tensor.matmul(s4[:, i4 * 128:(i4 + 1) * 128],
                                                            lhsT=Kz[:, (r * ntile + cc) * 128:(r * ntile + cc + 1) * 128],
                                                            rhs=QT[:, qcol:qcol + 128], start=True, stop=True),
                             reads=["Kz", "QT"], writes=[s4k])
                    e, ek = Etr.next()
                    S.op("act", lambda: nc.scalar.activation(out=e[:, lo * 128:hi * 128], in_=s4[:, lo * 128:hi * 128],
                                                             func=AF.Exp, scale=0.125), reads=[s4k], writes=[ek])
                    pt_, ptk = PTr.next()
                    S.op("dve", lambda: nc.vector.tensor_tensor(out=pt_[:, lo * 128:hi * 128], in0=e[:, lo * 128:hi * 128],
                                                                in1=K["mask4"][:, lo * 128:hi * 128], op=ALU.mult),
                         reads=[ek, "k_mask4"], writes=[ptk])
                    return (r, b, lo, hi, pt_, ptk)

                def stage2(r, b, lo, hi, pt_, ptk):
                    ud, udk = psr.next()
                    for i4 in range(lo, hi):
                        cc = 2 * b - 1 + i4
                        S.op("pe", lambda: nc.tensor.matmul(ud[:, 0:128], lhsT=Vbd[:, r * ntile + cc, :], rhs=pt_[:, i4 * 128:(i4 + 1) * 128],
                                                            start=(i4 == lo), stop=(i4 == hi - 1)),
                             reads=["Vbd", ptk], writes=[udk])
                    for i4 in range(lo, hi):
                        S.op("pe", lambda: nc.tensor.matmul(ud[:, 128:256], lhsT=K["bdones"][:], rhs=pt_[:, i4 * 128:(i4 + 1) * 128],
                                                            start=(i4 == lo), stop=(i4 == hi - 1)),
                             reads=["k_bdones", ptk], writes=[udk])
                    tok0 = r + dil * 128 * b
                    udst = _ap(Ug[g], SEQ, 0, 128, tok0, [[dil, 128]])
                    S.op("act", lambda: nc.scalar.activation(out=udst, in_=ud[:, 0:128], func=AF.Copy), reads=[udk], writes=[f"Ug{g}"])
                    ddst = _ap(Dacc, SEQ, 0, 128, tok0, [[dil, 128]])
                    if g == 0:
                        S.op("dve", lambda: nc.vector.tensor_copy(out=ddst, in_=ud[:, 128:256]), reads=[udk], writes=["Dacc"])
                    else:
                        S.op("dve", lambda: nc.vector.tensor_tensor(out=ddst, in0=ud[:, 128:256], in1=ddst, op=ALU.add),
                             reads=[udk, "Dacc"], writes=["Dacc"])

                pend = stage1(*blocks[0])
                for bi in range(len(blocks)):
                    nxt = stage1(*blocks[bi + 1]) if bi + 1 < len(blocks) else None
                    stage2(*pend)
                    pend = nxt
            S.op("dve", lambda: nc.vector.reciprocal(out=Rinv[:], in_=Dacc[:]), reads=["Dacc"], writes=["Dacc"])
            for g in range(3):
                j = g * 4 + s
                w, wk = next_w()
                for tn in range(8):
                    pj, pjk = proj_tile(w, wk, tn)
                    a, ak = stAr.next()
                    S.op("act", lambda: nc.scalar.activation(out=a[:], in_=pj[:], func=AF.Silu), reads=[pjk], writes=[ak])
                    sl = slice(tn * 512, (tn + 1) * 512)
                    tf, tfk = tmpfr.next()
                    S.op("dve", lambda: nc.vector.tensor_tensor(out=tf[:], in0=Ug[g][:, sl], in1=Rinv[:, sl], op=ALU.mult),
                         reads=[f"Ug{g}", "Dacc"], writes=[tfk])
                    o, ok = Ostr.next()
                    S.op("dve", lambda: nc.vector.tensor_tensor(out=o[:], in0=tf[:], in1=a[:], op=ALU.mult),
                         reads=[tfk, ak], writes=[ok])
                    S.dma("sp", o_d[j * 128:(j + 1) * 128, sl], o[:], reads=[ok], writes=["o_d"])
        S.barrier()

    if stop_after == "C":
        S.finish()
        return nc, dbg

    with contextlib.ExitStack() as Dx:
        wout = sb(Dx, "wout", [128, 12, D], BF16)
        wstg = [sb(Dx, f"wstgD{i}", [128, 1024], F32) for i in range(2)]
        wstgr = Rot(wstg, "wstgD")
        for j in range(12):
            load_cast(wstgr, wout[:, j, :], "wout", awout_d[j * 128:(j + 1) * 128, :], [128, 1024])
        OTs = [sb(Dx, f"OT{i}", [128, 12, 512], BF16) for i in range(2)]
        OTr = Rot(OTs, "OT")
        xts = [sb(Dx, f"xt{i}", [128, D], F32) for i in range(2)]
        xr = Rot(xts, "xt")
        ytm = [sb(Dx, f"ytm{i}", [128, D], F32) for i in range(2)]
        ytr = Rot(ytm, "ytm")
        x1s = [sb(Dx, f"x1t{i}", [128, D], F32) for i in range(2)]
        x1r = Rot(x1s, "x1t")
        o_v = o_d.rearrange("(j p) t -> p j t", p=128)
        for tb in range(8):
            ot, otk = OTr.next()
            S.dma("sp", ot[:], o_v[:, :, tb * 512:(tb + 1) * 512], reads=["o_d"], writes=[otk])
            for t4 in range(4):
                tt = tb * 4 + t4
                xt, xk = xr.next()
                S.dma("sp", xt[:], x_d[tt * 128:(tt + 1) * 128, :], writes=[xk])
                yt, ytk = ytr.next()
                for n in range(2):
                    py, pyk = psr.next()
                    for j in range(12):
                        S.op("pe", lambda: nc.tensor.matmul(py[:], lhsT=ot[:, j, t4 * 128:(t4 + 1) * 128], rhs=wout[:, j, n * 512:(n + 1) * 512],
                                                            start=(j == 0), stop=(j == 11)),
                             reads=[otk, "wout"], writes=[pyk])
                    S.op("dve", lambda: nc.vector.tensor_tensor(out=yt[:, n * 512:(n + 1) * 512], in0=py[:], in1=gate_b[0][:, n * 512:(n + 1) * 512],
                                                                op=ALU.mult), reads=[pyk, "gate_b0"], writes=[ytk])
                x1, x1k = x1r.next()
                S.op("dve", lambda: nc.vector.tensor_tensor(out=x1[:], in0=yt[:], in1=xt[:], op=ALU.add), reads=[ytk, xk], writes=[x1k])
                S.dma("sp", x1_d[tt * 128:(tt + 1) * 128, :], x1[:], reads=[x1k], writes=["x1_d"])
                norm_tile(x1, x1k, tt, 1)
        S.barrier()

    if stop_after == "D":
        hn_dbg = nc.dram_tensor("hn_dbg", [128, 8, SEQ], BF16, kind="ExternalOutput").ap()
        dbg["hn_dbg"] = hn_dbg
        for kc in range(8):
            S.dma("sp", hn_dbg[:, kc, :], hnT[:, kc, :], reads=HN)
        S.finish()
        return nc, dbg

    sz_d = scratch("sz_d", [SEQ, SSD_INNER], BF16)
    xs_d = scratch("xs_d", [SEQ, SSD_INNER], BF16)
    bt_d = scratch("bt_d", [SEQ, 1024], BF16)
    bT_d = scratch("bT_d", [1024, SEQ], BF16)
    cT_d = scratch("cT_d", [1024, SEQ], BF16)
    pb_d = scratch("pb_d", [NT, 128, SSD_INNER], BF16)
    dt_d = scratch("dt_d", [NT, 128, 64], F32)
    swv = swin_d.rearrange("(kc p) n -> p kc n", p=128)

    with contextlib.ExitStack() as E:
        dtb_b = sb(E, "dtb_b", [128, 64], F32)
        S.dma("sp", dtb_b[:], dtb_d.partition_broadcast(128), writes=["dtb_b"])
        with contextlib.ExitStack() as E1:
            wz = sb(E1, "wz", [128, 8, SSD_INNER], BF16)
            wdt = sb(E1, "wdt", [128, 8, 64], BF16)
            wstg1 = [sb(E1, f"wstgE1{i}", [128, 1024], F32) for i in range(2)]
            wstg1r = Rot(wstg1, "wstgE1")
            for kc in range(8):
                for hh in range(2):
                    load_cast(wstg1r, wz[:, kc, hh * 1024:(hh + 1) * 1024], f"wz{kc}", swin_d[kc * 128:(kc + 1) * 128, hh * 1024:(hh + 1) * 1024], [128, 1024])
            load_cast(wstg1r, wdt[:], "wdt", swv[:, :, 6144:6208], [128, 8, 64])
            szts = [sb(E1, f"szt{i}", [128, SSD_INNER], BF16) for i in range(2)]
            sztr = Rot(szts, "szt")
            dtt = [sb(E1, f"dtt{i}", [128, 64], F32) for i in range(2)]
            dttr = Rot(dtt, "dtt")
            dto = [sb(E1, f"dto{i}", [128, 64], F32) for i in range(2)]
            dtor = Rot(dto, "dto")
            for tt in range(NT):
                szt, sztk = sztr.next()
                for n in range(4):
                    pz, pzk = psr.next()
                    for kc in range(8):
                        S.op("pe", lambda: nc.tensor.matmul(pz[:], lhsT=hnT[:, kc, tt * 128:(tt + 1) * 128], rhs=wz[:, kc, n * 512:(n + 1) * 512],
                                                            start=(kc == 0), stop=(kc == 7)), reads=[f"hnT{kc}", f"wz{kc}"], writes=[pzk])
                    S.op("act", lambda: nc.scalar.activation(out=szt[:, n * 512:(n + 1) * 512], in_=pz[:], func=AF.Silu), reads=[pzk], writes=[sztk])
                S.dma("sp", sz_d[tt * 128:(tt + 1) * 128, :], szt[:], reads=[sztk], writes=["sz_d"])
                pd, pdk = psr.next()
                for kc in range(8):
                    S.op("pe", lambda: nc.tensor.matmul(pd[:, 0:64], lhsT=hnT[:, kc, tt * 128:(tt + 1) * 128], rhs=wdt[:, kc, :],
                                                        start=(kc == 0), stop=(kc == 7)), reads=[f"hnT{kc}", "wdt"], writes=[pdk])
                dtmp, dtk = dttr.next()
                S.op("dve", lambda: nc.vector.tensor_tensor(out=dtmp[:], in0=pd[:, 0:64], in1=dtb_b[:], op=ALU.add), reads=[pdk, "dtb_b"], writes=[dtk])
                S.op("act", lambda: nc.scalar.activation(out=dtmp[:], in_=dtmp[:], func=AF.Exp), reads=[dtk], writes=[dtk])
                dto_, dtok = dtor.next()
                S.op("act", lambda: nc.scalar.activation(out=dto_[:], in_=dtmp[:], func=AF.Ln, bias=onec[:, 0:1], scale=1.0),
                     reads=[dtk, "onec"], writes=[dtok])
                S.dma("sp", dt_d[tt], dto_[:], reads=[dtok], writes=["dt_d"])
            S.barrier()
        raws = [sb(E, f"raw{i}", [128, SEQ + 4], F32) for i in range(2)]
        for i in range(2):
            S.op("dve", lambda: nc.vector.memset(raws[i][:], 0.0), writes=[f"raw{i}"])
        rawr = Rot(raws, "raw")
        accs = [sb(E, f"acc{i}", [128, 1024], F32) for i in range(3)]
        accr = Rot(accs, "acc")
        xos = [sb(E, f"xo{i}", [128, SEQ], BF16) for i in range(4)]
        xor_ = Rot(xos, "xo")
        xtoks = [sb(E, f"xtok{i}", [128, 8, 128], BF16) for i in range(2)]
        xtokr = Rot(xtoks, "xtok")
        wxs = [sb(E, f"wx{i}", [128, 8, 128], BF16) for i in range(3)]
        wstgE = [sb(E, f"wstgE{i}", [128, 1024], F32) for i in range(2)]
        wstgEr = Rot(wstgE, "wstgE")
        wxr = Rot(wxs, "wx")
        xs_v = xs_d.rearrange("(t p) c -> p t c", p=128)
        bt_v = bt_d.rearrange("(t p) c -> p t c", p=128)
        def load_wx(cc_):
            wx_, wxk_ = wxr.next()
            load_cast(wstgEr, wx_[:], wxk_, swv[:, :, SSD_INNER + cc_ * 128:SSD_INNER + (cc_ + 1) * 128], [128, 8, 128])
            return wx_, wxk_

        pend_tok = []

        def to_tok(cc_, xo_, xok_):
            for tb in range(4):
                tp, tpk = ptr.next()
                for jj in range(8):
                    tt = tb * 8 + jj
                    S.op("pe", lambda: nc.tensor.transpose(tp[:, jj, :], xo_[:, tt * 128:(tt + 1) * 128], K["ident"][:]),
                         reads=[xok_, "k_ident"], writes=[tpk])
                xtok, xtokk = xtokr.next()
                S.op("act", lambda: nc.scalar.activation(out=xtok[:], in_=tp[:], func=AF.Copy), reads=[tpk], writes=[xtokk])
                if cc_ < 16:
                    S.dma("sp", xs_v[:, tb * 8:(tb + 1) * 8, cc_ * 128:(cc_ + 1) * 128], xtok[:], reads=[xtokk], writes=["xs_d"])
                else:
                    S.dma("sp", bt_v[:, tb * 8:(tb + 1) * 8, (cc_ - 16) * 128:(cc_ - 15) * 128], xtok[:], reads=[xtokk], writes=["bt_d"])

        wxq = {0: load_wx(0), 1: load_wx(1)}
        def e_proj(cc):
            wx, wxk = wxq.pop(cc)
            if cc + 2 < 32:
                wxq[cc + 2] = load_wx(cc + 2)
            raw, rawk = rawr.next()
            for tn in range(8):
                pj, pjk = psr.next()
                for kc in range(8):
                    S.op("pe", lambda: nc.tensor.matmul(pj[:], lhsT=wx[:, kc, :], rhs=hnT[:, kc, tn * 512:(tn + 1) * 512],
                                                        start=(kc == 0), stop=(kc == 7)), reads=[wxk, f"hnT{kc}"], writes=[pjk])
                S.op("act", lambda: nc.scalar.activation(out=raw[:, 2 + tn * 512:2 + (tn + 1) * 512], in_=pj[:], func=AF.Copy),
                     reads=[pjk], writes=[rawk])
            return raw, rawk

        def e_conv(cc, raw, rawk):
            xo, xok = xor_.next()
            for cb in range(4):
                acc, acck = accr.next()
                c0 = cb * 1024
                S.op("dve", lambda: nc.vector.tensor_scalar(out=acc[:], in0=raw[:, c0:c0 + 1024], scalar1=pcol[:, PC_CW + cc:PC_CW + cc + 1],
                                                            scalar2=None, op0=ALU.mult), reads=[rawk, "pcol"], writes=[acck])
                for k in range(1, 5):
                    S.op("dve", lambda: nc.vector.scalar_tensor_tensor(out=acc[:], in0=raw[:, c0 + k:c0 + k + 1024],
                                                                       scalar=pcol[:, PC_CW + k * 32 + cc:PC_CW + k * 32 + cc + 1],
                                                                       in1=acc[:], op0=ALU.mult, op1=ALU.add),
                         reads=[rawk, "pcol", acck], writes=[acck])
                S.op("act", lambda: nc.scalar.activation(out=xo[:, c0:c0 + 1024], in_=acc[:], func=AF.Silu,
                                                         bias=pcol[:, PC_CB + cc:PC_CB + cc + 1], scale=1.0), reads=[acck, "pcol"], writes=[xok])
            if cc >= 24:
                for hh in range(2):
                    S.dma("sp", cT_d[(cc - 24) * 128:(cc - 23) * 128, hh * 2048:(hh + 1) * 2048], xo[:, hh * 2048:(hh + 1) * 2048], reads=[xok], writes=["cT_d"])
            elif cc >= 16:
                for hh in range(2):
                    S.dma("sp", bT_d[(cc - 16) * 128:(cc - 15) * 128, hh * 2048:(hh + 1) * 2048], xo[:, hh * 2048:(hh + 1) * 2048], reads=[xok], writes=["bT_d"])
            if cc < 24:
                pend_tok.append((cc, xo, xok))

        prev_raw = None
        for cc in range(32):
            cur = e_proj(cc)
            if prev_raw is not None:
                e_conv(cc - 1, *prev_raw)
            prev_raw = cur
            while len(pend_tok) > 2 or (pend_tok and cc >= 24):
                to_tok(*pend_tok.pop(0))
        e_conv(31, *prev_raw)
        while pend_tok:
            to_tok(*pend_tok.pop(0))
        S.barrier()
    H.close()

    with contextlib.ExitStack() as G:
        arow = sb(G, "arow", [128, 64], F32)
        S.dma("sp", arow[:], alog_d.partition_broadcast(128), writes=["arow"])
        S.op("act", lambda: nc.scalar.activation(out=arow[:], in_=arow[:], func=AF.Exp), reads=["arow"], writes=["arow"])
        S.op("dve", lambda: nc.vector.tensor_scalar(out=arow[:], in0=arow[:], scalar1=-1.0, scalar2=None, op0=ALU.mult), reads=["arow"], writes=["arow"])
        drow = sb(G, "drow", [128, 32], F32)
        S.dma("sp", drow[:], sd_d.partition_broadcast(128), writes=["drow"])
        Dident = sb(G, "Dident", [128, 32, 128], BF16)
        for h in range(32):
            S.op("dve", lambda: nc.vector.tensor_scalar(out=Dident[:, h, :], in0=K["ident"][:], scalar1=drow[:, h:h + 1], scalar2=None, op0=ALU.mult),
                 reads=["k_ident", "drow"], writes=["Dident"])
        fnw_b = sb(G, "fnw_b", [128, D], F32)
        S.dma("sp", fnw_b[:], fnw_d.partition_broadcast(128), writes=["fnw_b"])
        ones128 = sb(G, "ones128", [128, 128], F32)
        S.op("dve", lambda: nc.vector.memset(ones128[:], 1.0), writes=["ones128"])
        ones1 = sb(G, "ones1", [2, 128], BF16)
        S.op("dve", lambda: nc.vector.memset(ones1[:], 1.0), writes=["ones1"])
        wout1 = sb(G, "wout1", [128, 16, D], BF16)
        with contextlib.ExitStack() as G0:
            wstg = [sb(G0, f"wstgG{i}", [128, 1024], F32) for i in range(2)]
            wstgr = Rot(wstg, "wstgG")
            for j in range(16):
                stg, stgk = wstgr.next()
                S.dma("sp", stg[:], swout_d[j * 128:(j + 1) * 128, :], writes=[stgk])
                S.op("dve", lambda: nc.vector.tensor_scalar(out=wout1[:, j, :], in0=stg[:], scalar1=pcol[:, PC_SNW + j:PC_SNW + j + 1], scalar2=None, op0=ALU.mult),
                     reads=[stgk, "pcol"], writes=["wout1"])
            S.barrier()
        carry = sb(G, "carry", [128, SSD_INNER], F32)
        xts_ = [sb(G, f"xc{i}", [128, SSD_INNER], BF16) for i in range(2)]
        xcr = Rot(xts_, "xc")
        bts_ = [sb(G, f"bc{i}", [128, 1024], BF16) for i in range(2)]
        bcr = Rot(bts_, "bc")
        dtcs = [sb(G, f"dtc{i}", [128, 64], F32) for i in range(4)]
        dtcr = Rot(dtcs, "dtc")
        das = [sb(G, f"da{i}", [128, 64], F32) for i in range(2)]
        dar = Rot(das, "da")
        scs = [sb(G, f"sc{i}", [128, 128], F32) for i in range(2)]
        scr_ = Rot(scs, "sc")
        smalls = [sb(G, f"sm{i}", [128, 6, 64], F32) for i in range(3)]
        smr = Rot(smalls, "sm")
        xws = [sb(G, f"xw{i}", [128, SSD_INNER], BF16) for i in range(1)]
        xwr = Rot(xws, "xw")
        cbfs = [sb(G, f"cbf{i}", [128, SSD_INNER], BF16) for i in range(1)]
        cbfr = Rot(cbfs, "cbf")

        def load_dt(c):
            dtc, dtck = dtcr.next()
            S.dma("sp", dtc[:], dt_d[c], reads=["dt_d"], writes=[dtck])
            return dtc, dtck

        def chunk_scalars(c, need_b_only, pre=None):
            dtc, dtck = pre if pre is not None else load_dt(c)
            da, dak = dar.next()
            S.op("dve", lambda: nc.vector.tensor_tensor(out=da[:], in0=dtc[:], in1=arow[:], op=ALU.mult), reads=[dtck, "arow"], writes=[dak])
            p0, p0k = psr.next()
            S.op("pe", lambda: nc.tensor.matmul(p0[:, 0:32], lhsT=K["triL"][:], rhs=da[:, 0:32], start=True, stop=True), reads=["k_triL", dak], writes=[p0k])
            S.op("pe", lambda: nc.tensor.matmul(p0[:, 32:64], lhsT=K["triU"][:], rhs=da[:, 32:64], start=True, stop=True), reads=["k_triU", dak], writes=[p0k])
            S.op("pe", lambda: nc.tensor.matmul(p0[:, 64:128], lhsT=ones128[:], rhs=da[:, 0:64], start=True, stop=True), reads=["ones128", dak], writes=[p0k])
            if not need_b_only:
                S.op("pe", lambda: nc.tensor.matmul(p0[0:32, 128:256], lhsT=da[:, 0:32], rhs=K["triL"][:], start=True, stop=True), reads=["k_triL", dak], writes=[p0k])
                S.op("pe", lambda: nc.tensor.matmul(p0[32:64, 128:256], lhsT=da[:, 32:64], rhs=K["triU"][:], start=True, stop=True), reads=["k_triU", dak], writes=[p0k])
            sc, sck = scr_.next()
            S.op("act", lambda: nc.scalar.activation(out=sc[:], in_=p0[:, 0:128], func=AF.Copy), reads=[p0k], writes=[sck])
            sm, smk = smr.next()
            S.op("dve", lambda: nc.vector.tensor_scalar(out=sm[:, 0, :], in0=sc[:, 0:64], scalar1=-1.0, scalar2=None, op0=ALU.mult), reads=[sck], writes=[smk])
            S.op("act", lambda: nc.scalar.activation(out=sm[:, 1, :], in_=sc[:, 0:64], func=AF.Exp), reads=[sck], writes=[smk])
            S.op("dve", lambda: nc.vector.tensor_tensor(out=sm[:, 4, :], in0=sc[:, 64:128], in1=sc[:, 0:64], op=ALU.subtract), reads=[sck], writes=[smk])
            S.op("act", lambda: nc.scalar.activation(out=sm[:, 5, :], in_=sm[:, 4, :], func=AF.Exp), reads=[smk], writes=[smk])
            S.op("dve", lambda: nc.vector.tensor_tensor(out=sm[:, 2, :], in0=sm[:, 5, :], in1=dtc[:], op=ALU.mult), reads=[smk, dtck], writes=[smk])
            S.op("act", lambda: nc.scalar.activation(out=sm[:, 3, :], in_=sc[:, 64:128], func=AF.Exp), reads=[sck], writes=[smk])
            return sm, smk, p0, p0k, dtc, dtck

        def state_update(c, xc, xck, bc, bck, sm, smk, d):
            xw, xwk = xwr.next()
            S.op("dve", lambda: nc.vector.tensor_tensor(out=xw[:].rearrange("p (h q) -> p h q", q=64), in0=xc[:].rearrange("p (h q) -> p h q", q=64),
                                                        in1=sm[:, 2, d * 32:(d + 1) * 32].unsqueeze(2).to_broadcast([128, 32, 64]), op=ALU.mult),
                 reads=[xck, smk], writes=[xwk])
            S.op("dve", lambda: nc.vector.tensor_tensor(out=carry[:].rearrange("p (h q) -> p h q", q=64), in0=carry[:].rearrange("p (h q) -> p h q", q=64),
                                                        in1=sm[:, 3, d * 32:(d + 1) * 32].unsqueeze(2).to_broadcast([128, 32, 64]), op=ALU.mult),
                 reads=["carry", smk], writes=["carry"])
            for q4 in range(4):
                pS, pSk = psr.next()
                for g2 in range(2):
                    g = q4 * 2 + g2
                    S.op("pe", lambda: nc.tensor.matmul(pS[:, g2 * 256:(g2 + 1) * 256], lhsT=bc[:, g * 128:(g + 1) * 128], rhs=xw[:, g * 256:(g + 1) * 256],
                                                        start=True, stop=True), reads=[bck, xwk], writes=[pSk])
                S.op("dve", lambda: nc.vector.tensor_tensor(out=carry[:, q4 * 512:(q4 + 1) * 512], in0=pS[:], in1=carry[:, q4 * 512:(q4 + 1) * 512], op=ALU.add),
                     reads=[pSk, "carry"], writes=["carry"])

        S.op("dve", lambda: nc.vector.memset(carry[:], 0.0), writes=["carry"])
        def f_prep(c):
            xc, xck = xcr.next()
            S.dma("sp", xc[:], xs_d[c * 128:(c + 1) * 128, :], reads=["xs_d"], writes=[xck])
            bc, bck = bcr.next()
            S.dma("sp", bc[:], bt_d[c * 128:(c + 1) * 128, :], reads=["bt_d"], writes=[bck])
            sm, smk, _, _, _, _ = chunk_scalars(c, True)
            return xc, xck, bc, bck, sm, smk

        fp_ = f_prep(NT - 1)
        for c in range(NT - 1, -1, -1):
            nfp = f_prep(c - 1) if c > 0 else None
            cbf, cbfk = cbfr.next()
            S.op("act", lambda: nc.scalar.activation(out=cbf[:], in_=carry[:], func=AF.Copy), reads=["carry"], writes=[cbfk])
            S.dma("sp", pb_d[c], cbf[:], reads=[cbfk], writes=["pb_d"])
            state_update(c, *fp_, 1)
            fp_ = nfp

        if stop_after == "F":
            S.barrier()
            S.finish()
            return nc, dbg

        S.op("dve", lambda: nc.vector.memset(carry[:], 0.0), writes=["carry"])
        BTs = [sb(G, f"BT{i}", [128, 8, 128], BF16) for i in range(1)]
        BTr = Rot(BTs, "BT")
        CTs = [sb(G, f"CT{i}", [128, 8, 128], BF16) for i in range(2)]
        CTr = Rot(CTs, "CT")
        szs = [sb(G, f"szc{i}", [128, SSD_INNER], BF16) for i in range(1)]
        szr = Rot(szs, "szc")
        pbs = [sb(G, f"pbc{i}", [128, SSD_INNER], BF16) for i in range(1)]
        pbr = Rot(pbs, "pbc")
        hls = [sb(G, f"hl{i}", [64, 2, 128], BF16) for i in range(3)]
        hlr = Rot(hls, "hl")
        rows = [sb(G, f"row{i}", [2, 8 * 128], BF16) for i in range(2)]
        rowr = Rot(rows, "row")
        Lts = [sb(G, f"Lt{i}", [128, 32, 128], BF16) for i in range(4)]
        Ltr = Rot(Lts, "Lt")
        CBs = [sb(G, f"CBs{i}", [128, 8, 128], BF16) for i in range(1)]
        CBr = Rot(CBs, "CBs")
        xdts = [sb(G, f"xdt{i}", [128, SSD_INNER], BF16) for i in range(2)]
        ycs = [sb(G, f"yc{i}", [128, 512], F32) for i in range(1)]
        ycr = Rot(ycs, "yc")
        yts = [sb(G, f"ytmp{i}", [128, 512], F32) for i in range(1)]
        ytr = Rot(yts, "ytmp")
        yg = sb(G, "yg", [128, SSD_INNER], F32)
        ynb = sb(G, "ynb", [128, SSD_INNER], BF16)
        ynT = sb(G, "ynT", [128, 16, 128], BF16)
        nst2 = [sb(G, f"nst2{i}", [128, 8], F32) for i in range(2)]
        nst2r = Rot(nst2, "nst2")
        x2s = [sb(G, f"x2{i}", [128, D], F32) for i in range(2)]
        x2r = Rot(x2s, "x2")
        outs = [sb(G, f"ot{i}", [128, D], F32) for i in range(2)]
        outr = Rot(outs, "ot")
        bT_v = bT_d.rearrange("(g p) t -> p g t", p=128)
        cT_v = cT_d.rearrange("(g p) t -> p g t", p=128)
        ident4 = K["ident"][:].unsqueeze(1).to_broadcast([128, 4, 128])
        def g_loads(c):
            tsl = slice(c * 128, (c + 1) * 128)
            xc, xck = xcr.next()
            S.dma("sp", xc[:], xs_d[tsl, :], reads=["xs_d"], writes=[xck])
            bc, bck = bcr.next()
            S.dma("sp", bc[:], bt_d[tsl, :], reads=["bt_d"], writes=[bck])
            BT, BTk = BTr.next()
            S.dma("sp", BT[:], bT_v[:, :, tsl], reads=["bT_d"], writes=[BTk])
            CT, CTk = CTr.next()
            S.dma("sp", CT[:], cT_v[:, :, tsl], reads=["cT_d"], writes=[CTk])
            dt_ = load_dt(c)
            return (tsl, xc, xck, bc, bck, BT, BTk, CT, CTk, dt_)

        def g_scal(c, dt_):
            sm, smk, p0, p0k, dtc, dtck = chunk_scalars(c, False, dt_)
            hl, hlk = hlr.next()
            S.op("act", lambda: nc.scalar.activation(out=hl[:, 0, :], in_=p0[0:64, 128:256], func=AF.Copy), reads=[p0k], writes=[hlk])
            S.op("dve", lambda: nc.vector.tensor_tensor(out=hl[:, 1, :], in0=p0[0:64, 128:256], in1=hl[:, 0, :], op=ALU.subtract), reads=[p0k, hlk], writes=[hlk])
            return (sm, smk, dtc, dtck, hl, hlk)

        def g_head(c, L, SC):
            tsl, xc, xck, bc, bck, BT, BTk, CT, CTk, dt_ = L
            sm, smk, dtc, dtck, hl, hlk = SC
            CB, CBk = CBr.next()
            for gh in range(2):
                pcb, pcbk = psr.next()
                for g4 in range(4):
                    g = gh * 4 + g4
                    S.op("pe", lambda: nc.tensor.matmul(pcb[:, g4 * 128:(g4 + 1) * 128], lhsT=BT[:, g, :], rhs=CT[:, g, :], start=True, stop=True),
                         reads=[BTk, CTk], writes=[pcbk])
                S.op("act", lambda: nc.scalar.activation(out=CB[:, gh * 4:(gh + 1) * 4, :].rearrange("p g l -> p (g l)"), in_=pcb[:], func=AF.Copy),
                     reads=[pcbk], writes=[CBk])
            Ms = []
            tasks = []
            mtasks = []
            rowbox = {}
            for d in range(2):
                Lt, Ltk = Ltr.next()
                Ms.append((Lt, Ltk))
                for r16 in range(4):
                    for hb4 in range(2):
                        def task(d=d, r16=r16, hb4=hb4, Lt=Lt, Ltk=Ltk):
                            nm = K["nmaskf"] if d == 0 else K["nmaskb"]
                            nmk = "k_nmaskf" if d == 0 else "k_nmaskb"
                            if hb4 == 0:
                                row, rowk = rowr.next()
                                p0_ = d * 32 + r16 * 8
                                for j_ in range(2):
                                    S.dma("sp", row[j_:j_ + 1, :].rearrange("o (p f) -> o p f", p=8), hl[p0_:p0_ + 8, j_, :],
                                          reads=[hlk], writes=[rowk])
                                rowbox[(d, r16)] = (row, rowk)
                            row, rowk = rowbox[(d, r16)]
                            hb = r16 * 2 + hb4
                            lp, lpk = psr.next()
                            S.op("pe", lambda: nc.tensor.matmul(lp[:], lhsT=ones1[:], rhs=row[:, hb4 * 512:(hb4 + 1) * 512], start=True, stop=False),
                                 reads=["ones1", rowk], writes=[lpk])
                            S.op("pe", lambda: nc.tensor.matmul(lp[:], lhsT=nm[:], rhs=ident4, start=False, stop=True), reads=[nmk, "k_ident"], writes=[lpk])
                            for j4 in range(4):
                                h = hb * 4 + j4
                                S.op("act", lambda: nc.scalar.activation(out=Lt[:, h, :], in_=lp[:, j4 * 128:(j4 + 1) * 128], func=AF.Exp,
                                                                         bias=sm[:, 0, d * 32 + h:d * 32 + h + 1], scale=1.0), reads=[lpk, smk], writes=[Ltk])
                        tasks.append(task)

                        def mtask(d=d, r16=r16, hb4=hb4, Lt=Lt, Ltk=Ltk):
                            hb = r16 * 2 + hb4
                            S.op("dve", lambda: nc.vector.tensor_tensor(out=Lt[:, hb * 4:(hb + 1) * 4, :], in0=Lt[:, hb * 4:(hb + 1) * 4, :],
                                                                        in1=CB[:, hb:hb + 1, :].to_broadcast([128, 4, 128]), op=ALU.mult),
                                 reads=[Ltk, CBk], writes=[Ltk])
                        mtasks.append(mtask)

            def fin():
                for d in range(2):
                    S.op("dve", lambda: nc.vector.tensor_tensor(out=xdts[d][:].rearrange("p (h q) -> p h q", q=64), in0=xc[:].rearrange("p (h q) -> p h q", q=64),
                                                                in1=dtc[:, d * 32:(d + 1) * 32].unsqueeze(2).to_broadcast([128, 32, 64]), op=ALU.mult),
                         reads=[xck, dtck], writes=[f"xdt{d}"])

            def late_loads():
                szc, szk = szr.next()
                S.dma("sp", szc[:], sz_d[tsl, :], reads=["sz_d"], writes=[szk])
                pbc, pbk = pbr.next()
                S.dma("sp", pbc[:], pb_d[c], reads=["pb_d"], writes=[pbk])
                x2, x2k = x2r.next()
                S.dma("sp", x2[:], x1_d[tsl, :], reads=["x1_d"], writes=[x2k])
                X.update(szc=szc, szk=szk, pbc=pbc, pbk=pbk, x2=x2, x2k=x2k)

            X = dict(c=c, tsl=tsl, xc=xc, xck=xck, bc=bc, bck=bck, CT=CT, CTk=CTk,
                     sm=sm, smk=smk, Ms=Ms, tasks=tasks, mtasks=mtasks, fin=fin, late_loads=late_loads)
            return X

        def g_mid(X, run):
            c, tsl, xc, xck, bc, bck, CT, CTk = X["c"], X["tsl"], X["xc"], X["xck"], X["bc"], X["bck"], X["CT"], X["CTk"]
            szc, szk, pbc, pbk, sm, smk, Ms = X["szc"], X["szk"], X["pbc"], X["pbk"], X["sm"], X["smk"], X["Ms"]
            cbf, cbfk = X["cbf"], X["cbfk"]
            for q4 in range(4):
                csl = slice(q4 * 512, (q4 + 1) * 512)
                pyd, pydk = psr.next()
                for h8 in range(8):
                    h = q4 * 8 + h8
                    for d in range(2):
                        S.op("pe", lambda: nc.tensor.matmul(pyd[:, h8 * 64:(h8 + 1) * 64], lhsT=Ms[d][0][:, h, :], rhs=xdts[d][:, h * 64:(h + 1) * 64],
                                                            start=(d == 0), stop=False), reads=[Ms[d][1], f"xdt{d}"], writes=[pydk])
                    S.op("pe", lambda: nc.tensor.matmul(pyd[:, h8 * 64:(h8 + 1) * 64], lhsT=Dident[:, h, :], rhs=xc[:, h * 64:(h + 1) * 64],
                                                        start=False, stop=True), reads=["Dident", xck], writes=[pydk])
                pof, pofk = psr.next()
                pob, pobk = psr.next()
                for g2 in range(2):
                    g = q4 * 2 + g2
                    S.op("pe", lambda: nc.tensor.matmul(pof[:, g2 * 256:(g2 + 1) * 256], lhsT=CT[:, g, :], rhs=cbf[:, g * 256:(g + 1) * 256], start=True, stop=True),
                         reads=[CTk, cbfk], writes=[pofk])
                    S.op("pe", lambda: nc.tensor.matmul(pob[:, g2 * 256:(g2 + 1) * 256], lhsT=CT[:, g, :], rhs=pbc[:, g * 256:(g + 1) * 256], start=True, stop=True),
                         reads=[CTk, pbk], writes=[pobk])
                yc, yck = ycr.next()
                yt_, ytk = ytr.next()
                v3 = lambda t_: t_[:].rearrange("p (h q) -> p h q", q=64)
                ef = sm[:, 1, q4 * 8:q4 * 8 + 8].unsqueeze(2).to_broadcast([128, 8, 64])
                eb = sm[:, 1, 32 + q4 * 8:32 + q4 * 8 + 8].unsqueeze(2).to_broadcast([128, 8, 64])
                S.op("dve", lambda: nc.vector.tensor_tensor(out=v3(yc), in0=pof[:].rearrange("p (h q) -> p h q", q=64), in1=ef, op=ALU.mult), reads=[pofk, smk], writes=[yck])
                S.op("dve", lambda: nc.vector.tensor_tensor(out=v3(yt_), in0=pob[:].rearrange("p (h q) -> p h q", q=64), in1=eb, op=ALU.mult), reads=[pobk, smk], writes=[ytk])
                S.op("dve", lambda: nc.vector.tensor_tensor(out=yc[:], in0=yc[:], in1=yt_[:], op=ALU.add), reads=[yck, ytk], writes=[yck])
                S.op("dve", lambda: nc.vector.tensor_tensor(out=yc[:], in0=pyd[:], in1=yc[:], op=ALU.add), reads=[pydk, yck], writes=[yck])
                S.op("dve", lambda: nc.vector.tensor_tensor(out=yg[:, csl], in0=yc[:], in1=szc[:, csl], op=ALU.mult), reads=[yck, szk], writes=["yg"])
                run(3)
            state_update(c, xc, xck, bc, bck, sm, smk, 0)

        def g_tail(X):
            c, tsl, x2, x2k = X["c"], X["tsl"], X["x2"], X["x2k"]
            ns, nsk = nst2r.next()
            S.op("act", lambda: nc.scalar.activation(out=ynb[:], in_=yg[:], func=AF.Square, accum_out=ns[:, 0:1]), reads=["yg"], writes=["ynb", nsk])
            S.op("act", lambda: nc.scalar.activation(out=ns[:, 1:2], in_=ns[:, 0:1], func=AF.Sqrt, bias=epsc[:, 0:1], scale=1.0 / SSD_INNER), reads=[nsk, "epsc"], writes=[nsk])
            S.op("dve", lambda: nc.vector.reciprocal(out=ns[:, 2:3], in_=ns[:, 1:2]), reads=[nsk], writes=[nsk])
            S.op("dve", lambda: nc.vector.tensor_scalar(out=ynb[:], in0=yg[:], scalar1=ns[:, 2:3], scalar2=None, op0=ALU.mult),
                 reads=["yg", nsk], writes=["ynb"])
            for jh in range(2):
                tp, tpk = ptr.next()
                for jj in range(8):
                    j = jh * 8 + jj
                    S.op("pe", lambda: nc.tensor.transpose(tp[:, jj, :], ynb[:, j * 128:(j + 1) * 128], K["ident"][:]), reads=["ynb", "k_ident"], writes=[tpk])
                S.op("act", lambda: nc.scalar.activation(out=ynT[:, jh * 8:(jh + 1) * 8, :], in_=tp[:], func=AF.Copy), reads=[tpk], writes=["ynT"])
            y2, y2k = yg, "yg"
            for n in range(2):
                po, pok = psr.next()
                for j in range(16):
                    S.op("pe", lambda: nc.tensor.matmul(po[:], lhsT=ynT[:, j, :], rhs=wout1[:, j, n * 512:(n + 1) * 512], start=(j == 0), stop=(j == 15)),
                         reads=["ynT", "wout1"], writes=[pok])
                S.op("dve", lambda: nc.vector.tensor_tensor(out=y2[:, n * 512:(n + 1) * 512], in0=po[:], in1=gate_b[1][:, n * 512:(n + 1) * 512], op=ALU.mult),
                     reads=[pok, "gate_b1"], writes=[y2k])
            S.op("dve", lambda: nc.vector.tensor_tensor(out=x2[:], in0=y2[:, 0:D], in1=x2[:], op=ALU.add), reads=[y2k, x2k], writes=[x2k])
            ns, nsk = nst2r.next()
            S.op("act", lambda: nc.scalar.activation(out=nrm_junk[:], in_=x2[:], func=AF.Square, accum_out=ns[:, 0:1]), reads=[x2k], writes=["nrm_junk", nsk])
            S.op("act", lambda: nc.scalar.activation(out=ns[:, 1:2], in_=ns[:, 0:1], func=AF.Sqrt, bias=epsc[:, 0:1], scale=1.0 / D), reads=[nsk, "epsc"], writes=[nsk])
            S.op("dve", lambda: nc.vector.reciprocal(out=ns[:, 2:3], in_=ns[:, 1:2]), reads=[nsk], writes=[nsk])
            ot, otk = outr.next()
            S.op("dve", lambda: nc.vector.scalar_tensor_tensor(out=ot[:], in0=x2[:], scalar=ns[:, 2:3], in1=fnw_b[:], op0=ALU.mult, op1=ALU.mult),
                 reads=[x2k, nsk, "fnw_b"], writes=[otk])
            X["store"] = lambda: S.dma("sp", out_d[tsl, :], ot[:], reads=[otk], writes=["out_d"])

        def make_runner(X):
            tl = list(X["tasks"]) if X is not None else []
            ml = list(X["mtasks"]) if X is not None else []
            done = [0]

            def run(n):
                for _ in range(n):
                    if tl:
                        tl.pop(0)()
                        done[0] += 1
                        if done[0] > 2 and ml:
                            ml.pop(0)()

            def flush():
                run(len(tl))
                while ml:
                    ml.pop(0)()
            return run, flush

        Ld = {0: g_loads(0)}
        Sc = {0: g_scal(0, Ld[0][-1])}
        ctx = g_head(0, Ld.pop(0), Sc.pop(0))
        run, flush = make_runner(ctx)
        flush()
        ctx["fin"]()
        ctx["late_loads"]()
        if NT > 1:
            Ld[1] = g_loads(1)
            Sc[1] = g_scal(1, Ld[1][-1])
        pend_store = None
        for c in range(NT):
            cbf, cbfk = cbfr.next()
            S.op("act", lambda: nc.scalar.activation(out=cbf[:], in_=carry[:], func=AF.Copy), reads=["carry"], writes=[cbfk])
            ctx["cbf"], ctx["cbfk"] = cbf, cbfk
            nxt = g_head(c + 1, Ld.pop(c + 1), Sc.pop(c + 1)) if c + 1 < NT else None
            run, flush = make_runner(nxt)
            g_mid(ctx, run)
            flush()
            if c + 2 < NT:
                Ld[c + 2] = g_loads(c + 2)
                Sc[c + 2] = g_scal(c + 2, Ld[c + 2][-1])
            if nxt is not None:
                nxt["late_loads"]()
            if pend_store is not None:
                pend_store()
            g_tail(ctx)
            pend_store = ctx["store"]
            if nxt is not None:
                nxt["fin"]()
            ctx = nxt
        pend_store()
        S.barrier()
    S.finish()
    return nc, dbg


def make_in_maps(inputs):
    cst = host_consts()
    pcol = host_pcol(inputs)
    f = lambda a: np.ascontiguousarray(np.asarray(a), dtype=np.float32)
    shared = {
        "mod_w": f(inputs["mod_w"]), "mod_b": f(inputs["mod_b"]),
        "attn_w_in": f(inputs["attn_w_in"][0]), "attn_w_out": f(inputs["attn_w_out"][0]),
        "ssd_w_in": f(inputs["ssd_w_in"][0]),
        "ssd_dt_bias": f(inputs["ssd_dt_bias"]).reshape(1, 64), "ssd_a_log": f(inputs["ssd_a_log"]).reshape(1, 64),
        "ssd_d": f(inputs["ssd_d"]).reshape(1, 32), "ssd_norm_w": f(inputs["ssd_norm_w"]).reshape(1, SSD_INNER),
        "ssd_w_out": f(inputs["ssd_w_out"][0]), "final_norm_w": f(inputs["final_norm_w"]).reshape(1, D),
        "pcol": pcol,
    }
    for k, v in cst.items():
        shared["k_" + k] = v
    x = f(inputs["x"])
    c = f(inputs["c"])
    pos = np.ascontiguousarray(np.asarray(inputs["positions"]), dtype=np.int32)
    maps = []
    for b in range(x.shape[0]):
        m = dict(shared)
        m["x"] = x[b]
        m["c"] = np.ascontiguousarray(c[b].reshape(8, 128).T)
        m["pos"] = pos[b:b + 1]
        maps.append(m)
    return maps


def kernel(**inputs):
    nc, _ = build()
    maps = make_in_maps(inputs)
    res = run_bass_kernel_spmd(nc, maps, core_ids=list(range(8)))
    return np.stack([np.asarray(r["out"], dtype=np.float32) for r in res.results], axis=0)
```

```python
import contextlib
import math
import numpy as np
import ml_dtypes
import concourse.bass as bass
import concourse.mybir as mybir
from concourse.bass_utils import run_bass_kernel_spmd

F32 = mybir.dt.float32
BF16 = mybir.dt.bfloat16
I32 = mybir.dt.int32
AF = mybir.ActivationFunctionType
ALU = mybir.AluOpType

D = 1024
SEQ = 4096
NT = SEQ // 128
AW = 1536
PATTERNS = ((128, 1), (512, 4), (2048, 16))
SSD_INNER = 2048
SSD_IN = 6208
EPS = 1e-6
import os
KSTEP = int(os.environ.get('KSTEP', '9'))
KPOOL = int(os.environ.get('KPOOL', '0'))


class Sched:
    NDMA = 10

    def __init__(self, nc, es, needed=None):
        self.nc = nc
        self.es = es
        self.needed = needed
        self.record = set()
        self.engs = {"pe": nc.tensor, "dve": nc.vector, "act": nc.scalar,
                     "pool": nc.gpsimd, "sp": nc.sync}
        self.sem = {}
        self.raw = {}
        self.pub = {}
        for e in self.engs:
            self.sem[e] = self.es.enter_context(nc.semaphore("s_" + e))
            self.raw[e] = 0
            self.pub[e] = 0
        self.dsem, self.dcnt, self.dnext = {}, {}, {}
        for q in ("sp",):
            self.dsem[q] = [self.es.enter_context(nc.semaphore(f"d_{q}{i}")) for i in range(self.NDMA)]
            self.dcnt[q] = [0] * self.NDMA
            self.dnext[q] = 0
        self.seen = {e: {} for e in self.engs}
        self.lastw = {}
        self.readers = {}
        self.ninstr = 0
        self.nwaits = 0

    def _wait(self, e, ev):
        sem, val, src, rid = ev
        if src == "pe" and e == "pe":
            return
        k = id(sem)
        if self.seen[e].get(k, 0) >= val:
            return
        if rid is not None:
            self.record.add(rid)
        self.engs[e].wait_ge(sem, val)
        self.seen[e][k] = val
        self.nwaits += 1

    def _deps(self, e, reads, writes):
        for k in reads:
            ev = self.lastw.get(k)
            if ev is not None:
                self._wait(e, ev)
            if k[:2] in ("ps", "pt"):
                for ev in self.readers.get(k, ()):
                    self._wait(e, ev)
        for k in writes:
            ev = self.lastw.get(k)
            if ev is not None:
                self._wait(e, ev)
            for ev in self.readers.get(k, ()):
                self._wait(e, ev)

    def _commit(self, ev, reads, writes):
        for k in reads:
            self.readers.setdefault(k, []).append(ev)
        for k in writes:
            self.lastw[k] = ev
            self.readers[k] = []

    def op(self, e, fn, reads=(), writes=()):
        self._deps(e, reads, writes)
        ins = fn()
        self.raw[e] += 1
        rid = (e, self.raw[e])
        if self.needed is None or rid in self.needed:
            self.pub[e] += 1
            ins.then_inc(self.sem[e], 1)
            ev = (self.sem[e], self.pub[e], e, rid)
        else:
            ev = (self.sem[e], self.pub[e] + 1, e, rid)
        self._commit(ev, reads, writes)
        self.ninstr += 1
        return ev

    def dma(self, q, out, in_, reads=(), writes=()):
        i = self.dnext[q]
        self.dnext[q] = (i + 1) % self.NDMA
        sem = self.dsem[q][i]
        if self.dcnt[q][i] > 0:
            self._wait(q, (sem, self.dcnt[q][i], None, None))
        self._deps(q, reads, writes)
        ins = self.engs[q].dma_start(out=out, in_=in_)
        self.dcnt[q][i] += 16
        ins.then_inc(sem, 16)
        ev = (sem, self.dcnt[q][i], None, None)
        self._commit(ev, reads, writes)
        self.ninstr += 1
        return ev

    def barrier(self):
        evs = []
        for e in self.engs:
            if self.raw[e] > 0:
                rid = (e, self.raw[e])
                pubd = self.needed is None or rid in self.needed
                evs.append((self.sem[e], self.pub[e] if pubd else self.pub[e] + 1, e, rid))
        for q in self.dsem:
            for i, sem in enumerate(self.dsem[q]):
                if self.dcnt[q][i] > 0:
                    evs.append((sem, self.dcnt[q][i], None, None))
        for e in self.engs:
            for ev in evs:
                if ev[2] == e:
                    continue
                self._wait(e, ev)
        self.lastw = {}
        self.readers = {}

    def finish(self):
        for q in self.dsem:
            for i, sem in enumerate(self.dsem[q]):
                if self.dcnt[q][i] > 0:
                    self._wait("sp", (sem, self.dcnt[q][i], None, None))


class Rot:
    def __init__(self, tiles, name):
        self.tiles = tiles
        self.name = name
        self.i = -1

    def next(self):
        self.i = (self.i + 1) % len(self.tiles)
        return self.tiles[self.i], f"{self.name}{self.i}"


def _ap(t, rowsize, p0, npart, col, dims):
    return bass.AP(t, p0 * rowsize + col, [[rowsize, npart]] + [list(d) for d in dims])


def _bf(a):
    return np.asarray(a, dtype=np.float32).astype(ml_dtypes.bfloat16)


def host_consts():
    c = {}
    p = np.arange(128)
    c["ident"] = _bf(np.eye(128))
    c["bdones"] = _bf((p[:, None] // 64) == (p[None, :] // 64))
    pm = np.zeros((128, 128), np.float32)
    for hb in (0, 64):
        for j in range(8):
            pm[hb + j + 8, hb + j] = 1.0
            pm[hb + j, hb + 8 + j] = 1.0
    c["pswap"] = _bf(pm)
    i = (p % 64)[:, None, None]
    i4 = np.arange(4)[None, :, None]
    jq = np.arange(128)[None, None, :]
    c["mask4"] = _bf(np.abs(64 * (i4 - 1) + i - jq) <= 64).reshape(128, 512)
    k = p[:, None]
    l = p[None, :]
    c["triL"] = (k <= l).astype(np.float32)
    c["triU"] = (k >= l).astype(np.float32)
    c["nmaskf"] = _bf(np.where(l <= k, 0.0, -30000.0))
    c["nmaskb"] = _bf(np.where(l >= k, 0.0, -30000.0))
    c["maskf"] = _bf(k <= l)
    c["maskb"] = _bf(k >= l)
    return c


def host_pcol(inp):
    cols = {}
    p = np.arange(128)
    inv = (500000.0 ** (-np.arange(0, 16, 2, dtype=np.float32) / 16.0)).astype(np.float32)
    pp = p % 64
    invf = np.where(pp < 16, inv[pp % 8], 0.0).astype(np.float32)
    sg = np.where(pp < 8, -1.0, np.where(pp < 16, 1.0, 0.0)).astype(np.float32)
    parts = [invf[:, None], sg[:, None]]
    parts.append(np.asarray(inp["norm_w"], np.float32).reshape(2, 8, 128).transpose(2, 0, 1).reshape(128, 16))
    parts.append(np.asarray(inp["mod_b"], np.float32).reshape(2, 24, 128).transpose(2, 0, 1).reshape(128, 48))
    parts.append(np.asarray(inp["ssd_conv_w"], np.float32).reshape(5, 32, 128).transpose(2, 0, 1).reshape(128, 160))
    parts.append(np.asarray(inp["ssd_conv_b"], np.float32).reshape(32, 128).T)
    parts.append(np.asarray(inp["ssd_norm_w"], np.float32).reshape(16, 128).T)
    return np.ascontiguousarray(np.concatenate(parts, axis=1), dtype=np.float32)


PC_INVF, PC_SG, PC_NW, PC_MODB, PC_CW, PC_CB, PC_SNW, PC_N = 0, 1, 2, 18, 66, 226, 258, 274


def build(stop_after=None):
    M = contextlib.ExitStack()
    rec = {}
    _build(M, stop_after, None, rec)
    M.close()
    M = contextlib.ExitStack()
    nc, dbg = _build(M, stop_after, rec["needed"], {})
    M.close()
    return nc, dbg


def _build(M, stop_after=None, needed=None, rec=None):
    nc = bass.Bass("TRN2", target_bir_lowering=False)
    S = Sched(nc, M, needed)
    rec["needed"] = S.record
    dbg = {}

    def din(name, shape, dt=F32):
        return nc.dram_tensor(name, list(shape), dt, kind="ExternalInput").ap()

    x_d = din("x", [SEQ, D])
    c_d = din("c", [128, 8])
    pos_d = din("pos", [1, SEQ], I32)
    modw_d = din("mod_w", [2, D, 3 * D])
    modb_d = din("mod_b", [2, 3 * D])
    awin_d = din("attn_w_in", [D, 4 * AW])
    awout_d = din("attn_w_out", [AW, D])
    swin_d = din("ssd_w_in", [D, SSD_IN])
    dtb_d = din("ssd_dt_bias", [1, 64])
    alog_d = din("ssd_a_log", [1, 64])
    sd_d = din("ssd_d", [1, 32])
    snw_d = din("ssd_norm_w", [1, SSD_INNER])
    swout_d = din("ssd_w_out", [SSD_INNER, D])
    fnw_d = din("final_norm_w", [1, D])
    pcol_d = din("pcol", [128, PC_N])
    cst = host_consts()
    cst_d = {k: din("k_" + k, list(v.shape), BF16 if v.dtype != np.float32 else F32) for k, v in cst.items()}
    out_d = nc.dram_tensor("out", [SEQ, D], F32, kind="ExternalOutput").ap()

    def scratch(name, shape, dt):
        kind = "ExternalOutput" if stop_after is not None else "Internal"
        t = nc.dram_tensor(name, list(shape), dt, kind=kind).ap()
        dbg[name] = t
        return t

    o_d = scratch("o_d", [AW, SEQ], BF16)
    x1_d = scratch("x1_d", [SEQ, D], F32)

    P = M

    uid = [0]

    def sb(stack, name, shape, dt):
        uid[0] += 1
        return stack.enter_context(nc.sbuf_tensor(f"sb{uid[0]}_{name}", list(shape), dt))

    ps = [P.enter_context(nc.psum_tensor(f"ps{i}", [128, 512], F32)) for i in range(6)]
    pt = [P.enter_context(nc.psum_tensor(f"pt{i}", [128, 8, 128], BF16)) for i in range(2)]
    psr = Rot(ps, "ps")
    ptr = Rot(pt, "pt")

    def load_cast(stg_rot, dst, dkey, src, shape):
        stg, stgk = stg_rot.next()
        n = 1
        for d_ in shape[1:]:
            n *= d_
        sv = stg[:, 0:n]
        if len(shape) == 3:
            sv = sv.rearrange("p (a b) -> p a b", a=shape[1])
        S.dma("sp", sv, src, writes=[stgk])
        S.op("act", lambda: nc.scalar.activation(out=dst, in_=sv, func=AF.Copy), reads=[stgk], writes=[dkey])

    pcol = sb(P, "pcol", [128, PC_N], F32)
    S.dma("sp", pcol[:], pcol_d, writes=["pcol"])
    K = {}
    for k, v in cst.items():
        K[k] = sb(P, "k_" + k, list(v.shape), BF16 if v.dtype != np.float32 else F32)
        S.dma("sp", K[k][:], cst_d[k], writes=["k_" + k])
    if stop_after is not None:
        junk = sb(P, "junk", [1, 16], F32)
        junki = sb(P, "junki", [1, 16], I32)
        for t_ in (awin_d, awout_d, swin_d, dtb_d, alog_d, sd_d, snw_d, swout_d, fnw_d):
            S.dma("sp", junk[:], t_[0:1, 0:16], writes=["junk"])
        S.dma("sp", junki[:], pos_d[0:1, 0:16], writes=["junki"])
    epsc = sb(P, "epsc", [128, 1], F32)
    S.op("dve", lambda: nc.vector.memset(epsc[:], EPS), writes=["epsc"])
    one11 = sb(P, "one11", [1, 1], F32)
    S.op("dve", lambda: nc.vector.memset(one11[:], 1.0), writes=["one11"])
    modA = sb(P, "modA", [128, 16], F32)
    modB = sb(P, "modB", [128, 16], F32)
    gate_b = [sb(P, f"gate_b{i}", [128, D], F32) for i in range(2)]
    nrm_junk = sb(P, "nrm_junk", [128, D], BF16)
    xn_t = [sb(P, f"xn{i}", [128, D], BF16) for i in range(2)]
    st_t = [sb(P, f"nst{i}", [128, 4], F32) for i in range(2)]
    onec = sb(P, "onec", [128, 1], F32)
    S.op("dve", lambda: nc.vector.memset(onec[:], 1.0), writes=["onec"])
    H = contextlib.ExitStack()
    M.enter_context(H)
    hnT = sb(H, "hnT", [128, 8, SEQ], BF16)

    with contextlib.ExitStack() as A:
        c_fm = sb(A, "c_fm", [128, 8], F32)
        S.dma("sp", c_fm[:], c_d, writes=["c_fm"])
        cond_f = sb(A, "cond_f", [128, 8], F32)
        cond_b = sb(A, "cond_b", [128, 8], BF16)
        condB = sb(A, "condB", [128, 8, 128], F32)
        S.op("act", lambda: nc.scalar.activation(out=cond_f[:], in_=c_fm[:], func=AF.Silu), reads=["c_fm"], writes=["cond_f"])
        if stop_after == "A0":
            dd = nc.dram_tensor("cond_dbg", [128, 8], F32, kind="ExternalOutput").ap()
            S.dma("sp", dd, cond_f[:], reads=["cond_f"])
            S.finish()
            return nc, dbg
        S.op("dve", lambda: nc.vector.tensor_copy(out=cond_b[:], in_=cond_f[:]), reads=["cond_f"], writes=["cond_b"])
        S.op("dve", lambda: nc.vector.tensor_copy(out=condB[:], in_=cond_f[:].unsqueeze(2).to_broadcast([128, 8, 128])),
             reads=["cond_f"], writes=["condB"])
        modw = sb(A, "modw", [128, 8, 3 * D], F32)
        modT = sb(A, "modT", [128, 24], F32)
        gb_bias = sb(A, "gb_bias", [128, D], F32)
        for li in range(2):
            for kc in range(8):
                S.dma("sp", modw[:, kc, :], modw_d[li, kc * 128:(kc + 1) * 128, :], writes=[f"modw{kc}"])
            pm, pmk = psr.next()
            for fc in range(24):
                for kc in range(8):
                    S.op("pe", lambda: nc.tensor.matmul(pm[:, fc:fc + 1], lhsT=modw[:, kc, fc * 128:(fc + 1) * 128],
                                                        rhs=cond_f[:, kc:kc + 1], start=(kc == 0), stop=(kc == 7)),
                         reads=[f"modw{kc}", "cond_f"], writes=[pmk])
            S.op("dve", lambda: nc.vector.tensor_tensor(out=modT[:], in0=pm[:, 0:24],
                                                        in1=pcol[:, PC_MODB + li * 24:PC_MODB + (li + 1) * 24], op=ALU.add),
                 reads=[pmk, "pcol"], writes=["modT"])
            S.op("dve", lambda: nc.vector.tensor_copy(out=modB[:, li * 8:(li + 1) * 8], in_=modT[:, 0:8]),
                 reads=["modT"], writes=["modB"])
            S.op("dve", lambda: nc.vector.scalar_tensor_tensor(out=modA[:, li * 8:(li + 1) * 8], in0=modT[:, 8:16], scalar=1.0,
                                                               in1=pcol[:, PC_NW + li * 8:PC_NW + (li + 1) * 8],
                                                               op0=ALU.add, op1=ALU.mult),
                 reads=["modT", "pcol"], writes=["modA"])
            S.dma("sp", gb_bias[:], modb_d[li:li + 1, 2 * D:3 * D].partition_broadcast(128), writes=["gb_bias"])
            for n in range(2):
                pg, pgk = psr.next()
                for kc in range(8):
                    S.op("pe", lambda: nc.tensor.matmul(pg[:], lhsT=condB[:, kc, :],
                                                        rhs=modw[:, kc, 2 * D + n * 512:2 * D + (n + 1) * 512],
                                                        start=(kc == 0), stop=(kc == 7)),
                         reads=[f"modw{kc}", "condB"], writes=[pgk])
                S.op("dve", lambda: nc.vector.tensor_tensor(out=gate_b[li][:, n * 512:(n + 1) * 512], in0=pg[:],
                                                            in1=gb_bias[:, n * 512:(n + 1) * 512], op=ALU.add),
                     reads=[pgk, "gb_bias"], writes=[f"gate_b{li}"])
        S.barrier()

    if stop_after == "A":
        for nm, t_, shp in (("modA_dbg", modA, [128, 16]), ("modB_dbg", modB, [128, 16]), ("gate0_dbg", gate_b[0], [128, D]), ("gate1_dbg", gate_b[1], [128, D])):
            dd = nc.dram_tensor(nm, shp, F32, kind="ExternalOutput").ap()
            S.dma("sp", dd, t_[:])
        S.finish()
        return nc, dbg

    xnr = Rot(xn_t, "xn")
    str_ = Rot(st_t, "nst")

    def norm_tile(xt, xk, tt, li):
        st, stk = str_.next()
        xn, xnk = xnr.next()
        S.op("act", lambda: nc.scalar.activation(out=nrm_junk[:], in_=xt[:], func=AF.Square, accum_out=st[:, 0:1]),
             reads=[xk], writes=["nrm_junk", stk])
        S.op("act", lambda: nc.scalar.activation(out=st[:, 1:2], in_=st[:, 0:1], func=AF.Sqrt, bias=epsc[:, 0:1], scale=1.0 / D),
             reads=[stk, "epsc"], writes=[stk])
        S.op("dve", lambda: nc.vector.reciprocal(out=st[:, 2:3], in_=st[:, 1:2]), reads=[stk], writes=[stk])
        S.op("dve", lambda: nc.vector.tensor_scalar(out=xn[:], in0=xt[:], scalar1=st[:, 2:3], scalar2=None, op0=ALU.mult),
             reads=[xk, stk], writes=[xnk])
        tp, tpk = ptr.next()
        for kc in range(8):
            S.op("pe", lambda: nc.tensor.transpose(tp[:, kc, :], xn[:, kc * 128:(kc + 1) * 128], K["ident"][:]),
                 reads=[xnk, "k_ident"], writes=[tpk])
        for kc in range(8):
            col = li * 8 + kc
            if True:
                S.op("dve", lambda: nc.vector.tensor_scalar(out=hnT[:, kc, tt * 128:(tt + 1) * 128], in0=tp[:, kc, :],
                                                            scalar1=modA[:, col:col + 1], scalar2=modB[:, col:col + 1],
                                                            op0=ALU.mult, op1=ALU.add),
                     reads=[tpk, "modA", "modB"], writes=[f"hnT{kc}"])
            else:
                S.op("act", lambda: nc.scalar.activation(out=hnT[:, kc, tt * 128:(tt + 1) * 128], in_=tp[:, kc, :],
                                                         func=AF.Identity, scale=modA[:, col:col + 1], bias=modB[:, col:col + 1]),
                     reads=[tpk, "modA", "modB"], writes=[f"hnT{kc}"])

    HN = [f"hnT{kc}" for kc in range(8)]

    with contextlib.ExitStack() as B:
        xts = [sb(B, f"xt{i}", [128, D], F32) for i in range(3)]
        xr = Rot(xts, "xt")
        for tt in range(NT):
            xt, xk = xr.next()
            S.dma("sp", xt[:], x_d[tt * 128:(tt + 1) * 128, :], writes=[xk])
            norm_tile(xt, xk, tt, 0)
        S.barrier()

    if stop_after == "B":
        hn_dbg = nc.dram_tensor("hn_dbg", [128, 8, SEQ], BF16, kind="ExternalOutput").ap()
        dbg["hn_dbg"] = hn_dbg
        for kc in range(8):
            S.dma("sp", hn_dbg[:, kc, :], hnT[:, kc, :], reads=HN)
        S.finish()
        return nc, dbg

    with contextlib.ExitStack() as C:
        Ct = sb(C, "Ct", [128, SEQ], BF16)
        St = sb(C, "St", [128, SEQ], BF16)
        with contextlib.ExitStack() as R:
            RB = 1024
            posi = sb(R, "posi", [128, RB], I32)
            posf = sb(R, "posf", [128, RB], F32)
            ang = sb(R, "ang", [128, RB], F32)
            ki = sb(R, "ki", [128, RB], I32)
            kf = sb(R, "kf", [128, RB], F32)
            for cb in range(SEQ // RB):
                csl = slice(cb * RB, (cb + 1) * RB)
                S.dma("sp", posi[:], pos_d[:, csl].partition_broadcast(128), writes=["posi"])
                S.op("dve", lambda: nc.vector.tensor_copy(out=posf[:], in_=posi[:]), reads=["posi"], writes=["posf"])
                for which, phase in ((0, 0.0), (1, math.pi / 2)):
                    S.op("dve", lambda: nc.vector.tensor_scalar(out=ang[:], in0=posf[:], scalar1=pcol[:, PC_INVF:PC_INVF + 1],
                                                                scalar2=phase, op0=ALU.mult, op1=ALU.add),
                         reads=["posf", "pcol"], writes=["ang"])
                    S.op("dve", lambda: nc.vector.tensor_scalar(out=ki[:], in0=ang[:], scalar1=1.0 / (2 * math.pi), scalar2=None,
                                                                op0=ALU.mult), reads=["ang"], writes=["ki"])
                    S.op("dve", lambda: nc.vector.tensor_copy(out=kf[:], in_=ki[:]), reads=["ki"], writes=["kf"])
                    S.op("dve", lambda: nc.vector.scalar_tensor_tensor(out=ang[:], in0=kf[:], scalar=-2 * math.pi, in1=ang[:],
                                                                       op0=ALU.mult, op1=ALU.add),
                         reads=["kf", "ang"], writes=["ang"])
                    if which == 0:
                        S.op("act", lambda: nc.scalar.activation(out=kf[:], in_=ang[:], func=AF.Sin), reads=["ang"], writes=["kf"])
                        S.op("dve", lambda: nc.vector.tensor_scalar(out=St[:, csl], in0=kf[:], scalar1=pcol[:, PC_SG:PC_SG + 1],
                                                                    scalar2=None, op0=ALU.mult), reads=["kf", "pcol"], writes=["St"])
                    else:
                        S.op("act", lambda: nc.scalar.activation(out=Ct[:, csl], in_=ang[:], func=AF.Sin), reads=["ang"], writes=["Ct"])
            S.barrier()
        def dump(nm, t_, shape, dt):
            dd = nc.dram_tensor(nm, shape, dt, kind="ExternalOutput").ap()
            if len(shape) == 2 and shape[1] * (2 if dt == BF16 else 4) > 32768:
                h = shape[1] // 2
                S.dma("sp", dd[:, 0:h], t_[:, 0:h])
                S.dma("sp", dd[:, h:], t_[:, h:])
            else:
                S.dma("sp", dd, t_[:])

        if stop_after == "Crope":
            dump("Ct_dbg", Ct, [128, SEQ], BF16)
            dump("St_dbg", St, [128, SEQ], BF16)
            S.finish()
            return nc, dbg
        Dacc = sb(C, "Dacc", [128, SEQ], F32)
        Rinv = Dacc
        tmpfs = [sb(C, f"tmpf{i}", [128, 512], F32) for i in range(1)]
        tmpfr = Rot(tmpfs, "tmpf")
        Ug = [sb(C, f"Ug{i}", [128, SEQ], BF16) for i in range(3)]
        QT = sb(C, "QT", [128, SEQ], BF16)
        Kz = sb(C, "Kz", [128, 2 * SEQ], BF16)
        Vbd = sb(C, "Vbd", [128, 64, 128], BF16)
        S.op("dve", lambda: nc.vector.memset(Kz[:], 0.0), writes=["Kz"])
        S.op("dve", lambda: nc.vector.memset(Vbd[:], 0.0), writes=["Vbd"])
        wts = [sb(C, f"wt{i}", [128, 8, 128], BF16) for i in range(3)]
        wstg = [sb(C, f"wstg{i}", [128, 1024], F32) for i in range(2)]
        wstgr = Rot(wstg, "wstg")
        wr = Rot(wts, "wt")
        stA = [sb(C, f"stA{i}", [128, 512], BF16) for i in range(2)]
        stAr = Rot(stA, "stA")
        st1 = [sb(C, f"st1{i}", [128, 512], BF16) for i in range(2)]
        st1r = Rot(st1, "st1")
        st2 = [sb(C, f"st2{i}", [128, 512], BF16) for i in range(2)]
        st2r = Rot(st2, "st2")
        Et = [sb(C, f"Et{i}", [128, 512], BF16) for i in range(2)]
        Etr = Rot(Et, "Et")
        PTt = [sb(C, f"PT{i}", [128, 512], BF16) for i in range(2)]
        PTr = Rot(PTt, "PT")
        Ost = [sb(C, f"Ost{i}", [128, 512], BF16) for i in range(2)]
        Ostr = Rot(Ost, "Ost")
        awv = awin_d.rearrange("(kc p) n -> p kc n", p=128)

        def load_w(col0):
            w, wk = wr.next()
            load_cast(wstgr, w[:], wk, awv[:, :, col0:col0 + 128], [128, 8, 128])
            return w, wk

        def proj_tile(w, wk, tn):
            pj, pjk = psr.next()
            for kc in range(8):
                S.op("pe", lambda: nc.tensor.matmul(pj[:], lhsT=w[:, kc, :], rhs=hnT[:, kc, tn * 512:(tn + 1) * 512],
                                                    start=(kc == 0), stop=(kc == 7)),
                     reads=[wk, f"hnT{kc}"], writes=[pjk])
            return pj, pjk

        jobs = []
        for s in range(4):
            for g in range(3):
                jobs += [(s, g, 0), (s, g, 1), (s, g, 2)]
            for g in range(3):
                jobs.append((s, g, 3))
        wq = {}
        PF = 2

        def prefetch(i):
            if i < len(jobs) and i not in wq:
                s_, g_, wh_ = jobs[i]
                wq[i] = load_w(wh_ * AW + (g_ * 4 + s_) * 128)

        for i in range(PF):
            prefetch(i)
        ji = [0]

        def next_w():
            w, wk = wq.pop(ji[0])
            prefetch(ji[0] + PF)
            ji[0] += 1
            return w, wk

        for s in range(4):
            for g in range(3):
                win, dil = PATTERNS[g]
                L = SEQ // dil
                j = g * 4 + s
                ntile = L // 64
                for which in range(2):
                    w, wk = next_w()

                    def post(tn, a, ak):
                        p2, p2k = psr.next()
                        S.op("pe", lambda: nc.tensor.matmul(p2[:], lhsT=K["pswap"][:], rhs=a[:], start=True, stop=True),
                             reads=["k_pswap", ak], writes=[p2k])
                        t1, t1k = st1r.next()
                        t2, t2k = st2r.next()
                        S.op("dve", lambda: nc.vector.tensor_tensor(out=t1[:], in0=a[:], in1=Ct[:, tn * 512:(tn + 1) * 512], op=ALU.mult),
                             reads=[ak, "Ct"], writes=[t1k])
                        S.op("dve", lambda: nc.vector.tensor_tensor(out=t2[:], in0=p2[:], in1=St[:, tn * 512:(tn + 1) * 512], op=ALU.mult),
                             reads=[p2k, "St"], writes=[t2k])
                        ni = 512 // dil
                        i0_ = tn * ni
                        if which == 0:
                            dst = QT[:].rearrange("p (r i) -> p i r", r=dil)[:, i0_:i0_ + ni, :]
                            S.op("dve", lambda: nc.vector.tensor_tensor(out=dst, in0=t1[:].rearrange("p (i r) -> p i r", r=dil),
                                                                        in1=t2[:].rearrange("p (i r) -> p i r", r=dil), op=ALU.add),
                                 reads=[t1k, t2k], writes=["QT"])
                        else:
                            bc_ = min(64, ni)
                            ac_ = max(1, ni // 64)
                            a0, b0 = divmod(i0_, 64)
                            for hh in range(2):
                                dst = bass.AP(Kz, hh * 64 * (2 * SEQ) + a0 * 128 + hh * 64 + b0,
                                              [[2 * SEQ, 64], [128, ac_], [1, bc_], [2 * L, dil]])
                                eng = "dve"
                                fn = nc.vector.tensor_tensor
                                S.op(eng, lambda: fn(out=dst, in0=t1[hh * 64:(hh + 1) * 64, :].rearrange("p (a b r) -> p a b r", a=ac_, b=bc_),
                                                     in1=t2[hh * 64:(hh + 1) * 64, :].rearrange("p (a b r) -> p a b r", a=ac_, b=bc_), op=ALU.add),
                                     reads=[t1k, t2k], writes=["Kz"])

                    pend = None
                    for tn in range(8):
                        pj, pjk = proj_tile(w, wk, tn)
                        a, ak = stAr.next()
                        S.op("act", lambda: nc.scalar.activation(out=a[:], in_=pj[:], func=AF.Copy), reads=[pjk], writes=[ak])
                        if pend is not None:
                            post(*pend)
                        pend = (tn, a, ak)
                    post(*pend)
                w, wk = next_w()
                VT, VTk = Ug[g], f"Ug{g}"
                for tn in range(8):
                    pj, pjk = proj_tile(w, wk, tn)
                    S.op("act", lambda: nc.scalar.activation(out=VT[:, tn * 512:(tn + 1) * 512], in_=pj[:], func=AF.Copy),
                         reads=[pjk], writes=[VTk])
                for t0 in range(0, 64, 4):
                    tp, tpk = psr.next()
                    for jj in range(4):
                        t = t0 + jj
                        r, cc = divmod(t, ntile)
                        src = _ap(VT, SEQ, 0, 128, r + dil * 64 * cc, [[dil, 64]])
                        for hh in range(2):
                            S.op("pe", lambda: nc.tensor.matmul(tp[hh * 64:(hh + 1) * 64, jj * 128:(jj + 1) * 128], lhsT=src, rhs=K["ident"][:],
                                                                start=True, stop=True),
                                 reads=[VTk, "k_ident"], writes=[tpk])
                    tpv = tp[:].rearrange("p (j f) -> p j f", j=4)
                    S.op("dve", lambda: nc.vector.tensor_copy(out=Vbd[0:64, t0:t0 + 4, 0:64], in_=tpv[0:64, :, 0:64]),
                         reads=[tpk], writes=["Vbd"])
                    S.op("act", lambda: nc.scalar.activation(out=Vbd[64:128, t0:t0 + 4, 64:128], in_=tpv[64:128, :, 64:128], func=AF.Copy),
                         reads=[tpk], writes=["Vbd"])
                nb = L // 128
                blocks = [(r, b) for r in range(dil) for b in range(nb)]

                def stage1(r, b):
                    lo = 1 if b == 0 else 0
                    hi = 3 if b == nb - 1 else 4
                    s4, s4k = psr.next()
                    qcol = r * L + 128 * b
                    for i4 in range(lo, hi):
                        cc = 2 * b - 1 + i4
                        S.op("pe", lambda: nc.tensor.matmul(s4[:, i4 * 128:(i4 + 1) * 128],
                                                            lhsT=Kz[:, (r * ntile + cc) * 128:(r * ntile + cc + 1) * 128],
                                                            rhs=QT[:, qcol:qcol + 128], start=True, stop=True),
                             reads=["Kz", "QT"], writes=[s4k])
                    e, ek = Etr.next()
                    S.op("act", lambda: nc.scalar.activation(out=e[:, lo * 128:hi * 128], in_=s4[:, lo * 128:hi * 128],
                                                             func=AF.Exp, scale=0.125), reads=[s4k], writes=[ek])
                    pt_, ptk = PTr.next()
                    S.op("dve", lambda: nc.vector.tensor_tensor(out=pt_[:, lo * 128:hi * 128], in0=e[:, lo * 128:hi * 128],
                                                                in1=K["mask4"][:, lo * 128:hi * 128], op=ALU.mult),
                         reads=[ek, "k_mask4"], writes=[ptk])
                    return (r, b, lo, hi, pt_, ptk)

                def stage2(r, b, lo, hi, pt_, ptk):
                    ud, udk = psr.next()
                    for i4 in range(lo, hi):
                        cc = 2 * b - 1 + i4
                        S.op("pe", lambda: nc.tensor.matmul(ud[:, 0:128], lhsT=Vbd[:, r * ntile + cc, :], rhs=pt_[:, i4 * 128:(i4 + 1) * 128],
                                                            start=(i4 == lo), stop=(i4 == hi - 1)),
                             reads=["Vbd", ptk], writes=[udk])
                    for i4 in range(lo, hi):
                        S.op("pe", lambda: nc.tensor.matmul(ud[:, 128:256], lhsT=K["bdones"][:], rhs=pt_[:, i4 * 128:(i4 + 1) * 128],
                                                            start=(i4 == lo), stop=(i4 == hi - 1)),
                             reads=["k_bdones", ptk], writes=[udk])
                    tok0 = r + dil * 128 * b
                    udst = _ap(Ug[g], SEQ, 0, 128, tok0, [[dil, 128]])
                    S.op("act", lambda: nc.scalar.activation(out=udst, in_=ud[:, 0:128], func=AF.Copy), reads=[udk], writes=[f"Ug{g}"])
                    ddst = _ap(Dacc, SEQ, 0, 128, tok0, [[dil, 128]])
                    if g == 0:
                        S.op("dve", lambda: nc.vector.tensor_copy(out=ddst, in_=ud[:, 128:256]), reads=[udk], writes=["Dacc"])
                    else:
                        S.op("dve", lambda: nc.vector.tensor_tensor(out=ddst, in0=ud[:, 128:256], in1=ddst, op=ALU.add),
                             reads=[udk, "Dacc"], writes=["Dacc"])

                pend = stage1(*blocks[0])
                for bi in range(len(blocks)):
                    nxt = stage1(*blocks[bi + 1]) if bi + 1 < len(blocks) else None
                    stage2(*pend)
                    pend = nxt
            S.op("dve", lambda: nc.vector.reciprocal(out=Rinv[:], in_=Dacc[:]), reads=["Dacc"], writes=["Dacc"])
            for g in range(3):
                j = g * 4 + s
                w, wk = next_w()
                for tn in range(8):
                    pj, pjk = proj_tile(w, wk, tn)
                    a, ak = stAr.next()
                    S.op("act", lambda: nc.scalar.activation(out=a[:], in_=pj[:], func=AF.Silu), reads=[pjk], writes=[ak])
                    sl = slice(tn * 512, (tn + 1) * 512)
                    tf, tfk = tmpfr.next()
                    S.op("dve", lambda: nc.vector.tensor_tensor(out=tf[:], in0=Ug[g][:, sl], in1=Rinv[:, sl], op=ALU.mult),
                         reads=[f"Ug{g}", "Dacc"], writes=[tfk])
                    o, ok = Ostr.next()
                    S.op("dve", lambda: nc.vector.tensor_tensor(out=o[:], in0=tf[:], in1=a[:], op=ALU.mult),
                         reads=[tfk, ak], writes=[ok])
                    S.dma("sp", o_d[j * 128:(j + 1) * 128, sl], o[:], reads=[ok], writes=["o_d"])
        S.barrier()

    if stop_after == "C":
        S.finish()
        return nc, dbg

    with contextlib.ExitStack() as Dx:
        wout = sb(Dx, "wout", [128, 12, D], BF16)
        wstg = [sb(Dx, f"wstgD{i}", [128, 1024], F32) for i in range(2)]
        wstgr = Rot(wstg, "wstgD")
        for j in range(12):
            load_cast(wstgr, wout[:, j, :], "wout", awout_d[j * 128:(j + 1) * 128, :], [128, 1024])
        OTs = [sb(Dx, f"OT{i}", [128, 12, 512], BF16) for i in range(2)]
        OTr = Rot(OTs, "OT")
        xts = [sb(Dx, f"xt{i}", [128, D], F32) for i in range(2)]
        xr = Rot(xts, "xt")
        ytm = [sb(Dx, f"ytm{i}", [128, D], F32) for i in range(2)]
        ytr = Rot(ytm, "ytm")
        x1s = [sb(Dx, f"x1t{i}", [128, D], F32) for i in range(2)]
        x1r = Rot(x1s, "x1t")
        o_v = o_d.rearrange("(j p) t -> p j t", p=128)
        for tb in range(8):
            ot, otk = OTr.next()
            S.dma("sp", ot[:], o_v[:, :, tb * 512:(tb + 1) * 512], reads=["o_d"], writes=[otk])
            for t4 in range(4):
                tt = tb * 4 + t4
                xt, xk = xr.next()
                S.dma("sp", xt[:], x_d[tt * 128:(tt + 1) * 128, :], writes=[xk])
                yt, ytk = ytr.next()
                for n in range(2):
                    py, pyk = psr.next()
                    for j in range(12):
                        S.op("pe", lambda: nc.tensor.matmul(py[:], lhsT=ot[:, j, t4 * 128:(t4 + 1) * 128], rhs=wout[:, j, n * 512:(n + 1) * 512],
                                                            start=(j == 0), stop=(j == 11)),
                             reads=[otk, "wout"], writes=[pyk])
                    S.op("dve", lambda: nc.vector.tensor_tensor(out=yt[:, n * 512:(n + 1) * 512], in0=py[:], in1=gate_b[0][:, n * 512:(n + 1) * 512],
                                                                op=ALU.mult), reads=[pyk, "gate_b0"], writes=[ytk])
                x1, x1k = x1r.next()
                S.op("dve", lambda: nc.vector.tensor_tensor(out=x1[:], in0=yt[:], in1=xt[:], op=ALU.add), reads=[ytk, xk], writes=[x1k])
                S.dma("sp", x1_d[tt * 128:(tt + 1) * 128, :], x1[:], reads=[x1k], writes=["x1_d"])
                norm_tile(x1, x1k, tt, 1)
        S.barrier()

    if stop_after == "D":
        hn_dbg = nc.dram_tensor("hn_dbg", [128, 8, SEQ], BF16, kind="ExternalOutput").ap()
        dbg["hn_dbg"] = hn_dbg
        for kc in range(8):
            S.dma("sp", hn_dbg[:, kc, :], hnT[:, kc, :], reads=HN)
        S.finish()
        return nc, dbg

    sz_d = scratch("sz_d", [SEQ, SSD_INNER], BF16)
    xs_d = scratch("xs_d", [SEQ, SSD_INNER], BF16)
    bt_d = scratch("bt_d", [SEQ, 1024], BF16)
    bT_d = scratch("bT_d", [1024, SEQ], BF16)
    cT_d = scratch("cT_d", [1024, SEQ], BF16)
    pb_d = scratch("pb_d", [NT, 128, SSD_INNER], BF16)
    dt_d = scratch("dt_d", [NT, 128, 64], F32)
    swv = swin_d.rearrange("(kc p) n -> p kc n", p=128)

    with contextlib.ExitStack() as E:
        dtb_b = sb(E, "dtb_b", [128, 64], F32)
        S.dma("sp", dtb_b[:], dtb_d.partition_broadcast(128), writes=["dtb_b"])
        with contextlib.ExitStack() as E1:
            wz = sb(E1, "wz", [128, 8, SSD_INNER], BF16)
            wdt = sb(E1, "wdt", [128, 8, 64], BF16)
            wstg1 = [sb(E1, f"wstgE1{i}", [128, 1024], F32) for i in range(2)]
            wstg1r = Rot(wstg1, "wstgE1")
            for kc in range(8):
                for hh in range(2):
                    load_cast(wstg1r, wz[:, kc, hh * 1024:(hh + 1) * 1024], f"wz{kc}", swin_d[kc * 128:(kc + 1) * 128, hh * 1024:(hh + 1) * 1024], [128, 1024])
            load_cast(wstg1r, wdt[:], "wdt", swv[:, :, 6144:6208], [128, 8, 64])
            szts = [sb(E1, f"szt{i}", [128, SSD_INNER], BF16) for i in range(2)]
            sztr = Rot(szts, "szt")
            dtt = [sb(E1, f"dtt{i}", [128, 64], F32) for i in range(2)]
            dttr = Rot(dtt, "dtt")
            dto = [sb(E1, f"dto{i}", [128, 64], F32) for i in range(2)]
            dtor = Rot(dto, "dto")
            for tt in range(NT):
                szt, sztk = sztr.next()
                for n in range(4):
                    pz, pzk = psr.next()
                    for kc in range(8):
                        S.op("pe", lambda: nc.tensor.matmul(pz[:], lhsT=hnT[:, kc, tt * 128:(tt + 1) * 128], rhs=wz[:, kc, n * 512:(n + 1) * 512],
                                                            start=(kc == 0), stop=(kc == 7)), reads=[f"hnT{kc}", f"wz{kc}"], writes=[pzk])
                    S.op("act", lambda: nc.scalar.activation(out=szt[:, n * 512:(n + 1) * 512], in_=pz[:], func=AF.Silu), reads=[pzk], writes=[sztk])
                S.dma("sp", sz_d[tt * 128:(tt + 1) * 128, :], szt[:], reads=[sztk], writes=["sz_d"])
            for tt in range(NT):
                pd, pdk = psr.next()
                for kc in range(8):
                    S.op("pe", lambda: nc.tensor.matmul(pd[:, 0:64], lhsT=hnT[:, kc, tt * 128:(tt + 1) * 128], rhs=wdt[:, kc, :],
                                                        start=(kc == 0), stop=(kc == 7)), reads=[f"hnT{kc}", "wdt"], writes=[pdk])
                dtmp, dtk = dttr.next()
                S.op("dve", lambda: nc.vector.tensor_tensor(out=dtmp[:], in0=pd[:, 0:64], in1=dtb_b[:], op=ALU.add), reads=[pdk, "dtb_b"], writes=[dtk])
                S.op("act", lambda: nc.scalar.activation(out=dtmp[:], in_=dtmp[:], func=AF.Exp), reads=[dtk], writes=[dtk])
                dto_, dtok = dtor.next()
                S.op("act", lambda: nc.scalar.activation(out=dto_[:], in_=dtmp[:], func=AF.Ln, bias=onec[:, 0:1], scale=1.0),
                     reads=[dtk, "onec"], writes=[dtok])
                S.dma("sp", dt_d[tt], dto_[:], reads=[dtok], writes=["dt_d"])
            S.barrier()
        raws = [sb(E, f"raw{i}", [128, SEQ + 4], F32) for i in range(2)]
        for i in range(2):
            S.op("dve", lambda: nc.vector.memset(raws[i][:], 0.0), writes=[f"raw{i}"])
        rawr = Rot(raws, "raw")
        accs = [sb(E, f"acc{i}", [128, 1024], F32) for i in range(3)]
        accr = Rot(accs, "acc")
        xos = [sb(E, f"xo{i}", [128, SEQ], BF16) for i in range(4)]
        xor_ = Rot(xos, "xo")
        xtoks = [sb(E, f"xtok{i}", [128, 8, 128], BF16) for i in range(2)]
        xtokr = Rot(xtoks, "xtok")
        wxs = [sb(E, f"wx{i}", [128, 8, 128], BF16) for i in range(3)]
        wstgE = [sb(E, f"wstgE{i}", [128, 1024], F32) for i in range(2)]
        wstgEr = Rot(wstgE, "wstgE")
        wxr = Rot(wxs, "wx")
        xs_v = xs_d.rearrange("(t p) c -> p t c", p=128)
        bt_v = bt_d.rearrange("(t p) c -> p t c", p=128)
        def load_wx(cc_):
            wx_, wxk_ = wxr.next()
            load_cast(wstgEr, wx_[:], wxk_, swv[:, :, SSD_INNER + cc_ * 128:SSD_INNER + (cc_ + 1) * 128], [128, 8, 128])
            return wx_, wxk_

        pend_tok = []

        def to_tok(cc_, xo_, xok_):
            for tb in range(4):
                tp, tpk = ptr.next()
                for jj in range(8):
                    tt = tb * 8 + jj
                    S.op("pe", lambda: nc.tensor.transpose(tp[:, jj, :], xo_[:, tt * 128:(tt + 1) * 128], K["ident"][:]),
                         reads=[xok_, "k_ident"], writes=[tpk])
                xtok, xtokk = xtokr.next()
                S.op("act", lambda: nc.scalar.activation(out=xtok[:], in_=tp[:], func=AF.Copy), reads=[tpk], writes=[xtokk])
                if cc_ < 16:
                    S.dma("sp", xs_v[:, tb * 8:(tb + 1) * 8, cc_ * 128:(cc_ + 1) * 128], xtok[:], reads=[xtokk], writes=["xs_d"])
                else:
                    S.dma("sp", bt_v[:, tb * 8:(tb + 1) * 8, (cc_ - 16) * 128:(cc_ - 15) * 128], xtok[:], reads=[xtokk], writes=["bt_d"])

        wxq = {0: load_wx(0), 1: load_wx(1)}
        def e_proj(cc):
            wx, wxk = wxq.pop(cc)
            if cc + 2 < 32:
                wxq[cc + 2] = load_wx(cc + 2)
            raw, rawk = rawr.next()
            for tn in range(8):
                pj, pjk = psr.next()
                for kc in range(8):
                    S.op("pe", lambda: nc.tensor.matmul(pj[:], lhsT=wx[:, kc, :], rhs=hnT[:, kc, tn * 512:(tn + 1) * 512],
                                                        start=(kc == 0), stop=(kc == 7)), reads=[wxk, f"hnT{kc}"], writes=[pjk])
                S.op("act", lambda: nc.scalar.activation(out=raw[:, 2 + tn * 512:2 + (tn + 1) * 512], in_=pj[:], func=AF.Copy),
                     reads=[pjk], writes=[rawk])
            return raw, rawk

        def e_conv(cc, raw, rawk):
            xo, xok = xor_.next()
            for cb in range(4):
                acc, acck = accr.next()
                c0 = cb * 1024
                S.op("dve", lambda: nc.vector.tensor_scalar(out=acc[:], in0=raw[:, c0:c0 + 1024], scalar1=pcol[:, PC_CW + cc:PC_CW + cc + 1],
                                                            scalar2=None, op0=ALU.mult), reads=[rawk, "pcol"], writes=[acck])
                for k in range(1, 5):
                    S.op("dve", lambda: nc.vector.scalar_tensor_tensor(out=acc[:], in0=raw[:, c0 + k:c0 + k + 1024],
                                                                       scalar=pcol[:, PC_CW + k * 32 + cc:PC_CW + k * 32 + cc + 1],
                                                                       in1=acc[:], op0=ALU.mult, op1=ALU.add),
                         reads=[rawk, "pcol", acck], writes=[acck])
                S.op("act", lambda: nc.scalar.activation(out=xo[:, c0:c0 + 1024], in_=acc[:], func=AF.Silu,
                                                         bias=pcol[:, PC_CB + cc:PC_CB + cc + 1], scale=1.0), reads=[acck, "pcol"], writes=[xok])
            if cc >= 24:
                for hh in range(2):
                    S.dma("sp", cT_d[(cc - 24) * 128:(cc - 23) * 128, hh * 2048:(hh + 1) * 2048], xo[:, hh * 2048:(hh + 1) * 2048], reads=[xok], writes=["cT_d"])
            elif cc >= 16:
                for hh in range(2):
                    S.dma("sp", bT_d[(cc - 16) * 128:(cc - 15) * 128, hh * 2048:(hh + 1) * 2048], xo[:, hh * 2048:(hh + 1) * 2048], reads=[xok], writes=["bT_d"])
            if cc < 24:
                pend_tok.append((cc, xo, xok))

        prev_raw = None
        for cc in range(32):
            cur = e_proj(cc)
            if prev_raw is not None:
                e_conv(cc - 1, *prev_raw)
            prev_raw = cur
            while len(pend_tok) > 2 or (pend_tok and cc >= 24):
                to_tok(*pend_tok.pop(0))
        e_conv(31, *prev_raw)
        while pend_tok:
            to_tok(*pend_tok.pop(0))
        S.barrier()
    H.close()

    with contextlib.ExitStack() as G:
        arow = sb(G, "arow", [128, 64], F32)
        S.dma("sp", arow[:], alog_d.partition_broadcast(128), writes=["arow"])
        S.op("act", lambda: nc.scalar.activation(out=arow[:], in_=arow[:], func=AF.Exp), reads=["arow"], writes=["arow"])
        S.op("dve", lambda: nc.vector.tensor_scalar(out=arow[:], in0=arow[:], scalar1=-1.0, scalar2=None, op0=ALU.mult), reads=["arow"], writes=["arow"])
        drow = sb(G, "drow", [128, 32], F32)
        S.dma("sp", drow[:], sd_d.partition_broadcast(128), writes=["drow"])
        Dident = sb(G, "Dident", [128, 32, 128], BF16)
        for h in range(32):
            S.op("dve", lambda: nc.vector.tensor_scalar(out=Dident[:, h, :], in0=K["ident"][:], scalar1=drow[:, h:h + 1], scalar2=None, op0=ALU.mult),
                 reads=["k_ident", "drow"], writes=["Dident"])
        fnw_b = sb(G, "fnw_b", [128, D], F32)
        S.dma("sp", fnw_b[:], fnw_d.partition_broadcast(128), writes=["fnw_b"])
        ones128 = sb(G, "ones128", [128, 128], F32)
        S.op("dve", lambda: nc.vector.memset(ones128[:], 1.0), writes=["ones128"])
        ones1 = sb(G, "ones1", [2, 128], BF16)
        S.op("dve", lambda: nc.vector.memset(ones1[:], 1.0), writes=["ones1"])
        wout1 = sb(G, "wout1", [128, 16, D], BF16)
        with contextlib.ExitStack() as G0:
            wstg = [sb(G0, f"wstgG{i}", [128, 1024], F32) for i in range(2)]
            wstgr = Rot(wstg, "wstgG")
            for j in range(16):
                stg, stgk = wstgr.next()
                S.dma("sp", stg[:], swout_d[j * 128:(j + 1) * 128, :], writes=[stgk])
                S.op("dve", lambda: nc.vector.tensor_scalar(out=wout1[:, j, :], in0=stg[:], scalar1=pcol[:, PC_SNW + j:PC_SNW + j + 1], scalar2=None, op0=ALU.mult),
                     reads=[stgk, "pcol"], writes=["wout1"])
            S.barrier()
        carry = sb(G, "carry", [128, SSD_INNER], F32)
        xts_ = [sb(G, f"xc{i}", [128, SSD_INNER], BF16) for i in range(2)]
        xcr = Rot(xts_, "xc")
        bts_ = [sb(G, f"bc{i}", [128, 1024], BF16) for i in range(2)]
        bcr = Rot(bts_, "bc")
        dtcs = [sb(G, f"dtc{i}", [128, 64], F32) for i in range(4)]
        dtcr = Rot(dtcs, "dtc")
        das = [sb(G, f"da{i}", [128, 64], F32) for i in range(2)]
        dar = Rot(das, "da")
        scs = [sb(G, f"sc{i}", [128, 128], F32) for i in range(2)]
        scr_ = Rot(scs, "sc")
        smalls = [sb(G, f"sm{i}", [128, 6, 64], F32) for i in range(3)]
        smr = Rot(smalls, "sm")
        xws = [sb(G, f"xw{i}", [128, SSD_INNER], BF16) for i in range(1)]
        xwr = Rot(xws, "xw")
        cbfs = [sb(G, f"cbf{i}", [128, SSD_INNER], BF16) for i in range(1)]
        cbfr = Rot(cbfs, "cbf")

        def load_dt(c):
            dtc, dtck = dtcr.next()
            S.dma("sp", dtc[:], dt_d[c], reads=["dt_d"], writes=[dtck])
            return dtc, dtck

        def chunk_scalars(c, need_b_only, pre=None):
            dtc, dtck = pre if pre is not None else load_dt(c)
            da, dak = dar.next()
            S.op("dve", lambda: nc.vector.tensor_tensor(out=da[:], in0=dtc[:], in1=arow[:], op=ALU.mult), reads=[dtck, "arow"], writes=[dak])
            p0, p0k = psr.next()
            S.op("pe", lambda: nc.tensor.matmul(p0[:, 0:32], lhsT=K["triL"][:], rhs=da[:, 0:32], start=True, stop=True), reads=["k_triL", dak], writes=[p0k])
            S.op("pe", lambda: nc.tensor.matmul(p0[:, 32:64], lhsT=K["triU"][:], rhs=da[:, 32:64], start=True, stop=True), reads=["k_triU", dak], writes=[p0k])
            S.op("pe", lambda: nc.tensor.matmul(p0[:, 64:128], lhsT=ones128[:], rhs=da[:, 0:64], start=True, stop=True), reads=["ones128", dak], writes=[p0k])
            if not need_b_only:
                S.op("pe", lambda: nc.tensor.matmul(p0[0:32, 128:256], lhsT=da[:, 0:32], rhs=K["triL"][:], start=True, stop=True), reads=["k_triL", dak], writes=[p0k])
                S.op("pe", lambda: nc.tensor.matmul(p0[32:64, 128:256], lhsT=da[:, 32:64], rhs=K["triU"][:], start=True, stop=True), reads=["k_triU", dak], writes=[p0k])
            sc, sck = scr_.next()
            S.op("act", lambda: nc.scalar.activation(out=sc[:], in_=p0[:, 0:128], func=AF.Copy), reads=[p0k], writes=[sck])
            sm, smk = smr.next()
            S.op("dve", lambda: nc.vector.tensor_scalar(out=sm[:, 0, :], in0=sc[:, 0:64], scalar1=-1.0, scalar2=None, op0=ALU.mult), reads=[sck], writes=[smk])
            S.op("act", lambda: nc.scalar.activation(out=sm[:, 1, :], in_=sc[:, 0:64], func=AF.Exp), reads=[sck], writes=[smk])
            S.op("dve", lambda: nc.vector.tensor_tensor(out=sm[:, 4, :], in0=sc[:, 64:128], in1=sc[:, 0:64], op=ALU.subtract), reads=[sck], writes=[smk])
            S.op("act", lambda: nc.scalar.activation(out=sm[:, 5, :], in_=sm[:, 4, :], func=AF.Exp), reads=[smk], writes=[smk])
            S.op("dve", lambda: nc.vector.tensor_tensor(out=sm[:, 2, :], in0=sm[:, 5, :], in1=dtc[:], op=ALU.mult), reads=[smk, dtck], writes=[smk])
            S.op("act", lambda: nc.scalar.activation(out=sm[:, 3, :], in_=sc[:, 64:128], func=AF.Exp), reads=[sck], writes=[smk])
            return sm, smk, p0, p0k, dtc, dtck

        def state_update(c, xc, xck, bc, bck, sm, smk, d):
            xw, xwk = xwr.next()
            S.op("dve", lambda: nc.vector.tensor_tensor(out=xw[:].rearrange("p (h q) -> p h q", q=64), in0=xc[:].rearrange("p (h q) -> p h q", q=64),
                                                        in1=sm[:, 2, d * 32:(d + 1) * 32].unsqueeze(2).to_broadcast([128, 32, 64]), op=ALU.mult),
                 reads=[xck, smk], writes=[xwk])
            S.op("dve", lambda: nc.vector.tensor_tensor(out=carry[:].rearrange("p (h q) -> p h q", q=64), in0=carry[:].rearrange("p (h q) -> p h q", q=64),
                                                        in1=sm[:, 3, d * 32:(d + 1) * 32].unsqueeze(2).to_broadcast([128, 32, 64]), op=ALU.mult),
                 reads=["carry", smk], writes=["carry"])
            for q4 in range(4):
                pS, pSk = psr.next()
                for g2 in range(2):
                    g = q4 * 2 + g2
                    S.op("pe", lambda: nc.tensor.matmul(pS[:, g2 * 256:(g2 + 1) * 256], lhsT=bc[:, g * 128:(g + 1) * 128], rhs=xw[:, g * 256:(g + 1) * 256],
                                                        start=True, stop=True), reads=[bck, xwk], writes=[pSk])
                S.op("dve", lambda: nc.vector.tensor_tensor(out=carry[:, q4 * 512:(q4 + 1) * 512], in0=pS[:], in1=carry[:, q4 * 512:(q4 + 1) * 512], op=ALU.add),
                     reads=[pSk, "carry"], writes=["carry"])

        S.op("dve", lambda: nc.vector.memset(carry[:], 0.0), writes=["carry"])
        def f_prep(c):
            xc, xck = xcr.next()
            S.dma("sp", xc[:], xs_d[c * 128:(c + 1) * 128, :], reads=["xs_d"], writes=[xck])
            bc, bck = bcr.next()
            S.dma("sp", bc[:], bt_d[c * 128:(c + 1) * 128, :], reads=["bt_d"], writes=[bck])
            sm, smk, _, _, _, _ = chunk_scalars(c, True)
            return xc, xck, bc, bck, sm, smk

        fp_ = f_prep(NT - 1)
        for c in range(NT - 1, -1, -1):
            nfp = f_prep(c - 1) if c > 0 else None
            cbf, cbfk = cbfr.next()
            S.op("act", lambda: nc.scalar.activation(out=cbf[:], in_=carry[:], func=AF.Copy), reads=["carry"], writes=[cbfk])
            S.dma("sp", pb_d[c], cbf[:], reads=[cbfk], writes=["pb_d"])
            state_update(c, *fp_, 1)
            fp_ = nfp

        if stop_after == "F":
            S.barrier()
            S.finish()
            return nc, dbg

        S.op("dve", lambda: nc.vector.memset(carry[:], 0.0), writes=["carry"])
        BTs = [sb(G, f"BT{i}", [128, 8, 128], BF16) for i in range(1)]
        BTr = Rot(BTs, "BT")
        CTs = [sb(G, f"CT{i}", [128, 8, 128], BF16) for i in range(2)]
        CTr = Rot(CTs, "CT")
        szs = [sb(G, f"szc{i}", [128, SSD_INNER], BF16) for i in range(1)]
        szr = Rot(szs, "szc")
        pbs = [sb(G, f"pbc{i}", [128, SSD_INNER], BF16) for i in range(1)]
        pbr = Rot(pbs, "pbc")
        hls = [sb(G, f"hl{i}", [64, 2, 128], BF16) for i in range(3)]
        hlr = Rot(hls, "hl")
        rows = [sb(G, f"row{i}", [2, 8 * 128], BF16) for i in range(2)]
        rowr = Rot(rows, "row")
        Lts = [sb(G, f"Lt{i}", [128, 32, 128], BF16) for i in range(4)]
        Ltr = Rot(Lts, "Lt")
        CBs = [sb(G, f"CBs{i}", [128, 8, 128], BF16) for i in range(1)]
        CBr = Rot(CBs, "CBs")
        xdts = [sb(G, f"xdt{i}", [128, SSD_INNER], BF16) for i in range(2)]
        ycs = [sb(G, f"yc{i}", [128, 512], F32) for i in range(1)]
        ycr = Rot(ycs, "yc")
        yts = [sb(G, f"ytmp{i}", [128, 512], F32) for i in range(1)]
        ytr = Rot(yts, "ytmp")
        yg = sb(G, "yg", [128, SSD_INNER], F32)
        ynb = sb(G, "ynb", [128, SSD_INNER], BF16)
        ynT = sb(G, "ynT", [128, 16, 128], BF16)
        nst2 = [sb(G, f"nst2{i}", [128, 8], F32) for i in range(2)]
        nst2r = Rot(nst2, "nst2")
        x2s = [sb(G, f"x2{i}", [128, D], F32) for i in range(2)]
        x2r = Rot(x2s, "x2")
        outs = [sb(G, f"ot{i}", [128, D], F32) for i in range(2)]
        outr = Rot(outs, "ot")
        bT_v = bT_d.rearrange("(g p) t -> p g t", p=128)
        cT_v = cT_d.rearrange("(g p) t -> p g t", p=128)
        ident4 = K["ident"][:].unsqueeze(1).to_broadcast([128, 4, 128])
        def g_loads(c):
            tsl = slice(c * 128, (c + 1) * 128)
            xc, xck = xcr.next()
            S.dma("sp", xc[:], xs_d[tsl, :], reads=["xs_d"], writes=[xck])
            bc, bck = bcr.next()
            S.dma("sp", bc[:], bt_d[tsl, :], reads=["bt_d"], writes=[bck])
            BT, BTk = BTr.next()
            S.dma("sp", BT[:], bT_v[:, :, tsl], reads=["bT_d"], writes=[BTk])
            CT, CTk = CTr.next()
            S.dma("sp", CT[:], cT_v[:, :, tsl], reads=["cT_d"], writes=[CTk])
            dt_ = load_dt(c)
            return (tsl, xc, xck, bc, bck, BT, BTk, CT, CTk, dt_)

        def g_scal(c, dt_):
            sm, smk, p0, p0k, dtc, dtck = chunk_scalars(c, False, dt_)
            hl, hlk = hlr.next()
            S.op("act", lambda: nc.scalar.activation(out=hl[:, 0, :], in_=p0[0:64, 128:256], func=AF.Copy), reads=[p0k], writes=[hlk])
            S.op("dve", lambda: nc.vector.tensor_tensor(out=hl[:, 1, :], in0=p0[0:64, 128:256], in1=hl[:, 0, :], op=ALU.subtract), reads=[p0k, hlk], writes=[hlk])
            return (sm, smk, dtc, dtck, hl, hlk)

        def g_head(c, L, SC):
            tsl, xc, xck, bc, bck, BT, BTk, CT, CTk, dt_ = L
            sm, smk, dtc, dtck, hl, hlk = SC
            CB, CBk = CBr.next()
            for gh in range(2):
                pcb, pcbk = psr.next()
                for g4 in range(4):
                    g = gh * 4 + g4
                    S.op("pe", lambda: nc.tensor.matmul(pcb[:, g4 * 128:(g4 + 1) * 128], lhsT=BT[:, g, :], rhs=CT[:, g, :], start=True, stop=True),
                         reads=[BTk, CTk], writes=[pcbk])
                S.op("act", lambda: nc.scalar.activation(out=CB[:, gh * 4:(gh + 1) * 4, :].rearrange("p g l -> p (g l)"), in_=pcb[:], func=AF.Copy),
                     reads=[pcbk], writes=[CBk])
            Ms = []
            tasks = []
            mtasks = []
            rowbox = {}
            for d in range(2):
                Lt, Ltk = Ltr.next()
                Ms.append((Lt, Ltk))
                for r16 in range(4):
                    for hb4 in range(2):
                        def task(d=d, r16=r16, hb4=hb4, Lt=Lt, Ltk=Ltk):
                            nm = K["nmaskf"] if d == 0 else K["nmaskb"]
                            nmk = "k_nmaskf" if d == 0 else "k_nmaskb"
                            if hb4 == 0:
                                row, rowk = rowr.next()
                                p0_ = d * 32 + r16 * 8
                                for j_ in range(2):
                                    S.dma("sp", row[j_:j_ + 1, :].rearrange("o (p f) -> o p f", p=8), hl[p0_:p0_ + 8, j_, :],
                                          reads=[hlk], writes=[rowk])
                                rowbox[(d, r16)] = (row, rowk)
                            row, rowk = rowbox[(d, r16)]
                            hb = r16 * 2 + hb4
                            lp, lpk = psr.next()
                            S.op("pe", lambda: nc.tensor.matmul(lp[:], lhsT=ones1[:], rhs=row[:, hb4 * 512:(hb4 + 1) * 512], start=True, stop=False),
                                 reads=["ones1", rowk], writes=[lpk])
                            S.op("pe", lambda: nc.tensor.matmul(lp[:], lhsT=nm[:], rhs=ident4, start=False, stop=True), reads=[nmk, "k_ident"], writes=[lpk])
                            for j4 in range(4):
                                h = hb * 4 + j4
                                S.op("act", lambda: nc.scalar.activation(out=Lt[:, h, :], in_=lp[:, j4 * 128:(j4 + 1) * 128], func=AF.Exp,
                                                                         bias=sm[:, 0, d * 32 + h:d * 32 + h + 1], scale=1.0), reads=[lpk, smk], writes=[Ltk])
                        tasks.append(task)

                        def mtask(d=d, r16=r16, hb4=hb4, Lt=Lt, Ltk=Ltk):
                            hb = r16 * 2 + hb4
                            S.op("dve", lambda: nc.vector.tensor_tensor(out=Lt[:, hb * 4:(hb + 1) * 4, :], in0=Lt[:, hb * 4:(hb + 1) * 4, :],
                                                                        in1=CB[:, hb:hb + 1, :].to_broadcast([128, 4, 128]), op=ALU.mult),
                                 reads=[Ltk, CBk], writes=[Ltk])
                        mtasks.append(mtask)

            def fin():
                for d in range(2):
                    S.op("dve", lambda: nc.vector.tensor_tensor(out=xdts[d][:].rearrange("p (h q) -> p h q", q=64), in0=xc[:].rearrange("p (h q) -> p h q", q=64),
                                                                in1=dtc[:, d * 32:(d + 1) * 32].unsqueeze(2).to_broadcast([128, 32, 64]), op=ALU.mult),
                         reads=[xck, dtck], writes=[f"xdt{d}"])

            def late_loads():
                szc, szk = szr.next()
                S.dma("sp", szc[:], sz_d[tsl, :], reads=["sz_d"], writes=[szk])
                pbc, pbk = pbr.next()
                S.dma("sp", pbc[:], pb_d[c], reads=["pb_d"], writes=[pbk])
                x2, x2k = x2r.next()
                S.dma("sp", x2[:], x1_d[tsl, :], reads=["x1_d"], writes=[x2k])
                X.update(szc=szc, szk=szk, pbc=pbc, pbk=pbk, x2=x2, x2k=x2k)

            X = dict(c=c, tsl=tsl, xc=xc, xck=xck, bc=bc, bck=bck, CT=CT, CTk=CTk,
                     sm=sm, smk=smk, Ms=Ms, tasks=tasks, mtasks=mtasks, fin=fin, late_loads=late_loads)
            return X

        def g_mid(X, run):
            c, tsl, xc, xck, bc, bck, CT, CTk = X["c"], X["tsl"], X["xc"], X["xck"], X["bc"], X["bck"], X["CT"], X["CTk"]
            szc, szk, pbc, pbk, sm, smk, Ms = X["szc"], X["szk"], X["pbc"], X["pbk"], X["sm"], X["smk"], X["Ms"]
            cbf, cbfk = X["cbf"], X["cbfk"]
            for q4 in range(4):
                csl = slice(q4 * 512, (q4 + 1) * 512)
                pyd, pydk = psr.next()
                for h8 in range(8):
                    h = q4 * 8 + h8
                    for d in range(2):
                        S.op("pe", lambda: nc.tensor.matmul(pyd[:, h8 * 64:(h8 + 1) * 64], lhsT=Ms[d][0][:, h, :], rhs=xdts[d][:, h * 64:(h + 1) * 64],
                                                            start=(d == 0), stop=False), reads=[Ms[d][1], f"xdt{d}"], writes=[pydk])
                    S.op("pe", lambda: nc.tensor.matmul(pyd[:, h8 * 64:(h8 + 1) * 64], lhsT=Dident[:, h, :], rhs=xc[:, h * 64:(h + 1) * 64],
                                                        start=False, stop=True), reads=["Dident", xck], writes=[pydk])
                pof, pofk = psr.next()
                pob, pobk = psr.next()
                for g2 in range(2):
                    g = q4 * 2 + g2
                    S.op("pe", lambda: nc.tensor.matmul(pof[:, g2 * 256:(g2 + 1) * 256], lhsT=CT[:, g, :], rhs=cbf[:, g * 256:(g + 1) * 256], start=True, stop=True),
                         reads=[CTk, cbfk], writes=[pofk])
                    S.op("pe", lambda: nc.tensor.matmul(pob[:, g2 * 256:(g2 + 1) * 256], lhsT=CT[:, g, :], rhs=pbc[:, g * 256:(g + 1) * 256], start=True, stop=True),
                         reads=[CTk, pbk], writes=[pobk])
                yc, yck = ycr.next()
                yt_, ytk = ytr.next()
                v3 = lambda t_: t_[:].rearrange("p (h q) -> p h q", q=64)
                ef = sm[:, 1, q4 * 8:q4 * 8 + 8].unsqueeze(2).to_broadcast([128, 8, 64])
                eb = sm[:, 1, 32 + q4 * 8:32 + q4 * 8 + 8].unsqueeze(2).to_broadcast([128, 8, 64])
                S.op("dve", lambda: nc.vector.tensor_tensor(out=v3(yc), in0=pof[:].rearrange("p (h q) -> p h q", q=64), in1=ef, op=ALU.mult), reads=[pofk, smk], writes=[yck])
                S.op("dve", lambda: nc.vector.tensor_tensor(out=v3(yt_), in0=pob[:].rearrange("p (h q) -> p h q", q=64), in1=eb, op=ALU.mult), reads=[pobk, smk], writes=[ytk])
                S.op("dve", lambda: nc.vector.tensor_tensor(out=yc[:], in0=yc[:], in1=yt_[:], op=ALU.add), reads=[yck, ytk], writes=[yck])
                S.op("dve", lambda: nc.vector.tensor_tensor(out=yc[:], in0=pyd[:], in1=yc[:], op=ALU.add), reads=[pydk, yck], writes=[yck])
                S.op("dve", lambda: nc.vector.tensor_tensor(out=yg[:, csl], in0=yc[:], in1=szc[:, csl], op=ALU.mult), reads=[yck, szk], writes=["yg"])
                run(3)
            state_update(c, xc, xck, bc, bck, sm, smk, 0)

        def g_tail(X):
            c, tsl, x2, x2k = X["c"], X["tsl"], X["x2"], X["x2k"]
            ns, nsk = nst2r.next()
            S.op("act", lambda: nc.scalar.activation(out=ynb[:], in_=yg[:], func=AF.Square, accum_out=ns[:, 0:1]), reads=["yg"], writes=["ynb", nsk])
            S.op("act", lambda: nc.scalar.activation(out=ns[:, 1:2], in_=ns[:, 0:1], func=AF.Ln, bias=epsc[:, 0:1], scale=1.0 / SSD_INNER), reads=[nsk, "epsc"], writes=[nsk])
            S.op("act", lambda: nc.scalar.activation(out=ns[:, 2:3], in_=ns[:, 1:2], func=AF.Exp, scale=-0.5), reads=[nsk], writes=[nsk])
            S.op("dve", lambda: nc.vector.tensor_scalar(out=ynb[:], in0=yg[:], scalar1=ns[:, 2:3], scalar2=None, op0=ALU.mult),
                 reads=["yg", nsk], writes=["ynb"])
            for jh in range(2):
                tp, tpk = ptr.next()
                for jj in range(8):
                    j = jh * 8 + jj
                    S.op("pe", lambda: nc.tensor.transpose(tp[:, jj, :], ynb[:, j * 128:(j + 1) * 128], K["ident"][:]), reads=["ynb", "k_ident"], writes=[tpk])
                S.op("act", lambda: nc.scalar.activation(out=ynT[:, jh * 8:(jh + 1) * 8, :], in_=tp[:], func=AF.Copy), reads=[tpk], writes=["ynT"])
            y2, y2k = yg, "yg"
            for n in range(2):
                po, pok = psr.next()
                for j in range(16):
                    S.op("pe", lambda: nc.tensor.matmul(po[:], lhsT=ynT[:, j, :], rhs=wout1[:, j, n * 512:(n + 1) * 512], start=(j == 0), stop=(j == 15)),
                         reads=["ynT", "wout1"], writes=[pok])
                S.op("dve", lambda: nc.vector.tensor_tensor(out=y2[:, n * 512:(n + 1) * 512], in0=po[:], in1=gate_b[1][:, n * 512:(n + 1) * 512], op=ALU.mult),
                     reads=[pok, "gate_b1"], writes=[y2k])
            S.op("dve", lambda: nc.vector.tensor_tensor(out=x2[:], in0=y2[:, 0:D], in1=x2[:], op=ALU.add), reads=[y2k, x2k], writes=[x2k])
            ns, nsk = nst2r.next()
            S.op("act", lambda: nc.scalar.activation(out=nrm_junk[:], in_=x2[:], func=AF.Square, accum_out=ns[:, 0:1]), reads=[x2k], writes=["nrm_junk", nsk])
            S.op("act", lambda: nc.scalar.activation(out=ns[:, 1:2], in_=ns[:, 0:1], func=AF.Ln, bias=epsc[:, 0:1], scale=1.0 / D), reads=[nsk, "epsc"], writes=[nsk])
            S.op("act", lambda: nc.scalar.activation(out=ns[:, 2:3], in_=ns[:, 1:2], func=AF.Exp, scale=-0.5), reads=[nsk], writes=[nsk])
            ot, otk = outr.next()
            S.op("dve", lambda: nc.vector.scalar_tensor_tensor(out=ot[:], in0=x2[:], scalar=ns[:, 2:3], in1=fnw_b[:], op0=ALU.mult, op1=ALU.mult),
                 reads=[x2k, nsk, "fnw_b"], writes=[otk])
            X["store"] = lambda: S.dma("sp", out_d[tsl, :], ot[:], reads=[otk], writes=["out_d"])

        def make_runner(X):
            tl = list(X["tasks"]) if X is not None else []
            ml = list(X["mtasks"]) if X is not None else []
            done = [0]

            def run(n):
                for _ in range(n):
                    if tl:
                        tl.pop(0)()
                        done[0] += 1
                        if done[0] > 2 and ml:
                            ml.pop(0)()

            def flush():
                run(len(tl))
                while ml:
                    ml.pop(0)()
            return run, flush

        Ld = {0: g_loads(0)}
        Sc = {0: g_scal(0, Ld[0][-1])}
        ctx = g_head(0, Ld.pop(0), Sc.pop(0))
        run, flush = make_runner(ctx)
        flush()
        ctx["fin"]()
        ctx["late_loads"]()
        if NT > 1:
            Ld[1] = g_loads(1)
            Sc[1] = g_scal(1, Ld[1][-1])
        pend_store = None
        for c in range(NT):
            cbf, cbfk = cbfr.next()
            S.op("act", lambda: nc.scalar.activation(out=cbf[:], in_=carry[:], func=AF.Copy), reads=["carry"], writes=[cbfk])
            ctx["cbf"], ctx["cbfk"] = cbf, cbfk
            nxt = g_head(c + 1, Ld.pop(c + 1), Sc.pop(c + 1)) if c + 1 < NT else None
            run, flush = make_runner(nxt)
            g_mid(ctx, run)
            flush()
            if c + 2 < NT:
                Ld[c + 2] = g_loads(c + 2)
                Sc[c + 2] = g_scal(c + 2, Ld[c + 2][-1])
            if nxt is not None:
                nxt["late_loads"]()
            if pend_store is not None:
                pend_store()
            g_tail(ctx)
            pend_store = ctx["store"]
            if nxt is not None:
                nxt["fin"]()
            ctx = nxt
        pend_store()
        S.barrier()
    S.finish()
    return nc, dbg


def make_in_maps(inputs):
    cst = host_consts()
    pcol = host_pcol(inputs)
    f = lambda a: np.ascontiguousarray(np.asarray(a), dtype=np.float32)
    shared = {
        "mod_w": f(inputs["mod_w"]), "mod_b": f(inputs["mod_b"]),
        "attn_w_in": f(inputs["attn_w_in"][0]), "attn_w_out": f(inputs["attn_w_out"][0]),
        "ssd_w_in": f(inputs["ssd_w_in"][0]),
        "ssd_dt_bias": f(inputs["ssd_dt_bias"]).reshape(1, 64), "ssd_a_log": f(inputs["ssd_a_log"]).reshape(1, 64),
        "ssd_d": f(inputs["ssd_d"]).reshape(1, 32), "ssd_norm_w": f(inputs["ssd_norm_w"]).reshape(1, SSD_INNER),
        "ssd_w_out": f(inputs["ssd_w_out"][0]), "final_norm_w": f(inputs["final_norm_w"]).reshape(1, D),
        "pcol": pcol,
    }
    for k, v in cst.items():
        shared["k_" + k] = v
    x = f(inputs["x"])
    c = f(inputs["c"])
    pos = np.ascontiguousarray(np.asarray(inputs["positions"]), dtype=np.int32)
    maps = []
    for b in range(x.shape[0]):
        m = dict(shared)
        m["x"] = x[b]
        m["c"] = np.ascontiguousarray(c[b].reshape(8, 128).T)
        m["pos"] = pos[b:b + 1]
        maps.append(m)
    return maps


def kernel(**inputs):
    nc, _ = build()
    maps = make_in_maps(inputs)
    res = run_bass_kernel_spmd(nc, maps, core_ids=list(range(8)))
    return np.stack([np.asarray(r["out"], dtype=np.float32) for r in res.results], axis=0)
```

```python
import contextlib
import math
import numpy as np
import ml_dtypes
import concourse.bass as bass
import concourse.mybir as mybir
from concourse.bass_utils import run_bass_kernel_spmd

F32 = mybir.dt.float32
BF16 = mybir.dt.bfloat16
I32 = mybir.dt.int32
AF = mybir.ActivationFunctionType
ALU = mybir.AluOpType

D = 1024
SEQ = 4096
NT = SEQ // 128
AW = 1536
PATTERNS = ((128, 1), (512, 4), (2048, 16))
SSD_INNER = 2048
SSD_IN = 6208
EPS = 1e-6
import os
KSTEP = int(os.environ.get('KSTEP', '9'))
KPOOL = int(os.environ.get('KPOOL', '0'))


class Sched:
    NDMA = 10

    def __init__(self, nc, es, needed=None):
        self.nc = nc
        self.es = es
        self.needed = needed
        self.record = set()
        self.engs = {"pe": nc.tensor, "dve": nc.vector, "act": nc.scalar,
                     "pool": nc.gpsimd, "sp": nc.sync}
        self.sem = {}
        self.raw = {}
        self.pub = {}
        for e in self.engs:
            self.sem[e] = self.es.enter_context(nc.semaphore("s_" + e))
            self.raw[e] = 0
            self.pub[e] = 0
        self.dsem, self.dcnt, self.dnext = {}, {}, {}
        for q in ("sp",):
            self.dsem[q] = [self.es.enter_context(nc.semaphore(f"d_{q}{i}")) for i in range(self.NDMA)]
            self.dcnt[q] = [0] * self.NDMA
            self.dnext[q] = 0
        self.seen = {e: {} for e in self.engs}
        self.lastw = {}
        self.readers = {}
        self.ninstr = 0
        self.nwaits = 0

    def _wait(self, e, ev):
        sem, val, src, rid = ev
        if src == "pe" and e == "pe":
            return
        k = id(sem)
        if self.seen[e].get(k, 0) >= val:
            return
        if rid is not None:
            self.record.add(rid)
        self.engs[e].wait_ge(sem, val)
        self.seen[e][k] = val
        self.nwaits += 1

    def _deps(self, e, reads, writes):
        for k in reads:
            ev = self.lastw.get(k)
            if ev is not None:
                self._wait(e, ev)
            if k[:2] in ("ps", "pt"):
                for ev in self.readers.get(k, ()):
                    self._wait(e, ev)
        for k in writes:
            ev = self.lastw.get(k)
            if ev is not None:
                self._wait(e, ev)
            for ev in self.readers.get(k, ()):
                self._wait(e, ev)

    def _commit(self, ev, reads, writes):
        for k in reads:
            self.readers.setdefault(k, []).append(ev)
        for k in writes:
            self.lastw[k] = ev
            self.readers[k] = []

    def op(self, e, fn, reads=(), writes=()):
        self._deps(e, reads, writes)
        ins = fn()
        self.raw[e] += 1
        rid = (e, self.raw[e])
        if self.needed is None or rid in self.needed:
            self.pub[e] += 1
            ins.then_inc(self.sem[e], 1)
            ev = (self.sem[e], self.pub[e], e, rid)
        else:
            ev = (self.sem[e], self.pub[e] + 1, e, rid)
        self._commit(ev, reads, writes)
        self.ninstr += 1
        return ev

    def dma(self, q, out, in_, reads=(), writes=()):
        i = self.dnext[q]
        self.dnext[q] = (i + 1) % self.NDMA
        sem = self.dsem[q][i]
        if self.dcnt[q][i] > 0:
            self._wait(q, (sem, self.dcnt[q][i], None, None))
        self._deps(q, reads, writes)
        ins = self.engs[q].dma_start(out=out, in_=in_)
        self.dcnt[q][i] += 16
        ins.then_inc(sem, 16)
        ev = (sem, self.dcnt[q][i], None, None)
        self._commit(ev, reads, writes)
        self.ninstr += 1
        return ev

    def barrier(self):
        evs = []
        for e in self.engs:
            if self.raw[e] > 0:
                rid = (e, self.raw[e])
                pubd = self.needed is None or rid in self.needed
                evs.append((self.sem[e], self.pub[e] if pubd else self.pub[e] + 1, e, rid))
        for q in self.dsem:
            for i, sem in enumerate(self.dsem[q]):
                if self.dcnt[q][i] > 0:
                    evs.append((sem, self.dcnt[q][i], None, None))
        for e in self.engs:
            for ev in evs:
                if ev[2] == e:
                    continue
                self._wait(e, ev)
        self.lastw = {}
        self.readers = {}

    def finish(self):
        for q in self.dsem:
            for i, sem in enumerate(self.dsem[q]):
                if self.dcnt[q][i] > 0:
                    self._wait("sp", (sem, self.dcnt[q][i], None, None))


class Rot:
    def __init__(self, tiles, name):
        self.tiles = tiles
        self.name = name
        self.i = -1

    def next(self):
        self.i = (self.i + 1) % len(self.tiles)
        return self.tiles[self.i], f"{self.name}{self.i}"


def _ap(t, rowsize, p0, npart, col, dims):
    return bass.AP(t, p0 * rowsize + col, [[rowsize, npart]] + [list(d) for d in dims])


def _bf(a):
    return np.asarray(a, dtype=np.float32).astype(ml_dtypes.bfloat16)


def host_consts():
    c = {}
    p = np.arange(128)
    c["ident"] = _bf(np.eye(128))
    c["bdones"] = _bf((p[:, None] // 64) == (p[None, :] // 64))
    pm = np.zeros((128, 128), np.float32)
    for hb in (0, 64):
        for j in range(8):
            pm[hb + j + 8, hb + j] = 1.0
            pm[hb + j, hb + 8 + j] = 1.0
    c["pswap"] = _bf(pm)
    i = (p % 64)[:, None, None]
    i4 = np.arange(4)[None, :, None]
    jq = np.arange(128)[None, None, :]
    c["mask4"] = _bf(np.abs(64 * (i4 - 1) + i - jq) <= 64).reshape(128, 512)
    k = p[:, None]
    l = p[None, :]
    c["triL"] = (k <= l).astype(np.float32)
    c["triU"] = (k >= l).astype(np.float32)
    c["nmaskf"] = _bf(np.where(l <= k, 0.0, -30000.0))
    c["nmaskb"] = _bf(np.where(l >= k, 0.0, -30000.0))
    c["maskf"] = _bf(k <= l)
    c["maskb"] = _bf(k >= l)
    return c


def host_pcol(inp):
    cols = {}
    p = np.arange(128)
    inv = (500000.0 ** (-np.arange(0, 16, 2, dtype=np.float32) / 16.0)).astype(np.float32)
    pp = p % 64
    invf = np.where(pp < 16, inv[pp % 8], 0.0).astype(np.float32)
    sg = np.where(pp < 8, -1.0, np.where(pp < 16, 1.0, 0.0)).astype(np.float32)
    parts = [invf[:, None], sg[:, None]]
    parts.append(np.asarray(inp["norm_w"], np.float32).reshape(2, 8, 128).transpose(2, 0, 1).reshape(128, 16))
    parts.append(np.asarray(inp["mod_b"], np.float32).reshape(2, 24, 128).transpose(2, 0, 1).reshape(128, 48))
    parts.append(np.asarray(inp["ssd_conv_w"], np.float32).reshape(5, 32, 128).transpose(2, 0, 1).reshape(128, 160))
    parts.append(np.asarray(inp["ssd_conv_b"], np.float32).reshape(32, 128).T)
    parts.append(np.asarray(inp["ssd_norm_w"], np.float32).reshape(16, 128).T)
    return np.ascontiguousarray(np.concatenate(parts, axis=1), dtype=np.float32)


PC_INVF, PC_SG, PC_NW, PC_MODB, PC_CW, PC_CB, PC_SNW, PC_N = 0, 1, 2, 18, 66, 226, 258, 274


def build(stop_after=None):
    M = contextlib.ExitStack()
    rec = {}
    _build(M, stop_after, None, rec)
    M.close()
    M = contextlib.ExitStack()
    nc, dbg = _build(M, stop_after, rec["needed"], {})
    M.close()
    return nc, dbg


def _build(M, stop_after=None, needed=None, rec=None):
    nc = bass.Bass("TRN2", target_bir_lowering=False)
    S = Sched(nc, M, needed)
    rec["needed"] = S.record
    dbg = {}

    def din(name, shape, dt=F32):
        return nc.dram_tensor(name, list(shape), dt, kind="ExternalInput").ap()

    x_d = din("x", [SEQ, D])
    c_d = din("c", [128, 8])
    pos_d = din("pos", [1, SEQ], I32)
    modw_d = din("mod_w", [2, D, 3 * D])
    modb_d = din("mod_b", [2, 3 * D])
    awin_d = din("attn_w_in", [D, 4 * AW])
    awout_d = din("attn_w_out", [AW, D])
    swin_d = din("ssd_w_in", [D, SSD_IN])
    dtb_d = din("ssd_dt_bias", [1, 64])
    alog_d = din("ssd_a_log", [1, 64])
    sd_d = din("ssd_d", [1, 32])
    snw_d = din("ssd_norm_w", [1, SSD_INNER])
    swout_d = din("ssd_w_out", [SSD_INNER, D])
    fnw_d = din("final_norm_w", [1, D])
    pcol_d = din("pcol", [128, PC_N])
    cst = host_consts()
    cst_d = {k: din("k_" + k, list(v.shape), BF16 if v.dtype != np.float32 else F32) for k, v in cst.items()}
    out_d = nc.dram_tensor("out", [SEQ, D], F32, kind="ExternalOutput").ap()

    def scratch(name, shape, dt):
        kind = "ExternalOutput" if stop_after is not None else "Internal"
        t = nc.dram_tensor(name, list(shape), dt, kind=kind).ap()
        dbg[name] = t
        return t

    o_d = scratch("o_d", [AW, SEQ], BF16)
    x1_d = scratch("x1_d", [SEQ, D], F32)

    P = M

    uid = [0]

    def sb(stack, name, shape, dt):
        uid[0] += 1
        return stack.enter_context(nc.sbuf_tensor(f"sb{uid[0]}_{name}", list(shape), dt))

    ps = [P.enter_context(nc.psum_tensor(f"ps{i}", [128, 512], F32)) for i in range(6)]
    pt = [P.enter_context(nc.psum_tensor(f"pt{i}", [128, 8, 128], BF16)) for i in range(2)]
    psr = Rot(ps, "ps")
    ptr = Rot(pt, "pt")

    def load_cast(stg_rot, dst, dkey, src, shape):
        stg, stgk = stg_rot.next()
        n = 1
        for d_ in shape[1:]:
            n *= d_
        sv = stg[:, 0:n]
        if len(shape) == 3:
            sv = sv.rearrange("p (a b) -> p a b", a=shape[1])
        S.dma("sp", sv, src, writes=[stgk])
        S.op("act", lambda: nc.scalar.activation(out=dst, in_=sv, func=AF.Copy), reads=[stgk], writes=[dkey])

    pcol = sb(P, "pcol", [128, PC_N], F32)
    S.dma("sp", pcol[:], pcol_d, writes=["pcol"])
    K = {}
    for k, v in cst.items():
        K[k] = sb(P, "k_" + k, list(v.shape), BF16 if v.dtype != np.float32 else F32)
        S.dma("sp", K[k][:], cst_d[k], writes=["k_" + k])
    if stop_after is not None:
        junk = sb(P, "junk", [1, 16], F32)
        junki = sb(P, "junki", [1, 16], I32)
        for t_ in (awin_d, awout_d, swin_d, dtb_d, alog_d, sd_d, snw_d, swout_d, fnw_d):
            S.dma("sp", junk[:], t_[0:1, 0:16], writes=["junk"])
        S.dma("sp", junki[:], pos_d[0:1, 0:16], writes=["junki"])
    epsc = sb(P, "epsc", [128, 1], F32)
    S.op("dve", lambda: nc.vector.memset(epsc[:], EPS), writes=["epsc"])
    one11 = sb(P, "one11", [1, 1], F32)
    S.op("dve", lambda: nc.vector.memset(one11[:], 1.0), writes=["one11"])
    modA = sb(P, "modA", [128, 16], F32)
    modB = sb(P, "modB", [128, 16], F32)
    gate_b = [sb(P, f"gate_b{i}", [128, D], F32) for i in range(2)]
    nrm_junk = sb(P, "nrm_junk", [128, D], BF16)
    xn_t = [sb(P, f"xn{i}", [128, D], BF16) for i in range(2)]
    st_t = [sb(P, f"nst{i}", [128, 4], F32) for i in range(2)]
    onec = sb(P, "onec", [128, 1], F32)
    S.op("dve", lambda: nc.vector.memset(onec[:], 1.0), writes=["onec"])
    H = contextlib.ExitStack()
    M.enter_context(H)
    hnT = sb(H, "hnT", [128, 8, SEQ], BF16)

    with contextlib.ExitStack() as A:
        c_fm = sb(A, "c_fm", [128, 8], F32)
        S.dma("sp", c_fm[:], c_d, writes=["c_fm"])
        cond_f = sb(A, "cond_f", [128, 8], F32)
        cond_b = sb(A, "cond_b", [128, 8], BF16)
        condB = sb(A, "condB", [128, 8, 128], F32)
        S.op("act", lambda: nc.scalar.activation(out=cond_f[:], in_=c_fm[:], func=AF.Silu), reads=["c_fm"], writes=["cond_f"])
        if stop_after == "A0":
            dd = nc.dram_tensor("cond_dbg", [128, 8], F32, kind="ExternalOutput").ap()
            S.dma("sp", dd, cond_f[:], reads=["cond_f"])
            S.finish()
            return nc, dbg
        S.op("dve", lambda: nc.vector.tensor_copy(out=cond_b[:], in_=cond_f[:]), reads=["cond_f"], writes=["cond_b"])
        S.op("dve", lambda: nc.vector.tensor_copy(out=condB[:], in_=cond_f[:].unsqueeze(2).to_broadcast([128, 8, 128])),
             reads=["cond_f"], writes=["condB"])
        modw = sb(A, "modw", [128, 8, 3 * D], F32)
        modT = sb(A, "modT", [128, 24], F32)
        gb_bias = sb(A, "gb_bias", [128, D], F32)
        for li in range(2):
            for kc in range(8):
                S.dma("sp", modw[:, kc, :], modw_d[li, kc * 128:(kc + 1) * 128, :], writes=[f"modw{kc}"])
            pm, pmk = psr.next()
            for fc in range(24):
                for kc in range(8):
                    S.op("pe", lambda: nc.tensor.matmul(pm[:, fc:fc + 1], lhsT=modw[:, kc, fc * 128:(fc + 1) * 128],
                                                        rhs=cond_f[:, kc:kc + 1], start=(kc == 0), stop=(kc == 7)),
                         reads=[f"modw{kc}", "cond_f"], writes=[pmk])
            S.op("dve", lambda: nc.vector.tensor_tensor(out=modT[:], in0=pm[:, 0:24],
                                                        in1=pcol[:, PC_MODB + li * 24:PC_MODB + (li + 1) * 24], op=ALU.add),
                 reads=[pmk, "pcol"], writes=["modT"])
            S.op("dve", lambda: nc.vector.tensor_copy(out=modB[:, li * 8:(li + 1) * 8], in_=modT[:, 0:8]),
                 reads=["modT"], writes=["modB"])
            S.op("dve", lambda: nc.vector.scalar_tensor_tensor(out=modA[:, li * 8:(li + 1) * 8], in0=modT[:, 8:16], scalar=1.0,
                                                               in1=pcol[:, PC_NW + li * 8:PC_NW + (li + 1) * 8],
                                                               op0=ALU.add, op1=ALU.mult),
                 reads=["modT", "pcol"], writes=["modA"])
            S.dma("sp", gb_bias[:], modb_d[li:li + 1, 2 * D:3 * D].partition_broadcast(128), writes=["gb_bias"])
            for n in range(2):
                pg, pgk = psr.next()
                for kc in range(8):
                    S.op("pe", lambda: nc.tensor.matmul(pg[:], lhsT=condB[:, kc, :],
                                                        rhs=modw[:, kc, 2 * D + n * 512:2 * D + (n + 1) * 512],
                                                        start=(kc == 0), stop=(kc == 7)),
                         reads=[f"modw{kc}", "condB"], writes=[pgk])
                S.op("dve", lambda: nc.vector.tensor_tensor(out=gate_b[li][:, n * 512:(n + 1) * 512], in0=pg[:],
                                                            in1=gb_bias[:, n * 512:(n + 1) * 512], op=ALU.add),
                     reads=[pgk, "gb_bias"], writes=[f"gate_b{li}"])
        S.barrier()

    if stop_after == "A":
        for nm, t_, shp in (("modA_dbg", modA, [128, 16]), ("modB_dbg", modB, [128, 16]), ("gate0_dbg", gate_b[0], [128, D]), ("gate1_dbg", gate_b[1], [128, D])):
            dd = nc.dram_tensor(nm, shp, F32, kind="ExternalOutput").ap()
            S.dma("sp", dd, t_[:])
        S.finish()
        return nc, dbg

    xnr = Rot(xn_t, "xn")
    str_ = Rot(st_t, "nst")

    def norm_tile(xt, xk, tt, li):
        st, stk = str_.next()
        xn, xnk = xnr.next()
        S.op("act", lambda: nc.scalar.activation(out=nrm_junk[:], in_=xt[:], func=AF.Square, accum_out=st[:, 0:1]),
             reads=[xk], writes=["nrm_junk", stk])
        S.op("act", lambda: nc.scalar.activation(out=st[:, 1:2], in_=st[:, 0:1], func=AF.Sqrt, bias=epsc[:, 0:1], scale=1.0 / D),
             reads=[stk, "epsc"], writes=[stk])
        S.op("dve", lambda: nc.vector.reciprocal(out=st[:, 2:3], in_=st[:, 1:2]), reads=[stk], writes=[stk])
        S.op("dve", lambda: nc.vector.tensor_scalar(out=xn[:], in0=xt[:], scalar1=st[:, 2:3], scalar2=None, op0=ALU.mult),
             reads=[xk, stk], writes=[xnk])
        tp, tpk = ptr.next()
        for kc in range(8):
            S.op("pe", lambda: nc.tensor.transpose(tp[:, kc, :], xn[:, kc * 128:(kc + 1) * 128], K["ident"][:]),
                 reads=[xnk, "k_ident"], writes=[tpk])
        for kc in range(8):
            col = li * 8 + kc
            if True:
                S.op("dve", lambda: nc.vector.tensor_scalar(out=hnT[:, kc, tt * 128:(tt + 1) * 128], in0=tp[:, kc, :],
                                                            scalar1=modA[:, col:col + 1], scalar2=modB[:, col:col + 1],
                                                            op0=ALU.mult, op1=ALU.add),
                     reads=[tpk, "modA", "modB"], writes=[f"hnT{kc}"])
            else:
                S.op("act", lambda: nc.scalar.activation(out=hnT[:, kc, tt * 128:(tt + 1) * 128], in_=tp[:, kc, :],
                                                         func=AF.Identity, scale=modA[:, col:col + 1], bias=modB[:, col:col + 1]),
                     reads=[tpk, "modA", "modB"], writes=[f"hnT{kc}"])

    HN = [f"hnT{kc}" for kc in range(8)]

    with contextlib.ExitStack() as B:
        xts = [sb(B, f"xt{i}", [128, D], F32) for i in range(3)]
        xr = Rot(xts, "xt")
        for tt in range(NT):
            xt, xk = xr.next()
            S.dma("sp", xt[:], x_d[tt * 128:(tt + 1) * 128, :], writes=[xk])
            norm_tile(xt, xk, tt, 0)
        S.barrier()

    if stop_after == "B":
        hn_dbg = nc.dram_tensor("hn_dbg", [128, 8, SEQ], BF16, kind="ExternalOutput").ap()
        dbg["hn_dbg"] = hn_dbg
        for kc in range(8):
            S.dma("sp", hn_dbg[:, kc, :], hnT[:, kc, :], reads=HN)
        S.finish()
        return nc, dbg

    with contextlib.ExitStack() as C:
        Ct = sb(C, "Ct", [128, SEQ], BF16)
        St = sb(C, "St", [128, SEQ], BF16)
        with contextlib.ExitStack() as R:
            RB = 1024
            posi = sb(R, "posi", [128, RB], I32)
            posf = sb(R, "posf", [128, RB], F32)
            ang = sb(R, "ang", [128, RB], F32)
            ki = sb(R, "ki", [128, RB], I32)
            kf = sb(R, "kf", [128, RB], F32)
            for cb in range(SEQ // RB):
                csl = slice(cb * RB, (cb + 1) * RB)
                S.dma("sp", posi[:], pos_d[:, csl].partition_broadcast(128), writes=["posi"])
                S.op("dve", lambda: nc.vector.tensor_copy(out=posf[:], in_=posi[:]), reads=["posi"], writes=["posf"])
                for which, phase in ((0, 0.0), (1, math.pi / 2)):
                    S.op("dve", lambda: nc.vector.tensor_scalar(out=ang[:], in0=posf[:], scalar1=pcol[:, PC_INVF:PC_INVF + 1],
                                                                scalar2=phase, op0=ALU.mult, op1=ALU.add),
                         reads=["posf", "pcol"], writes=["ang"])
                    S.op("dve", lambda: nc.vector.tensor_scalar(out=ki[:], in0=ang[:], scalar1=1.0 / (2 * math.pi), scalar2=None,
                                                                op0=ALU.mult), reads=["ang"], writes=["ki"])
                    S.op("dve", lambda: nc.vector.tensor_copy(out=kf[:], in_=ki[:]), reads=["ki"], writes=["kf"])
                    S.op("dve", lambda: nc.vector.scalar_tensor_tensor(out=ang[:], in0=kf[:], scalar=-2 * math.pi, in1=ang[:],
                                                                       op0=ALU.mult, op1=ALU.add),
                         reads=["kf", "ang"], writes=["ang"])
                    if which == 0:
                        S.op("act", lambda: nc.scalar.activation(out=kf[:], in_=ang[:], func=AF.Sin), reads=["ang"], writes=["kf"])
                        S.op("dve", lambda: nc.vector.tensor_scalar(out=St[:, csl], in0=kf[:], scalar1=pcol[:, PC_SG:PC_SG + 1],
                                                                    scalar2=None, op0=ALU.mult), reads=["kf", "pcol"], writes=["St"])
                    else:
                        S.op("act", lambda: nc.scalar.activation(out=Ct[:, csl], in_=ang[:], func=AF.Sin), reads=["ang"], writes=["Ct"])
            S.barrier()
        def dump(nm, t_, shape, dt):
            dd = nc.dram_tensor(nm, shape, dt, kind="ExternalOutput").ap()
            if len(shape) == 2 and shape[1] * (2 if dt == BF16 else 4) > 32768:
                h = shape[1] // 2
                S.dma("sp", dd[:, 0:h], t_[:, 0:h])
                S.dma("sp", dd[:, h:], t_[:, h:])
            else:
                S.dma("sp", dd, t_[:])

        if stop_after == "Crope":
            dump("Ct_dbg", Ct, [128, SEQ], BF16)
            dump("St_dbg", St, [128, SEQ], BF16)
            S.finish()
            return nc, dbg
        Dacc = sb(C, "Dacc", [128, SEQ], F32)
        Rinv = Dacc
        tmpfs = [sb(C, f"tmpf{i}", [128, 512], F32) for i in range(1)]
        tmpfr = Rot(tmpfs, "tmpf")
        Ug = [sb(C, f"Ug{i}", [128, SEQ], BF16) for i in range(3)]
        QT = sb(C, "QT", [128, SEQ], BF16)
        Kz = sb(C, "Kz", [128, 2 * SEQ], BF16)
        Vbd = sb(C, "Vbd", [128, 64, 128], BF16)
        S.op("dve", lambda: nc.vector.memset(Kz[:], 0.0), writes=["Kz"])
        S.op("dve", lambda: nc.vector.memset(Vbd[:], 0.0), writes=["Vbd"])
        wts = [sb(C, f"wt{i}", [128, 8, 128], BF16) for i in range(3)]
        wstg = [sb(C, f"wstg{i}", [128, 1024], F32) for i in range(2)]
        wstgr = Rot(wstg, "wstg")
        wr = Rot(wts, "wt")
        stA = [sb(C, f"stA{i}", [128, 512], BF16) for i in range(2)]
        stAr = Rot(stA, "stA")
        st1 = [sb(C, f"st1{i}", [128, 512], BF16) for i in range(2)]
        st1r = Rot(st1, "st1")
        st2 = [sb(C, f"st2{i}", [128, 512], BF16) for i in range(2)]
        st2r = Rot(st2, "st2")
        Et = [sb(C, f"Et{i}", [128, 512], BF16) for i in range(2)]
        Etr = Rot(Et, "Et")
        PTt = [sb(C, f"PT{i}", [128, 512], BF16) for i in range(2)]
        PTr = Rot(PTt, "PT")
        Ost = [sb(C, f"Ost{i}", [128, 512], BF16) for i in range(2)]
        Ostr = Rot(Ost, "Ost")
        awv = awin_d.rearrange("(kc p) n -> p kc n", p=128)

        def load_w(col0):
            w, wk = wr.next()
            load_cast(wstgr, w[:], wk, awv[:, :, col0:col0 + 128], [128, 8, 128])
            return w, wk

        def proj_tile(w, wk, tn):
            pj, pjk = psr.next()
            for kc in range(8):
                S.op("pe", lambda: nc.tensor.matmul(pj[:], lhsT=w[:, kc, :], rhs=hnT[:, kc, tn * 512:(tn + 1) * 512],
                                                    start=(kc == 0), stop=(kc == 7)),
                     reads=[wk, f"hnT{kc}"], writes=[pjk])
            return pj, pjk

        jobs = []
        for s in range(4):
            for g in range(3):
                jobs += [(s, g, 0), (s, g, 1), (s, g, 2)]
            for g in range(3):
                jobs.append((s, g, 3))
        wq = {}
        PF = 2

        def prefetch(i):
            if i < len(jobs) and i not in wq:
                s_, g_, wh_ = jobs[i]
                wq[i] = load_w(wh_ * AW + (g_ * 4 + s_) * 128)

        for i in range(PF):
            prefetch(i)
        ji = [0]

        def next_w():
            w, wk = wq.pop(ji[0])
            prefetch(ji[0] + PF)
            ji[0] += 1
            return w, wk

        for s in range(4):
            for g in range(3):
                win, dil = PATTERNS[g]
                L = SEQ // dil
                j = g * 4 + s
                ntile = L // 64
                for which in range(2):
                    w, wk = next_w()

                    def post(tn, a, ak):
                        p2, p2k = psr.next()
                        S.op("pe", lambda: nc.tensor.matmul(p2[:], lhsT=K["pswap"][:], rhs=a[:], start=True, stop=True),
                             reads=["k_pswap", ak], writes=[p2k])
                        t1, t1k = st1r.next()
                        t2, t2k = st2r.next()
                        S.op("dve", lambda: nc.vector.tensor_tensor(out=t1[:], in0=a[:], in1=Ct[:, tn * 512:(tn + 1) * 512], op=ALU.mult),
                             reads=[ak, "Ct"], writes=[t1k])
                        S.op("dve", lambda: nc.vector.tensor_tensor(out=t2[:], in0=p2[:], in1=St[:, tn * 512:(tn + 1) * 512], op=ALU.mult),
                             reads=[p2k, "St"], writes=[t2k])
                        ni = 512 // dil
                        i0_ = tn * ni
                        if which == 0:
                            dst = QT[:].rearrange("p (r i) -> p i r", r=dil)[:, i0_:i0_ + ni, :]
                            S.op("dve", lambda: nc.vector.tensor_tensor(out=dst, in0=t1[:].rearrange("p (i r) -> p i r", r=dil),
                                                                        in1=t2[:].rearrange("p (i r) -> p i r", r=dil), op=ALU.add),
                                 reads=[t1k, t2k], writes=["QT"])
                        else:
                            bc_ = min(64, ni)
                            ac_ = max(1, ni // 64)
                            a0, b0 = divmod(i0_, 64)
                            for hh in range(2):
                                dst = bass.AP(Kz, hh * 64 * (2 * SEQ) + a0 * 128 + hh * 64 + b0,
                                              [[2 * SEQ, 64], [128, ac_], [1, bc_], [2 * L, dil]])
                                eng = "dve"
                                fn = nc.vector.tensor_tensor
                                S.op(eng, lambda: fn(out=dst, in0=t1[hh * 64:(hh + 1) * 64, :].rearrange("p (a b r) -> p a b r", a=ac_, b=bc_),
                                                     in1=t2[hh * 64:(hh + 1) * 64, :].rearrange("p (a b r) -> p a b r", a=ac_, b=bc_), op=ALU.add),
                                     reads=[t1k, t2k], writes=["Kz"])

                    pend = None
                    for tn in range(8):
                        pj, pjk = proj_tile(w, wk, tn)
                        a, ak = stAr.next()
                        S.op("act", lambda: nc.scalar.activation(out=a[:], in_=pj[:], func=AF.Copy), reads=[pjk], writes=[ak])
                        if pend is not None:
                            post(*pend)
                        pend = (tn, a, ak)
                    post(*pend)
                w, wk = next_w()
                VT, VTk = Ug[g], f"Ug{g}"
                for tn in range(8):
                    pj, pjk = proj_tile(w, wk, tn)
                    S.op("act", lambda: nc.scalar.activation(out=VT[:, tn * 512:(tn + 1) * 512], in_=pj[:], func=AF.Copy),
                         reads=[pjk], writes=[VTk])
                for t0 in range(0, 64, 4):
                    tp, tpk = psr.next()
                    for jj in range(4):
                        t = t0 + jj
                        r, cc = divmod(t, ntile)
                        src = _ap(VT, SEQ, 0, 128, r + dil * 64 * cc, [[dil, 64]])
                        for hh in range(2):
                            S.op("pe", lambda: nc.tensor.matmul(tp[hh * 64:(hh + 1) * 64, jj * 128:(jj + 1) * 128], lhsT=src, rhs=K["ident"][:],
                                                                start=True, stop=True),
                                 reads=[VTk, "k_ident"], writes=[tpk])
                    tpv = tp[:].rearrange("p (j f) -> p j f", j=4)
                    S.op("dve", lambda: nc.vector.tensor_copy(out=Vbd[0:64, t0:t0 + 4, 0:64], in_=tpv[0:64, :, 0:64]),
                         reads=[tpk], writes=["Vbd"])
                    S.op("act", lambda: nc.scalar.activation(out=Vbd[64:128, t0:t0 + 4, 64:128], in_=tpv[64:128, :, 64:128], func=AF.Copy),
                         reads=[tpk], writes=["Vbd"])
                nb = L // 128
                blocks = [(r, b) for r in range(dil) for b in range(nb)]

                def stage1(r, b):
                    lo = 1 if b == 0 else 0
                    hi = 3 if b == nb - 1 else 4
                    s4, s4k = psr.next()
                    qcol = r * L + 128 * b
                    for i4 in range(lo, hi):
                        cc = 2 * b - 1 + i4
                        S.op("pe", lambda: nc.tensor.matmul(s4[:, i4 * 128:(i4 + 1) * 128],
                                                            lhsT=Kz[:, (r * ntile + cc) * 128:(r * ntile + cc + 1) * 128],
                                                            rhs=QT[:, qcol:qcol + 128], start=True, stop=True),
                             reads=["Kz", "QT"], writes=[s4k])
                    e, ek = Etr.next()
                    S.op("act", lambda: nc.scalar.activation(out=e[:, lo * 128:hi * 128], in_=s4[:, lo * 128:hi * 128],
                                                             func=AF.Exp, scale=0.125), reads=[s4k], writes=[ek])
                    pt_, ptk = PTr.next()
                    S.op("dve", lambda: nc.vector.tensor_tensor(out=pt_[:, lo * 128:hi * 128], in0=e[:, lo * 128:hi * 128],
                                                                in1=K["mask4"][:, lo * 128:hi * 128], op=ALU.mult),
                         reads=[ek, "k_mask4"], writes=[ptk])
                    return (r, b, lo, hi, pt_, ptk)

                def stage2(r, b, lo, hi, pt_, ptk):
                    ud, udk = psr.next()
                    for i4 in range(lo, hi):
                        cc = 2 * b - 1 + i4
                        S.op("pe", lambda: nc.tensor.matmul(ud[:, 0:128], lhsT=Vbd[:, r * ntile + cc, :], rhs=pt_[:, i4 * 128:(i4 + 1) * 128],
                                                            start=(i4 == lo), stop=(i4 == hi - 1)),
                             reads=["Vbd", ptk], writes=[udk])
                    for i4 in range(lo, hi):
                        S.op("pe", lambda: nc.tensor.matmul(ud[:, 128:256], lhsT=K["bdones"][:], rhs=pt_[:, i4 * 128:(i4 + 1) * 128],
                                                            start=(i4 == lo), stop=(i4 == hi - 1)),
                             reads=["k_bdones", ptk], writes=[udk])
                    tok0 = r + dil * 128 * b
                    udst = _ap(Ug[g], SEQ, 0, 128, tok0, [[dil, 128]])
                    S.op("act", lambda: nc.scalar.activation(out=udst, in_=ud[:, 0:128], func=AF.Copy), reads=[udk], writes=[f"Ug{g}"])
                    ddst = _ap(Dacc, SEQ, 0, 128, tok0, [[dil, 128]])
                    if g == 0:
                        S.op("dve", lambda: nc.vector.tensor_copy(out=ddst, in_=ud[:, 128:256]), reads=[udk], writes=["Dacc"])
                    else:
                        S.op("dve", lambda: nc.vector.tensor_tensor(out=ddst, in0=ud[:, 128:256], in1=ddst, op=ALU.add),
                             reads=[udk, "Dacc"], writes=["Dacc"])

                pend = stage1(*blocks[0])
                for bi in range(len(blocks)):
                    nxt = stage1(*blocks[bi + 1]) if bi + 1 < len(blocks) else None
                    stage2(*pend)
                    pend = nxt
            S.op("dve", lambda: nc.vector.reciprocal(out=Rinv[:], in_=Dacc[:]), reads=["Dacc"], writes=["Dacc"])
            for g in range(3):
                j = g * 4 + s
                w, wk = next_w()
                for tn in range(8):
                    pj, pjk = proj_tile(w, wk, tn)
                    a, ak = stAr.next()
                    S.op("act", lambda: nc.scalar.activation(out=a[:], in_=pj[:], func=AF.Silu), reads=[pjk], writes=[ak])
                    sl = slice(tn * 512, (tn + 1) * 512)
                    tf, tfk = tmpfr.next()
                    S.op("dve", lambda: nc.vector.tensor_tensor(out=tf[:], in0=Ug[g][:, sl], in1=Rinv[:, sl], op=ALU.mult),
                         reads=[f"Ug{g}", "Dacc"], writes=[tfk])
                    o, ok = Ostr.next()
                    S.op("dve", lambda: nc.vector.tensor_tensor(out=o[:], in0=tf[:], in1=a[:], op=ALU.mult),
                         reads=[tfk, ak], writes=[ok])
                    S.dma("sp", o_d[j * 128:(j + 1) * 128, sl], o[:], reads=[ok], writes=["o_d"])
        S.barrier()

    if stop_after == "C":
        S.finish()
        return nc, dbg

    with contextlib.ExitStack() as Dx:
        wout = sb(Dx, "wout", [128, 12, D], BF16)
        wstg = [sb(Dx, f"wstgD{i}", [128, 1024], F32) for i in range(2)]
        wstgr = Rot(wstg, "wstgD")
        for j in range(12):
            load_cast(wstgr, wout[:, j, :], "wout", awout_d[j * 128:(j + 1) * 128, :], [128, 1024])
        OTs = [sb(Dx, f"OT{i}", [128, 12, 512], BF16) for i in range(2)]
        OTr = Rot(OTs, "OT")
        xts = [sb(Dx, f"xt{i}", [128, D], F32) for i in range(2)]
        xr = Rot(xts, "xt")
        ytm = [sb(Dx, f"ytm{i}", [128, D], F32) for i in range(2)]
        ytr = Rot(ytm, "ytm")
        x1s = [sb(Dx, f"x1t{i}", [128, D], F32) for i in range(2)]
        x1r = Rot(x1s, "x1t")
        o_v = o_d.rearrange("(j p) t -> p j t", p=128)
        for tb in range(8):
            ot, otk = OTr.next()
            S.dma("sp", ot[:], o_v[:, :, tb * 512:(tb + 1) * 512], reads=["o_d"], writes=[otk])
            for t4 in range(4):
                tt = tb * 4 + t4
                xt, xk = xr.next()
                S.dma("sp", xt[:], x_d[tt * 128:(tt + 1) * 128, :], writes=[xk])
                yt, ytk = ytr.next()
                for n in range(2):
                    py, pyk = psr.next()
                    for j in range(12):
                        S.op("pe", lambda: nc.tensor.matmul(py[:], lhsT=ot[:, j, t4 * 128:(t4 + 1) * 128], rhs=wout[:, j, n * 512:(n + 1) * 512],
                                                            start=(j == 0), stop=(j == 11)),
                             reads=[otk, "wout"], writes=[pyk])
                    S.op("dve", lambda: nc.vector.tensor_tensor(out=yt[:, n * 512:(n + 1) * 512], in0=py[:], in1=gate_b[0][:, n * 512:(n + 1) * 512],
                                                                op=ALU.mult), reads=[pyk, "gate_b0"], writes=[ytk])
                x1, x1k = x1r.next()
                S.op("dve", lambda: nc.vector.tensor_tensor(out=x1[:], in0=yt[:], in1=xt[:], op=ALU.add), reads=[ytk, xk], writes=[x1k])
                S.dma("sp", x1_d[tt * 128:(tt + 1) * 128, :], x1[:], reads=[x1k], writes=["x1_d"])
                norm_tile(x1, x1k, tt, 1)
        S.barrier()

    if stop_after == "D":
        hn_dbg = nc.dram_tensor("hn_dbg", [128, 8, SEQ], BF16, kind="ExternalOutput").ap()
        dbg["hn_dbg"] = hn_dbg
        for kc in range(8):
            S.dma("sp", hn_dbg[:, kc, :], hnT[:, kc, :], reads=HN)
        S.finish()
        return nc, dbg

    sz_d = scratch("sz_d", [SEQ, SSD_INNER], BF16)
    xs_d = scratch("xs_d", [SEQ, SSD_INNER], BF16)
    bt_d = scratch("bt_d", [SEQ, 1024], BF16)
    bT_d = scratch("bT_d", [1024, SEQ], BF16)
    cT_d = scratch("cT_d", [1024, SEQ], BF16)
    pb_d = scratch("pb_d", [NT, 128, SSD_INNER], BF16)
    dt_d = scratch("dt_d", [NT, 128, 64], F32)
    swv = swin_d.rearrange("(kc p) n -> p kc n", p=128)

    with contextlib.ExitStack() as E:
        dtb_b = sb(E, "dtb_b", [128, 64], F32)
        S.dma("sp", dtb_b[:], dtb_d.partition_broadcast(128), writes=["dtb_b"])
        with contextlib.ExitStack() as E1:
            wz = sb(E1, "wz", [128, 8, SSD_INNER], BF16)
            wdt = sb(E1, "wdt", [128, 8, 64], BF16)
            wstg1 = [sb(E1, f"wstgE1{i}", [128, 1024], F32) for i in range(2)]
            wstg1r = Rot(wstg1, "wstgE1")
            for kc in range(8):
                for hh in range(2):
                    load_cast(wstg1r, wz[:, kc, hh * 1024:(hh + 1) * 1024], f"wz{kc}", swin_d[kc * 128:(kc + 1) * 128, hh * 1024:(hh + 1) * 1024], [128, 1024])
            load_cast(wstg1r, wdt[:], "wdt", swv[:, :, 6144:6208], [128, 8, 64])
            szts = [sb(E1, f"szt{i}", [128, SSD_INNER], BF16) for i in range(2)]
            sztr = Rot(szts, "szt")
            dtt = [sb(E1, f"dtt{i}", [128, 64], F32) for i in range(2)]
            dttr = Rot(dtt, "dtt")
            dto = [sb(E1, f"dto{i}", [128, 64], F32) for i in range(2)]
            dtor = Rot(dto, "dto")
            for tt in range(NT):
                szt, sztk = sztr.next()
                for n in range(4):
                    pz, pzk = psr.next()
                    for kc in range(8):
                        S.op("pe", lambda: nc.tensor.matmul(pz[:], lhsT=hnT[:, kc, tt * 128:(tt + 1) * 128], rhs=wz[:, kc, n * 512:(n + 1) * 512],
                                                            start=(kc == 0), stop=(kc == 7)), reads=[f"hnT{kc}", f"wz{kc}"], writes=[pzk])
                    S.op("act", lambda: nc.scalar.activation(out=szt[:, n * 512:(n + 1) * 512], in_=pz[:], func=AF.Silu), reads=[pzk], writes=[sztk])
                S.dma("sp", sz_d[tt * 128:(tt + 1) * 128, :], szt[:], reads=[sztk], writes=["sz_d"])
            for tt in range(NT):
                pd, pdk = psr.next()
                for kc in range(8):
                    S.op("pe", lambda: nc.tensor.matmul(pd[:, 0:64], lhsT=hnT[:, kc, tt * 128:(tt + 1) * 128], rhs=wdt[:, kc, :],
                                                        start=(kc == 0), stop=(kc == 7)), reads=[f"hnT{kc}", "wdt"], writes=[pdk])
                dtmp, dtk = dttr.next()
                S.op("dve", lambda: nc.vector.tensor_tensor(out=dtmp[:], in0=pd[:, 0:64], in1=dtb_b[:], op=ALU.add), reads=[pdk, "dtb_b"], writes=[dtk])
                S.op("act", lambda: nc.scalar.activation(out=dtmp[:], in_=dtmp[:], func=AF.Exp), reads=[dtk], writes=[dtk])
                dto_, dtok = dtor.next()
                S.op("act", lambda: nc.scalar.activation(out=dto_[:], in_=dtmp[:], func=AF.Ln, bias=onec[:, 0:1], scale=1.0),
                     reads=[dtk, "onec"], writes=[dtok])
                S.dma("sp", dt_d[tt], dto_[:], reads=[dtok], writes=["dt_d"])
            S.barrier()
        raws = [sb(E, f"raw{i}", [128, SEQ + 4], F32) for i in range(2)]
        for i in range(2):
            S.op("dve", lambda: nc.vector.memset(raws[i][:], 0.0), writes=[f"raw{i}"])
        rawr = Rot(raws, "raw")
        accs = [sb(E, f"acc{i}", [128, 1024], F32) for i in range(3)]
        accr = Rot(accs, "acc")
        xos = [sb(E, f"xo{i}", [128, SEQ], BF16) for i in range(4)]
        xor_ = Rot(xos, "xo")
        xtoks = [sb(E, f"xtok{i}", [128, 8, 128], BF16) for i in range(2)]
        xtokr = Rot(xtoks, "xtok")
        wxs = [sb(E, f"wx{i}", [128, 8, 128], BF16) for i in range(3)]
        wstgE = [sb(E, f"wstgE{i}", [128, 1024], F32) for i in range(2)]
        wstgEr = Rot(wstgE, "wstgE")
        wxr = Rot(wxs, "wx")
        xs_v = xs_d.rearrange("(t p) c -> p t c", p=128)
        bt_v = bt_d.rearrange("(t p) c -> p t c", p=128)
        def load_wx(cc_):
            wx_, wxk_ = wxr.next()
            load_cast(wstgEr, wx_[:], wxk_, swv[:, :, SSD_INNER + cc_ * 128:SSD_INNER + (cc_ + 1) * 128], [128, 8, 128])
            return wx_, wxk_

        pend_tok = []

        def to_tok(cc_, xo_, xok_):
            for tb in range(4):
                tp, tpk = ptr.next()
                for jj in range(8):
                    tt = tb * 8 + jj
                    S.op("pe", lambda: nc.tensor.transpose(tp[:, jj, :], xo_[:, tt * 128:(tt + 1) * 128], K["ident"][:]),
                         reads=[xok_, "k_ident"], writes=[tpk])
                xtok, xtokk = xtokr.next()
                S.op("act", lambda: nc.scalar.activation(out=xtok[:], in_=tp[:], func=AF.Copy), reads=[tpk], writes=[xtokk])
                if cc_ < 16:
                    S.dma("sp", xs_v[:, tb * 8:(tb + 1) * 8, cc_ * 128:(cc_ + 1) * 128], xtok[:], reads=[xtokk], writes=["xs_d"])
                else:
                    S.dma("sp", bt_v[:, tb * 8:(tb + 1) * 8, (cc_ - 16) * 128:(cc_ - 15) * 128], xtok[:], reads=[xtokk], writes=["bt_d"])

        wxq = {0: load_wx(0), 1: load_wx(1)}
        def e_proj(cc):
            wx, wxk = wxq.pop(cc)
            if cc + 2 < 32:
                wxq[cc + 2] = load_wx(cc + 2)
            raw, rawk = rawr.next()
            for tn in range(8):
                pj, pjk = psr.next()
                for kc in range(8):
                    S.op("pe", lambda: nc.tensor.matmul(pj[:], lhsT=wx[:, kc, :], rhs=hnT[:, kc, tn * 512:(tn + 1) * 512],
                                                        start=(kc == 0), stop=(kc == 7)), reads=[wxk, f"hnT{kc}"], writes=[pjk])
                S.op("act", lambda: nc.scalar.activation(out=raw[:, 2 + tn * 512:2 + (tn + 1) * 512], in_=pj[:], func=AF.Copy),
                     reads=[pjk], writes=[rawk])
            return raw, rawk

        def e_conv(cc, raw, rawk):
            xo, xok = xor_.next()
            for cb in range(4):
                acc, acck = accr.next()
                c0 = cb * 1024
                S.op("dve", lambda: nc.vector.tensor_scalar(out=acc[:], in0=raw[:, c0:c0 + 1024], scalar1=pcol[:, PC_CW + cc:PC_CW + cc + 1],
                                                            scalar2=None, op0=ALU.mult), reads=[rawk, "pcol"], writes=[acck])
                for k in range(1, 5):
                    S.op("dve", lambda: nc.vector.scalar_tensor_tensor(out=acc[:], in0=raw[:, c0 + k:c0 + k + 1024],
                                                                       scalar=pcol[:, PC_CW + k * 32 + cc:PC_CW + k * 32 + cc + 1],
                                                                       in1=acc[:], op0=ALU.mult, op1=ALU.add),
                         reads=[rawk, "pcol", acck], writes=[acck])
                S.op("act", lambda: nc.scalar.activation(out=xo[:, c0:c0 + 1024], in_=acc[:], func=AF.Silu,
                                                         bias=pcol[:, PC_CB + cc:PC_CB + cc + 1], scale=1.0), reads=[acck, "pcol"], writes=[xok])
            if cc >= 24:
                for hh in range(2):
                    S.dma("sp", cT_d[(cc - 24) * 128:(cc - 23) * 128, hh * 2048:(hh + 1) * 2048], xo[:, hh * 2048:(hh + 1) * 2048], reads=[xok], writes=["cT_d"])
            elif cc >= 16:
                for hh in range(2):
                    S.dma("sp", bT_d[(cc - 16) * 128:(cc - 15) * 128, hh * 2048:(hh + 1) * 2048], xo[:, hh * 2048:(hh + 1) * 2048], reads=[xok], writes=["bT_d"])
            if cc < 24:
                pend_tok.append((cc, xo, xok))

        prev_raw = None
        for cc in range(32):
            cur = e_proj(cc)
            if prev_raw is not None:
                e_conv(cc - 1, *prev_raw)
            prev_raw = cur
            while len(pend_tok) > 2 or (pend_tok and cc >= 24):
                to_tok(*pend_tok.pop(0))
        e_conv(31, *prev_raw)
        while pend_tok:
            to_tok(*pend_tok.pop(0))
        S.barrier()
    H.close()

    with contextlib.ExitStack() as G:
        arow = sb(G, "arow", [128, 64], F32)
        S.dma("sp", arow[:], alog_d.partition_broadcast(128), writes=["arow"])
        S.op("act", lambda: nc.scalar.activation(out=arow[:], in_=arow[:], func=AF.Exp), reads=["arow"], writes=["arow"])
        S.op("dve", lambda: nc.vector.tensor_scalar(out=arow[:], in0=arow[:], scalar1=-1.0, scalar2=None, op0=ALU.mult), reads=["arow"], writes=["arow"])
        drow = sb(G, "drow", [128, 32], F32)
        S.dma("sp", drow[:], sd_d.partition_broadcast(128), writes=["drow"])
        Dident = sb(G, "Dident", [128, 32, 128], BF16)
        for h in range(32):
            S.op("dve", lambda: nc.vector.tensor_scalar(out=Dident[:, h, :], in0=K["ident"][:], scalar1=drow[:, h:h + 1], scalar2=None, op0=ALU.mult),
                 reads=["k_ident", "drow"], writes=["Dident"])
        fnw_b = sb(G, "fnw_b", [128, D], F32)
        S.dma("sp", fnw_b[:], fnw_d.partition_broadcast(128), writes=["fnw_b"])
        ones128 = sb(G, "ones128", [128, 128], F32)
        S.op("dve", lambda: nc.vector.memset(ones128[:], 1.0), writes=["ones128"])
        ones1 = sb(G, "ones1", [2, 128], BF16)
        S.op("dve", lambda: nc.vector.memset(ones1[:], 1.0), writes=["ones1"])
        wout1 = sb(G, "wout1", [128, 16, D], BF16)
        with contextlib.ExitStack() as G0:
            wstg = [sb(G0, f"wstgG{i}", [128, 1024], F32) for i in range(2)]
            wstgr = Rot(wstg, "wstgG")
            for j in range(16):
                stg, stgk = wstgr.next()
                S.dma("sp", stg[:], swout_d[j * 128:(j + 1) * 128, :], writes=[stgk])
                S.op("dve", lambda: nc.vector.tensor_scalar(out=wout1[:, j, :], in0=stg[:], scalar1=pcol[:, PC_SNW + j:PC_SNW + j + 1], scalar2=None, op0=ALU.mult),
                     reads=[stgk, "pcol"], writes=["wout1"])
            S.barrier()
        carry = sb(G, "carry", [128, SSD_INNER], F32)
        xts_ = [sb(G, f"xc{i}", [128, SSD_INNER], BF16) for i in range(2)]
        xcr = Rot(xts_, "xc")
        bts_ = [sb(G, f"bc{i}", [128, 1024], BF16) for i in range(2)]
        bcr = Rot(bts_, "bc")
        dtcs = [sb(G, f"dtc{i}", [128, 64], F32) for i in range(4)]
        dtcr = Rot(dtcs, "dtc")
        das = [sb(G, f"da{i}", [128, 64], F32) for i in range(2)]
        dar = Rot(das, "da")
        scs = [sb(G, f"sc{i}", [128, 128], F32) for i in range(2)]
        scr_ = Rot(scs, "sc")
        smalls = [sb(G, f"sm{i}", [128, 6, 64], F32) for i in range(3)]
        smr = Rot(smalls, "sm")
        xws = [sb(G, f"xw{i}", [128, SSD_INNER], BF16) for i in range(1)]
        xwr = Rot(xws, "xw")
        cbfs = [sb(G, f"cbf{i}", [128, SSD_INNER], BF16) for i in range(1)]
        cbfr = Rot(cbfs, "cbf")

        def load_dt(c):
            dtc, dtck = dtcr.next()
            S.dma("sp", dtc[:], dt_d[c], reads=["dt_d"], writes=[dtck])
            return dtc, dtck

        def chunk_scalars(c, need_b_only, pre=None):
            dtc, dtck = pre if pre is not None else load_dt(c)
            da, dak = dar.next()
            S.op("dve", lambda: nc.vector.tensor_tensor(out=da[:], in0=dtc[:], in1=arow[:], op=ALU.mult), reads=[dtck, "arow"], writes=[dak])
            p0, p0k = psr.next()
            S.op("pe", lambda: nc.tensor.matmul(p0[:, 0:32], lhsT=K["triL"][:], rhs=da[:, 0:32], start=True, stop=True), reads=["k_triL", dak], writes=[p0k])
            S.op("pe", lambda: nc.tensor.matmul(p0[:, 32:64], lhsT=K["triU"][:], rhs=da[:, 32:64], start=True, stop=True), reads=["k_triU", dak], writes=[p0k])
            S.op("pe", lambda: nc.tensor.matmul(p0[:, 64:128], lhsT=ones128[:], rhs=da[:, 0:64], start=True, stop=True), reads=["ones128", dak], writes=[p0k])
            if not need_b_only:
                S.op("pe", lambda: nc.tensor.matmul(p0[0:32, 128:256], lhsT=da[:, 0:32], rhs=K["triL"][:], start=True, stop=True), reads=["k_triL", dak], writes=[p0k])
                S.op("pe", lambda: nc.tensor.matmul(p0[32:64, 128:256], lhsT=da[:, 32:64], rhs=K["triU"][:], start=True, stop=True), reads=["k_triU", dak], writes=[p0k])
            sc, sck = scr_.next()
            S.op("act", lambda: nc.scalar.activation(out=sc[:], in_=p0[:, 0:128], func=AF.Copy), reads=[p0k], writes=[sck])
            sm, smk = smr.next()
            S.op("dve", lambda: nc.vector.tensor_scalar(out=sm[:, 0, :], in0=sc[:, 0:64], scalar1=-1.0, scalar2=None, op0=ALU.mult), reads=[sck], writes=[smk])
            S.op("act", lambda: nc.scalar.activation(out=sm[:, 1, :], in_=sc[:, 0:64], func=AF.Exp), reads=[sck], writes=[smk])
            S.op("dve", lambda: nc.vector.tensor_tensor(out=sm[:, 4, :], in0=sc[:, 64:128], in1=sc[:, 0:64], op=ALU.subtract), reads=[sck], writes=[smk])
            S.op("act", lambda: nc.scalar.activation(out=sm[:, 5, :], in_=sm[:, 4, :], func=AF.Exp), reads=[smk], writes=[smk])
            S.op("dve", lambda: nc.vector.tensor_tensor(out=sm[:, 2, :], in0=sm[:, 5, :], in1=dtc[:], op=ALU.mult), reads=[smk, dtck], writes=[smk])
            S.op("act", lambda: nc.scalar.activation(out=sm[:, 3, :], in_=sc[:, 64:128], func=AF.Exp), reads=[sck], writes=[smk])
            return sm, smk, p0, p0k, dtc, dtck

        def state_update(c, xc, xck, bc, bck, sm, smk, d):
            xw, xwk = xwr.next()
            S.op("dve", lambda: nc.vector.tensor_tensor(out=xw[:].rearrange("p (h q) -> p h q", q=64), in0=xc[:].rearrange("p (h q) -> p h q", q=64),
                                                        in1=sm[:, 2, d * 32:(d + 1) * 32].unsqueeze(2).to_broadcast([128, 32, 64]), op=ALU.mult),
                 reads=[xck, smk], writes=[xwk])
            S.op("dve", lambda: nc.vector.tensor_tensor(out=carry[:].rearrange("p (h q) -> p h q", q=64), in0=carry[:].rearrange("p (h q) -> p h q", q=64),
                                                        in1=sm[:, 3, d * 32:(d + 1) * 32].unsqueeze(2).to_broadcast([128, 32, 64]), op=ALU.mult),
                 reads=["carry", smk], writes=["carry"])
            for q4 in range(4):
                pS, pSk = psr.next()
                for g2 in range(2):
                    g = q4 * 2 + g2
                    S.op("pe", lambda: nc.tensor.matmul(pS[:, g2 * 256:(g2 + 1) * 256], lhsT=bc[:, g * 128:(g + 1) * 128], rhs=xw[:, g * 256:(g + 1) * 256],
                                                        start=True, stop=True), reads=[bck, xwk], writes=[pSk])
                S.op("dve", lambda: nc.vector.tensor_tensor(out=carry[:, q4 * 512:(q4 + 1) * 512], in0=pS[:], in1=carry[:, q4 * 512:(q4 + 1) * 512], op=ALU.add),
                     reads=[pSk, "carry"], writes=["carry"])

        S.op("dve", lambda: nc.vector.memset(carry[:], 0.0), writes=["carry"])
        def f_prep(c):
            xc, xck = xcr.next()
            S.dma("sp", xc[:], xs_d[c * 128:(c + 1) * 128, :], reads=["xs_d"], writes=[xck])
            bc, bck = bcr.next()
            S.dma("sp", bc[:], bt_d[c * 128:(c + 1) * 128, :], reads=["bt_d"], writes=[bck])
            sm, smk, _, _, _, _ = chunk_scalars(c, True)
            return xc, xck, bc, bck, sm, smk

        fp_ = f_prep(NT - 1)
        for c in range(NT - 1, -1, -1):
            nfp = f_prep(c - 1) if c > 0 else None
            cbf, cbfk = cbfr.next()
            S.op("act", lambda: nc.scalar.activation(out=cbf[:], in_=carry[:], func=AF.Copy), reads=["carry"], writes=[cbfk])
            S.dma("sp", pb_d[c], cbf[:], reads=[cbfk], writes=["pb_d"])
            state_update(c, *fp_, 1)
            fp_ = nfp

        if stop_after == "F":
            S.barrier()
            S.finish()
            return nc, dbg

        S.op("dve", lambda: nc.vector.memset(carry[:], 0.0), writes=["carry"])
        BTs = [sb(G, f"BT{i}", [128, 8, 128], BF16) for i in range(1)]
        BTr = Rot(BTs, "BT")
        CTs = [sb(G, f"CT{i}", [128, 8, 128], BF16) for i in range(2)]
        CTr = Rot(CTs, "CT")
        szs = [sb(G, f"szc{i}", [128, SSD_INNER], BF16) for i in range(1)]
        szr = Rot(szs, "szc")
        pbs = [sb(G, f"pbc{i}", [128, SSD_INNER], BF16) for i in range(1)]
        pbr = Rot(pbs, "pbc")
        hls = [sb(G, f"hl{i}", [64, 2, 128], BF16) for i in range(3)]
        hlr = Rot(hls, "hl")
        rows = [sb(G, f"row{i}", [2, 8 * 128], BF16) for i in range(2)]
        rowr = Rot(rows, "row")
        Lts = [sb(G, f"Lt{i}", [128, 32, 128], BF16) for i in range(4)]
        Ltr = Rot(Lts, "Lt")
        CBs = [sb(G, f"CBs{i}", [128, 8, 128], BF16) for i in range(1)]
        CBr = Rot(CBs, "CBs")
        xdts = [sb(G, f"xdt{i}", [128, SSD_INNER], BF16) for i in range(2)]
        ycs = [sb(G, f"yc{i}", [128, 512], F32) for i in range(1)]
        ycr = Rot(ycs, "yc")
        yts = [sb(G, f"ytmp{i}", [128, 512], F32) for i in range(1)]
        ytr = Rot(yts, "ytmp")
        yg = sb(G, "yg", [128, SSD_INNER], F32)
        ynb = sb(G, "ynb", [128, SSD_INNER], BF16)
        ynT = sb(G, "ynT", [128, 16, 128], BF16)
        nst2 = [sb(G, f"nst2{i}", [128, 8], F32) for i in range(2)]
        nst2r = Rot(nst2, "nst2")
        x2s = [sb(G, f"x2{i}", [128, D], F32) for i in range(2)]
        x2r = Rot(x2s, "x2")
        outs = [sb(G, f"ot{i}", [128, D], F32) for i in range(2)]
        outr = Rot(outs, "ot")
        bT_v = bT_d.rearrange("(g p) t -> p g t", p=128)
        cT_v = cT_d.rearrange("(g p) t -> p g t", p=128)
        ident4 = K["ident"][:].unsqueeze(1).to_broadcast([128, 4, 128])
        def g_loads(c):
            tsl = slice(c * 128, (c + 1) * 128)
            xc, xck = xcr.next()
            S.dma("sp", xc[:], xs_d[tsl, :], reads=["xs_d"], writes=[xck])
            bc, bck = bcr.next()
            S.dma("sp", bc[:], bt_d[tsl, :], reads=["bt_d"], writes=[bck])
            BT, BTk = BTr.next()
            S.dma("sp", BT[:], bT_v[:, :, tsl], reads=["bT_d"], writes=[BTk])
            CT, CTk = CTr.next()
            S.dma("sp", CT[:], cT_v[:, :, tsl], reads=["cT_d"], writes=[CTk])
            dt_ = load_dt(c)
            return (tsl, xc, xck, bc, bck, BT, BTk, CT, CTk, dt_)

        def g_scal(c, dt_):
            sm, smk, p0, p0k, dtc, dtck = chunk_scalars(c, False, dt_)
            hl, hlk = hlr.next()
            S.op("act", lambda: nc.scalar.activation(out=hl[:, 0, :], in_=p0[0:64, 128:256], func=AF.Copy), reads=[p0k], writes=[hlk])
            S.op("dve", lambda: nc.vector.tensor_tensor(out=hl[:, 1, :], in0=p0[0:64, 128:256], in1=hl[:, 0, :], op=ALU.subtract), reads=[p0k, hlk], writes=[hlk])
            return (sm, smk, dtc, dtck, hl, hlk)

        def g_head(c, L, SC):
            tsl, xc, xck, bc, bck, BT, BTk, CT, CTk, dt_ = L
            sm, smk, dtc, dtck, hl, hlk = SC
            CB, CBk = CBr.next()
            for gh in range(2):
                pcb, pcbk = psr.next()
                for g4 in range(4):
                    g = gh * 4 + g4
                    S.op("pe", lambda: nc.tensor.matmul(pcb[:, g4 * 128:(g4 + 1) * 128], lhsT=BT[:, g, :], rhs=CT[:, g, :], start=True, stop=True),
                         reads=[BTk, CTk], writes=[pcbk])
                S.op("act", lambda: nc.scalar.activation(out=CB[:, gh * 4:(gh + 1) * 4, :].rearrange("p g l -> p (g l)"), in_=pcb[:], func=AF.Copy),
                     reads=[pcbk], writes=[CBk])
            Ms = []
            tasks = []
            mtasks = []
            rowbox = {}
            for d in range(2):
                Lt, Ltk = Ltr.next()
                Ms.append((Lt, Ltk))
                for r16 in range(4):
                    for hb4 in range(2):
                        def task(d=d, r16=r16, hb4=hb4, Lt=Lt, Ltk=Ltk):
                            nm = K["nmaskf"] if d == 0 else K["nmaskb"]
                            nmk = "k_nmaskf" if d == 0 else "k_nmaskb"
                            if hb4 == 0:
                                row, rowk = rowr.next()
                                p0_ = d * 32 + r16 * 8
                                for j_ in range(2):
                                    S.dma("sp", row[j_:j_ + 1, :].rearrange("o (p f) -> o p f", p=8), hl[p0_:p0_ + 8, j_, :],
                                          reads=[hlk], writes=[rowk])
                                rowbox[(d, r16)] = (row, rowk)
                            row, rowk = rowbox[(d, r16)]
                            hb = r16 * 2 + hb4
                            lp, lpk = psr.next()
                            S.op("pe", lambda: nc.tensor.matmul(lp[:], lhsT=ones1[:], rhs=row[:, hb4 * 512:(hb4 + 1) * 512], start=True, stop=False),
                                 reads=["ones1", rowk], writes=[lpk])
                            S.op("pe", lambda: nc.tensor.matmul(lp[:], lhsT=nm[:], rhs=ident4, start=False, stop=True), reads=[nmk, "k_ident"], writes=[lpk])
                            for j4 in range(4):
                                h = hb * 4 + j4
                                S.op("act", lambda: nc.scalar.activation(out=Lt[:, h, :], in_=lp[:, j4 * 128:(j4 + 1) * 128], func=AF.Exp,
                                                                         bias=sm[:, 0, d * 32 + h:d * 32 + h + 1], scale=1.0), reads=[lpk, smk], writes=[Ltk])
                        tasks.append(task)

                        def mtask(d=d, r16=r16, hb4=hb4, Lt=Lt, Ltk=Ltk):
                            hb = r16 * 2 + hb4
                            S.op("dve", lambda: nc.vector.tensor_tensor(out=Lt[:, hb * 4:(hb + 1) * 4, :], in0=Lt[:, hb * 4:(hb + 1) * 4, :],
                                                                        in1=CB[:, hb:hb + 1, :].to_broadcast([128, 4, 128]), op=ALU.mult),
                                 reads=[Ltk, CBk], writes=[Ltk])
                        mtasks.append(mtask)

            def fin():
                for d in range(2):
                    S.op("dve", lambda: nc.vector.tensor_tensor(out=xdts[d][:].rearrange("p (h q) -> p h q", q=64), in0=xc[:].rearrange("p (h q) -> p h q", q=64),
                                                                in1=dtc[:, d * 32:(d + 1) * 32].unsqueeze(2).to_broadcast([128, 32, 64]), op=ALU.mult),
                         reads=[xck, dtck], writes=[f"xdt{d}"])

            def late_loads():
                szc, szk = szr.next()
                S.dma("sp", szc[:], sz_d[tsl, :], reads=["sz_d"], writes=[szk])
                pbc, pbk = pbr.next()
                S.dma("sp", pbc[:], pb_d[c], reads=["pb_d"], writes=[pbk])
                x2, x2k = x2r.next()
                S.dma("sp", x2[:], x1_d[tsl, :], reads=["x1_d"], writes=[x2k])
                X.update(szc=szc, szk=szk, pbc=pbc, pbk=pbk, x2=x2, x2k=x2k)

            X = dict(c=c, tsl=tsl, xc=xc, xck=xck, bc=bc, bck=bck, CT=CT, CTk=CTk,
                     sm=sm, smk=smk, Ms=Ms, tasks=tasks, mtasks=mtasks, fin=fin, late_loads=late_loads)
            return X

        def g_mid(X, run, pre_update):
            c, tsl, xc, xck, bc, bck, CT, CTk = X["c"], X["tsl"], X["xc"], X["xck"], X["bc"], X["bck"], X["CT"], X["CTk"]
            szc, szk, pbc, pbk, sm, smk, Ms = X["szc"], X["szk"], X["pbc"], X["pbk"], X["sm"], X["smk"], X["Ms"]
            cbf, cbfk = X["cbf"], X["cbfk"]
            for q4 in range(4):
                csl = slice(q4 * 512, (q4 + 1) * 512)
                pyd, pydk = psr.next()
                for h8 in range(8):
                    h = q4 * 8 + h8
                    for d in range(2):
                        S.op("pe", lambda: nc.tensor.matmul(pyd[:, h8 * 64:(h8 + 1) * 64], lhsT=Ms[d][0][:, h, :], rhs=xdts[d][:, h * 64:(h + 1) * 64],
                                                            start=(d == 0), stop=False), reads=[Ms[d][1], f"xdt{d}"], writes=[pydk])
                    S.op("pe", lambda: nc.tensor.matmul(pyd[:, h8 * 64:(h8 + 1) * 64], lhsT=Dident[:, h, :], rhs=xc[:, h * 64:(h + 1) * 64],
                                                        start=False, stop=True), reads=["Dident", xck], writes=[pydk])
                pof, pofk = psr.next()
                pob, pobk = psr.next()
                for g2 in range(2):
                    g = q4 * 2 + g2
                    S.op("pe", lambda: nc.tensor.matmul(pof[:, g2 * 256:(g2 + 1) * 256], lhsT=CT[:, g, :], rhs=cbf[:, g * 256:(g + 1) * 256], start=True, stop=True),
                         reads=[CTk, cbfk], writes=[pofk])
                    S.op("pe", lambda: nc.tensor.matmul(pob[:, g2 * 256:(g2 + 1) * 256], lhsT=CT[:, g, :], rhs=pbc[:, g * 256:(g + 1) * 256], start=True, stop=True),
                         reads=[CTk, pbk], writes=[pobk])
                yc, yck = ycr.next()
                yt_, ytk = ytr.next()
                v3 = lambda t_: t_[:].rearrange("p (h q) -> p h q", q=64)
                ef = sm[:, 1, q4 * 8:q4 * 8 + 8].unsqueeze(2).to_broadcast([128, 8, 64])
                eb = sm[:, 1, 32 + q4 * 8:32 + q4 * 8 + 8].unsqueeze(2).to_broadcast([128, 8, 64])
                S.op("dve", lambda: nc.vector.tensor_tensor(out=v3(yc), in0=pof[:].rearrange("p (h q) -> p h q", q=64), in1=ef, op=ALU.mult), reads=[pofk, smk], writes=[yck])
                S.op("dve", lambda: nc.vector.tensor_tensor(out=v3(yt_), in0=pob[:].rearrange("p (h q) -> p h q", q=64), in1=eb, op=ALU.mult), reads=[pobk, smk], writes=[ytk])
                S.op("dve", lambda: nc.vector.tensor_tensor(out=yc[:], in0=yc[:], in1=yt_[:], op=ALU.add), reads=[yck, ytk], writes=[yck])
                S.op("dve", lambda: nc.vector.tensor_tensor(out=yc[:], in0=pyd[:], in1=yc[:], op=ALU.add), reads=[pydk, yck], writes=[yck])
                S.op("dve", lambda: nc.vector.tensor_tensor(out=yg[:, csl], in0=yc[:], in1=szc[:, csl], op=ALU.mult), reads=[yck, szk], writes=["yg"])
                run(3)
            pre_update()
            state_update(c, xc, xck, bc, bck, sm, smk, 0)

        def g_tail_a(X):
            ns, nsk = nst2r.next()
            S.op("act", lambda: nc.scalar.activation(out=ynb[:], in_=yg[:], func=AF.Square, accum_out=ns[:, 0:1]), reads=["yg"], writes=["ynb", nsk])
            S.op("act", lambda: nc.scalar.activation(out=ns[:, 1:2], in_=ns[:, 0:1], func=AF.Ln, bias=epsc[:, 0:1], scale=1.0 / SSD_INNER), reads=[nsk, "epsc"], writes=[nsk])
            S.op("act", lambda: nc.scalar.activation(out=ns[:, 2:3], in_=ns[:, 1:2], func=AF.Exp, scale=-0.5), reads=[nsk], writes=[nsk])
            S.op("dve", lambda: nc.vector.tensor_scalar(out=ynb[:], in0=yg[:], scalar1=ns[:, 2:3], scalar2=None, op0=ALU.mult),
                 reads=["yg", nsk], writes=["ynb"])
            for jh in range(2):
                tp, tpk = ptr.next()
                for jj in range(8):
                    j = jh * 8 + jj
                    S.op("pe", lambda: nc.tensor.transpose(tp[:, jj, :], ynb[:, j * 128:(j + 1) * 128], K["ident"][:]), reads=["ynb", "k_ident"], writes=[tpk])
                S.op("act", lambda: nc.scalar.activation(out=ynT[:, jh * 8:(jh + 1) * 8, :], in_=tp[:], func=AF.Copy), reads=[tpk], writes=["ynT"])

        def g_tail_b(X):
            c, tsl, x2, x2k = X["c"], X["tsl"], X["x2"], X["x2k"]
            y2, y2k = yg, "yg"
            for n in range(2):
                po, pok = psr.next()
                for j in range(16):
                    S.op("pe", lambda: nc.tensor.matmul(po[:], lhsT=ynT[:, j, :], rhs=wout1[:, j, n * 512:(n + 1) * 512], start=(j == 0), stop=(j == 15)),
                         reads=["ynT", "wout1"], writes=[pok])
                S.op("dve", lambda: nc.vector.tensor_tensor(out=y2[:, n * 512:(n + 1) * 512], in0=po[:], in1=gate_b[1][:, n * 512:(n + 1) * 512], op=ALU.mult),
                     reads=[pok, "gate_b1"], writes=[y2k])
            S.op("dve", lambda: nc.vector.tensor_tensor(out=x2[:], in0=y2[:, 0:D], in1=x2[:], op=ALU.add), reads=[y2k, x2k], writes=[x2k])
            ns, nsk = nst2r.next()
            S.op("act", lambda: nc.scalar.activation(out=nrm_junk[:], in_=x2[:], func=AF.Square, accum_out=ns[:, 0:1]), reads=[x2k], writes=["nrm_junk", nsk])
            S.op("act", lambda: nc.scalar.activation(out=ns[:, 1:2], in_=ns[:, 0:1], func=AF.Ln, bias=epsc[:, 0:1], scale=1.0 / D), reads=[nsk, "epsc"], writes=[nsk])
            S.op("act", lambda: nc.scalar.activation(out=ns[:, 2:3], in_=ns[:, 1:2], func=AF.Exp, scale=-0.5), reads=[nsk], writes=[nsk])
            ot, otk = outr.next()
            S.op("dve", lambda: nc.vector.scalar_tensor_tensor(out=ot[:], in0=x2[:], scalar=ns[:, 2:3], in1=fnw_b[:], op0=ALU.mult, op1=ALU.mult),
                 reads=[x2k, nsk, "fnw_b"], writes=[otk])
            X["store"] = lambda: S.dma("sp", out_d[tsl, :], ot[:], reads=[otk], writes=["out_d"])

        def make_runner(X):
            tl = list(X["tasks"]) if X is not None else []
            ml = list(X["mtasks"]) if X is not None else []
            done = [0]

            def run(n):
                for _ in range(n):
                    if tl:
                        tl.pop(0)()
                        done[0] += 1
                        if done[0] > 2 and ml:
                            ml.pop(0)()

            def flush():
                run(len(tl))
                while ml:
                    ml.pop(0)()
            return run, flush

        Ld = {0: g_loads(0)}
        Sc = {0: g_scal(0, Ld[0][-1])}
        ctx = g_head(0, Ld.pop(0), Sc.pop(0))
        run, flush = make_runner(ctx)
        flush()
        ctx["fin"]()
        ctx["late_loads"]()
        if NT > 1:
            Ld[1] = g_loads(1)
            Sc[1] = g_scal(1, Ld[1][-1])
        pend_store = None

        def emit_cbf(X):
            cbf, cbfk = cbfr.next()
            S.op("act", lambda: nc.scalar.activation(out=cbf[:], in_=carry[:], func=AF.Copy), reads=["carry"], writes=[cbfk])
            X["cbf"], X["cbfk"] = cbf, cbfk

        emit_cbf(ctx)
        for c in range(NT):
            nxt = g_head(c + 1, Ld.pop(c + 1), Sc.pop(c + 1)) if c + 1 < NT else None
            run, flush = make_runner(nxt)
            g_mid(ctx, run, lambda: g_tail_a(ctx))
            if nxt is not None:
                emit_cbf(nxt)
            flush()
            if c + 2 < NT:
                Ld[c + 2] = g_loads(c + 2)
                Sc[c + 2] = g_scal(c + 2, Ld[c + 2][-1])
            if nxt is not None:
                nxt["late_loads"]()
            if pend_store is not None:
                pend_store()
            g_tail_b(ctx)
            pend_store = ctx["store"]
            if nxt is not None:
                nxt["fin"]()
            ctx = nxt
        pend_store()
        S.barrier()
    S.finish()
    return nc, dbg


def make_in_maps(inputs):
    cst = host_consts()
    pcol = host_pcol(inputs)
    f = lambda a: np.ascontiguousarray(np.asarray(a), dtype=np.float32)
    shared = {
        "mod_w": f(inputs["mod_w"]), "mod_b": f(inputs["mod_b"]),
        "attn_w_in": f(inputs["attn_w_in"][0]), "attn_w_out": f(inputs["attn_w_out"][0]),
        "ssd_w_in": f(inputs["ssd_w_in"][0]),
        "ssd_dt_bias": f(inputs["ssd_dt_bias"]).reshape(1, 64), "ssd_a_log": f(inputs["ssd_a_log"]).reshape(1, 64),
        "ssd_d": f(inputs["ssd_d"]).reshape(1, 32), "ssd_norm_w": f(inputs["ssd_norm_w"]).reshape(1, SSD_INNER),
        "ssd_w_out": f(inputs["ssd_w_out"][0]), "final_norm_w": f(inputs["final_norm_w"]).reshape(1, D),
        "pcol": pcol,
    }
    for k, v in cst.items():
        shared["k_" + k] = v
    x = f(inputs["x"])
    c = f(inputs["c"])
    pos = np.ascontiguousarray(np.asarray(inputs["positions"]), dtype=np.int32)
    maps = []
    for b in range(x.shape[0]):
        m = dict(shared)
        m["x"] = x[b]
        m["c"] = np.ascontiguousarray(c[b].reshape(8, 128).T)
        m["pos"] = pos[b:b + 1]
        maps.append(m)
    return maps


def kernel(**inputs):
    nc, _ = build()
    maps = make_in_maps(inputs)
    res = run_bass_kernel_spmd(nc, maps, core_ids=list(range(8)))
    return np.stack([np.asarray(r["out"], dtype=np.float32) for r in res.results], axis=0)
```

```python
import contextlib
import math
import numpy as np
import ml_dtypes
import concourse.bass as bass
import concourse.mybir as mybir
from concourse.bass_utils import run_bass_kernel_spmd

F32 = mybir.dt.float32
BF16 = mybir.dt.bfloat16
I32 = mybir.dt.int32
AF = mybir.ActivationFunctionType
ALU = mybir.AluOpType

D = 1024
SEQ = 4096
NT = SEQ // 128
AW = 1536
PATTERNS = ((128, 1), (512, 4), (2048, 16))
SSD_INNER = 2048
SSD_IN = 6208
EPS = 1e-6
import os
KSTEP = int(os.environ.get('KSTEP', '9'))
KPOOL = int(os.environ.get('KPOOL', '0'))


class Sched:
    NDMA = 10

    def __init__(self, nc, es, needed=None):
        self.nc = nc
        self.es = es
        self.needed = needed
        self.record = set()
        self.engs = {"pe": nc.tensor, "dve": nc.vector, "act": nc.scalar,
                     "pool": nc.gpsimd, "sp": nc.sync}
        self.sem = {}
        self.raw = {}
        self.pub = {}
        for e in self.engs:
            self.sem[e] = self.es.enter_context(nc.semaphore("s_" + e))
            self.raw[e] = 0
            self.pub[e] = 0
        self.dsem, self.dcnt, self.dnext = {}, {}, {}
        for q in ("sp",):
            self.dsem[q] = [self.es.enter_context(nc.semaphore(f"d_{q}{i}")) for i in range(self.NDMA)]
            self.dcnt[q] = [0] * self.NDMA
            self.dnext[q] = 0
        self.seen = {e: {} for e in self.engs}
        self.lastw = {}
        self.readers = {}
        self.ninstr = 0
        self.nwaits = 0

    def _wait(self, e, ev):
        sem, val, src, rid = ev
        if src == "pe" and e == "pe":
            return
        k = id(sem)
        if self.seen[e].get(k, 0) >= val:
            return
        if rid is not None:
            self.record.add(rid)
        self.engs[e].wait_ge(sem, val)
        self.seen[e][k] = val
        self.nwaits += 1

    def _deps(self, e, reads, writes):
        for k in reads:
            ev = self.lastw.get(k)
            if ev is not None:
                self._wait(e, ev)
            if k[:2] in ("ps", "pt"):
                for ev in self.readers.get(k, ()):
                    self._wait(e, ev)
        for k in writes:
            ev = self.lastw.get(k)
            if ev is not None:
                self._wait(e, ev)
            for ev in self.readers.get(k, ()):
                self._wait(e, ev)

    def _commit(self, ev, reads, writes):
        for k in reads:
            self.readers.setdefault(k, []).append(ev)
        for k in writes:
            self.lastw[k] = ev
            self.readers[k] = []

    def op(self, e, fn, reads=(), writes=()):
        self._deps(e, reads, writes)
        ins = fn()
        self.raw[e] += 1
        rid = (e, self.raw[e])
        if self.needed is None or rid in self.needed:
            self.pub[e] += 1
            ins.then_inc(self.sem[e], 1)
            ev = (self.sem[e], self.pub[e], e, rid)
        else:
            ev = (self.sem[e], self.pub[e] + 1, e, rid)
        self._commit(ev, reads, writes)
        self.ninstr += 1
        return ev

    def dma(self, q, out, in_, reads=(), writes=()):
        i = self.dnext[q]
        self.dnext[q] = (i + 1) % self.NDMA
        sem = self.dsem[q][i]
        if self.dcnt[q][i] > 0:
            self._wait(q, (sem, self.dcnt[q][i], None, None))
        self._deps(q, reads, writes)
        ins = self.engs[q].dma_start(out=out, in_=in_)
        self.dcnt[q][i] += 16
        ins.then_inc(sem, 16)
        ev = (sem, self.dcnt[q][i], None, None)
        self._commit(ev, reads, writes)
        self.ninstr += 1
        return ev

    def barrier(self):
        evs = []
        for e in self.engs:
            if self.raw[e] > 0:
                rid = (e, self.raw[e])
                pubd = self.needed is None or rid in self.needed
                evs.append((self.sem[e], self.pub[e] if pubd else self.pub[e] + 1, e, rid))
        for q in self.dsem:
            for i, sem in enumerate(self.dsem[q]):
                if self.dcnt[q][i] > 0:
                    evs.append((sem, self.dcnt[q][i], None, None))
        for e in self.engs:
            for ev in evs:
                if ev[2] == e:
                    continue
                self._wait(e, ev)
        self.lastw = {}
        self.readers = {}

    def finish(self):
        for q in self.dsem:
            for i, sem in enumerate(self.dsem[q]):
                if self.dcnt[q][i] > 0:
                    self._wait("sp", (sem, self.dcnt[q][i], None, None))


class Rot:
    def __init__(self, tiles, name):
        self.tiles = tiles
        self.name = name
        self.i = -1

    def next(self):
        self.i = (self.i + 1) % len(self.tiles)
        return self.tiles[self.i], f"{self.name}{self.i}"


def _ap(t, rowsize, p0, npart, col, dims):
    return bass.AP(t, p0 * rowsize + col, [[rowsize, npart]] + [list(d) for d in dims])


def _bf(a):
    return np.asarray(a, dtype=np.float32).astype(ml_dtypes.bfloat16)


def host_consts():
    c = {}
    p = np.arange(128)
    c["ident"] = _bf(np.eye(128))
    c["bdones"] = _bf((p[:, None] // 64) == (p[None, :] // 64))
    pm = np.zeros((128, 128), np.float32)
    for hb in (0, 64):
        for j in range(8):
            pm[hb + j + 8, hb + j] = 1.0
            pm[hb + j, hb + 8 + j] = 1.0
    c["pswap"] = _bf(pm)
    i = (p % 64)[:, None, None]
    i4 = np.arange(4)[None, :, None]
    jq = np.arange(128)[None, None, :]
    c["mask4"] = _bf(np.abs(64 * (i4 - 1) + i - jq) <= 64).reshape(128, 512)
    k = p[:, None]
    l = p[None, :]
    c["triL"] = (k <= l).astype(np.float32)
    c["triU"] = (k >= l).astype(np.float32)
    c["nmaskf"] = _bf(np.where(l <= k, 0.0, -30000.0))
    c["nmaskb"] = _bf(np.where(l >= k, 0.0, -30000.0))
    c["maskf"] = _bf(k <= l)
    c["maskb"] = _bf(k >= l)
    return c


def host_pcol(inp):
    cols = {}
    p = np.arange(128)
    inv = (500000.0 ** (-np.arange(0, 16, 2, dtype=np.float32) / 16.0)).astype(np.float32)
    pp = p % 64
    invf = np.where(pp < 16, inv[pp % 8], 0.0).astype(np.float32)
    sg = np.where(pp < 8, -1.0, np.where(pp < 16, 1.0, 0.0)).astype(np.float32)
    parts = [invf[:, None], sg[:, None]]
    parts.append(np.asarray(inp["norm_w"], np.float32).reshape(2, 8, 128).transpose(2, 0, 1).reshape(128, 16))
    parts.append(np.asarray(inp["mod_b"], np.float32).reshape(2, 24, 128).transpose(2, 0, 1).reshape(128, 48))
    parts.append(np.asarray(inp["ssd_conv_w"], np.float32).reshape(5, 32, 128).transpose(2, 0, 1).reshape(128, 160))
    parts.append(np.asarray(inp["ssd_conv_b"], np.float32).reshape(32, 128).T)
    parts.append(np.asarray(inp["ssd_norm_w"], np.float32).reshape(16, 128).T)
    return np.ascontiguousarray(np.concatenate(parts, axis=1), dtype=np.float32)


PC_INVF, PC_SG, PC_NW, PC_MODB, PC_CW, PC_CB, PC_SNW, PC_N = 0, 1, 2, 18, 66, 226, 258, 274


def build(stop_after=None):
    M = contextlib.ExitStack()
    rec = {}
    _build(M, stop_after, None, rec)
    M.close()
    M = contextlib.ExitStack()
    nc, dbg = _build(M, stop_after, rec["needed"], {})
    M.close()
    return nc, dbg


def _build(M, stop_after=None, needed=None, rec=None):
    nc = bass.Bass("TRN2", target_bir_lowering=False)
    S = Sched(nc, M, needed)
    rec["needed"] = S.record
    dbg = {}

    def din(name, shape, dt=F32):
        return nc.dram_tensor(name, list(shape), dt, kind="ExternalInput").ap()

    x_d = din("x", [SEQ, D])
    c_d = din("c", [128, 8])
    pos_d = din("pos", [1, SEQ], I32)
    modw_d = din("mod_w", [2, D, 3 * D])
    modb_d = din("mod_b", [2, 3 * D])
    awin_d = din("attn_w_in", [D, 4 * AW])
    awout_d = din("attn_w_out", [AW, D])
    swin_d = din("ssd_w_in", [D, SSD_IN])
    dtb_d = din("ssd_dt_bias", [1, 64])
    alog_d = din("ssd_a_log", [1, 64])
    sd_d = din("ssd_d", [1, 32])
    snw_d = din("ssd_norm_w", [1, SSD_INNER])
    swout_d = din("ssd_w_out", [SSD_INNER, D])
    fnw_d = din("final_norm_w", [1, D])
    pcol_d = din("pcol", [128, PC_N])
    cst = host_consts()
    cst_d = {k: din("k_" + k, list(v.shape), BF16 if v.dtype != np.float32 else F32) for k, v in cst.items()}
    out_d = nc.dram_tensor("out", [SEQ, D], F32, kind="ExternalOutput").ap()

    def scratch(name, shape, dt):
        kind = "ExternalOutput" if stop_after is not None else "Internal"
        t = nc.dram_tensor(name, list(shape), dt, kind=kind).ap()
        dbg[name] = t
        return t

    o_d = scratch("o_d", [AW, SEQ], BF16)
    x1_d = scratch("x1_d", [SEQ, D], F32)

    P = M

    uid = [0]

    def sb(stack, name, shape, dt):
        uid[0] += 1
        return stack.enter_context(nc.sbuf_tensor(f"sb{uid[0]}_{name}", list(shape), dt))

    ps = [P.enter_context(nc.psum_tensor(f"ps{i}", [128, 512], F32)) for i in range(6)]
    pt = [P.enter_context(nc.psum_tensor(f"pt{i}", [128, 8, 128], BF16)) for i in range(2)]
    psr = Rot(ps, "ps")
    ptr = Rot(pt, "pt")

    def load_cast(stg_rot, dst, dkey, src, shape):
        stg, stgk = stg_rot.next()
        n = 1
        for d_ in shape[1:]:
            n *= d_
        sv = stg[:, 0:n]
        if len(shape) == 3:
            sv = sv.rearrange("p (a b) -> p a b", a=shape[1])
        S.dma("sp", sv, src, writes=[stgk])
        S.op("act", lambda: nc.scalar.activation(out=dst, in_=sv, func=AF.Copy), reads=[stgk], writes=[dkey])

    pcol = sb(P, "pcol", [128, PC_N], F32)
    S.dma("sp", pcol[:], pcol_d, writes=["pcol"])
    K = {}
    for k, v in cst.items():
        K[k] = sb(P, "k_" + k, list(v.shape), BF16 if v.dtype != np.float32 else F32)
        S.dma("sp", K[k][:], cst_d[k], writes=["k_" + k])
    if stop_after is not None:
        junk = sb(P, "junk", [1, 16], F32)
        junki = sb(P, "junki", [1, 16], I32)
        for t_ in (awin_d, awout_d, swin_d, dtb_d, alog_d, sd_d, snw_d, swout_d, fnw_d):
            S.dma("sp", junk[:], t_[0:1, 0:16], writes=["junk"])
        S.dma("sp", junki[:], pos_d[0:1, 0:16], writes=["junki"])
    epsc = sb(P, "epsc", [128, 1], F32)
    S.op("dve", lambda: nc.vector.memset(epsc[:], EPS), writes=["epsc"])
    one11 = sb(P, "one11", [1, 1], F32)
    S.op("dve", lambda: nc.vector.memset(one11[:], 1.0), writes=["one11"])
    modA = sb(P, "modA", [128, 16], F32)
    modB = sb(P, "modB", [128, 16], F32)
    gate_b = [sb(P, f"gate_b{i}", [128, D], F32) for i in range(2)]
    nrm_junk = sb(P, "nrm_junk", [128, D], BF16)
    xn_t = [sb(P, f"xn{i}", [128, D], BF16) for i in range(2)]
    st_t = [sb(P, f"nst{i}", [128, 4], F32) for i in range(2)]
    onec = sb(P, "onec", [128, 1], F32)
    S.op("dve", lambda: nc.vector.memset(onec[:], 1.0), writes=["onec"])
    H = contextlib.ExitStack()
    M.enter_context(H)
    hnT = sb(H, "hnT", [128, 8, SEQ], BF16)

    with contextlib.ExitStack() as A:
        c_fm = sb(A, "c_fm", [128, 8], F32)
        S.dma("sp", c_fm[:], c_d, writes=["c_fm"])
        cond_f = sb(A, "cond_f", [128, 8], F32)
        cond_b = sb(A, "cond_b", [128, 8], BF16)
        condB = sb(A, "condB", [128, 8, 128], F32)
        S.op("act", lambda: nc.scalar.activation(out=cond_f[:], in_=c_fm[:], func=AF.Silu), reads=["c_fm"], writes=["cond_f"])
        if stop_after == "A0":
            dd = nc.dram_tensor("cond_dbg", [128, 8], F32, kind="ExternalOutput").ap()
            S.dma("sp", dd, cond_f[:], reads=["cond_f"])
            S.finish()
            return nc, dbg
        S.op("dve", lambda: nc.vector.tensor_copy(out=cond_b[:], in_=cond_f[:]), reads=["cond_f"], writes=["cond_b"])
        S.op("dve", lambda: nc.vector.tensor_copy(out=condB[:], in_=cond_f[:].unsqueeze(2).to_broadcast([128, 8, 128])),
             reads=["cond_f"], writes=["condB"])
        modw = sb(A, "modw", [128, 8, 3 * D], F32)
        modT = sb(A, "modT", [128, 24], F32)
        gb_bias = sb(A, "gb_bias", [128, D], F32)
        for li in range(2):
            for kc in range(8):
                S.dma("sp", modw[:, kc, :], modw_d[li, kc * 128:(kc + 1) * 128, :], writes=[f"modw{kc}"])
            pm, pmk = psr.next()
            for fc in range(24):
                for kc in range(8):
                    S.op("pe", lambda: nc.tensor.matmul(pm[:, fc:fc + 1], lhsT=modw[:, kc, fc * 128:(fc + 1) * 128],
                                                        rhs=cond_f[:, kc:kc + 1], start=(kc == 0), stop=(kc == 7)),
                         reads=[f"modw{kc}", "cond_f"], writes=[pmk])
            S.op("dve", lambda: nc.vector.tensor_tensor(out=modT[:], in0=pm[:, 0:24],
                                                        in1=pcol[:, PC_MODB + li * 24:PC_MODB + (li + 1) * 24], op=ALU.add),
                 reads=[pmk, "pcol"], writes=["modT"])
            S.op("dve", lambda: nc.vector.tensor_copy(out=modB[:, li * 8:(li + 1) * 8], in_=modT[:, 0:8]),
                 reads=["modT"], writes=["modB"])
            S.op("dve", lambda: nc.vector.scalar_tensor_tensor(out=modA[:, li * 8:(li + 1) * 8], in0=modT[:, 8:16], scalar=1.0,
                                                               in1=pcol[:, PC_NW + li * 8:PC_NW + (li + 1) * 8],
                                                               op0=ALU.add, op1=ALU.mult),
                 reads=["modT", "pcol"], writes=["modA"])
            S.dma("sp", gb_bias[:], modb_d[li:li + 1, 2 * D:3 * D].partition_broadcast(128), writes=["gb_bias"])
            for n in range(2):
                pg, pgk = psr.next()
                for kc in range(8):
                    S.op("pe", lambda: nc.tensor.matmul(pg[:], lhsT=condB[:, kc, :],
                                                        rhs=modw[:, kc, 2 * D + n * 512:2 * D + (n + 1) * 512],
                                                        start=(kc == 0), stop=(kc == 7)),
                         reads=[f"modw{kc}", "condB"], writes=[pgk])
                S.op("dve", lambda: nc.vector.tensor_tensor(out=gate_b[li][:, n * 512:(n + 1) * 512], in0=pg[:],
                                                            in1=gb_bias[:, n * 512:(n + 1) * 512], op=ALU.add),
                     reads=[pgk, "gb_bias"], writes=[f"gate_b{li}"])
        S.barrier()

    if stop_after == "A":
        for nm, t_, shp in (("modA_dbg", modA, [128, 16]), ("modB_dbg", modB, [128, 16]), ("gate0_dbg", gate_b[0], [128, D]), ("gate1_dbg", gate_b[1], [128, D])):
            dd = nc.dram_tensor(nm, shp, F32, kind="ExternalOutput").ap()
            S.dma("sp", dd, t_[:])
        S.finish()
        return nc, dbg

    xnr = Rot(xn_t, "xn")
    str_ = Rot(st_t, "nst")

    def norm_tile(xt, xk, tt, li):
        st, stk = str_.next()
        xn, xnk = xnr.next()
        S.op("act", lambda: nc.scalar.activation(out=nrm_junk[:], in_=xt[:], func=AF.Square, accum_out=st[:, 0:1]),
             reads=[xk], writes=["nrm_junk", stk])
        S.op("act", lambda: nc.scalar.activation(out=st[:, 1:2], in_=st[:, 0:1], func=AF.Sqrt, bias=epsc[:, 0:1], scale=1.0 / D),
             reads=[stk, "epsc"], writes=[stk])
        S.op("dve", lambda: nc.vector.reciprocal(out=st[:, 2:3], in_=st[:, 1:2]), reads=[stk], writes=[stk])
        S.op("dve", lambda: nc.vector.tensor_scalar(out=xn[:], in0=xt[:], scalar1=st[:, 2:3], scalar2=None, op0=ALU.mult),
             reads=[xk, stk], writes=[xnk])
        tp, tpk = ptr.next()
        for kc in range(8):
            S.op("pe", lambda: nc.tensor.transpose(tp[:, kc, :], xn[:, kc * 128:(kc + 1) * 128], K["ident"][:]),
                 reads=[xnk, "k_ident"], writes=[tpk])
        for kc in range(8):
            col = li * 8 + kc
            if True:
                S.op("dve", lambda: nc.vector.tensor_scalar(out=hnT[:, kc, tt * 128:(tt + 1) * 128], in0=tp[:, kc, :],
                                                            scalar1=modA[:, col:col + 1], scalar2=modB[:, col:col + 1],
                                                            op0=ALU.mult, op1=ALU.add),
                     reads=[tpk, "modA", "modB"], writes=[f"hnT{kc}"])
            else:
                S.op("act", lambda: nc.scalar.activation(out=hnT[:, kc, tt * 128:(tt + 1) * 128], in_=tp[:, kc, :],
                                                         func=AF.Identity, scale=modA[:, col:col + 1], bias=modB[:, col:col + 1]),
                     reads=[tpk, "modA", "modB"], writes=[f"hnT{kc}"])

    HN = [f"hnT{kc}" for kc in range(8)]

    with contextlib.ExitStack() as B:
        xts = [sb(B, f"xt{i}", [128, D], F32) for i in range(3)]
        xr = Rot(xts, "xt")
        for tt in range(NT):
            xt, xk = xr.next()
            S.dma("sp", xt[:], x_d[tt * 128:(tt + 1) * 128, :], writes=[xk])
            norm_tile(xt, xk, tt, 0)
        S.barrier()

    if stop_after == "B":
        hn_dbg = nc.dram_tensor("hn_dbg", [128, 8, SEQ], BF16, kind="ExternalOutput").ap()
        dbg["hn_dbg"] = hn_dbg
        for kc in range(8):
            S.dma("sp", hn_dbg[:, kc, :], hnT[:, kc, :], reads=HN)
        S.finish()
        return nc, dbg

    with contextlib.ExitStack() as C:
        Ct = sb(C, "Ct", [128, SEQ], BF16)
        St = sb(C, "St", [128, SEQ], BF16)
        with contextlib.ExitStack() as R:
            RB = 1024
            posi = sb(R, "posi", [128, RB], I32)
            posf = sb(R, "posf", [128, RB], F32)
            ang = sb(R, "ang", [128, RB], F32)
            ki = sb(R, "ki", [128, RB], I32)
            kf = sb(R, "kf", [128, RB], F32)
            for cb in range(SEQ // RB):
                csl = slice(cb * RB, (cb + 1) * RB)
                S.dma("sp", posi[:], pos_d[:, csl].partition_broadcast(128), writes=["posi"])
                S.op("dve", lambda: nc.vector.tensor_copy(out=posf[:], in_=posi[:]), reads=["posi"], writes=["posf"])
                for which, phase in ((0, 0.0), (1, math.pi / 2)):
                    S.op("dve", lambda: nc.vector.tensor_scalar(out=ang[:], in0=posf[:], scalar1=pcol[:, PC_INVF:PC_INVF + 1],
                                                                scalar2=phase, op0=ALU.mult, op1=ALU.add),
                         reads=["posf", "pcol"], writes=["ang"])
                    S.op("dve", lambda: nc.vector.tensor_scalar(out=ki[:], in0=ang[:], scalar1=1.0 / (2 * math.pi), scalar2=None,
                                                                op0=ALU.mult), reads=["ang"], writes=["ki"])
                    S.op("dve", lambda: nc.vector.tensor_copy(out=kf[:], in_=ki[:]), reads=["ki"], writes=["kf"])
                    S.op("dve", lambda: nc.vector.scalar_tensor_tensor(out=ang[:], in0=kf[:], scalar=-2 * math.pi, in1=ang[:],
                                                                       op0=ALU.mult, op1=ALU.add),
                         reads=["kf", "ang"], writes=["ang"])
                    if which == 0:
                        S.op("act", lambda: nc.scalar.activation(out=kf[:], in_=ang[:], func=AF.Sin), reads=["ang"], writes=["kf"])
                        S.op("dve", lambda: nc.vector.tensor_scalar(out=St[:, csl], in0=kf[:], scalar1=pcol[:, PC_SG:PC_SG + 1],
                                                                    scalar2=None, op0=ALU.mult), reads=["kf", "pcol"], writes=["St"])
                    else:
                        S.op("act", lambda: nc.scalar.activation(out=Ct[:, csl], in_=ang[:], func=AF.Sin), reads=["ang"], writes=["Ct"])
            S.barrier()
        def dump(nm, t_, shape, dt):
            dd = nc.dram_tensor(nm, shape, dt, kind="ExternalOutput").ap()
            if len(shape) == 2 and shape[1] * (2 if dt == BF16 else 4) > 32768:
                h = shape[1] // 2
                S.dma("sp", dd[:, 0:h], t_[:, 0:h])
                S.dma("sp", dd[:, h:], t_[:, h:])
            else:
                S.dma("sp", dd, t_[:])

        if stop_after == "Crope":
            dump("Ct_dbg", Ct, [128, SEQ], BF16)
            dump("St_dbg", St, [128, SEQ], BF16)
            S.finish()
            return nc, dbg
        Dacc = sb(C, "Dacc", [128, SEQ], F32)
        Rinv = Dacc
        tmpfs = [sb(C, f"tmpf{i}", [128, 512], F32) for i in range(1)]
        tmpfr = Rot(tmpfs, "tmpf")
        Ug = [sb(C, f"Ug{i}", [128, SEQ], BF16) for i in range(3)]
        QT = sb(C, "QT", [128, SEQ], BF16)
        Kz = sb(C, "Kz", [128, 2 * SEQ], BF16)
        Vbd = sb(C, "Vbd", [128, 64, 128], BF16)
        S.op("dve", lambda: nc.vector.memset(Kz[:], 0.0), writes=["Kz"])
        S.op("dve", lambda: nc.vector.memset(Vbd[:], 0.0), writes=["Vbd"])
        wts = [sb(C, f"wt{i}", [128, 8, 128], BF16) for i in range(3)]
        wstg = [sb(C, f"wstg{i}", [128, 1024], F32) for i in range(2)]
        wstgr = Rot(wstg, "wstg")
        wr = Rot(wts, "wt")
        stA = [sb(C, f"stA{i}", [128, 512], BF16) for i in range(2)]
        stAr = Rot(stA, "stA")
        st1 = [sb(C, f"st1{i}", [128, 512], BF16) for i in range(2)]
        st1r = Rot(st1, "st1")
        st2 = [sb(C, f"st2{i}", [128, 512], BF16) for i in range(2)]
        st2r = Rot(st2, "st2")
        Et = [sb(C, f"Et{i}", [128, 512], BF16) for i in range(2)]
        Etr = Rot(Et, "Et")
        PTt = [sb(C, f"PT{i}", [128, 512], BF16) for i in range(2)]
        PTr = Rot(PTt, "PT")
        Ost = [sb(C, f"Ost{i}", [128, 512], BF16) for i in range(2)]
        Ostr = Rot(Ost, "Ost")
        awv = awin_d.rearrange("(kc p) n -> p kc n", p=128)

        def load_w(col0):
            w, wk = wr.next()
            load_cast(wstgr, w[:], wk, awv[:, :, col0:col0 + 128], [128, 8, 128])
            return w, wk

        def proj_tile(w, wk, tn):
            pj, pjk = psr.next()
            for kc in range(8):
                S.op("pe", lambda: nc.tensor.matmul(pj[:], lhsT=w[:, kc, :], rhs=hnT[:, kc, tn * 512:(tn + 1) * 512],
                                                    start=(kc == 0), stop=(kc == 7)),
                     reads=[wk, f"hnT{kc}"], writes=[pjk])
            return pj, pjk

        jobs = []
        for s in range(4):
            for g in range(3):
                jobs += [(s, g, 0), (s, g, 1), (s, g, 2)]
            for g in range(3):
                jobs.append((s, g, 3))
        wq = {}
        PF = 2

        def prefetch(i):
            if i < len(jobs) and i not in wq:
                s_, g_, wh_ = jobs[i]
                wq[i] = load_w(wh_ * AW + (g_ * 4 + s_) * 128)

        for i in range(PF):
            prefetch(i)
        ji = [0]

        def next_w():
            w, wk = wq.pop(ji[0])
            prefetch(ji[0] + PF)
            ji[0] += 1
            return w, wk

        for s in range(4):
            for g in range(3):
                win, dil = PATTERNS[g]
                L = SEQ // dil
                j = g * 4 + s
                ntile = L // 64
                for which in range(2):
                    w, wk = next_w()

                    def post(tn, a, ak):
                        p2, p2k = psr.next()
                        S.op("pe", lambda: nc.tensor.matmul(p2[:], lhsT=K["pswap"][:], rhs=a[:], start=True, stop=True),
                             reads=["k_pswap", ak], writes=[p2k])
                        t1, t1k = st1r.next()
                        t2, t2k = st2r.next()
                        S.op("dve", lambda: nc.vector.tensor_tensor(out=t1[:], in0=a[:], in1=Ct[:, tn * 512:(tn + 1) * 512], op=ALU.mult),
                             reads=[ak, "Ct"], writes=[t1k])
                        S.op("dve", lambda: nc.vector.tensor_tensor(out=t2[:], in0=p2[:], in1=St[:, tn * 512:(tn + 1) * 512], op=ALU.mult),
                             reads=[p2k, "St"], writes=[t2k])
                        ni = 512 // dil
                        i0_ = tn * ni
                        if which == 0:
                            dst = QT[:].rearrange("p (r i) -> p i r", r=dil)[:, i0_:i0_ + ni, :]
                            S.op("dve", lambda: nc.vector.tensor_tensor(out=dst, in0=t1[:].rearrange("p (i r) -> p i r", r=dil),
                                                                        in1=t2[:].rearrange("p (i r) -> p i r", r=dil), op=ALU.add),
                                 reads=[t1k, t2k], writes=["QT"])
                        else:
                            bc_ = min(64, ni)
                            ac_ = max(1, ni // 64)
                            a0, b0 = divmod(i0_, 64)
                            for hh in range(2):
                                dst = bass.AP(Kz, hh * 64 * (2 * SEQ) + a0 * 128 + hh * 64 + b0,
                                              [[2 * SEQ, 64], [128, ac_], [1, bc_], [2 * L, dil]])
                                eng = "dve"
                                fn = nc.vector.tensor_tensor
                                S.op(eng, lambda: fn(out=dst, in0=t1[hh * 64:(hh + 1) * 64, :].rearrange("p (a b r) -> p a b r", a=ac_, b=bc_),
                                                     in1=t2[hh * 64:(hh + 1) * 64, :].rearrange("p (a b r) -> p a b r", a=ac_, b=bc_), op=ALU.add),
                                     reads=[t1k, t2k], writes=["Kz"])

                    pend = None
                    for tn in range(8):
                        pj, pjk = proj_tile(w, wk, tn)
                        a, ak = stAr.next()
                        S.op("act", lambda: nc.scalar.activation(out=a[:], in_=pj[:], func=AF.Copy), reads=[pjk], writes=[ak])
                        if pend is not None:
                            post(*pend)
                        pend = (tn, a, ak)
                    post(*pend)
                w, wk = next_w()
                VT, VTk = Ug[g], f"Ug{g}"
                for tn in range(8):
                    pj, pjk = proj_tile(w, wk, tn)
                    S.op("act", lambda: nc.scalar.activation(out=VT[:, tn * 512:(tn + 1) * 512], in_=pj[:], func=AF.Copy),
                         reads=[pjk], writes=[VTk])
                for t0 in range(0, 64, 4):
                    tp, tpk = psr.next()
                    for jj in range(4):
                        t = t0 + jj
                        r, cc = divmod(t, ntile)
                        src = _ap(VT, SEQ, 0, 128, r + dil * 64 * cc, [[dil, 64]])
                        for hh in range(2):
                            S.op("pe", lambda: nc.tensor.matmul(tp[hh * 64:(hh + 1) * 64, jj * 128:(jj + 1) * 128], lhsT=src, rhs=K["ident"][:],
                                                                start=True, stop=True),
                                 reads=[VTk, "k_ident"], writes=[tpk])
                    tpv = tp[:].rearrange("p (j f) -> p j f", j=4)
                    S.op("dve", lambda: nc.vector.tensor_copy(out=Vbd[0:64, t0:t0 + 4, 0:64], in_=tpv[0:64, :, 0:64]),
                         reads=[tpk], writes=["Vbd"])
                    S.op("act", lambda: nc.scalar.activation(out=Vbd[64:128, t0:t0 + 4, 64:128], in_=tpv[64:128, :, 64:128], func=AF.Copy),
                         reads=[tpk], writes=["Vbd"])
                nb = L // 128
                blocks = [(r, b) for r in range(dil) for b in range(nb)]

                def stage1(r, b):
                    lo = 1 if b == 0 else 0
                    hi = 3 if b == nb - 1 else 4
                    s4, s4k = psr.next()
                    qcol = r * L + 128 * b
                    for i4 in range(lo, hi):
                        cc = 2 * b - 1 + i4
                        S.op("pe", lambda: nc.tensor.matmul(s4[:, i4 * 128:(i4 + 1) * 128],
                                                            lhsT=Kz[:, (r * ntile + cc) * 128:(r * ntile + cc + 1) * 128],
                                                            rhs=QT[:, qcol:qcol + 128], start=True, stop=True),
                             reads=["Kz", "QT"], writes=[s4k])
                    e, ek = Etr.next()
                    S.op("act", lambda: nc.scalar.activation(out=e[:, lo * 128:hi * 128], in_=s4[:, lo * 128:hi * 128],
                                                             func=AF.Exp, scale=0.125), reads=[s4k], writes=[ek])
                    pt_, ptk = PTr.next()
                    S.op("dve", lambda: nc.vector.tensor_tensor(out=pt_[:, lo * 128:hi * 128], in0=e[:, lo * 128:hi * 128],
                                                                in1=K["mask4"][:, lo * 128:hi * 128], op=ALU.mult),
                         reads=[ek, "k_mask4"], writes=[ptk])
                    return (r, b, lo, hi, pt_, ptk)

                def stage2(r, b, lo, hi, pt_, ptk):
                    ud, udk = psr.next()
                    for i4 in range(lo, hi):
                        cc = 2 * b - 1 + i4
                        S.op("pe", lambda: nc.tensor.matmul(ud[:, 0:128], lhsT=Vbd[:, r * ntile + cc, :], rhs=pt_[:, i4 * 128:(i4 + 1) * 128],
                                                            start=(i4 == lo), stop=(i4 == hi - 1)),
                             reads=["Vbd", ptk], writes=[udk])
                    for i4 in range(lo, hi):
                        S.op("pe", lambda: nc.tensor.matmul(ud[:, 128:256], lhsT=K["bdones"][:], rhs=pt_[:, i4 * 128:(i4 + 1) * 128],
                                                            start=(i4 == lo), stop=(i4 == hi - 1)),
                             reads=["k_bdones", ptk], writes=[udk])
                    tok0 = r + dil * 128 * b
                    udst = _ap(Ug[g], SEQ, 0, 128, tok0, [[dil, 128]])
                    S.op("act", lambda: nc.scalar.activation(out=udst, in_=ud[:, 0:128], func=AF.Copy), reads=[udk], writes=[f"Ug{g}"])
                    ddst = _ap(Dacc, SEQ, 0, 128, tok0, [[dil, 128]])
                    if g == 0:
                        S.op("dve", lambda: nc.vector.tensor_copy(out=ddst, in_=ud[:, 128:256]), reads=[udk], writes=["Dacc"])
                    else:
                        S.op("dve", lambda: nc.vector.tensor_tensor(out=ddst, in0=ud[:, 128:256], in1=ddst, op=ALU.add),
                             reads=[udk, "Dacc"], writes=["Dacc"])

                pend = stage1(*blocks[0])
                for bi in range(len(blocks)):
                    nxt = stage1(*blocks[bi + 1]) if bi + 1 < len(blocks) else None
                    stage2(*pend)
                    pend = nxt
            S.op("dve", lambda: nc.vector.reciprocal(out=Rinv[:], in_=Dacc[:]), reads=["Dacc"], writes=["Dacc"])
            for g in range(3):
                j = g * 4 + s
                w, wk = next_w()
                for tn in range(8):
                    pj, pjk = proj_tile(w, wk, tn)
                    a, ak = stAr.next()
                    S.op("act", lambda: nc.scalar.activation(out=a[:], in_=pj[:], func=AF.Silu), reads=[pjk], writes=[ak])
                    sl = slice(tn * 512, (tn + 1) * 512)
                    tf, tfk = tmpfr.next()
                    S.op("dve", lambda: nc.vector.tensor_tensor(out=tf[:], in0=Ug[g][:, sl], in1=Rinv[:, sl], op=ALU.mult),
                         reads=[f"Ug{g}", "Dacc"], writes=[tfk])
                    o, ok = Ostr.next()
                    S.op("dve", lambda: nc.vector.tensor_tensor(out=o[:], in0=tf[:], in1=a[:], op=ALU.mult),
                         reads=[tfk, ak], writes=[ok])
                    S.dma("sp", o_d[j * 128:(j + 1) * 128, sl], o[:], reads=[ok], writes=["o_d"])
        S.barrier()

    if stop_after == "C":
        S.finish()
        return nc, dbg

    with contextlib.ExitStack() as Dx:
        wout = sb(Dx, "wout", [128, 12, D], BF16)
        wstg = [sb(Dx, f"wstgD{i}", [128, 1024], F32) for i in range(2)]
        wstgr = Rot(wstg, "wstgD")
        for j in range(12):
            load_cast(wstgr, wout[:, j, :], "wout", awout_d[j * 128:(j + 1) * 128, :], [128, 1024])
        OTs = [sb(Dx, f"OT{i}", [128, 12, 512], BF16) for i in range(2)]
        OTr = Rot(OTs, "OT")
        xts = [sb(Dx, f"xt{i}", [128, D], F32) for i in range(2)]
        xr = Rot(xts, "xt")
        ytm = [sb(Dx, f"ytm{i}", [128, D], F32) for i in range(2)]
        ytr = Rot(ytm, "ytm")
        x1s = [sb(Dx, f"x1t{i}", [128, D], F32) for i in range(2)]
        x1r = Rot(x1s, "x1t")
        o_v = o_d.rearrange("(j p) t -> p j t", p=128)
        for tb in range(8):
            ot, otk = OTr.next()
            S.dma("sp", ot[:], o_v[:, :, tb * 512:(tb + 1) * 512], reads=["o_d"], writes=[otk])
            for t4 in range(4):
                tt = tb * 4 + t4
                xt, xk = xr.next()
                S.dma("sp", xt[:], x_d[tt * 128:(tt + 1) * 128, :], writes=[xk])
                yt, ytk = ytr.next()
                for n in range(2):
                    py, pyk = psr.next()
                    for j in range(12):
                        S.op("pe", lambda: nc.tensor.matmul(py[:], lhsT=ot[:, j, t4 * 128:(t4 + 1) * 128], rhs=wout[:, j, n * 512:(n + 1) * 512],
                                                            start=(j == 0), stop=(j == 11)),
                             reads=[otk, "wout"], writes=[pyk])
                    S.op("dve", lambda: nc.vector.tensor_tensor(out=yt[:, n * 512:(n + 1) * 512], in0=py[:], in1=gate_b[0][:, n * 512:(n + 1) * 512],
                                                                op=ALU.mult), reads=[pyk, "gate_b0"], writes=[ytk])
                x1, x1k = x1r.next()
                S.op("dve", lambda: nc.vector.tensor_tensor(out=x1[:], in0=yt[:], in1=xt[:], op=ALU.add), reads=[ytk, xk], writes=[x1k])
                S.dma("sp", x1_d[tt * 128:(tt + 1) * 128, :], x1[:], reads=[x1k], writes=["x1_d"])
                norm_tile(x1, x1k, tt, 1)
        S.barrier()

    if stop_after == "D":
        hn_dbg = nc.dram_tensor("hn_dbg", [128, 8, SEQ], BF16, kind="ExternalOutput").ap()
        dbg["hn_dbg"] = hn_dbg
        for kc in range(8):
            S.dma("sp", hn_dbg[:, kc, :], hnT[:, kc, :], reads=HN)
        S.finish()
        return nc, dbg

    sz_d = scratch("sz_d", [SEQ, SSD_INNER], BF16)
    xs_d = scratch("xs_d", [SEQ, SSD_INNER], BF16)
    bt_d = scratch("bt_d", [SEQ, 1024], BF16)
    bT_d = scratch("bT_d", [1024, SEQ], BF16)
    cT_d = scratch("cT_d", [1024, SEQ], BF16)
    pb_d = scratch("pb_d", [NT, 128, SSD_INNER], BF16)
    dt_d = scratch("dt_d", [NT, 128, 64], F32)
    swv = swin_d.rearrange("(kc p) n -> p kc n", p=128)

    with contextlib.ExitStack() as E:
        dtb_b = sb(E, "dtb_b", [128, 64], F32)
        S.dma("sp", dtb_b[:], dtb_d.partition_broadcast(128), writes=["dtb_b"])
        with contextlib.ExitStack() as E1:
            wz = sb(E1, "wz", [128, 8, SSD_INNER], BF16)
            wdt = sb(E1, "wdt", [128, 8, 64], BF16)
            wstg1 = [sb(E1, f"wstgE1{i}", [128, 1024], F32) for i in range(2)]
            wstg1r = Rot(wstg1, "wstgE1")
            for kc in range(8):
                for hh in range(2):
                    load_cast(wstg1r, wz[:, kc, hh * 1024:(hh + 1) * 1024], f"wz{kc}", swin_d[kc * 128:(kc + 1) * 128, hh * 1024:(hh + 1) * 1024], [128, 1024])
            load_cast(wstg1r, wdt[:], "wdt", swv[:, :, 6144:6208], [128, 8, 64])
            szts = [sb(E1, f"szt{i}", [128, SSD_INNER], BF16) for i in range(2)]
            sztr = Rot(szts, "szt")
            dtt = [sb(E1, f"dtt{i}", [128, 64], F32) for i in range(2)]
            dttr = Rot(dtt, "dtt")
            dto = [sb(E1, f"dto{i}", [128, 64], F32) for i in range(2)]
            dtor = Rot(dto, "dto")
            for tt in range(NT):
                szt, sztk = sztr.next()
                for n in range(4):
                    pz, pzk = psr.next()
                    for kc in range(8):
                        S.op("pe", lambda: nc.tensor.matmul(pz[:], lhsT=hnT[:, kc, tt * 128:(tt + 1) * 128], rhs=wz[:, kc, n * 512:(n + 1) * 512],
                                                            start=(kc == 0), stop=(kc == 7)), reads=[f"hnT{kc}", f"wz{kc}"], writes=[pzk])
                    S.op("act", lambda: nc.scalar.activation(out=szt[:, n * 512:(n + 1) * 512], in_=pz[:], func=AF.Silu), reads=[pzk], writes=[sztk])
                S.dma("sp", sz_d[tt * 128:(tt + 1) * 128, :], szt[:], reads=[sztk], writes=["sz_d"])
            for tt in range(NT):
                pd, pdk = psr.next()
                for kc in range(8):
                    S.op("pe", lambda: nc.tensor.matmul(pd[:, 0:64], lhsT=hnT[:, kc, tt * 128:(tt + 1) * 128], rhs=wdt[:, kc, :],
                                                        start=(kc == 0), stop=(kc == 7)), reads=[f"hnT{kc}", "wdt"], writes=[pdk])
                dtmp, dtk = dttr.next()
                S.op("dve", lambda: nc.vector.tensor_tensor(out=dtmp[:], in0=pd[:, 0:64], in1=dtb_b[:], op=ALU.add), reads=[pdk, "dtb_b"], writes=[dtk])
                S.op("act", lambda: nc.scalar.activation(out=dtmp[:], in_=dtmp[:], func=AF.Exp), reads=[dtk], writes=[dtk])
                dto_, dtok = dtor.next()
                S.op("act", lambda: nc.scalar.activation(out=dto_[:], in_=dtmp[:], func=AF.Ln, bias=onec[:, 0:1], scale=1.0),
                     reads=[dtk, "onec"], writes=[dtok])
                S.dma("sp", dt_d[tt], dto_[:], reads=[dtok], writes=["dt_d"])
            S.barrier()
        raws = [sb(E, f"raw{i}", [128, SEQ + 4], F32) for i in range(2)]
        for i in range(2):
            S.op("dve", lambda: nc.vector.memset(raws[i][:], 0.0), writes=[f"raw{i}"])
        rawr = Rot(raws, "raw")
        accs = [sb(E, f"acc{i}", [128, 1024], F32) for i in range(3)]
        accr = Rot(accs, "acc")
        xos = [sb(E, f"xo{i}", [128, SEQ], BF16) for i in range(4)]
        xor_ = Rot(xos, "xo")
        xtoks = [sb(E, f"xtok{i}", [128, 8, 128], BF16) for i in range(2)]
        xtokr = Rot(xtoks, "xtok")
        wxs = [sb(E, f"wx{i}", [128, 8, 128], BF16) for i in range(3)]
        wstgE = [sb(E, f"wstgE{i}", [128, 1024], F32) for i in range(2)]
        wstgEr = Rot(wstgE, "wstgE")
        wxr = Rot(wxs, "wx")
        xs_v = xs_d.rearrange("(t p) c -> p t c", p=128)
        bt_v = bt_d.rearrange("(t p) c -> p t c", p=128)
        def load_wx(cc_):
            wx_, wxk_ = wxr.next()
            load_cast(wstgEr, wx_[:], wxk_, swv[:, :, SSD_INNER + cc_ * 128:SSD_INNER + (cc_ + 1) * 128], [128, 8, 128])
            return wx_, wxk_

        pend_tok = []

        def to_tok(cc_, xo_, xok_):
            for tb in range(4):
                tp, tpk = ptr.next()
                for jj in range(8):
                    tt = tb * 8 + jj
                    S.op("pe", lambda: nc.tensor.transpose(tp[:, jj, :], xo_[:, tt * 128:(tt + 1) * 128], K["ident"][:]),
                         reads=[xok_, "k_ident"], writes=[tpk])
                xtok, xtokk = xtokr.next()
                S.op("act", lambda: nc.scalar.activation(out=xtok[:], in_=tp[:], func=AF.Copy), reads=[tpk], writes=[xtokk])
                if cc_ < 16:
                    S.dma("sp", xs_v[:, tb * 8:(tb + 1) * 8, cc_ * 128:(cc_ + 1) * 128], xtok[:], reads=[xtokk], writes=["xs_d"])
                else:
                    S.dma("sp", bt_v[:, tb * 8:(tb + 1) * 8, (cc_ - 16) * 128:(cc_ - 15) * 128], xtok[:], reads=[xtokk], writes=["bt_d"])

        wxq = {0: load_wx(0), 1: load_wx(1)}
        def e_proj(cc):
            wx, wxk = wxq.pop(cc)
            if cc + 2 < 32:
                wxq[cc + 2] = load_wx(cc + 2)
            raw, rawk = rawr.next()
            for tn in range(8):
                pj, pjk = psr.next()
                for kc in range(8):
                    S.op("pe", lambda: nc.tensor.matmul(pj[:], lhsT=wx[:, kc, :], rhs=hnT[:, kc, tn * 512:(tn + 1) * 512],
                                                        start=(kc == 0), stop=(kc == 7)), reads=[wxk, f"hnT{kc}"], writes=[pjk])
                S.op("act", lambda: nc.scalar.activation(out=raw[:, 2 + tn * 512:2 + (tn + 1) * 512], in_=pj[:], func=AF.Copy),
                     reads=[pjk], writes=[rawk])
            return raw, rawk

        def e_conv(cc, raw, rawk):
            xo, xok = xor_.next()
            for cb in range(4):
                acc, acck = accr.next()
                c0 = cb * 1024
                S.op("dve", lambda: nc.vector.tensor_scalar(out=acc[:], in0=raw[:, c0:c0 + 1024], scalar1=pcol[:, PC_CW + cc:PC_CW + cc + 1],
                                                            scalar2=None, op0=ALU.mult), reads=[rawk, "pcol"], writes=[acck])
                for k in range(1, 5):
                    S.op("dve", lambda: nc.vector.scalar_tensor_tensor(out=acc[:], in0=raw[:, c0 + k:c0 + k + 1024],
                                                                       scalar=pcol[:, PC_CW + k * 32 + cc:PC_CW + k * 32 + cc + 1],
                                                                       in1=acc[:], op0=ALU.mult, op1=ALU.add),
                         reads=[rawk, "pcol", acck], writes=[acck])
                S.op("act", lambda: nc.scalar.activation(out=xo[:, c0:c0 + 1024], in_=acc[:], func=AF.Silu,
                                                         bias=pcol[:, PC_CB + cc:PC_CB + cc + 1], scale=1.0), reads=[acck, "pcol"], writes=[xok])
            if cc >= 24:
                for hh in range(2):
                    S.dma("sp", cT_d[(cc - 24) * 128:(cc - 23) * 128, hh * 2048:(hh + 1) * 2048], xo[:, hh * 2048:(hh + 1) * 2048], reads=[xok], writes=["cT_d"])
            elif cc >= 16:
                for hh in range(2):
                    S.dma("sp", bT_d[(cc - 16) * 128:(cc - 15) * 128, hh * 2048:(hh + 1) * 2048], xo[:, hh * 2048:(hh + 1) * 2048], reads=[xok], writes=["bT_d"])
            if cc < 24:
                pend_tok.append((cc, xo, xok))

        prev_raw = None
        for cc in range(32):
            cur = e_proj(cc)
            if prev_raw is not None:
                e_conv(cc - 1, *prev_raw)
            prev_raw = cur
            while len(pend_tok) > 2 or (pend_tok and cc >= 24):
                to_tok(*pend_tok.pop(0))
        e_conv(31, *prev_raw)
        while pend_tok:
            to_tok(*pend_tok.pop(0))
        S.barrier()
    H.close()

    with contextlib.ExitStack() as G:
        arow = sb(G, "arow", [128, 64], F32)
        S.dma("sp", arow[:], alog_d.partition_broadcast(128), writes=["arow"])
        S.op("act", lambda: nc.scalar.activation(out=arow[:], in_=arow[:], func=AF.Exp), reads=["arow"], writes=["arow"])
        S.op("dve", lambda: nc.vector.tensor_scalar(out=arow[:], in0=arow[:], scalar1=-1.0, scalar2=None, op0=ALU.mult), reads=["arow"], writes=["arow"])
        drow = sb(G, "drow", [128, 32], F32)
        S.dma("sp", drow[:], sd_d.partition_broadcast(128), writes=["drow"])
        Dident = sb(G, "Dident", [128, 32, 128], BF16)
        for h in range(32):
            S.op("dve", lambda: nc.vector.tensor_scalar(out=Dident[:, h, :], in0=K["ident"][:], scalar1=drow[:, h:h + 1], scalar2=None, op0=ALU.mult),
                 reads=["k_ident", "drow"], writes=["Dident"])
        fnw_b = sb(G, "fnw_b", [128, D], F32)
        S.dma("sp", fnw_b[:], fnw_d.partition_broadcast(128), writes=["fnw_b"])
        ones128 = sb(G, "ones128", [128, 128], F32)
        S.op("dve", lambda: nc.vector.memset(ones128[:], 1.0), writes=["ones128"])
        ones1 = sb(G, "ones1", [2, 128], BF16)
        S.op("dve", lambda: nc.vector.memset(ones1[:], 1.0), writes=["ones1"])
        wout1 = sb(G, "wout1", [128, 16, D], BF16)
        with contextlib.ExitStack() as G0:
            wstg = [sb(G0, f"wstgG{i}", [128, 1024], F32) for i in range(2)]
            wstgr = Rot(wstg, "wstgG")
            for j in range(16):
                stg, stgk = wstgr.next()
                S.dma("sp", stg[:], swout_d[j * 128:(j + 1) * 128, :], writes=[stgk])
                S.op("dve", lambda: nc.vector.tensor_scalar(out=wout1[:, j, :], in0=stg[:], scalar1=pcol[:, PC_SNW + j:PC_SNW + j + 1], scalar2=None, op0=ALU.mult),
                     reads=[stgk, "pcol"], writes=["wout1"])
            S.barrier()
        carry = sb(G, "carry", [128, SSD_INNER], F32)
        xts_ = [sb(G, f"xc{i}", [128, SSD_INNER], BF16) for i in range(2)]
        xcr = Rot(xts_, "xc")
        bts_ = [sb(G, f"bc{i}", [128, 1024], BF16) for i in range(2)]
        bcr = Rot(bts_, "bc")
        dtcs = [sb(G, f"dtc{i}", [128, 64], F32) for i in range(4)]
        dtcr = Rot(dtcs, "dtc")
        das = [sb(G, f"da{i}", [128, 64], F32) for i in range(2)]
        dar = Rot(das, "da")
        scs = [sb(G, f"sc{i}", [128, 128], F32) for i in range(2)]
        scr_ = Rot(scs, "sc")
        smalls = [sb(G, f"sm{i}", [128, 6, 64], F32) for i in range(3)]
        smr = Rot(smalls, "sm")
        xws = [sb(G, f"xw{i}", [128, SSD_INNER], BF16) for i in range(1)]
        xwr = Rot(xws, "xw")
        cbfs = [sb(G, f"cbf{i}", [128, SSD_INNER], BF16) for i in range(1)]
        cbfr = Rot(cbfs, "cbf")

        def load_dt(c):
            dtc, dtck = dtcr.next()
            S.dma("sp", dtc[:], dt_d[c], reads=["dt_d"], writes=[dtck])
            return dtc, dtck

        def chunk_scalars(c, need_b_only, pre=None):
            dtc, dtck = pre if pre is not None else load_dt(c)
            da, dak = dar.next()
            S.op("dve", lambda: nc.vector.tensor_tensor(out=da[:], in0=dtc[:], in1=arow[:], op=ALU.mult), reads=[dtck, "arow"], writes=[dak])
            p0, p0k = psr.next()
            S.op("pe", lambda: nc.tensor.matmul(p0[:, 0:32], lhsT=K["triL"][:], rhs=da[:, 0:32], start=True, stop=True), reads=["k_triL", dak], writes=[p0k])
            S.op("pe", lambda: nc.tensor.matmul(p0[:, 32:64], lhsT=K["triU"][:], rhs=da[:, 32:64], start=True, stop=True), reads=["k_triU", dak], writes=[p0k])
            S.op("pe", lambda: nc.tensor.matmul(p0[:, 64:128], lhsT=ones128[:], rhs=da[:, 0:64], start=True, stop=True), reads=["ones128", dak], writes=[p0k])
            if not need_b_only:
                S.op("pe", lambda: nc.tensor.matmul(p0[0:32, 128:256], lhsT=da[:, 0:32], rhs=K["triL"][:], start=True, stop=True), reads=["k_triL", dak], writes=[p0k])
                S.op("pe", lambda: nc.tensor.matmul(p0[32:64, 128:256], lhsT=da[:, 32:64], rhs=K["triU"][:], start=True, stop=True), reads=["k_triU", dak], writes=[p0k])
            sc, sck = scr_.next()
            S.op("act", lambda: nc.scalar.activation(out=sc[:], in_=p0[:, 0:128], func=AF.Copy), reads=[p0k], writes=[sck])
            sm, smk = smr.next()
            S.op("dve", lambda: nc.vector.tensor_scalar(out=sm[:, 0, :], in0=sc[:, 0:64], scalar1=-1.0, scalar2=None, op0=ALU.mult), reads=[sck], writes=[smk])
            S.op("act", lambda: nc.scalar.activation(out=sm[:, 1, :], in_=sc[:, 0:64], func=AF.Exp), reads=[sck], writes=[smk])
            S.op("dve", lambda: nc.vector.tensor_tensor(out=sm[:, 4, :], in0=sc[:, 64:128], in1=sc[:, 0:64], op=ALU.subtract), reads=[sck], writes=[smk])
            S.op("act", lambda: nc.scalar.activation(out=sm[:, 5, :], in_=sm[:, 4, :], func=AF.Exp), reads=[smk], writes=[smk])
            S.op("dve", lambda: nc.vector.tensor_tensor(out=sm[:, 2, :], in0=sm[:, 5, :], in1=dtc[:], op=ALU.mult), reads=[smk, dtck], writes=[smk])
            S.op("act", lambda: nc.scalar.activation(out=sm[:, 3, :], in_=sc[:, 64:128], func=AF.Exp), reads=[sck], writes=[smk])
            return sm, smk, p0, p0k, dtc, dtck

        def state_update(c, xc, xck, bc, bck, sm, smk, d):
            xw, xwk = xwr.next()
            S.op("dve", lambda: nc.vector.tensor_tensor(out=xw[:].rearrange("p (h q) -> p h q", q=64), in0=xc[:].rearrange("p (h q) -> p h q", q=64),
                                                        in1=sm[:, 2, d * 32:(d + 1) * 32].unsqueeze(2).to_broadcast([128, 32, 64]), op=ALU.mult),
                 reads=[xck, smk], writes=[xwk])
            S.op("dve", lambda: nc.vector.tensor_tensor(out=carry[:].rearrange("p (h q) -> p h q", q=64), in0=carry[:].rearrange("p (h q) -> p h q", q=64),
                                                        in1=sm[:, 3, d * 32:(d + 1) * 32].unsqueeze(2).to_broadcast([128, 32, 64]), op=ALU.mult),
                 reads=["carry", smk], writes=["carry"])
            for q4 in range(4):
                pS, pSk = psr.next()
                for g2 in range(2):
                    g = q4 * 2 + g2
                    S.op("pe", lambda: nc.tensor.matmul(pS[:, g2 * 256:(g2 + 1) * 256], lhsT=bc[:, g * 128:(g + 1) * 128], rhs=xw[:, g * 256:(g + 1) * 256],
                                                        start=True, stop=True), reads=[bck, xwk], writes=[pSk])
                S.op("dve", lambda: nc.vector.tensor_tensor(out=carry[:, q4 * 512:(q4 + 1) * 512], in0=pS[:], in1=carry[:, q4 * 512:(q4 + 1) * 512], op=ALU.add),
                     reads=[pSk, "carry"], writes=["carry"])

        S.op("dve", lambda: nc.vector.memset(carry[:], 0.0), writes=["carry"])
        def f_prep(c):
            xc, xck = xcr.next()
            S.dma("sp", xc[:], xs_d[c * 128:(c + 1) * 128, :], reads=["xs_d"], writes=[xck])
            bc, bck = bcr.next()
            S.dma("sp", bc[:], bt_d[c * 128:(c + 1) * 128, :], reads=["bt_d"], writes=[bck])
            sm, smk, _, _, _, _ = chunk_scalars(c, True)
            return xc, xck, bc, bck, sm, smk

        fp_ = f_prep(NT - 1)
        for c in range(NT - 1, -1, -1):
            nfp = f_prep(c - 1) if c > 0 else None
            cbf, cbfk = cbfr.next()
            S.op("act", lambda: nc.scalar.activation(out=cbf[:], in_=carry[:], func=AF.Copy), reads=["carry"], writes=[cbfk])
            S.dma("sp", pb_d[c], cbf[:], reads=[cbfk], writes=["pb_d"])
            state_update(c, *fp_, 1)
            fp_ = nfp

        if stop_after == "F":
            S.barrier()
            S.finish()
            return nc, dbg

        S.op("dve", lambda: nc.vector.memset(carry[:], 0.0), writes=["carry"])
        BTs = [sb(G, f"BT{i}", [128, 8, 128], BF16) for i in range(1)]
        BTr = Rot(BTs, "BT")
        CTs = [sb(G, f"CT{i}", [128, 8, 128], BF16) for i in range(2)]
        CTr = Rot(CTs, "CT")
        szs = [sb(G, f"szc{i}", [128, SSD_INNER], BF16) for i in range(1)]
        szr = Rot(szs, "szc")
        pbs = [sb(G, f"pbc{i}", [128, SSD_INNER], BF16) for i in range(1)]
        pbr = Rot(pbs, "pbc")
        hls = [sb(G, f"hl{i}", [64, 2, 128], BF16) for i in range(3)]
        hlr = Rot(hls, "hl")
        rows = [sb(G, f"row{i}", [2, 8 * 128], BF16) for i in range(2)]
        rowr = Rot(rows, "row")
        Lts = [sb(G, f"Lt{i}", [128, 32, 128], BF16) for i in range(4)]
        Ltr = Rot(Lts, "Lt")
        CBs = [sb(G, f"CBs{i}", [128, 8, 128], BF16) for i in range(1)]
        CBr = Rot(CBs, "CBs")
        xdts = [sb(G, f"xdt{i}", [128, SSD_INNER], BF16) for i in range(2)]
        ycs = [sb(G, f"yc{i}", [128, 512], F32) for i in range(1)]
        ycr = Rot(ycs, "yc")
        yts = [sb(G, f"ytmp{i}", [128, 512], F32) for i in range(1)]
        ytr = Rot(yts, "ytmp")
        yg = sb(G, "yg", [128, SSD_INNER], F32)
        ynb = sb(G, "ynb", [128, SSD_INNER], BF16)
        ynT = sb(G, "ynT", [128, 16, 128], BF16)
        nst2 = [sb(G, f"nst2{i}", [128, 8], F32) for i in range(2)]
        nst2r = Rot(nst2, "nst2")
        x2s = [sb(G, f"x2{i}", [128, D], F32) for i in range(2)]
        x2r = Rot(x2s, "x2")
        outs = [sb(G, f"ot{i}", [128, D], F32) for i in range(2)]
        outr = Rot(outs, "ot")
        bT_v = bT_d.rearrange("(g p) t -> p g t", p=128)
        cT_v = cT_d.rearrange("(g p) t -> p g t", p=128)
        ident4 = K["ident"][:].unsqueeze(1).to_broadcast([128, 4, 128])
        def g_loads(c):
            tsl = slice(c * 128, (c + 1) * 128)
            xc, xck = xcr.next()
            S.dma("sp", xc[:], xs_d[tsl, :], reads=["xs_d"], writes=[xck])
            bc, bck = bcr.next()
            S.dma("sp", bc[:], bt_d[tsl, :], reads=["bt_d"], writes=[bck])
            BT, BTk = BTr.next()
            S.dma("sp", BT[:], bT_v[:, :, tsl], reads=["bT_d"], writes=[BTk])
            CT, CTk = CTr.next()
            S.dma("sp", CT[:], cT_v[:, :, tsl], reads=["cT_d"], writes=[CTk])
            dt_ = load_dt(c)
            return (tsl, xc, xck, bc, bck, BT, BTk, CT, CTk, dt_)

        def g_scal(c, dt_):
            sm, smk, p0, p0k, dtc, dtck = chunk_scalars(c, False, dt_)
            hl, hlk = hlr.next()
            S.op("act", lambda: nc.scalar.activation(out=hl[:, 0, :], in_=p0[0:64, 128:256], func=AF.Copy), reads=[p0k], writes=[hlk])
            S.op("dve", lambda: nc.vector.tensor_tensor(out=hl[:, 1, :], in0=p0[0:64, 128:256], in1=hl[:, 0, :], op=ALU.subtract), reads=[p0k, hlk], writes=[hlk])
            return (sm, smk, dtc, dtck, hl, hlk)

        def g_head(c, L, SC):
            tsl, xc, xck, bc, bck, BT, BTk, CT, CTk, dt_ = L
            sm, smk, dtc, dtck, hl, hlk = SC
            CB, CBk = CBr.next()
            for gh in range(2):
                pcb, pcbk = psr.next()
                for g4 in range(4):
                    g = gh * 4 + g4
                    S.op("pe", lambda: nc.tensor.matmul(pcb[:, g4 * 128:(g4 + 1) * 128], lhsT=BT[:, g, :], rhs=CT[:, g, :], start=True, stop=True),
                         reads=[BTk, CTk], writes=[pcbk])
                S.op("act", lambda: nc.scalar.activation(out=CB[:, gh * 4:(gh + 1) * 4, :].rearrange("p g l -> p (g l)"), in_=pcb[:], func=AF.Copy),
                     reads=[pcbk], writes=[CBk])
            Ms = []
            tasks = []
            mtasks = []
            rowbox = {}
            for d in range(2):
                Lt, Ltk = Ltr.next()
                Ms.append((Lt, Ltk))
                for r16 in range(4):
                    for hb4 in range(2):
                        def task(d=d, r16=r16, hb4=hb4, Lt=Lt, Ltk=Ltk):
                            nm = K["nmaskf"] if d == 0 else K["nmaskb"]
                            nmk = "k_nmaskf" if d == 0 else "k_nmaskb"
                            if hb4 == 0:
                                row, rowk = rowr.next()
                                p0_ = d * 32 + r16 * 8
                                for j_ in range(2):
                                    S.dma("sp", row[j_:j_ + 1, :].rearrange("o (p f) -> o p f", p=8), hl[p0_:p0_ + 8, j_, :],
                                          reads=[hlk], writes=[rowk])
                                rowbox[(d, r16)] = (row, rowk)
                            row, rowk = rowbox[(d, r16)]
                            hb = r16 * 2 + hb4
                            lp, lpk = psr.next()
                            S.op("pe", lambda: nc.tensor.matmul(lp[:], lhsT=ones1[:], rhs=row[:, hb4 * 512:(hb4 + 1) * 512], start=True, stop=False),
                                 reads=["ones1", rowk], writes=[lpk])
                            S.op("pe", lambda: nc.tensor.matmul(lp[:], lhsT=nm[:], rhs=ident4, start=False, stop=True), reads=[nmk, "k_ident"], writes=[lpk])
                            for j4 in range(4):
                                h = hb * 4 + j4
                                S.op("act", lambda: nc.scalar.activation(out=Lt[:, h, :], in_=lp[:, j4 * 128:(j4 + 1) * 128], func=AF.Exp,
                                                                         bias=sm[:, 0, d * 32 + h:d * 32 + h + 1], scale=1.0), reads=[lpk, smk], writes=[Ltk])
                        tasks.append(task)

                        def mtask(d=d, r16=r16, hb4=hb4, Lt=Lt, Ltk=Ltk):
                            hb = r16 * 2 + hb4
                            S.op("dve", lambda: nc.vector.tensor_tensor(out=Lt[:, hb * 4:(hb + 1) * 4, :], in0=Lt[:, hb * 4:(hb + 1) * 4, :],
                                                                        in1=CB[:, hb:hb + 1, :].to_broadcast([128, 4, 128]), op=ALU.mult),
                                 reads=[Ltk, CBk], writes=[Ltk])
                        mtasks.append(mtask)

            def fin():
                for d in range(2):
                    S.op("dve", lambda: nc.vector.tensor_tensor(out=xdts[d][:].rearrange("p (h q) -> p h q", q=64), in0=xc[:].rearrange("p (h q) -> p h q", q=64),
                                                                in1=dtc[:, d * 32:(d + 1) * 32].unsqueeze(2).to_broadcast([128, 32, 64]), op=ALU.mult),
                         reads=[xck, dtck], writes=[f"xdt{d}"])

            def late_loads():
                szc, szk = szr.next()
                S.dma("sp", szc[:], sz_d[tsl, :], reads=["sz_d"], writes=[szk])
                pbc, pbk = pbr.next()
                S.dma("sp", pbc[:], pb_d[c], reads=["pb_d"], writes=[pbk])
                x2, x2k = x2r.next()
                S.dma("sp", x2[:], x1_d[tsl, :], reads=["x1_d"], writes=[x2k])
                X.update(szc=szc, szk=szk, pbc=pbc, pbk=pbk, x2=x2, x2k=x2k)

            X = dict(c=c, tsl=tsl, xc=xc, xck=xck, bc=bc, bck=bck, CT=CT, CTk=CTk,
                     sm=sm, smk=smk, Ms=Ms, tasks=tasks, mtasks=mtasks, fin=fin, late_loads=late_loads)
            return X

        def g_mid(X, run, pre_update):
            c, tsl, xc, xck, bc, bck, CT, CTk = X["c"], X["tsl"], X["xc"], X["xck"], X["bc"], X["bck"], X["CT"], X["CTk"]
            szc, szk, pbc, pbk, sm, smk, Ms = X["szc"], X["szk"], X["pbc"], X["pbk"], X["sm"], X["smk"], X["Ms"]
            cbf, cbfk = X["cbf"], X["cbfk"]
            for q4 in range(4):
                csl = slice(q4 * 512, (q4 + 1) * 512)
                pyd, pydk = psr.next()
                for h8 in range(8):
                    h = q4 * 8 + h8
                    for d in range(2):
                        S.op("pe", lambda: nc.tensor.matmul(pyd[:, h8 * 64:(h8 + 1) * 64], lhsT=Ms[d][0][:, h, :], rhs=xdts[d][:, h * 64:(h + 1) * 64],
                                                            start=(d == 0), stop=False), reads=[Ms[d][1], f"xdt{d}"], writes=[pydk])
                    S.op("pe", lambda: nc.tensor.matmul(pyd[:, h8 * 64:(h8 + 1) * 64], lhsT=Dident[:, h, :], rhs=xc[:, h * 64:(h + 1) * 64],
                                                        start=False, stop=True), reads=["Dident", xck], writes=[pydk])
                pof, pofk = psr.next()
                pob, pobk = psr.next()
                for g2 in range(2):
                    g = q4 * 2 + g2
                    S.op("pe", lambda: nc.tensor.matmul(pof[:, g2 * 256:(g2 + 1) * 256], lhsT=CT[:, g, :], rhs=cbf[:, g * 256:(g + 1) * 256], start=True, stop=True),
                         reads=[CTk, cbfk], writes=[pofk])
                    S.op("pe", lambda: nc.tensor.matmul(pob[:, g2 * 256:(g2 + 1) * 256], lhsT=CT[:, g, :], rhs=pbc[:, g * 256:(g + 1) * 256], start=True, stop=True),
                         reads=[CTk, pbk], writes=[pobk])
                yc, yck = ycr.next()
                yt_, ytk = ytr.next()
                v3 = lambda t_: t_[:].rearrange("p (h q) -> p h q", q=64)
                ef = sm[:, 1, q4 * 8:q4 * 8 + 8].unsqueeze(2).to_broadcast([128, 8, 64])
                eb = sm[:, 1, 32 + q4 * 8:32 + q4 * 8 + 8].unsqueeze(2).to_broadcast([128, 8, 64])
                S.op("dve", lambda: nc.vector.tensor_tensor(out=v3(yc), in0=pof[:].rearrange("p (h q) -> p h q", q=64), in1=ef, op=ALU.mult), reads=[pofk, smk], writes=[yck])
                S.op("dve", lambda: nc.vector.tensor_tensor(out=v3(yt_), in0=pob[:].rearrange("p (h q) -> p h q", q=64), in1=eb, op=ALU.mult), reads=[pobk, smk], writes=[ytk])
                S.op("dve", lambda: nc.vector.tensor_tensor(out=yc[:], in0=yc[:], in1=yt_[:], op=ALU.add), reads=[yck, ytk], writes=[yck])
                S.op("dve", lambda: nc.vector.tensor_tensor(out=yc[:], in0=pyd[:], in1=yc[:], op=ALU.add), reads=[pydk, yck], writes=[yck])
                S.op("dve", lambda: nc.vector.tensor_tensor(out=yg[:, csl], in0=yc[:], in1=szc[:, csl], op=ALU.mult), reads=[yck, szk], writes=["yg"])
                run(4)
            pre_update()
            state_update(c, xc, xck, bc, bck, sm, smk, 0)

        def g_tail_a(X):
            ns, nsk = nst2r.next()
            S.op("act", lambda: nc.scalar.activation(out=ynb[:], in_=yg[:], func=AF.Square, accum_out=ns[:, 0:1]), reads=["yg"], writes=["ynb", nsk])
            S.op("act", lambda: nc.scalar.activation(out=ns[:, 1:2], in_=ns[:, 0:1], func=AF.Ln, bias=epsc[:, 0:1], scale=1.0 / SSD_INNER), reads=[nsk, "epsc"], writes=[nsk])
            S.op("act", lambda: nc.scalar.activation(out=ns[:, 2:3], in_=ns[:, 1:2], func=AF.Exp, scale=-0.5), reads=[nsk], writes=[nsk])
            S.op("dve", lambda: nc.vector.tensor_scalar(out=ynb[:], in0=yg[:], scalar1=ns[:, 2:3], scalar2=None, op0=ALU.mult),
                 reads=["yg", nsk], writes=["ynb"])
            for jh in range(2):
                tp, tpk = ptr.next()
                for jj in range(8):
                    j = jh * 8 + jj
                    S.op("pe", lambda: nc.tensor.transpose(tp[:, jj, :], ynb[:, j * 128:(j + 1) * 128], K["ident"][:]), reads=["ynb", "k_ident"], writes=[tpk])
                S.op("act", lambda: nc.scalar.activation(out=ynT[:, jh * 8:(jh + 1) * 8, :], in_=tp[:], func=AF.Copy), reads=[tpk], writes=["ynT"])

        def g_tail_b(X):
            c, tsl, x2, x2k = X["c"], X["tsl"], X["x2"], X["x2k"]
            y2, y2k = yg, "yg"
            for n in range(2):
                po, pok = psr.next()
                for j in range(16):
                    S.op("pe", lambda: nc.tensor.matmul(po[:], lhsT=ynT[:, j, :], rhs=wout1[:, j, n * 512:(n + 1) * 512], start=(j == 0), stop=(j == 15)),
                         reads=["ynT", "wout1"], writes=[pok])
                S.op("dve", lambda: nc.vector.tensor_tensor(out=y2[:, n * 512:(n + 1) * 512], in0=po[:], in1=gate_b[1][:, n * 512:(n + 1) * 512], op=ALU.mult),
                     reads=[pok, "gate_b1"], writes=[y2k])
            S.op("dve", lambda: nc.vector.tensor_tensor(out=x2[:], in0=y2[:, 0:D], in1=x2[:], op=ALU.add), reads=[y2k, x2k], writes=[x2k])
            ns, nsk = nst2r.next()
            S.op("act", lambda: nc.scalar.activation(out=nrm_junk[:], in_=x2[:], func=AF.Square, accum_out=ns[:, 0:1]), reads=[x2k], writes=["nrm_junk", nsk])
            S.op("act", lambda: nc.scalar.activation(out=ns[:, 1:2], in_=ns[:, 0:1], func=AF.Ln, bias=epsc[:, 0:1], scale=1.0 / D), reads=[nsk, "epsc"], writes=[nsk])
            S.op("act", lambda: nc.scalar.activation(out=ns[:, 2:3], in_=ns[:, 1:2], func=AF.Exp, scale=-0.5), reads=[nsk], writes=[nsk])
            ot, otk = outr.next()
            S.op("dve", lambda: nc.vector.scalar_tensor_tensor(out=ot[:], in0=x2[:], scalar=ns[:, 2:3], in1=fnw_b[:], op0=ALU.mult, op1=ALU.mult),
                 reads=[x2k, nsk, "fnw_b"], writes=[otk])
            X["store"] = lambda: S.dma("sp", out_d[tsl, :], ot[:], reads=[otk], writes=["out_d"])

        def make_runner(X):
            tl = list(X["tasks"]) if X is not None else []
            ml = list(X["mtasks"]) if X is not None else []
            done = [0]

            def run(n):
                for _ in range(n):
                    if tl:
                        tl.pop(0)()
                        done[0] += 1
                        if done[0] > 2 and ml:
                            ml.pop(0)()

            def flush():
                run(len(tl))
                while ml:
                    ml.pop(0)()
            return run, flush

        Ld = {0: g_loads(0)}
        Sc = {0: g_scal(0, Ld[0][-1])}
        ctx = g_head(0, Ld.pop(0), Sc.pop(0))
        run, flush = make_runner(ctx)
        flush()
        ctx["fin"]()
        ctx["late_loads"]()
        if NT > 1:
            Ld[1] = g_loads(1)
            Sc[1] = g_scal(1, Ld[1][-1])
        pend_store = None

        def emit_cbf(X):
            cbf, cbfk = cbfr.next()
            S.op("act", lambda: nc.scalar.activation(out=cbf[:], in_=carry[:], func=AF.Copy), reads=["carry"], writes=[cbfk])
            X["cbf"], X["cbfk"] = cbf, cbfk

        emit_cbf(ctx)
        for c in range(NT):
            nxt = g_head(c + 1, Ld.pop(c + 1), Sc.pop(c + 1)) if c + 1 < NT else None
            run, flush = make_runner(nxt)
            g_mid(ctx, run, lambda: g_tail_a(ctx))
            if nxt is not None:
                emit_cbf(nxt)
            flush()
            if c + 2 < NT:
                Ld[c + 2] = g_loads(c + 2)
                Sc[c + 2] = g_scal(c + 2, Ld[c + 2][-1])
            if nxt is not None:
                nxt["late_loads"]()
            if pend_store is not None:
                pend_store()
            if nxt is not None:
                nxt["fin"]()
            g_tail_b(ctx)
            pend_store = ctx["store"]
            ctx = nxt
        pend_store()
        S.barrier()
    S.finish()
    return nc, dbg


def make_in_maps(inputs):
    cst = host_consts()
    pcol = host_pcol(inputs)
    f = lambda a: np.ascontiguousarray(np.asarray(a), dtype=np.float32)
    shared = {
        "mod_w": f(inputs["mod_w"]), "mod_b": f(inputs["mod_b"]),
        "attn_w_in": f(inputs["attn_w_in"][0]), "attn_w_out": f(inputs["attn_w_out"][0]),
        "ssd_w_in": f(inputs["ssd_w_in"][0]),
        "ssd_dt_bias": f(inputs["ssd_dt_bias"]).reshape(1, 64), "ssd_a_log": f(inputs["ssd_a_log"]).reshape(1, 64),
        "ssd_d": f(inputs["ssd_d"]).reshape(1, 32), "ssd_norm_w": f(inputs["ssd_norm_w"]).reshape(1, SSD_INNER),
        "ssd_w_out": f(inputs["ssd_w_out"][0]), "final_norm_w": f(inputs["final_norm_w"]).reshape(1, D),
        "pcol": pcol,
    }
    for k, v in cst.items():
        shared["k_" + k] = v
    x = f(inputs["x"])
    c = f(inputs["c"])
    pos = np.ascontiguousarray(np.asarray(inputs["positions"]), dtype=np.int32)
    maps = []
    for b in range(x.shape[0]):
        m = dict(shared)
        m["x"] = x[b]
        m["c"] = np.ascontiguousarray(c[b].reshape(8, 128).T)
        m["pos"] = pos[b:b + 1]
        maps.append(m)
    return maps


def kernel(**inputs):
    nc, _ = build()
    maps = make_in_maps(inputs)
    res = run_bass_kernel_spmd(nc, maps, core_ids=list(range(8)))
    return np.stack([np.asarray(r["out"], dtype=np.float32) for r in res.results], axis=0)
```

```python
import contextlib
import math
import numpy as np
import ml_dtypes
import concourse.bass as bass
import concourse.mybir as mybir
from concourse.bass_utils import run_bass_kernel_spmd

F32 = mybir.dt.float32
BF16 = mybir.dt.bfloat16
I32 = mybir.dt.int32
AF = mybir.ActivationFunctionType
ALU = mybir.AluOpType

D = 1024
SEQ = 4096
NT = SEQ // 128
AW = 1536
PATTERNS = ((128, 1), (512, 4), (2048, 16))
SSD_INNER = 2048
SSD_IN = 6208
EPS = 1e-6
import os
KSTEP = int(os.environ.get('KSTEP', '9'))
KPOOL = int(os.environ.get('KPOOL', '0'))


class Sched:
    NDMA = 10

    def __init__(self, nc, es, needed=None):
        self.nc = nc
        self.es = es
        self.needed = needed
        self.record = set()
        self.engs = {"pe": nc.tensor, "dve": nc.vector, "act": nc.scalar,
                     "pool": nc.gpsimd, "sp": nc.sync}
        self.sem = {}
        self.raw = {}
        self.pub = {}
        for e in self.engs:
            self.sem[e] = self.es.enter_context(nc.semaphore("s_" + e))
            self.raw[e] = 0
            self.pub[e] = 0
        self.dsem, self.dcnt, self.dnext = {}, {}, {}
        for q in ("sp",):
            self.dsem[q] = [self.es.enter_context(nc.semaphore(f"d_{q}{i}")) for i in range(self.NDMA)]
            self.dcnt[q] = [0] * self.NDMA
            self.dnext[q] = 0
        self.seen = {e: {} for e in self.engs}
        self.lastw = {}
        self.readers = {}
        self.ninstr = 0
        self.nwaits = 0

    def _wait(self, e, ev):
        sem, val, src, rid = ev
        if src == "pe" and e == "pe":
            return
        k = id(sem)
        if self.seen[e].get(k, 0) >= val:
            return
        if rid is not None:
            self.record.add(rid)
        self.engs[e].wait_ge(sem, val)
        self.seen[e][k] = val
        self.nwaits += 1

    def _deps(self, e, reads, writes):
        for k in reads:
            ev = self.lastw.get(k)
            if ev is not None:
                self._wait(e, ev)
            if k[:2] in ("ps", "pt"):
                for ev in self.readers.get(k, ()):
                    self._wait(e, ev)
        for k in writes:
            ev = self.lastw.get(k)
            if ev is not None:
                self._wait(e, ev)
            for ev in self.readers.get(k, ()):
                self._wait(e, ev)

    def _commit(self, ev, reads, writes):
        for k in reads:
            self.readers.setdefault(k, []).append(ev)
        for k in writes:
            self.lastw[k] = ev
            self.readers[k] = []

    def op(self, e, fn, reads=(), writes=()):
        self._deps(e, reads, writes)
        ins = fn()
        self.raw[e] += 1
        rid = (e, self.raw[e])
        if self.needed is None or rid in self.needed:
            self.pub[e] += 1
            ins.then_inc(self.sem[e], 1)
            ev = (self.sem[e], self.pub[e], e, rid)
        else:
            ev = (self.sem[e], self.pub[e] + 1, e, rid)
        self._commit(ev, reads, writes)
        self.ninstr += 1
        return ev

    def dma(self, q, out, in_, reads=(), writes=()):
        i = self.dnext[q]
        self.dnext[q] = (i + 1) % self.NDMA
        sem = self.dsem[q][i]
        if self.dcnt[q][i] > 0:
            self._wait(q, (sem, self.dcnt[q][i], None, None))
        self._deps(q, reads, writes)
        ins = self.engs[q].dma_start(out=out, in_=in_)
        self.dcnt[q][i] += 16
        ins.then_inc(sem, 16)
        ev = (sem, self.dcnt[q][i], None, None)
        self._commit(ev, reads, writes)
        self.ninstr += 1
        return ev

    def barrier(self):
        evs = []
        for e in self.engs:
            if self.raw[e] > 0:
                rid = (e, self.raw[e])
                pubd = self.needed is None or rid in self.needed
                evs.append((self.sem[e], self.pub[e] if pubd else self.pub[e] + 1, e, rid))
        for q in self.dsem:
            for i, sem in enumerate(self.dsem[q]):
                if self.dcnt[q][i] > 0:
                    evs.append((sem, self.dcnt[q][i], None, None))
        for e in self.engs:
            for ev in evs:
                if ev[2] == e:
                    continue
                self._wait(e, ev)
        self.lastw = {}
        self.readers = {}

    def finish(self):
        for q in self.dsem:
            for i, sem in enumerate(self.dsem[q]):
                if self.dcnt[q][i] > 0:
                    self._wait("sp", (sem, self.dcnt[q][i], None, None))


class Rot:
    def __init__(self, tiles, name):
        self.tiles = tiles
        self.name = name
        self.i = -1

    def next(self):
        self.i = (self.i + 1) % len(self.tiles)
        return self.tiles[self.i], f"{self.name}{self.i}"


def _ap(t, rowsize, p0, npart, col, dims):
    return bass.AP(t, p0 * rowsize + col, [[rowsize, npart]] + [list(d) for d in dims])


def _bf(a):
    return np.asarray(a, dtype=np.float32).astype(ml_dtypes.bfloat16)


def host_consts():
    c = {}
    p = np.arange(128)
    c["ident"] = _bf(np.eye(128))
    c["bdones"] = _bf((p[:, None] // 64) == (p[None, :] // 64))
    pm = np.zeros((128, 128), np.float32)
    for hb in (0, 64):
        for j in range(8):
            pm[hb + j + 8, hb + j] = 1.0
            pm[hb + j, hb + 8 + j] = 1.0
    c["pswap"] = _bf(pm)
    i = (p % 64)[:, None, None]
    i4 = np.arange(4)[None, :, None]
    jq = np.arange(128)[None, None, :]
    c["mask4"] = _bf(np.abs(64 * (i4 - 1) + i - jq) <= 64).reshape(128, 512)
    k = p[:, None]
    l = p[None, :]
    c["triL"] = (k <= l).astype(np.float32)
    c["triU"] = (k >= l).astype(np.float32)
    c["nmaskf"] = _bf(np.where(l <= k, 0.0, -30000.0))
    c["nmaskb"] = _bf(np.where(l >= k, 0.0, -30000.0))
    c["maskf"] = _bf(k <= l)
    c["maskb"] = _bf(k >= l)
    return c


def host_pcol(inp):
    cols = {}
    p = np.arange(128)
    inv = (500000.0 ** (-np.arange(0, 16, 2, dtype=np.float32) / 16.0)).astype(np.float32)
    pp = p % 64
    invf = np.where(pp < 16, inv[pp % 8], 0.0).astype(np.float32)
    sg = np.where(pp < 8, -1.0, np.where(pp < 16, 1.0, 0.0)).astype(np.float32)
    parts = [invf[:, None], sg[:, None]]
    parts.append(np.asarray(inp["norm_w"], np.float32).reshape(2, 8, 128).transpose(2, 0, 1).reshape(128, 16))
    parts.append(np.asarray(inp["mod_b"], np.float32).reshape(2, 24, 128).transpose(2, 0, 1).reshape(128, 48))
    parts.append(np.asarray(inp["ssd_conv_w"], np.float32).reshape(5, 32, 128).transpose(2, 0, 1).reshape(128, 160))
    parts.append(np.asarray(inp["ssd_conv_b"], np.float32).reshape(32, 128).T)
    parts.append(np.asarray(inp["ssd_norm_w"], np.float32).reshape(16, 128).T)
    return np.ascontiguousarray(np.concatenate(parts, axis=1), dtype=np.float32)


PC_INVF, PC_SG, PC_NW, PC_MODB, PC_CW, PC_CB, PC_SNW, PC_N = 0, 1, 2, 18, 66, 226, 258, 274


def build(stop_after=None):
    M = contextlib.ExitStack()
    rec = {}
    _build(M, stop_after, None, rec)
    M.close()
    M = contextlib.ExitStack()
    nc, dbg = _build(M, stop_after, rec["needed"], {})
    M.close()
    return nc, dbg


def _build(M, stop_after=None, needed=None, rec=None):
    nc = bass.Bass("TRN2", target_bir_lowering=False)
    S = Sched(nc, M, needed)
    rec["needed"] = S.record
    dbg = {}

    def din(name, shape, dt=F32):
        return nc.dram_tensor(name, list(shape), dt, kind="ExternalInput").ap()

    x_d = din("x", [SEQ, D])
    c_d = din("c", [128, 8])
    pos_d = din("pos", [1, SEQ], I32)
    modw_d = din("mod_w", [2, D, 3 * D])
    modb_d = din("mod_b", [2, 3 * D])
    awin_d = din("attn_w_in", [D, 4 * AW])
    awout_d = din("attn_w_out", [AW, D])
    swin_d = din("ssd_w_in", [D, SSD_IN])
    dtb_d = din("ssd_dt_bias", [1, 64])
    alog_d = din("ssd_a_log", [1, 64])
    sd_d = din("ssd_d", [1, 32])
    snw_d = din("ssd_norm_w", [1, SSD_INNER])
    swout_d = din("ssd_w_out", [SSD_INNER, D])
    fnw_d = din("final_norm_w", [1, D])
    pcol_d = din("pcol", [128, PC_N])
    cst = host_consts()
    cst_d = {k: din("k_" + k, list(v.shape), BF16 if v.dtype != np.float32 else F32) for k, v in cst.items()}
    out_d = nc.dram_tensor("out", [SEQ, D], F32, kind="ExternalOutput").ap()

    def scratch(name, shape, dt):
        kind = "ExternalOutput" if stop_after is not None else "Internal"
        t = nc.dram_tensor(name, list(shape), dt, kind=kind).ap()
        dbg[name] = t
        return t

    o_d = scratch("o_d", [AW, SEQ], BF16)
    x1_d = scratch("x1_d", [SEQ, D], F32)

    P = M

    uid = [0]

    def sb(stack, name, shape, dt):
        uid[0] += 1
        return stack.enter_context(nc.sbuf_tensor(f"sb{uid[0]}_{name}", list(shape), dt))

    ps = [P.enter_context(nc.psum_tensor(f"ps{i}", [128, 512], F32)) for i in range(6)]
    pt = [P.enter_context(nc.psum_tensor(f"pt{i}", [128, 8, 128], BF16)) for i in range(2)]
    psr = Rot(ps, "ps")
    ptr = Rot(pt, "pt")

    def load_cast(stg_rot, dst, dkey, src, shape):
        stg, stgk = stg_rot.next()
        n = 1
        for d_ in shape[1:]:
            n *= d_
        sv = stg[:, 0:n]
        if len(shape) == 3:
            sv = sv.rearrange("p (a b) -> p a b", a=shape[1])
        S.dma("sp", sv, src, writes=[stgk])
        S.op("act", lambda: nc.scalar.activation(out=dst, in_=sv, func=AF.Copy), reads=[stgk], writes=[dkey])

    pcol = sb(P, "pcol", [128, PC_N], F32)
    S.dma("sp", pcol[:], pcol_d, writes=["pcol"])
    K = {}
    for k, v in cst.items():
        K[k] = sb(P, "k_" + k, list(v.shape), BF16 if v.dtype != np.float32 else F32)
        S.dma("sp", K[k][:], cst_d[k], writes=["k_" + k])
    if stop_after is not None:
        junk = sb(P, "junk", [1, 16], F32)
        junki = sb(P, "junki", [1, 16], I32)
        for t_ in (awin_d, awout_d, swin_d, dtb_d, alog_d, sd_d, snw_d, swout_d, fnw_d):
            S.dma("sp", junk[:], t_[0:1, 0:16], writes=["junk"])
        S.dma("sp", junki[:], pos_d[0:1, 0:16], writes=["junki"])
    epsc = sb(P, "epsc", [128, 1], F32)
    S.op("dve", lambda: nc.vector.memset(epsc[:], EPS), writes=["epsc"])
    one11 = sb(P, "one11", [1, 1], F32)
    S.op("dve", lambda: nc.vector.memset(one11[:], 1.0), writes=["one11"])
    modA = sb(P, "modA", [128, 16], F32)
    modB = sb(P, "modB", [128, 16], F32)
    gate_b = [sb(P, f"gate_b{i}", [128, D], F32) for i in range(2)]
    nrm_junk = sb(P, "nrm_junk", [128, D], BF16)
    xn_t = [sb(P, f"xn{i}", [128, D], BF16) for i in range(2)]
    st_t = [sb(P, f"nst{i}", [128, 4], F32) for i in range(2)]
    onec = sb(P, "onec", [128, 1], F32)
    S.op("dve", lambda: nc.vector.memset(onec[:], 1.0), writes=["onec"])
    H = contextlib.ExitStack()
    M.enter_context(H)
    hnT = sb(H, "hnT", [128, 8, SEQ], BF16)

    with contextlib.ExitStack() as A:
        c_fm = sb(A, "c_fm", [128, 8], F32)
        S.dma("sp", c_fm[:], c_d, writes=["c_fm"])
        cond_f = sb(A, "cond_f", [128, 8], F32)
        cond_b = sb(A, "cond_b", [128, 8], BF16)
        condB = sb(A, "condB", [128, 8, 128], F32)
        S.op("act", lambda: nc.scalar.activation(out=cond_f[:], in_=c_fm[:], func=AF.Silu), reads=["c_fm"], writes=["cond_f"])
        if stop_after == "A0":
            dd = nc.dram_tensor("cond_dbg", [128, 8], F32, kind="ExternalOutput").ap()
            S.dma("sp", dd, cond_f[:], reads=["cond_f"])
            S.finish()
            return nc, dbg
        S.op("dve", lambda: nc.vector.tensor_copy(out=cond_b[:], in_=cond_f[:]), reads=["cond_f"], writes=["cond_b"])
        S.op("dve", lambda: nc.vector.tensor_copy(out=condB[:], in_=cond_f[:].unsqueeze(2).to_broadcast([128, 8, 128])),
             reads=["cond_f"], writes=["condB"])
        modw = sb(A, "modw", [128, 8, 3 * D], F32)
        modT = sb(A, "modT", [128, 24], F32)
        gb_bias = sb(A, "gb_bias", [128, D], F32)
        for li in range(2):
            for kc in range(8):
                S.dma("sp", modw[:, kc, :], modw_d[li, kc * 128:(kc + 1) * 128, :], writes=[f"modw{kc}"])
            pm, pmk = psr.next()
            for fc in range(24):
                for kc in range(8):
                    S.op("pe", lambda: nc.tensor.matmul(pm[:, fc:fc + 1], lhsT=modw[:, kc, fc * 128:(fc + 1) * 128],
                                                        rhs=cond_f[:, kc:kc + 1], start=(kc == 0), stop=(kc == 7)),
                         reads=[f"modw{kc}", "cond_f"], writes=[pmk])
            S.op("dve", lambda: nc.vector.tensor_tensor(out=modT[:], in0=pm[:, 0:24],
                                                        in1=pcol[:, PC_MODB + li * 24:PC_MODB + (li + 1) * 24], op=ALU.add),
                 reads=[pmk, "pcol"], writes=["modT"])
            S.op("dve", lambda: nc.vector.tensor_copy(out=modB[:, li * 8:(li + 1) * 8], in_=modT[:, 0:8]),
                 reads=["modT"], writes=["modB"])
            S.op("dve", lambda: nc.vector.scalar_tensor_tensor(out=modA[:, li * 8:(li + 1) * 8], in0=modT[:, 8:16], scalar=1.0,
                                                               in1=pcol[:, PC_NW + li * 8:PC_NW + (li + 1) * 8],
                                                               op0=ALU.add, op1=ALU.mult),
                 reads=["modT", "pcol"], writes=["modA"])
            S.dma("sp", gb_bias[:], modb_d[li:li + 1, 2 * D:3 * D].partition_broadcast(128), writes=["gb_bias"])
            for n in range(2):
                pg, pgk = psr.next()
                for kc in range(8):
                    S.op("pe", lambda: nc.tensor.matmul(pg[:], lhsT=condB[:, kc, :],
                                                        rhs=modw[:, kc, 2 * D + n * 512:2 * D + (n + 1) * 512],
                                                        start=(kc == 0), stop=(kc == 7)),
                         reads=[f"modw{kc}", "condB"], writes=[pgk])
                S.op("dve", lambda: nc.vector.tensor_tensor(out=gate_b[li][:, n * 512:(n + 1) * 512], in0=pg[:],
                                                            in1=gb_bias[:, n * 512:(n + 1) * 512], op=ALU.add),
                     reads=[pgk, "gb_bias"], writes=[f"gate_b{li}"])
        S.barrier()

    if stop_after == "A":
        for nm, t_, shp in (("modA_dbg", modA, [128, 16]), ("modB_dbg", modB, [128, 16]), ("gate0_dbg", gate_b[0], [128, D]), ("gate1_dbg", gate_b[1], [128, D])):
            dd = nc.dram_tensor(nm, shp, F32, kind="ExternalOutput").ap()
            S.dma("sp", dd, t_[:])
        S.finish()
        return nc, dbg

    xnr = Rot(xn_t, "xn")
    str_ = Rot(st_t, "nst")

    def norm_tile(xt, xk, tt, li):
        st, stk = str_.next()
        xn, xnk = xnr.next()
        S.op("act", lambda: nc.scalar.activation(out=nrm_junk[:], in_=xt[:], func=AF.Square, accum_out=st[:, 0:1]),
             reads=[xk], writes=["nrm_junk", stk])
        S.op("act", lambda: nc.scalar.activation(out=st[:, 1:2], in_=st[:, 0:1], func=AF.Sqrt, bias=epsc[:, 0:1], scale=1.0 / D),
             reads=[stk, "epsc"], writes=[stk])
        S.op("dve", lambda: nc.vector.reciprocal(out=st[:, 2:3], in_=st[:, 1:2]), reads=[stk], writes=[stk])
        S.op("dve", lambda: nc.vector.tensor_scalar(out=xn[:], in0=xt[:], scalar1=st[:, 2:3], scalar2=None, op0=ALU.mult),
             reads=[xk, stk], writes=[xnk])
        tp, tpk = ptr.next()
        for kc in range(8):
            S.op("pe", lambda: nc.tensor.transpose(tp[:, kc, :], xn[:, kc * 128:(kc + 1) * 128], K["ident"][:]),
                 reads=[xnk, "k_ident"], writes=[tpk])
        for kc in range(8):
            col = li * 8 + kc
            if True:
                S.op("dve", lambda: nc.vector.tensor_scalar(out=hnT[:, kc, tt * 128:(tt + 1) * 128], in0=tp[:, kc, :],
                                                            scalar1=modA[:, col:col + 1], scalar2=modB[:, col:col + 1],
                                                            op0=ALU.mult, op1=ALU.add),
                     reads=[tpk, "modA", "modB"], writes=[f"hnT{kc}"])
            else:
                S.op("act", lambda: nc.scalar.activation(out=hnT[:, kc, tt * 128:(tt + 1) * 128], in_=tp[:, kc, :],
                                                         func=AF.Identity, scale=modA[:, col:col + 1], bias=modB[:, col:col + 1]),
                     reads=[tpk, "modA", "modB"], writes=[f"hnT{kc}"])

    HN = [f"hnT{kc}" for kc in range(8)]

    with contextlib.ExitStack() as B:
        xts = [sb(B, f"xt{i}", [128, D], F32) for i in range(3)]
        xr = Rot(xts, "xt")
        for tt in range(NT):
            xt, xk = xr.next()
            S.dma("sp", xt[:], x_d[tt * 128:(tt + 1) * 128, :], writes=[xk])
            norm_tile(xt, xk, tt, 0)
        S.barrier()

    if stop_after == "B":
        hn_dbg = nc.dram_tensor("hn_dbg", [128, 8, SEQ], BF16, kind="ExternalOutput").ap()
        dbg["hn_dbg"] = hn_dbg
        for kc in range(8):
            S.dma("sp", hn_dbg[:, kc, :], hnT[:, kc, :], reads=HN)
        S.finish()
        return nc, dbg

    with contextlib.ExitStack() as C:
        Ct = sb(C, "Ct", [128, SEQ], BF16)
        St = sb(C, "St", [128, SEQ], BF16)
        with contextlib.ExitStack() as R:
            RB = 1024
            posi = sb(R, "posi", [128, RB], I32)
            posf = sb(R, "posf", [128, RB], F32)
            ang = sb(R, "ang", [128, RB], F32)
            ki = sb(R, "ki", [128, RB], I32)
            kf = sb(R, "kf", [128, RB], F32)
            for cb in range(SEQ // RB):
                csl = slice(cb * RB, (cb + 1) * RB)
                S.dma("sp", posi[:], pos_d[:, csl].partition_broadcast(128), writes=["posi"])
                S.op("dve", lambda: nc.vector.tensor_copy(out=posf[:], in_=posi[:]), reads=["posi"], writes=["posf"])
                for which, phase in ((0, 0.0), (1, math.pi / 2)):
                    S.op("dve", lambda: nc.vector.tensor_scalar(out=ang[:], in0=posf[:], scalar1=pcol[:, PC_INVF:PC_INVF + 1],
                                                                scalar2=phase, op0=ALU.mult, op1=ALU.add),
                         reads=["posf", "pcol"], writes=["ang"])
                    S.op("dve", lambda: nc.vector.tensor_scalar(out=ki[:], in0=ang[:], scalar1=1.0 / (2 * math.pi), scalar2=None,
                                                                op0=ALU.mult), reads=["ang"], writes=["ki"])
                    S.op("dve", lambda: nc.vector.tensor_copy(out=kf[:], in_=ki[:]), reads=["ki"], writes=["kf"])
                    S.op("dve", lambda: nc.vector.scalar_tensor_tensor(out=ang[:], in0=kf[:], scalar=-2 * math.pi, in1=ang[:],
                                                                       op0=ALU.mult, op1=ALU.add),
                         reads=["kf", "ang"], writes=["ang"])
                    if which == 0:
                        S.op("act", lambda: nc.scalar.activation(out=kf[:], in_=ang[:], func=AF.Sin), reads=["ang"], writes=["kf"])
                        S.op("dve", lambda: nc.vector.tensor_scalar(out=St[:, csl], in0=kf[:], scalar1=pcol[:, PC_SG:PC_SG + 1],
                                                                    scalar2=None, op0=ALU.mult), reads=["kf", "pcol"], writes=["St"])
                    else:
                        S.op("act", lambda: nc.scalar.activation(out=Ct[:, csl], in_=ang[:], func=AF.Sin), reads=["ang"], writes=["Ct"])
            S.barrier()
        def dump(nm, t_, shape, dt):
            dd = nc.dram_tensor(nm, shape, dt, kind="ExternalOutput").ap()
            if len(shape) == 2 and shape[1] * (2 if dt == BF16 else 4) > 32768:
                h = shape[1] // 2
                S.dma("sp", dd[:, 0:h], t_[:, 0:h])
                S.dma("sp", dd[:, h:], t_[:, h:])
            else:
                S.dma("sp", dd, t_[:])

        if stop_after == "Crope":
            dump("Ct_dbg", Ct, [128, SEQ], BF16)
            dump("St_dbg", St, [128, SEQ], BF16)
            S.finish()
            return nc, dbg
        Dacc = sb(C, "Dacc", [128, SEQ], F32)
        Rinv = Dacc
        tmpfs = [sb(C, f"tmpf{i}", [128, 512], F32) for i in range(1)]
        tmpfr = Rot(tmpfs, "tmpf")
        Ug = [sb(C, f"Ug{i}", [128, SEQ], BF16) for i in range(3)]
        QT = sb(C, "QT", [128, SEQ], BF16)
        Kz = sb(C, "Kz", [128, 2 * SEQ], BF16)
        Vbd = sb(C, "Vbd", [128, 64, 128], BF16)
        S.op("dve", lambda: nc.vector.memset(Kz[:], 0.0), writes=["Kz"])
        S.op("dve", lambda: nc.vector.memset(Vbd[:], 0.0), writes=["Vbd"])
        wts = [sb(C, f"wt{i}", [128, 8, 128], BF16) for i in range(3)]
        wstg = [sb(C, f"wstg{i}", [128, 1024], F32) for i in range(2)]
        wstgr = Rot(wstg, "wstg")
        wr = Rot(wts, "wt")
        stA = [sb(C, f"stA{i}", [128, 512], BF16) for i in range(2)]
        stAr = Rot(stA, "stA")
        st1 = [sb(C, f"st1{i}", [128, 512], BF16) for i in range(2)]
        st1r = Rot(st1, "st1")
        st2 = [sb(C, f"st2{i}", [128, 512], BF16) for i in range(2)]
        st2r = Rot(st2, "st2")
        Et = [sb(C, f"Et{i}", [128, 512], BF16) for i in range(2)]
        Etr = Rot(Et, "Et")
        PTt = [sb(C, f"PT{i}", [128, 512], BF16) for i in range(2)]
        PTr = Rot(PTt, "PT")
        Ost = [sb(C, f"Ost{i}", [128, 512], BF16) for i in range(2)]
        Ostr = Rot(Ost, "Ost")
        awv = awin_d.rearrange("(kc p) n -> p kc n", p=128)

        def load_w(col0):
            w, wk = wr.next()
            load_cast(wstgr, w[:], wk, awv[:, :, col0:col0 + 128], [128, 8, 128])
            return w, wk

        def proj_tile(w, wk, tn):
            pj, pjk = psr.next()
            for kc in range(8):
                S.op("pe", lambda: nc.tensor.matmul(pj[:], lhsT=w[:, kc, :], rhs=hnT[:, kc, tn * 512:(tn + 1) * 512],
                                                    start=(kc == 0), stop=(kc == 7)),
                     reads=[wk, f"hnT{kc}"], writes=[pjk])
            return pj, pjk

        jobs = []
        for s in range(4):
            for g in range(3):
                jobs += [(s, g, 0), (s, g, 1), (s, g, 2)]
            for g in range(3):
                jobs.append((s, g, 3))
        wq = {}
        PF = 2

        def prefetch(i):
            if i < len(jobs) and i not in wq:
                s_, g_, wh_ = jobs[i]
                wq[i] = load_w(wh_ * AW + (g_ * 4 + s_) * 128)

        for i in range(PF):
            prefetch(i)
        ji = [0]

        def next_w():
            w, wk = wq.pop(ji[0])
            prefetch(ji[0] + PF)
            ji[0] += 1
            return w, wk

        for s in range(4):
            for g in range(3):
                win, dil = PATTERNS[g]
                L = SEQ // dil
                j = g * 4 + s
                ntile = L // 64
                for which in range(2):
                    w, wk = next_w()

                    def post(tn, a, ak):
                        p2, p2k = psr.next()
                        S.op("pe", lambda: nc.tensor.matmul(p2[:], lhsT=K["pswap"][:], rhs=a[:], start=True, stop=True),
                             reads=["k_pswap", ak], writes=[p2k])
                        t1, t1k = st1r.next()
                        t2, t2k = st2r.next()
                        S.op("dve", lambda: nc.vector.tensor_tensor(out=t2[:], in0=p2[:], in1=St[:, tn * 512:(tn + 1) * 512], op=ALU.mult),
                             reads=[p2k, "St"], writes=[t2k])
                        S.op("dve", lambda: nc.vector.tensor_tensor(out=t1[:], in0=a[:], in1=Ct[:, tn * 512:(tn + 1) * 512], op=ALU.mult),
                             reads=[ak, "Ct"], writes=[t1k])
                        ni = 512 // dil
                        i0_ = tn * ni
                        if which == 0:
                            dst = QT[:].rearrange("p (r i) -> p i r", r=dil)[:, i0_:i0_ + ni, :]
                            S.op("dve", lambda: nc.vector.tensor_tensor(out=dst, in0=t1[:].rearrange("p (i r) -> p i r", r=dil),
                                                                        in1=t2[:].rearrange("p (i r) -> p i r", r=dil), op=ALU.add),
                                 reads=[t1k, t2k], writes=["QT"])
                        else:
                            bc_ = min(64, ni)
                            ac_ = max(1, ni // 64)
                            a0, b0 = divmod(i0_, 64)
                            for hh in range(2):
                                dst = bass.AP(Kz, hh * 64 * (2 * SEQ) + a0 * 128 + hh * 64 + b0,
                                              [[2 * SEQ, 64], [128, ac_], [1, bc_], [2 * L, dil]])
                                eng = "dve"
                                fn = nc.vector.tensor_tensor
                                S.op(eng, lambda: fn(out=dst, in0=t1[hh * 64:(hh + 1) * 64, :].rearrange("p (a b r) -> p a b r", a=ac_, b=bc_),
                                                     in1=t2[hh * 64:(hh + 1) * 64, :].rearrange("p (a b r) -> p a b r", a=ac_, b=bc_), op=ALU.add),
                                     reads=[t1k, t2k], writes=["Kz"])

                    pend = None
                    for tn in range(8):
                        pj, pjk = proj_tile(w, wk, tn)
                        a, ak = stAr.next()
                        S.op("act", lambda: nc.scalar.activation(out=a[:], in_=pj[:], func=AF.Copy), reads=[pjk], writes=[ak])
                        if pend is not None:
                            post(*pend)
                        pend = (tn, a, ak)
                    post(*pend)
                w, wk = next_w()
                VT, VTk = Ug[g], f"Ug{g}"
                for tn in range(8):
                    pj, pjk = proj_tile(w, wk, tn)
                    S.op("act", lambda: nc.scalar.activation(out=VT[:, tn * 512:(tn + 1) * 512], in_=pj[:], func=AF.Copy),
                         reads=[pjk], writes=[VTk])
                for t0 in range(0, 64, 4):
                    tp, tpk = psr.next()
                    for jj in range(4):
                        t = t0 + jj
                        r, cc = divmod(t, ntile)
                        src = _ap(VT, SEQ, 0, 128, r + dil * 64 * cc, [[dil, 64]])
                        for hh in range(2):
                            S.op("pe", lambda: nc.tensor.matmul(tp[hh * 64:(hh + 1) * 64, jj * 128:(jj + 1) * 128], lhsT=src, rhs=K["ident"][:],
                                                                start=True, stop=True),
                                 reads=[VTk, "k_ident"], writes=[tpk])
                    tpv = tp[:].rearrange("p (j f) -> p j f", j=4)
                    S.op("dve", lambda: nc.vector.tensor_copy(out=Vbd[0:64, t0:t0 + 4, 0:64], in_=tpv[0:64, :, 0:64]),
                         reads=[tpk], writes=["Vbd"])
                    S.op("act", lambda: nc.scalar.activation(out=Vbd[64:128, t0:t0 + 4, 64:128], in_=tpv[64:128, :, 64:128], func=AF.Copy),
                         reads=[tpk], writes=["Vbd"])
                nb = L // 128
                blocks = [(r, b) for r in range(dil) for b in range(nb)]

                def stage1(r, b):
                    lo = 1 if b == 0 else 0
                    hi = 3 if b == nb - 1 else 4
                    s4, s4k = psr.next()
                    qcol = r * L + 128 * b
                    for i4 in range(lo, hi):
                        cc = 2 * b - 1 + i4
                        S.op("pe", lambda: nc.tensor.matmul(s4[:, i4 * 128:(i4 + 1) * 128],
                                                            lhsT=Kz[:, (r * ntile + cc) * 128:(r * ntile + cc + 1) * 128],
                                                            rhs=QT[:, qcol:qcol + 128], start=True, stop=True),
                             reads=["Kz", "QT"], writes=[s4k])
                    e, ek = Etr.next()
                    S.op("act", lambda: nc.scalar.activation(out=e[:, lo * 128:hi * 128], in_=s4[:, lo * 128:hi * 128],
                                                             func=AF.Exp, scale=0.125), reads=[s4k], writes=[ek])
                    pt_, ptk = PTr.next()
                    S.op("dve", lambda: nc.vector.tensor_tensor(out=pt_[:, lo * 128:hi * 128], in0=e[:, lo * 128:hi * 128],
                                                                in1=K["mask4"][:, lo * 128:hi * 128], op=ALU.mult),
                         reads=[ek, "k_mask4"], writes=[ptk])
                    return (r, b, lo, hi, pt_, ptk)

                def stage2(r, b, lo, hi, pt_, ptk):
                    ud, udk = psr.next()
                    for i4 in range(lo, hi):
                        cc = 2 * b - 1 + i4
                        S.op("pe", lambda: nc.tensor.matmul(ud[:, 0:128], lhsT=Vbd[:, r * ntile + cc, :], rhs=pt_[:, i4 * 128:(i4 + 1) * 128],
                                                            start=(i4 == lo), stop=(i4 == hi - 1)),
                             reads=["Vbd", ptk], writes=[udk])
                    for i4 in range(lo, hi):
                        S.op("pe", lambda: nc.tensor.matmul(ud[:, 128:256], lhsT=K["bdones"][:], rhs=pt_[:, i4 * 128:(i4 + 1) * 128],
                                                            start=(i4 == lo), stop=(i4 == hi - 1)),
                             reads=["k_bdones", ptk], writes=[udk])
                    tok0 = r + dil * 128 * b
                    udst = _ap(Ug[g], SEQ, 0, 128, tok0, [[dil, 128]])
                    S.op("act", lambda: nc.scalar.activation(out=udst, in_=ud[:, 0:128], func=AF.Copy), reads=[udk], writes=[f"Ug{g}"])
                    ddst = _ap(Dacc, SEQ, 0, 128, tok0, [[dil, 128]])
                    if g == 0:
                        S.op("dve", lambda: nc.vector.tensor_copy(out=ddst, in_=ud[:, 128:256]), reads=[udk], writes=["Dacc"])
                    else:
                        S.op("dve", lambda: nc.vector.tensor_tensor(out=ddst, in0=ud[:, 128:256], in1=ddst, op=ALU.add),
                             reads=[udk, "Dacc"], writes=["Dacc"])

                pend = stage1(*blocks[0])
                for bi in range(len(blocks)):
                    nxt = stage1(*blocks[bi + 1]) if bi + 1 < len(blocks) else None
                    stage2(*pend)
                    pend = nxt
            S.op("dve", lambda: nc.vector.reciprocal(out=Rinv[:], in_=Dacc[:]), reads=["Dacc"], writes=["Dacc"])
            for g in range(3):
                j = g * 4 + s
                w, wk = next_w()
                for tn in range(8):
                    pj, pjk = proj_tile(w, wk, tn)
                    a, ak = stAr.next()
                    S.op("act", lambda: nc.scalar.activation(out=a[:], in_=pj[:], func=AF.Silu), reads=[pjk], writes=[ak])
                    sl = slice(tn * 512, (tn + 1) * 512)
                    tf, tfk = tmpfr.next()
                    S.op("dve", lambda: nc.vector.tensor_tensor(out=tf[:], in0=Ug[g][:, sl], in1=Rinv[:, sl], op=ALU.mult),
                         reads=[f"Ug{g}", "Dacc"], writes=[tfk])
                    o, ok = Ostr.next()
                    S.op("dve", lambda: nc.vector.tensor_tensor(out=o[:], in0=tf[:], in1=a[:], op=ALU.mult),
                         reads=[tfk, ak], writes=[ok])
                    S.dma("sp", o_d[j * 128:(j + 1) * 128, sl], o[:], reads=[ok], writes=["o_d"])
        S.barrier()

    if stop_after == "C":
        S.finish()
        return nc, dbg

    with contextlib.ExitStack() as Dx:
        wout = sb(Dx, "wout", [128, 12, D], BF16)
        wstg = [sb(Dx, f"wstgD{i}", [128, 1024], F32) for i in range(2)]
        wstgr = Rot(wstg, "wstgD")
        for j in range(12):
            load_cast(wstgr, wout[:, j, :], "wout", awout_d[j * 128:(j + 1) * 128, :], [128, 1024])
        OTs = [sb(Dx, f"OT{i}", [128, 12, 512], BF16) for i in range(2)]
        OTr = Rot(OTs, "OT")
        xts = [sb(Dx, f"xt{i}", [128, D], F32) for i in range(2)]
        xr = Rot(xts, "xt")
        ytm = [sb(Dx, f"ytm{i}", [128, D], F32) for i in range(2)]
        ytr = Rot(ytm, "ytm")
        x1s = [sb(Dx, f"x1t{i}", [128, D], F32) for i in range(2)]
        x1r = Rot(x1s, "x1t")
        o_v = o_d.rearrange("(j p) t -> p j t", p=128)
        for tb in range(8):
            ot, otk = OTr.next()
            S.dma("sp", ot[:], o_v[:, :, tb * 512:(tb + 1) * 512], reads=["o_d"], writes=[otk])
            for t4 in range(4):
                tt = tb * 4 + t4
                xt, xk = xr.next()
                S.dma("sp", xt[:], x_d[tt * 128:(tt + 1) * 128, :], writes=[xk])
                yt, ytk = ytr.next()
                for n in range(2):
                    py, pyk = psr.next()
                    for j in range(12):
                        S.op("pe", lambda: nc.tensor.matmul(py[:], lhsT=ot[:, j, t4 * 128:(t4 + 1) * 128], rhs=wout[:, j, n * 512:(n + 1) * 512],
                                                            start=(j == 0), stop=(j == 11)),
                             reads=[otk, "wout"], writes=[pyk])
                    S.op("dve", lambda: nc.vector.tensor_tensor(out=yt[:, n * 512:(n + 1) * 512], in0=py[:], in1=gate_b[0][:, n * 512:(n + 1) * 512],
                                                                op=ALU.mult), reads=[pyk, "gate_b0"], writes=[ytk])
                x1, x1k = x1r.next()
                S.op("dve", lambda: nc.vector.tensor_tensor(out=x1[:], in0=yt[:], in1=xt[:], op=ALU.add), reads=[ytk, xk], writes=[x1k])
                S.dma("sp", x1_d[tt * 128:(tt + 1) * 128, :], x1[:], reads=[x1k], writes=["x1_d"])
                norm_tile(x1, x1k, tt, 1)
        S.barrier()

    if stop_after == "D":
        hn_dbg = nc.dram_tensor("hn_dbg", [128, 8, SEQ], BF16, kind="ExternalOutput").ap()
        dbg["hn_dbg"] = hn_dbg
        for kc in range(8):
            S.dma("sp", hn_dbg[:, kc, :], hnT[:, kc, :], reads=HN)
        S.finish()
        return nc, dbg

    sz_d = scratch("sz_d", [SEQ, SSD_INNER], BF16)
    xs_d = scratch("xs_d", [SEQ, SSD_INNER], BF16)
    bt_d = scratch("bt_d", [SEQ, 1024], BF16)
    bT_d = scratch("bT_d", [1024, SEQ], BF16)
    cT_d = scratch("cT_d", [1024, SEQ], BF16)
    pb_d = scratch("pb_d", [NT, 128, SSD_INNER], BF16)
    dt_d = scratch("dt_d", [NT, 128, 64], F32)
    swv = swin_d.rearrange("(kc p) n -> p kc n", p=128)

    with contextlib.ExitStack() as E:
        dtb_b = sb(E, "dtb_b", [128, 64], F32)
        S.dma("sp", dtb_b[:], dtb_d.partition_broadcast(128), writes=["dtb_b"])
        with contextlib.ExitStack() as E1:
            wz = sb(E1, "wz", [128, 8, SSD_INNER], BF16)
            wdt = sb(E1, "wdt", [128, 8, 64], BF16)
            wstg1 = [sb(E1, f"wstgE1{i}", [128, 1024], F32) for i in range(2)]
            wstg1r = Rot(wstg1, "wstgE1")
            for kc in range(8):
                for hh in range(2):
                    load_cast(wstg1r, wz[:, kc, hh * 1024:(hh + 1) * 1024], f"wz{kc}", swin_d[kc * 128:(kc + 1) * 128, hh * 1024:(hh + 1) * 1024], [128, 1024])
            load_cast(wstg1r, wdt[:], "wdt", swv[:, :, 6144:6208], [128, 8, 64])
            szts = [sb(E1, f"szt{i}", [128, SSD_INNER], BF16) for i in range(2)]
            sztr = Rot(szts, "szt")
            dtt = [sb(E1, f"dtt{i}", [128, 64], F32) for i in range(2)]
            dttr = Rot(dtt, "dtt")
            dto = [sb(E1, f"dto{i}", [128, 64], F32) for i in range(2)]
            dtor = Rot(dto, "dto")
            for tt in range(NT):
                szt, sztk = sztr.next()
                for n in range(4):
                    pz, pzk = psr.next()
                    for kc in range(8):
                        S.op("pe", lambda: nc.tensor.matmul(pz[:], lhsT=hnT[:, kc, tt * 128:(tt + 1) * 128], rhs=wz[:, kc, n * 512:(n + 1) * 512],
                                                            start=(kc == 0), stop=(kc == 7)), reads=[f"hnT{kc}", f"wz{kc}"], writes=[pzk])
                    S.op("act", lambda: nc.scalar.activation(out=szt[:, n * 512:(n + 1) * 512], in_=pz[:], func=AF.Silu), reads=[pzk], writes=[sztk])
                S.dma("sp", sz_d[tt * 128:(tt + 1) * 128, :], szt[:], reads=[sztk], writes=["sz_d"])
            for tt in range(NT):
                pd, pdk = psr.next()
                for kc in range(8):
                    S.op("pe", lambda: nc.tensor.matmul(pd[:, 0:64], lhsT=hnT[:, kc, tt * 128:(tt + 1) * 128], rhs=wdt[:, kc, :],
                                                        start=(kc == 0), stop=(kc == 7)), reads=[f"hnT{kc}", "wdt"], writes=[pdk])
                dtmp, dtk = dttr.next()
                S.op("dve", lambda: nc.vector.tensor_tensor(out=dtmp[:], in0=pd[:, 0:64], in1=dtb_b[:], op=ALU.add), reads=[pdk, "dtb_b"], writes=[dtk])
                S.op("act", lambda: nc.scalar.activation(out=dtmp[:], in_=dtmp[:], func=AF.Exp), reads=[dtk], writes=[dtk])
                dto_, dtok = dtor.next()
                S.op("act", lambda: nc.scalar.activation(out=dto_[:], in_=dtmp[:], func=AF.Ln, bias=onec[:, 0:1], scale=1.0),
                     reads=[dtk, "onec"], writes=[dtok])
                S.dma("sp", dt_d[tt], dto_[:], reads=[dtok], writes=["dt_d"])
            S.barrier()
        raws = [sb(E, f"raw{i}", [128, SEQ + 4], F32) for i in range(2)]
        for i in range(2):
            S.op("dve", lambda: nc.vector.memset(raws[i][:], 0.0), writes=[f"raw{i}"])
        rawr = Rot(raws, "raw")
        accs = [sb(E, f"acc{i}", [128, 1024], F32) for i in range(3)]
        accr = Rot(accs, "acc")
        xos = [sb(E, f"xo{i}", [128, SEQ], BF16) for i in range(4)]
        xor_ = Rot(xos, "xo")
        xtoks = [sb(E, f"xtok{i}", [128, 8, 128], BF16) for i in range(2)]
        xtokr = Rot(xtoks, "xtok")
        wxs = [sb(E, f"wx{i}", [128, 8, 128], BF16) for i in range(3)]
        wstgE = [sb(E, f"wstgE{i}", [128, 1024], F32) for i in range(2)]
        wstgEr = Rot(wstgE, "wstgE")
        wxr = Rot(wxs, "wx")
        xs_v = xs_d.rearrange("(t p) c -> p t c", p=128)
        bt_v = bt_d.rearrange("(t p) c -> p t c", p=128)
        def load_wx(cc_):
            wx_, wxk_ = wxr.next()
            load_cast(wstgEr, wx_[:], wxk_, swv[:, :, SSD_INNER + cc_ * 128:SSD_INNER + (cc_ + 1) * 128], [128, 8, 128])
            return wx_, wxk_

        pend_tok = []

        def to_tok(cc_, xo_, xok_):
            for tb in range(4):
                tp, tpk = ptr.next()
                for jj in range(8):
                    tt = tb * 8 + jj
                    S.op("pe", lambda: nc.tensor.transpose(tp[:, jj, :], xo_[:, tt * 128:(tt + 1) * 128], K["ident"][:]),
                         reads=[xok_, "k_ident"], writes=[tpk])
                xtok, xtokk = xtokr.next()
                S.op("act", lambda: nc.scalar.activation(out=xtok[:], in_=tp[:], func=AF.Copy), reads=[tpk], writes=[xtokk])
                if cc_ < 16:
                    S.dma("sp", xs_v[:, tb * 8:(tb + 1) * 8, cc_ * 128:(cc_ + 1) * 128], xtok[:], reads=[xtokk], writes=["xs_d"])
                else:
                    S.dma("sp", bt_v[:, tb * 8:(tb + 1) * 8, (cc_ - 16) * 128:(cc_ - 15) * 128], xtok[:], reads=[xtokk], writes=["bt_d"])

        wxq = {0: load_wx(0), 1: load_wx(1)}
        def e_proj(cc):
            wx, wxk = wxq.pop(cc)
            if cc + 2 < 32:
                wxq[cc + 2] = load_wx(cc + 2)
            raw, rawk = rawr.next()
            for tn in range(8):
                pj, pjk = psr.next()
                for kc in range(8):
                    S.op("pe", lambda: nc.tensor.matmul(pj[:], lhsT=wx[:, kc, :], rhs=hnT[:, kc, tn * 512:(tn + 1) * 512],
                                                        start=(kc == 0), stop=(kc == 7)), reads=[wxk, f"hnT{kc}"], writes=[pjk])
                S.op("act", lambda: nc.scalar.activation(out=raw[:, 2 + tn * 512:2 + (tn + 1) * 512], in_=pj[:], func=AF.Copy),
                     reads=[pjk], writes=[rawk])
            return raw, rawk

        def e_conv(cc, raw, rawk):
            xo, xok = xor_.next()
            for cb in range(4):
                acc, acck = accr.next()
                c0 = cb * 1024
                S.op("dve", lambda: nc.vector.tensor_scalar(out=acc[:], in0=raw[:, c0:c0 + 1024], scalar1=pcol[:, PC_CW + cc:PC_CW + cc + 1],
                                                            scalar2=None, op0=ALU.mult), reads=[rawk, "pcol"], writes=[acck])
                for k in range(1, 5):
                    S.op("dve", lambda: nc.vector.scalar_tensor_tensor(out=acc[:], in0=raw[:, c0 + k:c0 + k + 1024],
                                                                       scalar=pcol[:, PC_CW + k * 32 + cc:PC_CW + k * 32 + cc + 1],
                                                                       in1=acc[:], op0=ALU.mult, op1=ALU.add),
                         reads=[rawk, "pcol", acck], writes=[acck])
                S.op("act", lambda: nc.scalar.activation(out=xo[:, c0:c0 + 1024], in_=acc[:], func=AF.Silu,
                                                         bias=pcol[:, PC_CB + cc:PC_CB + cc + 1], scale=1.0), reads=[acck, "pcol"], writes=[xok])
            if cc >= 24:
                for hh in range(2):
                    S.dma("sp", cT_d[(cc - 24) * 128:(cc - 23) * 128, hh * 2048:(hh + 1) * 2048], xo[:, hh * 2048:(hh + 1) * 2048], reads=[xok], writes=["cT_d"])
            elif cc >= 16:
                for hh in range(2):
                    S.dma("sp", bT_d[(cc - 16) * 128:(cc - 15) * 128, hh * 2048:(hh + 1) * 2048], xo[:, hh * 2048:(hh + 1) * 2048], reads=[xok], writes=["bT_d"])
            if cc < 24:
                pend_tok.append((cc, xo, xok))

        prev_raw = None
        for cc in range(32):
            cur = e_proj(cc)
            if prev_raw is not None:
                e_conv(cc - 1, *prev_raw)
            prev_raw = cur
            while len(pend_tok) > 2 or (pend_tok and cc >= 24):
                to_tok(*pend_tok.pop(0))
        e_conv(31, *prev_raw)
        while pend_tok:
            to_tok(*pend_tok.pop(0))
        S.barrier()
    H.close()

    with contextlib.ExitStack() as G:
        arow = sb(G, "arow", [128, 64], F32)
        S.dma("sp", arow[:], alog_d.partition_broadcast(128), writes=["arow"])
        S.op("act", lambda: nc.scalar.activation(out=arow[:], in_=arow[:], func=AF.Exp), reads=["arow"], writes=["arow"])
        S.op("dve", lambda: nc.vector.tensor_scalar(out=arow[:], in0=arow[:], scalar1=-1.0, scalar2=None, op0=ALU.mult), reads=["arow"], writes=["arow"])
        drow = sb(G, "drow", [128, 32], F32)
        S.dma("sp", drow[:], sd_d.partition_broadcast(128), writes=["drow"])
        Dident = sb(G, "Dident", [128, 32, 128], BF16)
        for h in range(32):
            S.op("dve", lambda: nc.vector.tensor_scalar(out=Dident[:, h, :], in0=K["ident"][:], scalar1=drow[:, h:h + 1], scalar2=None, op0=ALU.mult),
                 reads=["k_ident", "drow"], writes=["Dident"])
        fnw_b = sb(G, "fnw_b", [128, D], F32)
        S.dma("sp", fnw_b[:], fnw_d.partition_broadcast(128), writes=["fnw_b"])
        ones128 = sb(G, "ones128", [128, 128], F32)
        S.op("dve", lambda: nc.vector.memset(ones128[:], 1.0), writes=["ones128"])
        ones1 = sb(G, "ones1", [2, 128], BF16)
        S.op("dve", lambda: nc.vector.memset(ones1[:], 1.0), writes=["ones1"])
        wout1 = sb(G, "wout1", [128, 16, D], BF16)
        with contextlib.ExitStack() as G0:
            wstg = [sb(G0, f"wstgG{i}", [128, 1024], F32) for i in range(2)]
            wstgr = Rot(wstg, "wstgG")
            for j in range(16):
                stg, stgk = wstgr.next()
                S.dma("sp", stg[:], swout_d[j * 128:(j + 1) * 128, :], writes=[stgk])
                S.op("dve", lambda: nc.vector.tensor_scalar(out=wout1[:, j, :], in0=stg[:], scalar1=pcol[:, PC_SNW + j:PC_SNW + j + 1], scalar2=None, op0=ALU.mult),
                     reads=[stgk, "pcol"], writes=["wout1"])
            S.barrier()
        carry = sb(G, "carry", [128, SSD_INNER], F32)
        xts_ = [sb(G, f"xc{i}", [128, SSD_INNER], BF16) for i in range(2)]
        xcr = Rot(xts_, "xc")
        bts_ = [sb(G, f"bc{i}", [128, 1024], BF16) for i in range(2)]
        bcr = Rot(bts_, "bc")
        dtcs = [sb(G, f"dtc{i}", [128, 64], F32) for i in range(4)]
        dtcr = Rot(dtcs, "dtc")
        das = [sb(G, f"da{i}", [128, 64], F32) for i in range(2)]
        dar = Rot(das, "da")
        scs = [sb(G, f"sc{i}", [128, 128], F32) for i in range(2)]
        scr_ = Rot(scs, "sc")
        smalls = [sb(G, f"sm{i}", [128, 6, 64], F32) for i in range(3)]
        smr = Rot(smalls, "sm")
        xws = [sb(G, f"xw{i}", [128, SSD_INNER], BF16) for i in range(1)]
        xwr = Rot(xws, "xw")
        cbfs = [sb(G, f"cbf{i}", [128, SSD_INNER], BF16) for i in range(1)]
        cbfr = Rot(cbfs, "cbf")

        def load_dt(c):
            dtc, dtck = dtcr.next()
            S.dma("sp", dtc[:], dt_d[c], reads=["dt_d"], writes=[dtck])
            return dtc, dtck

        def chunk_scalars(c, need_b_only, pre=None):
            dtc, dtck = pre if pre is not None else load_dt(c)
            da, dak = dar.next()
            S.op("dve", lambda: nc.vector.tensor_tensor(out=da[:], in0=dtc[:], in1=arow[:], op=ALU.mult), reads=[dtck, "arow"], writes=[dak])
            p0, p0k = psr.next()
            S.op("pe", lambda: nc.tensor.matmul(p0[:, 0:32], lhsT=K["triL"][:], rhs=da[:, 0:32], start=True, stop=True), reads=["k_triL", dak], writes=[p0k])
            S.op("pe", lambda: nc.tensor.matmul(p0[:, 32:64], lhsT=K["triU"][:], rhs=da[:, 32:64], start=True, stop=True), reads=["k_triU", dak], writes=[p0k])
            S.op("pe", lambda: nc.tensor.matmul(p0[:, 64:128], lhsT=ones128[:], rhs=da[:, 0:64], start=True, stop=True), reads=["ones128", dak], writes=[p0k])
            if not need_b_only:
                S.op("pe", lambda: nc.tensor.matmul(p0[0:32, 128:256], lhsT=da[:, 0:32], rhs=K["triL"][:], start=True, stop=True), reads=["k_triL", dak], writes=[p0k])
                S.op("pe", lambda: nc.tensor.matmul(p0[32:64, 128:256], lhsT=da[:, 32:64], rhs=K["triU"][:], start=True, stop=True), reads=["k_triU", dak], writes=[p0k])
            sc, sck = scr_.next()
            S.op("act", lambda: nc.scalar.activation(out=sc[:], in_=p0[:, 0:128], func=AF.Copy), reads=[p0k], writes=[sck])
            sm, smk = smr.next()
            S.op("dve", lambda: nc.vector.tensor_scalar(out=sm[:, 0, :], in0=sc[:, 0:64], scalar1=-1.0, scalar2=None, op0=ALU.mult), reads=[sck], writes=[smk])
            S.op("act", lambda: nc.scalar.activation(out=sm[:, 1, :], in_=sc[:, 0:64], func=AF.Exp), reads=[sck], writes=[smk])
            S.op("dve", lambda: nc.vector.tensor_tensor(out=sm[:, 4, :], in0=sc[:, 64:128], in1=sc[:, 0:64], op=ALU.subtract), reads=[sck], writes=[smk])
            S.op("act", lambda: nc.scalar.activation(out=sm[:, 5, :], in_=sm[:, 4, :], func=AF.Exp), reads=[smk], writes=[smk])
            S.op("dve", lambda: nc.vector.tensor_tensor(out=sm[:, 2, :], in0=sm[:, 5, :], in1=dtc[:], op=ALU.mult), reads=[smk, dtck], writes=[smk])
            S.op("act", lambda: nc.scalar.activation(out=sm[:, 3, :], in_=sc[:, 64:128], func=AF.Exp), reads=[sck], writes=[smk])
            return sm, smk, p0, p0k, dtc, dtck

        def state_update(c, xc, xck, bc, bck, sm, smk, d):
            xw, xwk = xwr.next()
            S.op("dve", lambda: nc.vector.tensor_tensor(out=xw[:].rearrange("p (h q) -> p h q", q=64), in0=xc[:].rearrange("p (h q) -> p h q", q=64),
                                                        in1=sm[:, 2, d * 32:(d + 1) * 32].unsqueeze(2).to_broadcast([128, 32, 64]), op=ALU.mult),
                 reads=[xck, smk], writes=[xwk])
            S.op("dve", lambda: nc.vector.tensor_tensor(out=carry[:].rearrange("p (h q) -> p h q", q=64), in0=carry[:].rearrange("p (h q) -> p h q", q=64),
                                                        in1=sm[:, 3, d * 32:(d + 1) * 32].unsqueeze(2).to_broadcast([128, 32, 64]), op=ALU.mult),
                 reads=["carry", smk], writes=["carry"])
            for q4 in range(4):
                pS, pSk = psr.next()
                for g2 in range(2):
                    g = q4 * 2 + g2
                    S.op("pe", lambda: nc.tensor.matmul(pS[:, g2 * 256:(g2 + 1) * 256], lhsT=bc[:, g * 128:(g + 1) * 128], rhs=xw[:, g * 256:(g + 1) * 256],
                                                        start=True, stop=True), reads=[bck, xwk], writes=[pSk])
                S.op("dve", lambda: nc.vector.tensor_tensor(out=carry[:, q4 * 512:(q4 + 1) * 512], in0=pS[:], in1=carry[:, q4 * 512:(q4 + 1) * 512], op=ALU.add),
                     reads=[pSk, "carry"], writes=["carry"])

        S.op("dve", lambda: nc.vector.memset(carry[:], 0.0), writes=["carry"])
        def f_prep(c):
            xc, xck = xcr.next()
            S.dma("sp", xc[:], xs_d[c * 128:(c + 1) * 128, :], reads=["xs_d"], writes=[xck])
            bc, bck = bcr.next()
            S.dma("sp", bc[:], bt_d[c * 128:(c + 1) * 128, :], reads=["bt_d"], writes=[bck])
            sm, smk, _, _, _, _ = chunk_scalars(c, True)
            return xc, xck, bc, bck, sm, smk

        fp_ = f_prep(NT - 1)
        for c in range(NT - 1, -1, -1):
            nfp = f_prep(c - 1) if c > 0 else None
            cbf, cbfk = cbfr.next()
            S.op("act", lambda: nc.scalar.activation(out=cbf[:], in_=carry[:], func=AF.Copy), reads=["carry"], writes=[cbfk])
            S.dma("sp", pb_d[c], cbf[:], reads=[cbfk], writes=["pb_d"])
            state_update(c, *fp_, 1)
            fp_ = nfp

        if stop_after == "F":
            S.barrier()
            S.finish()
            return nc, dbg

        S.op("dve", lambda: nc.vector.memset(carry[:], 0.0), writes=["carry"])
        BTs = [sb(G, f"BT{i}", [128, 8, 128], BF16) for i in range(1)]
        BTr = Rot(BTs, "BT")
        CTs = [sb(G, f"CT{i}", [128, 8, 128], BF16) for i in range(2)]
        CTr = Rot(CTs, "CT")
        szs = [sb(G, f"szc{i}", [128, SSD_INNER], BF16) for i in range(1)]
        szr = Rot(szs, "szc")
        pbs = [sb(G, f"pbc{i}", [128, SSD_INNER], BF16) for i in range(1)]
        pbr = Rot(pbs, "pbc")
        hls = [sb(G, f"hl{i}", [64, 2, 128], BF16) for i in range(3)]
        hlr = Rot(hls, "hl")
        rows = [sb(G, f"row{i}", [2, 8 * 128], BF16) for i in range(2)]
        rowr = Rot(rows, "row")
        Lts = [sb(G, f"Lt{i}", [128, 32, 128], BF16) for i in range(4)]
        Ltr = Rot(Lts, "Lt")
        CBs = [sb(G, f"CBs{i}", [128, 8, 128], BF16) for i in range(1)]
        CBr = Rot(CBs, "CBs")
        xdts = [sb(G, f"xdt{i}", [128, SSD_INNER], BF16) for i in range(2)]
        ycs = [sb(G, f"yc{i}", [128, 512], F32) for i in range(1)]
        ycr = Rot(ycs, "yc")
        yts = [sb(G, f"ytmp{i}", [128, 512], F32) for i in range(1)]
        ytr = Rot(yts, "ytmp")
        yg = sb(G, "yg", [128, SSD_INNER], F32)
        ynb = sb(G, "ynb", [128, SSD_INNER], BF16)
        ynT = sb(G, "ynT", [128, 16, 128], BF16)
        nst2 = [sb(G, f"nst2{i}", [128, 8], F32) for i in range(2)]
        nst2r = Rot(nst2, "nst2")
        x2s = [sb(G, f"x2{i}", [128, D], F32) for i in range(2)]
        x2r = Rot(x2s, "x2")
        outs = [sb(G, f"ot{i}", [128, D], F32) for i in range(2)]
        outr = Rot(outs, "ot")
        bT_v = bT_d.rearrange("(g p) t -> p g t", p=128)
        cT_v = cT_d.rearrange("(g p) t -> p g t", p=128)
        ident4 = K["ident"][:].unsqueeze(1).to_broadcast([128, 4, 128])
        def g_loads(c):
            tsl = slice(c * 128, (c + 1) * 128)
            xc, xck = xcr.next()
            S.dma("sp", xc[:], xs_d[tsl, :], reads=["xs_d"], writes=[xck])
            bc, bck = bcr.next()
            S.dma("sp", bc[:], bt_d[tsl, :], reads=["bt_d"], writes=[bck])
            BT, BTk = BTr.next()
            S.dma("sp", BT[:], bT_v[:, :, tsl], reads=["bT_d"], writes=[BTk])
            CT, CTk = CTr.next()
            S.dma("sp", CT[:], cT_v[:, :, tsl], reads=["cT_d"], writes=[CTk])
            dt_ = load_dt(c)
            return (tsl, xc, xck, bc, bck, BT, BTk, CT, CTk, dt_)

        def g_scal(c, dt_):
            sm, smk, p0, p0k, dtc, dtck = chunk_scalars(c, False, dt_)
            hl, hlk = hlr.next()
            S.op("act", lambda: nc.scalar.activation(out=hl[:, 0, :], in_=p0[0:64, 128:256], func=AF.Copy), reads=[p0k], writes=[hlk])
            S.op("dve", lambda: nc.vector.tensor_tensor(out=hl[:, 1, :], in0=p0[0:64, 128:256], in1=hl[:, 0, :], op=ALU.subtract), reads=[p0k, hlk], writes=[hlk])
            return (sm, smk, dtc, dtck, hl, hlk)

        def g_head(c, L, SC):
            tsl, xc, xck, bc, bck, BT, BTk, CT, CTk, dt_ = L
            sm, smk, dtc, dtck, hl, hlk = SC
            CB, CBk = CBr.next()
            for gh in range(2):
                pcb, pcbk = psr.next()
                for g4 in range(4):
                    g = gh * 4 + g4
                    S.op("pe", lambda: nc.tensor.matmul(pcb[:, g4 * 128:(g4 + 1) * 128], lhsT=BT[:, g, :], rhs=CT[:, g, :], start=True, stop=True),
                         reads=[BTk, CTk], writes=[pcbk])
                S.op("act", lambda: nc.scalar.activation(out=CB[:, gh * 4:(gh + 1) * 4, :].rearrange("p g l -> p (g l)"), in_=pcb[:], func=AF.Copy),
                     reads=[pcbk], writes=[CBk])
            Ms = []
            tasks = []
            mtasks = []
            rowbox = {}
            for d in range(2):
                Lt, Ltk = Ltr.next()
                Ms.append((Lt, Ltk))
                for r16 in range(4):
                    for hb4 in range(2):
                        def task(d=d, r16=r16, hb4=hb4, Lt=Lt, Ltk=Ltk):
                            nm = K["nmaskf"] if d == 0 else K["nmaskb"]
                            nmk = "k_nmaskf" if d == 0 else "k_nmaskb"
                            if hb4 == 0:
                                row, rowk = rowr.next()
                                p0_ = d * 32 + r16 * 8
                                for j_ in range(2):
                                    S.dma("sp", row[j_:j_ + 1, :].rearrange("o (p f) -> o p f", p=8), hl[p0_:p0_ + 8, j_, :],
                                          reads=[hlk], writes=[rowk])
                                rowbox[(d, r16)] = (row, rowk)
                            row, rowk = rowbox[(d, r16)]
                            hb = r16 * 2 + hb4
                            lp, lpk = psr.next()
                            S.op("pe", lambda: nc.tensor.matmul(lp[:], lhsT=ones1[:], rhs=row[:, hb4 * 512:(hb4 + 1) * 512], start=True, stop=False),
                                 reads=["ones1", rowk], writes=[lpk])
                            S.op("pe", lambda: nc.tensor.matmul(lp[:], lhsT=nm[:], rhs=ident4, start=False, stop=True), reads=[nmk, "k_ident"], writes=[lpk])
                            for j4 in range(4):
                                h = hb * 4 + j4
                                S.op("act", lambda: nc.scalar.activation(out=Lt[:, h, :], in_=lp[:, j4 * 128:(j4 + 1) * 128], func=AF.Exp,
                                                                         bias=sm[:, 0, d * 32 + h:d * 32 + h + 1], scale=1.0), reads=[lpk, smk], writes=[Ltk])
                        tasks.append(task)

                        def mtask(d=d, r16=r16, hb4=hb4, Lt=Lt, Ltk=Ltk):
                            hb = r16 * 2 + hb4
                            S.op("dve", lambda: nc.vector.tensor_tensor(out=Lt[:, hb * 4:(hb + 1) * 4, :], in0=Lt[:, hb * 4:(hb + 1) * 4, :],
                                                                        in1=CB[:, hb:hb + 1, :].to_broadcast([128, 4, 128]), op=ALU.mult),
                                 reads=[Ltk, CBk], writes=[Ltk])
                        mtasks.append(mtask)

            def fin():
                for d in range(2):
                    S.op("dve", lambda: nc.vector.tensor_tensor(out=xdts[d][:].rearrange("p (h q) -> p h q", q=64), in0=xc[:].rearrange("p (h q) -> p h q", q=64),
                                                                in1=dtc[:, d * 32:(d + 1) * 32].unsqueeze(2).to_broadcast([128, 32, 64]), op=ALU.mult),
                         reads=[xck, dtck], writes=[f"xdt{d}"])

            def late_loads():
                szc, szk = szr.next()
                S.dma("sp", szc[:], sz_d[tsl, :], reads=["sz_d"], writes=[szk])
                pbc, pbk = pbr.next()
                S.dma("sp", pbc[:], pb_d[c], reads=["pb_d"], writes=[pbk])
                x2, x2k = x2r.next()
                S.dma("sp", x2[:], x1_d[tsl, :], reads=["x1_d"], writes=[x2k])
                X.update(szc=szc, szk=szk, pbc=pbc, pbk=pbk, x2=x2, x2k=x2k)

            X = dict(c=c, tsl=tsl, xc=xc, xck=xck, bc=bc, bck=bck, CT=CT, CTk=CTk,
                     sm=sm, smk=smk, Ms=Ms, tasks=tasks, mtasks=mtasks, fin=fin, late_loads=late_loads)
            return X

        def g_mid(X, run, pre_update):
            c, tsl, xc, xck, bc, bck, CT, CTk = X["c"], X["tsl"], X["xc"], X["xck"], X["bc"], X["bck"], X["CT"], X["CTk"]
            szc, szk, pbc, pbk, sm, smk, Ms = X["szc"], X["szk"], X["pbc"], X["pbk"], X["sm"], X["smk"], X["Ms"]
            cbf, cbfk = X["cbf"], X["cbfk"]
            for q4 in range(4):
                csl = slice(q4 * 512, (q4 + 1) * 512)
                pyd, pydk = psr.next()
                for h8 in range(8):
                    h = q4 * 8 + h8
                    for d in range(2):
                        S.op("pe", lambda: nc.tensor.matmul(pyd[:, h8 * 64:(h8 + 1) * 64], lhsT=Ms[d][0][:, h, :], rhs=xdts[d][:, h * 64:(h + 1) * 64],
                                                            start=(d == 0), stop=False), reads=[Ms[d][1], f"xdt{d}"], writes=[pydk])
                    S.op("pe", lambda: nc.tensor.matmul(pyd[:, h8 * 64:(h8 + 1) * 64], lhsT=Dident[:, h, :], rhs=xc[:, h * 64:(h + 1) * 64],
                                                        start=False, stop=True), reads=["Dident", xck], writes=[pydk])
                pof, pofk = psr.next()
                pob, pobk = psr.next()
                for g2 in range(2):
                    g = q4 * 2 + g2
                    S.op("pe", lambda: nc.tensor.matmul(pof[:, g2 * 256:(g2 + 1) * 256], lhsT=CT[:, g, :], rhs=cbf[:, g * 256:(g + 1) * 256], start=True, stop=True),
                         reads=[CTk, cbfk], writes=[pofk])
                    S.op("pe", lambda: nc.tensor.matmul(pob[:, g2 * 256:(g2 + 1) * 256], lhsT=CT[:, g, :], rhs=pbc[:, g * 256:(g + 1) * 256], start=True, stop=True),
                         reads=[CTk, pbk], writes=[pobk])
                yc, yck = ycr.next()
                yt_, ytk = ytr.next()
                v3 = lambda t_: t_[:].rearrange("p (h q) -> p h q", q=64)
                ef = sm[:, 1, q4 * 8:q4 * 8 + 8].unsqueeze(2).to_broadcast([128, 8, 64])
                eb = sm[:, 1, 32 + q4 * 8:32 + q4 * 8 + 8].unsqueeze(2).to_broadcast([128, 8, 64])
                S.op("dve", lambda: nc.vector.tensor_tensor(out=v3(yc), in0=pof[:].rearrange("p (h q) -> p h q", q=64), in1=ef, op=ALU.mult), reads=[pofk, smk], writes=[yck])
                S.op("dve", lambda: nc.vector.tensor_tensor(out=v3(yt_), in0=pob[:].rearrange("p (h q) -> p h q", q=64), in1=eb, op=ALU.mult), reads=[pobk, smk], writes=[ytk])
                S.op("dve", lambda: nc.vector.tensor_tensor(out=yc[:], in0=yc[:], in1=yt_[:], op=ALU.add), reads=[yck, ytk], writes=[yck])
                S.op("dve", lambda: nc.vector.tensor_tensor(out=yc[:], in0=pyd[:], in1=yc[:], op=ALU.add), reads=[pydk, yck], writes=[yck])
                S.op("dve", lambda: nc.vector.tensor_tensor(out=yg[:, csl], in0=yc[:], in1=szc[:, csl], op=ALU.mult), reads=[yck, szk], writes=["yg"])
                run(4)
            pre_update()
            state_update(c, xc, xck, bc, bck, sm, smk, 0)

        def g_tail_a(X):
            ns, nsk = nst2r.next()
            S.op("act", lambda: nc.scalar.activation(out=ynb[:], in_=yg[:], func=AF.Square, accum_out=ns[:, 0:1]), reads=["yg"], writes=["ynb", nsk])
            S.op("act", lambda: nc.scalar.activation(out=ns[:, 1:2], in_=ns[:, 0:1], func=AF.Ln, bias=epsc[:, 0:1], scale=1.0 / SSD_INNER), reads=[nsk, "epsc"], writes=[nsk])
            S.op("act", lambda: nc.scalar.activation(out=ns[:, 2:3], in_=ns[:, 1:2], func=AF.Exp, scale=-0.5), reads=[nsk], writes=[nsk])
            S.op("dve", lambda: nc.vector.tensor_scalar(out=ynb[:], in0=yg[:], scalar1=ns[:, 2:3], scalar2=None, op0=ALU.mult),
                 reads=["yg", nsk], writes=["ynb"])
            for jh in range(2):
                tp, tpk = ptr.next()
                for jj in range(8):
                    j = jh * 8 + jj
                    S.op("pe", lambda: nc.tensor.transpose(tp[:, jj, :], ynb[:, j * 128:(j + 1) * 128], K["ident"][:]), reads=["ynb", "k_ident"], writes=[tpk])
                S.op("act", lambda: nc.scalar.activation(out=ynT[:, jh * 8:(jh + 1) * 8, :], in_=tp[:], func=AF.Copy), reads=[tpk], writes=["ynT"])

        def g_tail_b(X):
            c, tsl, x2, x2k = X["c"], X["tsl"], X["x2"], X["x2k"]
            y2, y2k = yg, "yg"
            for n in range(2):
                po, pok = psr.next()
                for j in range(16):
                    S.op("pe", lambda: nc.tensor.matmul(po[:], lhsT=ynT[:, j, :], rhs=wout1[:, j, n * 512:(n + 1) * 512], start=(j == 0), stop=(j == 15)),
                         reads=["ynT", "wout1"], writes=[pok])
                S.op("dve", lambda: nc.vector.tensor_tensor(out=y2[:, n * 512:(n + 1) * 512], in0=po[:], in1=gate_b[1][:, n * 512:(n + 1) * 512], op=ALU.mult),
                     reads=[pok, "gate_b1"], writes=[y2k])
            S.op("dve", lambda: nc.vector.tensor_tensor(out=x2[:], in0=y2[:, 0:D], in1=x2[:], op=ALU.add), reads=[y2k, x2k], writes=[x2k])
            ns, nsk = nst2r.next()
            S.op("act", lambda: nc.scalar.activation(out=nrm_junk[:], in_=x2[:], func=AF.Square, accum_out=ns[:, 0:1]), reads=[x2k], writes=["nrm_junk", nsk])
            S.op("act", lambda: nc.scalar.activation(out=ns[:, 1:2], in_=ns[:, 0:1], func=AF.Ln, bias=epsc[:, 0:1], scale=1.0 / D), reads=[nsk, "epsc"], writes=[nsk])
            S.op("act", lambda: nc.scalar.activation(out=ns[:, 2:3], in_=ns[:, 1:2], func=AF.Exp, scale=-0.5), reads=[nsk], writes=[nsk])
            ot, otk = outr.next()
            S.op("dve", lambda: nc.vector.scalar_tensor_tensor(out=ot[:], in0=x2[:], scalar=ns[:, 2:3], in1=fnw_b[:], op0=ALU.mult, op1=ALU.mult),
                 reads=[x2k, nsk, "fnw_b"], writes=[otk])
            X["store"] = lambda: S.dma("sp", out_d[tsl, :], ot[:], reads=[otk], writes=["out_d"])

        def make_runner(X):
            tl = list(X["tasks"]) if X is not None else []
            ml = list(X["mtasks"]) if X is not None else []
            done = [0]

            def run(n):
                for _ in range(n):
                    if tl:
                        tl.pop(0)()
                        done[0] += 1
                        if done[0] > 2 and ml:
                            ml.pop(0)()

            def flush():
                run(len(tl))
                while ml:
                    ml.pop(0)()
            return run, flush

        Ld = {0: g_loads(0)}
        Sc = {0: g_scal(0, Ld[0][-1])}
        ctx = g_head(0, Ld.pop(0), Sc.pop(0))
        run, flush = make_runner(ctx)
        flush()
        ctx["fin"]()
        ctx["late_loads"]()
        if NT > 1:
            Ld[1] = g_loads(1)
            Sc[1] = g_scal(1, Ld[1][-1])
        pend_store = None

        def emit_cbf(X):
            cbf, cbfk = cbfr.next()
            S.op("act", lambda: nc.scalar.activation(out=cbf[:], in_=carry[:], func=AF.Copy), reads=["carry"], writes=[cbfk])
            X["cbf"], X["cbfk"] = cbf, cbfk

        emit_cbf(ctx)
        for c in range(NT):
            nxt = g_head(c + 1, Ld.pop(c + 1), Sc.pop(c + 1)) if c + 1 < NT else None
            run, flush = make_runner(nxt)
            g_mid(ctx, run, lambda: g_tail_a(ctx))
            if nxt is not None:
                emit_cbf(nxt)
            flush()
            if c + 2 < NT:
                Ld[c + 2] = g_loads(c + 2)
                Sc[c + 2] = g_scal(c + 2, Ld[c + 2][-1])
            if nxt is not None:
                nxt["late_loads"]()
            if pend_store is not None:
                pend_store()
            if nxt is not None:
                nxt["fin"]()
            g_tail_b(ctx)
            pend_store = ctx["store"]
            ctx = nxt
        pend_store()
        S.barrier()
    S.finish()
    return nc, dbg


def make_in_maps(inputs):
    cst = host_consts()
    pcol = host_pcol(inputs)
    f = lambda a: np.ascontiguousarray(np.asarray(a), dtype=np.float32)
    shared = {
        "mod_w": f(inputs["mod_w"]), "mod_b": f(inputs["mod_b"]),
        "attn_w_in": f(inputs["attn_w_in"][0]), "attn_w_out": f(inputs["attn_w_out"][0]),
        "ssd_w_in": f(inputs["ssd_w_in"][0]),
        "ssd_dt_bias": f(inputs["ssd_dt_bias"]).reshape(1, 64), "ssd_a_log": f(inputs["ssd_a_log"]).reshape(1, 64),
        "ssd_d": f(inputs["ssd_d"]).reshape(1, 32), "ssd_norm_w": f(inputs["ssd_norm_w"]).reshape(1, SSD_INNER),
        "ssd_w_out": f(inputs["ssd_w_out"][0]), "final_norm_w": f(inputs["final_norm_w"]).reshape(1, D),
        "pcol": pcol,
    }
    for k, v in cst.items():
        shared["k_" + k] = v
    x = f(inputs["x"])
    c = f(inputs["c"])
    pos = np.ascontiguousarray(np.asarray(inputs["positions"]), dtype=np.int32)
    maps = []
    for b in range(x.shape[0]):
        m = dict(shared)
        m["x"] = x[b]
        m["c"] = np.ascontiguousarray(c[b].reshape(8, 128).T)
        m["pos"] = pos[b:b + 1]
        maps.append(m)
    return maps


def kernel(**inputs):
    nc, _ = build()
    maps = make_in_maps(inputs)
    res = run_bass_kernel_spmd(nc, maps, core_ids=list(range(8)))
    return np.stack([np.asarray(r["out"], dtype=np.float32) for r in res.results], axis=0)
```
